# Optimizing a Trainium2 kernel written in Bass

```python
import jax, jax.numpy as jnp
from jax import lax
import numpy as np

D_MODEL = 1024
BATCH = 8
SEQ = 4096
DEPTH = 2

CTX_LEN = 256
GRID_W = 64
EPS = 1e-6
NEG_INF = -1e30

FOURIER_WIDTH = D_MODEL // 4
FOURIER_GROUPS = 4
FOURIER_GD = FOURIER_WIDTH // FOURIER_GROUPS
HGRN_WIDTH = D_MODEL // 4
HGRN_DK = 64
HGRN_HEADS = HGRN_WIDTH // HGRN_DK
HGRN_CHUNK = 32
NA_WIDTH = D_MODEL - FOURIER_WIDTH - HGRN_WIDTH
NA_HEAD_DIM = 64
NA_HEADS = NA_WIDTH // NA_HEAD_DIM
NA_KH = 8
NA_KW = 16
NA_QCB = 16
NA_KCB = NA_QCB + NA_KW
MIX_WIDTH = FOURIER_WIDTH + HGRN_WIDTH + NA_WIDTH
IN_WIDTH = FOURIER_WIDTH + 5 * HGRN_WIDTH + 3 * NA_WIDTH
D_FF = -(-8 * D_MODEL // (3 * 256)) * 256

kernel_name = "hybrid_fourier_hgrn2_natten_dit_block"


def rmsnorm(x, g):
    xf = x.astype(jnp.float32)
    y = xf * lax.rsqrt(jnp.mean(xf * xf, axis=-1, keepdims=True) + EPS)
    return (y * g.astype(jnp.float32)).astype(x.dtype)


def modulate(h, shift, scale):
    return h * (1 + scale) + shift


def swiglu(h, w_gate, w_up, w_down):
    return (jax.nn.silu(h @ w_gate) * (h @ w_up)) @ w_down


def fourier_mix(u, w_f):
    B, T, _ = u.shape
    ug = u.astype(jnp.float32).reshape(B, T, FOURIER_GROUPS, FOURIER_GD)
    spec = jnp.fft.fft2(ug, axes=(1, 3), norm="ortho").real.astype(u.dtype)
    out = jnp.einsum('btgc,gcd->btgd', spec, w_f)
    return out.reshape(B, T, FOURIER_WIDTH)


def gla_chunk_scan(q, k, v, log_f, s0):
    B, H, T, dk = q.shape
    C = HGRN_CHUNK
    n = T // C

    def blocks(a):
        return a.reshape(B, H, n, C, a.shape[-1]).transpose(2, 0, 1, 3, 4)

    qb, kb, vb, gb = blocks(q), blocks(k), blocks(v), blocks(log_f)
    a = jnp.cumsum(gb, axis=3)
    a_last = a[:, :, :, -1:, :]
    q_dec = qb * jnp.exp(a)
    k_dec = kb * jnp.exp(-a)
    k_state = kb * jnp.exp(a_last - a)
    incl = jnp.tril(jnp.ones((C, C), dtype=bool))
    scores = jnp.where(incl, jnp.einsum('nbhtd,nbhsd->nbhts', q_dec, k_dec), 0.0)
    o_intra = jnp.einsum('nbhts,nbhsv->nbhtv', scores, vb)

    def step(s, inp):
        qd, ks, vv, al = inp
        o_inter = jnp.einsum('bhtd,bhdv->bhtv', qd, s)
        s_new = jnp.exp(al[:, :, 0, :])[..., None] * s + jnp.einsum('bhsd,bhsv->bhdv', ks, vv)
        return s_new, o_inter

    s_final, o_inter = lax.scan(step, s0, (q_dec, k_state, vb, a_last))
    o = (o_intra + o_inter).transpose(1, 2, 0, 3, 4).reshape(B, H, T, vb.shape[-1])
    return o, s_final


def hgrn2_mix(u_lat, u_ctx, lb, norm_g, with_ctx):
    def heads(a):
        B, T, _ = a.shape
        return a.astype(jnp.float32).reshape(B, T, HGRN_HEADS, HGRN_DK).transpose(0, 2, 1, 3)

    def split(u):
        q, f_fw, f_bw, i, g = jnp.split(u, 5, axis=-1)
        return heads(q), heads(f_fw), heads(f_bw), heads(i), g

    def gate(z, lb_d):
        f = lb_d + (1.0 - lb_d) * jax.nn.sigmoid(z)
        return 1.0 - f, jnp.log(f)

    def rev(a):
        return jnp.flip(a, axis=2)

    q_l, ff_l, fb_l, i_l, g_l = split(u_lat)
    q_c, ff_c, fb_c, i_c, g_c = split(u_ctx)
    lb_f = lb[0].reshape(HGRN_HEADS, 1, HGRN_DK)
    lb_b = lb[1].reshape(HGRN_HEADS, 1, HGRN_DK)
    s0 = jnp.zeros((u_lat.shape[0], HGRN_HEADS, HGRN_DK, HGRN_DK), jnp.float32)

    k, lf = gate(ff_c, lb_f)
    o_cf, s_cf = gla_chunk_scan(q_c, k, i_c, lf, s0)
    k, lf = gate(ff_l, lb_f)
    o_lf, _ = gla_chunk_scan(q_l, k, i_l, lf, s_cf)
    k, lf = gate(rev(fb_c), lb_b)
    o_cb, s_cb = gla_chunk_scan(rev(q_c), k, rev(i_c), lf, s0)
    k, lf = gate(rev(fb_l), lb_b)
    o_lb, _ = gla_chunk_scan(rev(q_l), k, rev(i_l), lf, s_cb)

    def readout(o, g):
        B, H, T, dv = o.shape
        o = o * lax.rsqrt(jnp.mean(o * o, axis=-1, keepdims=True) + EPS) * norm_g.astype(jnp.float32)
        o = o.transpose(0, 2, 1, 3).reshape(B, T, H * dv)
        return (o * jax.nn.silu(g.astype(jnp.float32))).astype(g.dtype)

    out_lat = readout(o_lf + rev(o_lb), g_l)
    out_ctx = readout(o_cf + rev(o_cb), g_c) if with_ctx else None
    return out_lat, out_ctx


def na_heads(u):
    B, T, _ = u.shape
    q, k, v = jnp.split(u, 3, axis=-1)
    return (q.reshape(B, T, NA_HEADS, NA_HEAD_DIM), k.reshape(B, T, NA_HEADS, NA_HEAD_DIM),
            v.reshape(B, T, NA_HEADS, NA_HEAD_DIM))


def na_latent(q, k, v, k_c, v_c, rpb):
    B, S, H, hd = q.shape
    rows = S // GRID_W
    kh = min(NA_KH, rows)
    ncb = GRID_W // NA_QCB
    scale = hd ** -0.5
    qg = q.reshape(B, rows, GRID_W, H, hd)
    kg = k.reshape(B, rows, GRID_W, H, hd)
    vg = v.reshape(B, rows, GRID_W, H, hd)

    band_start = np.clip(np.arange(ncb) * NA_QCB - NA_KW // 2, 0, GRID_W - NA_KCB)
    key_cols = band_start[:, None] + np.arange(NA_KCB)
    q_cols = np.arange(GRID_W).reshape(ncb, NA_QCB)
    win_col = np.clip(q_cols - NA_KW // 2, 0, GRID_W - NA_KW)[..., None]
    kc = key_cols[:, None, :]
    col_valid = (kc >= win_col) & (kc < win_col + NA_KW)
    col_idx = np.clip(kc - q_cols[..., None] + NA_KW - 1, 0, 2 * NA_KW - 2)
    col_valid = col_valid[:, :, None, :]
    col_idx = col_idx[:, :, None, :]
    row_start = jnp.clip(jnp.arange(rows) - kh // 2, 0, rows - kh)
    rpb32 = rpb.astype(jnp.float32)

    def one_row(args):
        r, rs = args
        qr = lax.dynamic_index_in_dim(qg, r, axis=1, keepdims=False)
        kr = lax.dynamic_slice_in_dim(kg, rs, kh, axis=1)
        vr = lax.dynamic_slice_in_dim(vg, rs, kh, axis=1)
        kb = kr[:, :, key_cols]
        vb = vr[:, :, key_cols]
        qb = qr.reshape(B, ncb, NA_QCB, H, hd)
        s_win = jnp.einsum('bjqhd,bkjchd->bhjqkc', qb, kb).astype(jnp.float32) * scale
        row_idx = (rs + jnp.arange(kh) - r + NA_KH - 1)[None, None, :, None]
        bias = rpb32[:, row_idx, col_idx]
        s_win = jnp.where(col_valid, s_win + bias[None], NEG_INF)
        s_ctx = jnp.einsum('bjqhd,blhd->bhjql', qb, k_c).astype(jnp.float32) * scale
        s_all = jnp.concatenate([s_win.reshape(B, H, ncb, NA_QCB, kh * NA_KCB), s_ctx], axis=-1)
        p = jax.nn.softmax(s_all, axis=-1).astype(v.dtype)
        p_win = p[..., :kh * NA_KCB].reshape(B, H, ncb, NA_QCB, kh, NA_KCB)
        p_ctx = p[..., kh * NA_KCB:]
        o = (jnp.einsum('bhjqkc,bkjchd->bjqhd', p_win, vb)
             + jnp.einsum('bhjql,blhd->bjqhd', p_ctx, v_c))
        return o.reshape(B, GRID_W, H * hd)

    out = lax.map(one_row, (jnp.arange(rows), row_start))
    return out.transpose(1, 0, 2, 3).reshape(B, S, H * hd)


def na_context(q_c, k_c, v_c):
    B, L, H, hd = q_c.shape
    s = jnp.einsum('blhd,bmhd->bhlm', q_c, k_c).astype(jnp.float32) * hd ** -0.5
    p = jax.nn.softmax(s, axis=-1).astype(v_c.dtype)
    return jnp.einsum('bhlm,bmhd->blhd', p, v_c).reshape(B, L, H * hd)


def mix_heads(a_lat, a_ctx, w_in, fourier_w, lb, hgrn_norm_g, rpb, w_out, with_ctx):
    u_lat = a_lat @ w_in
    u_ctx = a_ctx @ w_in
    f_end = FOURIER_WIDTH
    h_end = f_end + 5 * HGRN_WIDTH
    hg_lat, hg_ctx = hgrn2_mix(u_lat[..., f_end:h_end], u_ctx[..., f_end:h_end], lb, hgrn_norm_g, with_ctx)
    q_l, k_l, v_l = na_heads(u_lat[..., h_end:])
    q_c, k_c, v_c = na_heads(u_ctx[..., h_end:])
    o_lat = jnp.concatenate([fourier_mix(u_lat[..., :f_end], fourier_w), hg_lat,
                             na_latent(q_l, k_l, v_l, k_c, v_c, rpb)], axis=-1) @ w_out
    if not with_ctx:
        return o_lat, None
    o_ctx = jnp.concatenate([fourier_mix(u_ctx[..., :f_end], fourier_w), hg_ctx,
                             na_context(q_c, k_c, v_c)], axis=-1) @ w_out
    return o_lat, o_ctx


def setup_inputs(seed: int = 0) -> dict:
    key = jax.random.key(seed)
    ks = jax.random.split(key, 18)
    D = D_MODEL

    def nrm(k, shape, s):
        return jax.random.normal(k, shape, jnp.float32) * s

    return {
        "x": nrm(ks[0], (BATCH, SEQ, D), 1.0),
        "c": nrm(ks[1], (BATCH, D), 1.0),
        "ctx": nrm(ks[2], (BATCH, CTX_LEN, D), 1.0),
        "c_ctx": nrm(ks[3], (D,), 1.0),
        "w_mod": nrm(ks[4], (DEPTH, D, 6 * D), 0.5 * D ** -0.5),
        "b_mod": nrm(ks[5], (DEPTH, 6 * D), 0.02),
        "norm1_g": 1.0 + nrm(ks[6], (DEPTH, D), 0.02),
        "w_in": nrm(ks[7], (DEPTH, D, IN_WIDTH), D ** -0.5),
        "fourier_w": nrm(ks[8], (DEPTH, FOURIER_GROUPS, FOURIER_GD, FOURIER_GD), FOURIER_GD ** -0.5),
        "hgrn_lb": nrm(ks[9], (DEPTH, 2, HGRN_WIDTH), 0.5),
        "hgrn_norm_g": 1.0 + nrm(ks[10], (DEPTH, HGRN_DK), 0.02),
        "na_rpb": nrm(ks[11], (DEPTH, NA_HEADS, 2 * NA_KH - 1, 2 * NA_KW - 1), 0.1),
        "w_out": nrm(ks[12], (DEPTH, MIX_WIDTH, D), MIX_WIDTH ** -0.5),
        "norm2_g": 1.0 + nrm(ks[13], (DEPTH, D), 0.02),
        "w_ffn_gate": nrm(ks[14], (DEPTH, D, D_FF), D ** -0.5),
        "w_ffn_up": nrm(ks[15], (DEPTH, D, D_FF), D ** -0.5),
        "w_ffn_down": nrm(ks[16], (DEPTH, D_FF, D), D_FF ** -0.5),
        "final_norm_g": 1.0 + nrm(ks[17], (D,), 0.02),
    }


def reference(x, c, ctx, c_ctx, w_mod, b_mod, norm1_g, w_in, fourier_w, hgrn_lb, hgrn_norm_g,
              na_rpb, w_out, norm2_g, w_ffn_gate, w_ffn_up, w_ffn_down, final_norm_g):
    lbp = jax.nn.softmax(hgrn_lb.astype(jnp.float32), axis=0)
    lower_bounds = jnp.cumsum(lbp, axis=0) - lbp[0:1]
    h_ctx = ctx
    for l in range(DEPTH):
        with_ctx = l < DEPTH - 1
        mod = jax.nn.silu(c) @ w_mod[l] + b_mod[l]
        mod_c = jax.nn.silu(c_ctx) @ w_mod[l] + b_mod[l]
        sh1, sc1, g1, sh2, sc2, g2 = jnp.split(mod[:, None, :], 6, axis=-1)
        csh1, csc1, cg1, csh2, csc2, cg2 = jnp.split(mod_c, 6, axis=-1)

        a_lat = modulate(rmsnorm(x, norm1_g[l]), sh1, sc1)
        a_ctx = modulate(rmsnorm(h_ctx, norm1_g[l]), csh1, csc1)
        y_lat, y_ctx = mix_heads(a_lat, a_ctx, w_in[l], fourier_w[l], lower_bounds[l],
                                 hgrn_norm_g[l], na_rpb[l], w_out[l], with_ctx)
        x = x + g1 * y_lat
        x = x + g2 * swiglu(modulate(rmsnorm(x, norm2_g[l]), sh2, sc2),
                            w_ffn_gate[l], w_ffn_up[l], w_ffn_down[l])
        if with_ctx:
            h_ctx = h_ctx + cg1 * y_ctx
            h_ctx = h_ctx + cg2 * swiglu(modulate(rmsnorm(h_ctx, norm2_g[l]), csh2, csc2),
                                         w_ffn_gate[l], w_ffn_up[l], w_ffn_down[l])
    return rmsnorm(x, final_norm_g)
```

```python
import numpy as np
import ml_dtypes
from contextlib import ExitStack
import concourse.bass as bass
import concourse.mybir as mybir
from concourse.bass_utils import run_bass_kernel_spmd

F32 = mybir.dt.float32
BF16 = mybir.dt.bfloat16
AF = mybir.ActivationFunctionType
ALU = mybir.AluOpType

D = 1024
S = 4096
L = 256
T = S + L
NB = T // 128
DFF = 2816
NJ = DFF // 128
INW = 3072
EPS = 1e-6
NEG = -1e30
DEPTH = 2


class _Op:
    __slots__ = ("eng", "fn", "args", "kw", "deps", "signal", "sem", "val", "dma", "epoch")


class Buf:
    __slots__ = ("w", "r")

    def __init__(self):
        self.w = {}
        self.r = {}


class Sched:
    CENG = ("pe", "act", "dve", "pool")
    SEMCAP = 30000

    def __init__(self, nc, es):
        self.nc = nc
        self.eobj = dict(pe=nc.tensor, act=nc.scalar, dve=nc.vector, pool=nc.gpsimd, sp=nc.sync)
        self.pending = []
        self.lastc = {}
        self.dma_since = []
        self.bar_deps = {}
        ncs = dict(pe=3, act=3, dve=4, pool=2)
        self.csem = {e: [es.enter_context(nc.semaphore(f"c{e}{i}")) for i in range(ncs[e])] for e in self.CENG}
        self.ccount = {e: 0 for e in self.CENG}
        self.dsem = {q: [es.enter_context(nc.semaphore(f"d{q}{i}")) for i in range(8)] for q in ("sp", "pool")}
        self.dcount = {q: [0] * 8 for q in ("sp", "pool")}
        self.drr = {q: 0 for q in ("sp", "pool")}
        self.waited = {e: {} for e in self.eobj}
        self.nops = 0
        self.nwaits = 0
        self.epoch = 0
        self.pe_mode = "f"

    def op(self, eng, fn, *args, reads=(), writes=(), dma=False, force=False, **kw):
        o = _Op()
        o.eng, o.fn, o.args, o.kw, o.dma = eng, fn, args, kw, dma
        o.signal = dma or force
        o.sem = None
        o.val = 0
        o.epoch = self.epoch
        deps = {}
        if eng == "pe":
            mode = kw.pop("pemode", "f")
            if mode != self.pe_mode and "pe" in self.lastc:
                deps[id(self.lastc["pe"])] = self.lastc["pe"]
            self.pe_mode = mode

        def need(d, same_ok, bar=False):
            if d.epoch < self.epoch and not bar:
                return
            if same_ok and (not d.dma) and (not dma) and d.eng == eng:
                return
            deps[id(d)] = d

        for b in reads:
            for d in b.w.values():
                need(d, eng == "pe")
        for b in writes:
            if b.r:
                for d in b.r.values():
                    need(d, True)
                for d in b.w.values():
                    need(d, True)
                b.r = {}
                b.w = {}
        for d in self.bar_deps.pop(eng, ()):
            need(d, True, True)
        for d in deps.values():
            d.signal = True
        o.deps = list(deps.values())
        key = ("d", id(o)) if dma else eng
        for b in reads:
            b.r[key] = o
        for b in writes:
            b.w[key] = o
        if dma:
            self.dma_since.append(o)
        else:
            self.lastc[eng] = o
        self.pending.append(o)
        return o

    def dma(self, q, out, in_, reads=(), writes=()):
        fn = self.nc.sync.dma_start if q == "sp" else self.nc.gpsimd.dma_start
        return self.op(q, fn, reads=reads, writes=writes, dma=True, out=out, in_=in_)

    def flush(self):
        for o in self.pending:
            e = self.eobj[o.eng]
            w = self.waited[o.eng]
            for d in o.deps:
                assert d.sem is not None, "dependency on unsignalled op"
                k = id(d.sem)
                if w.get(k, 0) >= d.val:
                    continue
                e.wait_ge(d.sem, d.val)
                w[k] = d.val
                self.nwaits += 1
            ins = o.fn(*o.args, **o.kw)
            self.nops += 1
            if o.dma:
                q = o.eng
                i = self.drr[q]
                self.drr[q] = (i + 1) % 8
                self.dcount[q][i] += 16
                o.sem, o.val = self.dsem[q][i], self.dcount[q][i]
                ins.then_inc(o.sem, 16)
            elif o.signal:
                n = self.ccount[o.eng]
                self.ccount[o.eng] = n + 1
                sems = self.csem[o.eng]
                si = n // self.SEMCAP
                assert si < len(sems), "out of compute semaphores"
                o.sem, o.val = sems[si], n % self.SEMCAP + 1
                ins.then_inc(o.sem, 1)
            o.fn = o.args = o.kw = None
        self.pending = []

    def barrier(self):
        deps = [self.lastc[e] for e in self.CENG if e in self.lastc] + self.dma_since
        for d in deps:
            d.signal = True
        self.flush()
        old = self.bar_deps
        self.bar_deps = {e: list(deps) + list(old.get(e, ())) for e in self.eobj}
        self.dma_since = []
        self.epoch += 1

    def finish(self):
        self.barrier()
        sp = self.nc.sync
        for d in self.bar_deps["sp"]:
            k = id(d.sem)
            if self.waited["sp"].get(k, 0) >= d.val:
                continue
            sp.wait_ge(d.sem, d.val)
            self.waited["sp"][k] = d.val


class Rot:
    def __init__(self, tiles):
        self.t = [(t, Buf()) for t in tiles]
        self.i = 0

    def next(self):
        r = self.t[self.i]
        self.i = (self.i + 1) % len(self.t)
        return r


_CONST = {}


def _consts():
    if _CONST:
        return _CONST
    bf = ml_dtypes.bfloat16
    t = np.arange(S, dtype=np.int64)
    m = (t[:, None] * t[None, :]) % S
    ang = 2.0 * np.pi * m.astype(np.float64) / S
    _CONST["C4"] = (np.cos(ang) / 64.0).astype(np.float32).astype(bf)
    _CONST["S4"] = (np.sin(ang) / 64.0).astype(np.float32).astype(bf)
    del ang, m
    t = np.arange(L, dtype=np.int64)
    ang = 2.0 * np.pi * ((t[:, None] * t[None, :]) % L).astype(np.float64) / L
    _CONST["C2"] = (np.cos(ang) / 16.0).astype(np.float32).astype(bf)
    _CONST["S2"] = (np.sin(ang) / 16.0).astype(np.float32).astype(bf)
    t = np.arange(64, dtype=np.int64)
    ang = 2.0 * np.pi * ((t[:, None] * t[None, :]) % 64).astype(np.float64) / 64
    c64 = np.cos(ang) / 8.0
    s64 = np.sin(ang) / 8.0
    z = np.zeros((64, 64))
    _CONST["C64bd"] = np.block([[c64, z], [z, c64]]).astype(np.float32)
    _CONST["S64bdn"] = (-np.block([[s64, z], [z, s64]])).astype(np.float32)
    _CONST["ident"] = np.eye(128, dtype=np.float32).astype(bf)
    bd = np.zeros((128, 128), np.float32)
    bd[:64, :64] = 1.0
    bd[64:, 64:] = 1.0
    _CONST["bdmask"] = bd
    _CONST["bd64"] = (bd / 64.0).astype(np.float32)
    s_ = np.arange(128)[:, None]
    t_ = np.arange(128)[None, :]
    same = (s_ // 32) == (t_ // 32)
    _CONST["mfwd"] = (same & (s_ <= t_)).astype(np.float32)
    _CONST["mbwd"] = (same & (s_ >= t_)).astype(np.float32)
    cm = np.zeros((128, 4, 128), np.float32)
    for c_ in range(4):
        cm[32 * c_:32 * c_ + 32, c_, :] = 1.0
    _CONST["cm3"] = cm.reshape(128, 512)
    return _CONST


def _na_bias_tiles(rpb):
    H = rpb.shape[0]
    kc = np.arange(64)[:, None]
    qc = np.arange(64)[None, :]
    wc = np.clip(qc - 8, 0, 48)
    colvalid = (kc >= wc) & (kc < wc + 16)
    cidx = np.clip(kc - qc + 15, 0, 30)
    specs = [(8, 8 - 4 + 2 * po) for po in range(8)] + [(0, 2 * p) for p in range(6)] + [(56, 52 + 2 * p) for p in range(6)]
    out = np.full((H, 128, 20, 512), NEG, np.float32)
    for ti, (r0, kr0) in enumerate(specs):
        for kr2 in range(2):
            kr = kr0 + kr2
            for qr in range(8):
                r = r0 + qr
                rs = min(max(r - 4, 0), 56)
                if not (rs <= kr < rs + 8):
                    continue
                ridx = kr - r + 7
                blk = np.where(colvalid[None], rpb[:, ridx][:, cidx], NEG)
                out[:, kr2 * 64:(kr2 + 1) * 64, ti, qr * 64:(qr + 1) * 64] = blk
    return out.reshape(H, 128, 20 * 512)


def build(debug=False, stop=None):
    nc = bass.Bass("TRN2", target_bir_lowering=False)
    skind = "ExternalOutput" if debug else "Internal"

    def din(name, shape, dt=F32):
        return nc.dram_tensor(name, list(shape), dt, kind="ExternalInput").ap()

    def dscr(name, shape, dt=F32):
        return nc.dram_tensor(name, list(shape), dt, kind=skind).ap()

    xin = din("xin", [T, D])
    cT = din("cT", [128, 16])
    w_mod = din("w_mod", [DEPTH, D, 6 * D])
    b_mod = din("b_mod", [DEPTH, 6 * D])
    norm1_g = din("norm1_g", [DEPTH, D])
    w_in = din("w_in", [DEPTH, D, INW])
    fourier_w = din("fourier_w", [DEPTH, 4, 64, 64])
    lbT = din("lbT", [128, 8])
    ngT = din("ngT", [128, DEPTH])
    nabias = din("nabias", [DEPTH, 8, 128, 20 * 512])
    w_out = din("w_out", [DEPTH, D, D])
    norm2_g = din("norm2_g", [DEPTH, D])
    w_gate = din("w_gate", [DEPTH, D, DFF])
    w_up = din("w_up", [DEPTH, D, DFF])
    w_down = din("w_down", [DEPTH, DFF, D])
    final_g = din("final_g", [D])
    C4 = din("C4", [S, S], BF16)
    S4 = din("S4", [S, S], BF16)
    C2 = din("C2", [L, L], BF16)
    S2 = din("S2", [L, L], BF16)
    C64bd_d = din("C64bd", [128, 128])
    S64bdn_d = din("S64bdn", [128, 128])
    ident_d = din("ident", [128, 128], BF16)
    bdmask_d = din("bdmask", [128, 128])
    bd64_d = din("bd64", [128, 128])
    mfwd_d = din("mfwd", [128, 128])
    mbwd_d = din("mbwd", [128, 128])
    cm3_d = din("cm3", [128, 512])
    out = nc.dram_tensor("out", [S, D], F32, kind="ExternalOutput").ap()

    mod2d = dscr("mod2d", [2 * DEPTH, 6 * D])
    fz = dscr("fz", [512, T])
    fb = dscr("fb", [1536, T], BF16)
    tm = dscr("tm", [T, 1024], BF16)
    catT = dscr("catT", [1024, T], BF16)
    xres = dscr("xres", [T, D])
    w_in_b = dscr("w_in_b", [DEPTH, D, INW], BF16)
    w_out_b = dscr("w_out_b", [DEPTH, D, D], BF16)
    w_gate_b = dscr("w_gate_b", [DEPTH, D, DFF], BF16)
    w_up_b = dscr("w_up_b", [DEPTH, D, DFF], BF16)
    w_down_b = dscr("w_down_b", [DEPTH, DFF, D], BF16)

    with ExitStack() as ges:
        sch = Sched(nc, ges)
        V, A, P, G = nc.vector, nc.scalar, nc.tensor, nc.gpsimd

        uid = [0]

        def sbt(es, name, shape, dt=F32):
            uid[0] += 1
            return es.enter_context(nc.sbuf_tensor(f"s{uid[0]}_{name}", list(shape), dt))

        def pst(es, name, shape, dt=F32):
            uid[0] += 1
            return es.enter_context(nc.psum_tensor(f"p{uid[0]}_{name}", list(shape), dt))

        ident = sbt(ges, "ident", [128, 128], BF16)
        bdmask = sbt(ges, "bdmask", [128, 128])
        onesb = sbt(ges, "onesb", [128, 64], BF16)
        B_const = Buf()
        sch.dma("sp", ident[:], ident_d, writes=[B_const])
        sch.dma("sp", bdmask[:], bdmask_d, writes=[B_const])
        sch.op("pool", G.memset, onesb[:], 1.0, writes=[B_const], force=True)

        def evac(i, outap, inap, reads, writes, scale=None):
            if i % 2 == 0:
                if scale is None:
                    sch.op("act", A.copy, outap, inap, reads=reads, writes=writes)
                else:
                    sch.op("act", A.mul, outap, inap, scale, reads=reads, writes=writes)
            else:
                if scale is None:
                    sch.op("dve", V.tensor_copy, outap, inap, reads=reads, writes=writes)
                else:
                    sch.op("dve", V.tensor_scalar, outap, inap, scale, None, ALU.mult, reads=reads, writes=writes)

        def phase_mod():
            with ExitStack() as es:
                cTt = sbt(es, "cTt", [128, 16])
                sT = sbt(es, "sT", [128, 16])
                mod2s = sbt(es, "mod2s", [2, 6 * D])
                bt = sbt(es, "bt", [2, 6 * D])
                wm = Rot([sbt(es, f"wm{i}", [128, 8, 512]) for i in range(3)])
                pm = Rot([pst(es, f"pm{i}", [128, 512]) for i in range(2)])
                B_c, B_s, B_m, B_b = Buf(), Buf(), Buf(), Buf()
                sch.dma("sp", cTt[:], cT, writes=[B_c])
                sch.op("act", A.activation, sT[:], cTt[:], AF.Silu, reads=[B_c], writes=[B_s])
                for l in range(DEPTH):
                    sch.dma("sp", bt[:], b_mod[l].partition_broadcast(2), writes=[B_b])
                    for nb in range(12):
                        wt, Bw = wm.next()
                        sch.dma("sp", wt[:], w_mod[l][:, nb * 512:(nb + 1) * 512].rearrange("(k p) n -> p k n", p=128), writes=[Bw])
                        pt, Bp = pm.next()
                        for k in range(8):
                            sch.op("pe", P.matmul, pt[0:2, :], lhsT=sT[:, 2 * k:2 * k + 2], rhs=wt[:, k, :],
                                   start=(k == 0), stop=(k == 7), pemode="m2", reads=[B_s, Bw], writes=[Bp])
                        sch.op("dve", V.tensor_tensor, mod2s[:, nb * 512:(nb + 1) * 512], pt[0:2, :], bt[:, nb * 512:(nb + 1) * 512],
                               ALU.add, reads=[Bp, B_b], writes=[B_m])
                    sch.dma("pool", mod2d[2 * l:2 * l + 2, :], mod2s[:], reads=[B_m])
                sch.barrier()

        def norm_block(es_tiles, xt, Bx, Gt, SHt, B_mod, a_bf, Ba):
            junk, Bj, ssqr, tmpr = es_tiles
            sq, Bs = ssqr.next()
            sch.op("act", A.activation, junk[:], xt[:], AF.Square, accum_out=sq[:, 0:1], reads=[Bx], writes=[Bj, Bs])
            sch.op("act", A.activation, sq[:, 1:2], sq[:, 0:1], AF.Sqrt, bias=EPS, scale=1.0 / D, reads=[Bs], writes=[Bs])
            sch.op("dve", V.reciprocal, sq[:, 2:3], sq[:, 1:2], reads=[Bs], writes=[Bs])
            tp, Bt = tmpr.next()
            sch.op("dve", V.scalar_tensor_tensor, tp[:], xt[:], sq[:, 2:3], Gt[:], ALU.mult, ALU.mult,
                   reads=[Bx, Bs, B_mod], writes=[Bt])
            sch.op("dve", V.tensor_tensor, a_bf[:], tp[:], SHt[:], ALU.add, reads=[Bt, B_mod], writes=[Ba])

        def transpose_block(ptrr, a_bf, Ba, aT, BaT, tb, ei):
            for half in range(2):
                pt, Bp = ptrr.next()
                for j in range(4):
                    c = (half * 4 + j) * 128
                    sch.op("pe", P.transpose, pt[:, j, :], a_bf[:, c:c + 128], ident[:], reads=[Ba, B_const], writes=[Bp])
                evac(ei + half, aT[:, half * 4:(half + 1) * 4, tb * 128:(tb + 1) * 128], pt[:], [Bp], [BaT])

        def load_weight(es, name, src_kpn, nk, ncols):
            wb = sbt(es, name, [128, nk, ncols], BF16)
            Bw = Buf()
            for k in range(nk):
                sch.dma("sp", wb[:, k, :], src_kpn[:, k, :], writes=[Bw])
            return wb, Bw

        def phase_wcast():
            with ExitStack() as es:
                sin = Rot([sbt(es, f"wci{i}", [128, 2048]) for i in range(3)])
                sout = Rot([sbt(es, f"wco{i}", [128, 2048], BF16) for i in range(3)])
                i = 0
                for l in range(DEPTH):
                    for (src, dst, R, C) in ((w_in[l], w_in_b[l], D, INW), (w_out[l], w_out_b[l], D, D), (w_gate[l], w_gate_b[l], D, DFF),
                                             (w_up[l], w_up_b[l], D, DFF), (w_down[l], w_down_b[l], DFF, D)):
                        for r0 in range(0, R, 128):
                            for c0 in range(0, C, 2048):
                                c1 = min(C, c0 + 2048)
                                n = c1 - c0
                                ti, Bi = sin.next()
                                to, Bo = sout.next()
                                sch.dma("sp", ti[:, 0:n], src[r0:r0 + 128, c0:c1], writes=[Bi])
                                e = ("pool", "dve", "act")[i % 3]
                                i += 1
                                if e == "pool":
                                    sch.op("pool", G.tensor_copy, to[:, 0:n], ti[:, 0:n], reads=[Bi], writes=[Bo])
                                elif e == "dve":
                                    sch.op("dve", V.tensor_copy, to[:, 0:n], ti[:, 0:n], reads=[Bi], writes=[Bo])
                                else:
                                    sch.op("act", A.copy, to[:, 0:n], ti[:, 0:n], reads=[Bi], writes=[Bo])
                                sch.dma("pool", dst[r0:r0 + 128, c0:c1], to[:, 0:n], reads=[Bo])
                sch.barrier()

        def mod_tiles(es, l, v, gsrc, i_sh, i_sc, tag):
            Gt = sbt(es, f"G{tag}", [128, D])
            SHt = sbt(es, f"SH{tag}", [128, D])
            Bm = Buf()
            sch.dma("pool", SHt[:], mod2d[2 * l + v, i_sh * D:(i_sh + 1) * D].partition_broadcast(128), writes=[Bm])
            sch.dma("pool", Gt[:], gsrc.partition_broadcast(128), writes=[Bm])
            return Gt, SHt, Bm

        def phase_ab(l):
            xsrc = xin if l == 0 else xres
            with ExitStack() as es:
                wbf, Bw = load_weight(es, "winbf", w_in_b[l].rearrange("(k p) n -> p k n", p=128), 8, INW)
                mods = []
                sct = sbt(es, "sct", [128, D])
                Bsc = Buf()
                for v in range(2):
                    Gt, SHt, Bm = mod_tiles(es, l, v, norm1_g[l], 0, 1, f"1_{v}")
                    sch.dma("pool", sct[:], mod2d[2 * l + v, D:2 * D].partition_broadcast(128), writes=[Bsc])
                    sch.op("dve", V.scalar_tensor_tensor, Gt[:], sct[:], 1.0, Gt[:], ALU.add, ALU.mult, reads=[Bsc, Bm], writes=[Bm])
                    mods.append((Gt, SHt, Bm))
                junk = sbt(es, "junk", [128, D], BF16)
                nt = (junk, Buf(), Rot([sbt(es, f"ssq{i}", [128, 4]) for i in range(4)]),
                      Rot([sbt(es, f"ntmp{i}", [128, D]) for i in range(2)]))
                xr = Rot([sbt(es, f"xt{i}", [128, D]) for i in range(3)])
                ar = Rot([sbt(es, f"abf{i}", [128, D], BF16) for i in range(2)])
                aTr = Rot([sbt(es, f"aT{i}", [128, 8, 512], BF16) for i in range(2)])
                sfr = Rot([sbt(es, f"sf{i}", [128, 512]) for i in range(2)])
                sbr = Rot([sbt(es, f"sb{i}", [128, 512], BF16) for i in range(3)])
                tmr = Rot([sbt(es, f"tms{i}", [128, 1024], BF16) for i in range(2)])
                ptrr = Rot([pst(es, f"ptr{i}", [128, 4, 128], BF16) for i in range(2)])
                pfr = Rot([pst(es, f"pf{i}", [128, 512]) for i in range(3)])
                ptmr = Rot([pst(es, f"ptm{i}", [128, 512]) for i in range(2)])
                fm = []
                for c0, dst, r0, sc in ((256, fb, 0, None), (512, fz, 0, None), (768, fz, 256, None), (1280, fb, 256, None),
                                        (1536, fb, 512, 0.125), (2048, fb, 1024, None)):
                    wd = 512 if c0 >= 1536 else 256
                    for g in range(wd // 128):
                        fm.append((c0 + g * 128, dst, r0 + g * 128, sc))
                sbs = [(i * 512, 512, 0) for i in range(8)] + [(S, L, 1)]
                ei = 0
                for (t0, ntok, v) in sbs:
                    Gt, SHt, Bm = mods[v]
                    aT, BaT = aTr.next()
                    for tb in range(ntok // 128):
                        xt, Bx = xr.next()
                        sch.dma("sp", xt[:], xsrc[t0 + tb * 128:t0 + (tb + 1) * 128, :], writes=[Bx])
                        a_bf, Ba = ar.next()
                        norm_block(nt, xt, Bx, Gt, SHt, Bm, a_bf, Ba)
                        transpose_block(ptrr, a_bf, Ba, aT, BaT, tb, ei)
                    for (c0, dst, r0, sc) in fm:
                        pf, Bp = pfr.next()
                        for k in range(8):
                            sch.op("pe", P.matmul, pf[:, 0:ntok], lhsT=wbf[:, k, c0:c0 + 128], rhs=aT[:, k, 0:ntok],
                                   start=(k == 0), stop=(k == 7), reads=[Bw, BaT], writes=[Bp])
                        if dst is fz:
                            st, Bs = sfr.next()
                        else:
                            st, Bs = sbr.next()
                        ei += 1
                        evac(ei, st[:, 0:ntok], pf[:, 0:ntok], [Bp], [Bs], scale=sc)
                        sch.dma("pool", dst[r0:r0 + 128, t0:t0 + ntok], st[:, 0:ntok], reads=[Bs])
                    for tb in range(ntok // 128):
                        tms, Bts = tmr.next()
                        tsl = slice(tb * 128, (tb + 1) * 128)
                        p0, Bp0 = ptmr.next()
                        for (cc, oc) in ((0, 0), (1024, 256)):
                            for k in range(8):
                                sch.op("pe", P.matmul, p0[:, oc:oc + 256], lhsT=aT[:, k, tsl], rhs=wbf[:, k, cc:cc + 256],
                                       start=(k == 0), stop=(k == 7), reads=[Bw, BaT], writes=[Bp0])
                        ei += 1
                        evac(ei, tms[:, 0:512], p0[:], [Bp0], [Bts])
                        p1, Bp1 = ptmr.next()
                        for k in range(8):
                            sch.op("pe", P.matmul, p1[:], lhsT=aT[:, k, tsl], rhs=wbf[:, k, 2560:3072],
                                   start=(k == 0), stop=(k == 7), reads=[Bw, BaT], writes=[Bp1])
                        ei += 1
                        evac(ei, tms[:, 512:1024], p1[:], [Bp1], [Bts])
                        sch.dma("pool", tm[t0 + tb * 128:t0 + (tb + 1) * 128, :], tms[:], reads=[Bts])
                sch.barrier()

        def phase_c(l):
            with ExitStack() as es:
                U = sbt(es, "fU", [128, NB, 256], BF16)
                BU = Buf()
                sch.dma("sp", U[:], tm[:, 0:256].rearrange("(k p) c -> p k c", p=128), writes=[BU])
                c64 = sbt(es, "c64", [128, 2, 128])
                wf = sbt(es, "wf", [128, 2, 64])
                Mbd = sbt(es, "Mbd", [128, 2, 2, 128], BF16)
                Bc, Bwf, BM = Buf(), Buf(), Buf()
                sch.dma("sp", c64[:, 0, :], C64bd_d, writes=[Bc])
                sch.dma("sp", c64[:, 1, :], S64bdn_d, writes=[Bc])
                for cb in range(2):
                    sch.dma("sp", wf[:, cb, :], fourier_w[l][2 * cb:2 * cb + 2].rearrange("g c d -> (g c) d"), writes=[Bwf])
                sch.op("pool", G.memset, Mbd[:], 0.0, writes=[BM])
                pmr = Rot([pst(es, f"pM{i}", [128, 512]) for i in range(1)])
                pM, BpM = pmr.next()
                for w in range(2):
                    for cb in range(2):
                        o0 = (w * 2 + cb) * 64
                        sch.op("pe", P.matmul, pM[:, o0:o0 + 64], lhsT=c64[:, w, :], rhs=wf[:, cb, :], start=True, stop=True,
                               pemode="f32", reads=[Bc, Bwf], writes=[BpM])
                for w in range(2):
                    for cb in range(2):
                        o0 = (w * 2 + cb) * 64
                        sch.op("dve", V.tensor_copy, Mbd[0:64, w, cb, 0:64], pM[0:64, o0:o0 + 64], reads=[BpM, BM], writes=[BM])
                        sch.op("dve", V.tensor_copy, Mbd[64:128, w, cb, 64:128], pM[64:128, o0:o0 + 64], reads=[BpM, BM], writes=[BM])
                csr = Rot([sbt(es, f"cs{i}", [128, 2, 4, 512], BF16) for i in range(3)])
                pqr = Rot([sbt(es, f"pq{i}", [128, 4, 512], BF16) for i in range(2)])
                osr = Rot([sbt(es, f"fo{i}", [128, 512], BF16) for i in range(2)])
                pacc = [pst(es, f"pacc{i}", [128, 512]) for i in range(4)]
                Bacc = [Buf() for _ in range(4)]
                por = Rot([pst(es, f"po{i}", [128, 512]) for i in range(2)])

                def mix(pq, Bpq, n, tcol0):
                    for cb in range(2):
                        po, Bpo = por.next()
                        sch.op("pe", P.matmul, po[:, 0:n], lhsT=Mbd[:, 0, cb, :], rhs=pq[:, cb, 0:n], start=True, stop=False,
                               reads=[BM, Bpq], writes=[Bpo])
                        sch.op("pe", P.matmul, po[:, 0:n], lhsT=Mbd[:, 1, cb, :], rhs=pq[:, 2 + cb, 0:n], start=False, stop=True,
                               reads=[BM, Bpq], writes=[Bpo])
                        ot, Bo = osr.next()
                        evac(cb, ot[:, 0:n], po[:, 0:n], [Bpo], [Bo])
                        sch.dma("pool", catT[cb * 128:(cb + 1) * 128, tcol0:tcol0 + n], ot[:, 0:n], reads=[Bo])

                for nb in range(8):
                    for kg in range(8):
                        cs, Bcs = csr.next()
                        sch.dma("sp", cs[:, 0], C4[kg * 512:(kg + 1) * 512, nb * 512:(nb + 1) * 512].rearrange("(k p) n -> p k n", p=128), writes=[Bcs])
                        sch.dma("sp", cs[:, 1], S4[kg * 512:(kg + 1) * 512, nb * 512:(nb + 1) * 512].rearrange("(k p) n -> p k n", p=128), writes=[Bcs])
                        for kk in range(4):
                            k = kg * 4 + kk
                            for w in range(2):
                                for cb in range(2):
                                    i = w * 2 + cb
                                    sch.op("pe", P.matmul, pacc[i][:], lhsT=U[:, k, cb * 128:(cb + 1) * 128], rhs=cs[:, w, kk, :],
                                           start=(k == 0), stop=(k == 31), reads=[BU, Bcs], writes=[Bacc[i]])
                    pq, Bpq = pqr.next()
                    for i in range(4):
                        evac(i, pq[:, i, :], pacc[i][:], [Bacc[i]], [Bpq])
                    mix(pq, Bpq, 512, nb * 512)
                if l == 0:
                    c2 = sbt(es, "c2", [128, 2, 2, 256], BF16)
                    Bc2 = Buf()
                    sch.dma("sp", c2[:, 0], C2.rearrange("(k p) n -> p k n", p=128), writes=[Bc2])
                    sch.dma("sp", c2[:, 1], S2.rearrange("(k p) n -> p k n", p=128), writes=[Bc2])
                    for w in range(2):
                        for cb in range(2):
                            i = w * 2 + cb
                            for k in range(2):
                                sch.op("pe", P.matmul, pacc[i][:, 0:256], lhsT=U[:, 32 + k, cb * 128:(cb + 1) * 128], rhs=c2[:, w, k, :],
                                       start=(k == 0), stop=(k == 1), reads=[BU, Bc2], writes=[Bacc[i]])
                    pq, Bpq = pqr.next()
                    for i in range(4):
                        evac(i, pq[:, i, 0:256], pacc[i][:, 0:256], [Bacc[i]], [Bpq])
                    mix(pq, Bpq, 256, S)
                sch.barrier()

        def phase_d(l):
            HW = T // 2
            NCH = T // 32
            with ExitStack() as es:
                lbt = sbt(es, "lbt", [128, 8])
                lbv = sbt(es, "lbv", [128, 8])
                ngt = sbt(es, "ngt", [128, DEPTH])
                mdir = sbt(es, "mdir", [128, 2, 128])
                bd64 = sbt(es, "bd64", [128, 128])
                cm3 = sbt(es, "cm3", [128, 4, 128])
                Bl, Bmd = Buf(), Buf()
                sch.dma("sp", lbt[:], lbT, writes=[Bl])
                sch.dma("sp", ngt[:], ngT, writes=[Bl])
                sch.dma("sp", mdir[:, 0, :], mfwd_d, writes=[Bmd])
                sch.dma("sp", mdir[:, 1, :], mbwd_d, writes=[Bmd])
                sch.dma("sp", bd64[:], bd64_d, writes=[Bmd])
                sch.dma("sp", cm3[:], cm3_d.rearrange("p (c n) -> p c n", n=128), writes=[Bmd])
                if l == 0:
                    sch.op("pool", G.memset, lbv[:, 0:4], 0.0, reads=[Bl], writes=[Bl])
                    sch.op("pool", G.memset, lbv[:, 4:8], 1.0, reads=[Bl], writes=[Bl])
                else:
                    ex = sbt(es, "lbex", [128, 8])
                    sch.op("act", A.activation, ex[:], lbt[:], AF.Exp, reads=[Bl], writes=[Bl])
                    sch.op("dve", V.tensor_tensor, lbv[:, 4:8], ex[:, 0:4], ex[:, 4:8], ALU.add, reads=[Bl], writes=[Bl])
                    sch.op("dve", V.reciprocal, lbv[:, 4:8], lbv[:, 4:8], reads=[Bl], writes=[Bl])
                    sch.op("dve", V.tensor_tensor, lbv[:, 0:4], ex[:, 4:8], lbv[:, 4:8], ALU.mult, reads=[Bl], writes=[Bl])
                    sch.op("dve", V.tensor_scalar, lbv[:, 4:8], lbv[:, 0:4], -1.0, 1.0, ALU.mult, ALU.add, reads=[Bl], writes=[Bl])
                for hp in range(2):
                    with ExitStack() as hs:
                        vtm = sbt(hs, "hv", [128, NB, 128], BF16)
                        vm = sbt(hs, "hvm", [128, 2, NB, 128], BF16)
                        Bv, Bvm = Buf(), Buf()
                        sch.dma("sp", vtm[:], tm[:, 256 + hp * 128:256 + (hp + 1) * 128].rearrange("(k p) c -> p k c", p=128), writes=[Bv])
                        sch.op("pool", G.memset, vm[:], 0.0, writes=[Bvm])
                        sch.op("pool", G.tensor_copy, vm[:, 0, :, 0:64], vtm[:, :, 0:64], reads=[Bv, Bvm], writes=[Bvm])
                        sch.op("pool", G.tensor_copy, vm[:, 1, :, 64:128], vtm[:, :, 64:128], reads=[Bv, Bvm], writes=[Bvm])
                        chains = []
                        for d in range(2):
                            ch = dict(d=d)
                            ch["qdm"] = sbt(hs, f"qdm{d}", [128, 2, T], BF16)
                            ch["kd"] = sbt(hs, f"kd{d}", [128, T], BF16)
                            ch["kst"] = sbt(hs, f"kst{d}", [128, NB, 128], BF16)
                            ch["dec"] = sbt(hs, f"dec{d}", [128, NCH])
                            ch["o"] = sbt(hs, f"o{d}", [128, T])
                            ch["B"] = {k: Buf() for k in ("qdm", "kd", "kst", "dec", "o")}
                            sch.op("pool", G.memset, ch["qdm"][:], 0.0, writes=[ch["B"]["qdm"]])
                            chains.append(ch)
                        with ExitStack() as gs:
                            Ar = [sbt(gs, f"gA{i}", [128, HW]) for i in range(4)]
                            BA = [Buf() for _ in range(4)]
                            qh = sbt(gs, "gq", [128, HW], BF16)
                            ksh = sbt(gs, "gks", [128, HW], BF16)
                            Bq, Bks = Buf(), Buf()
                            tot = sbt(gs, "gtot", [128, HW // 32])
                            Btot = Buf()
                            ptk = Rot([pst(gs, f"ptk{i}", [128, 4, 128], BF16) for i in range(2)])
                            ei = 0
                            for ch in chains:
                                d = ch["d"]
                                B = ch["B"]
                                lbc = lbv[:, d * 2 + hp:d * 2 + hp + 1]
                                omc = lbv[:, 4 + d * 2 + hp:4 + d * 2 + hp + 1]
                                for h in range(2):
                                    c0 = h * HW
                                    cs = slice(c0, c0 + HW)
                                    A1, A2, A3, A4 = Ar
                                    B1, B2, B3, B4 = BA
                                    zr = 256 * d + hp * 128
                                    sch.dma("sp", A1[:], fz[zr:zr + 128, cs], writes=[B1])
                                    sch.dma("sp", qh[:], fb[hp * 128:(hp + 1) * 128, cs], writes=[Bq])
                                    sch.op("act", A.activation, A1[:], A1[:], AF.Sigmoid, reads=[B1], writes=[B1])
                                    sch.op("dve", V.tensor_scalar, A1[:], A1[:], omc, lbc, ALU.mult, ALU.add, reads=[B1, Bl], writes=[B1])
                                    sch.op("act", A.activation, A2[:], A1[:], AF.Ln, reads=[B1], writes=[B2])
                                    sch.op("dve", V.tensor_scalar, A1[:], A1[:], -1.0, 1.0, ALU.mult, ALU.add, reads=[B1, B2], writes=[B1])
                                    sch.op("pool", G.memset, A4[:], 1.0, writes=[B4])
                                    sch.op("pool", G.memset, A4[:].rearrange("p (n c) -> p n c", c=32)[:, :, 0:1], 0.0, reads=[B4], writes=[B4])
                                    sch.op("dve", V.tensor_tensor_scan, A3[:], A4[:], A2[:], 0.0, ALU.mult, ALU.add, reads=[B4, B2], writes=[B3])
                                    A3v = A3[:].rearrange("p (n c) -> p n c", c=32)
                                    A2v = A2[:].rearrange("p (n c) -> p n c", c=32)
                                    totb = tot[:].rearrange("p (n o) -> p n o", o=1).broadcast_to([128, HW // 32, 32])
                                    sch.op("dve", V.tensor_copy, tot[:], A3v[:, :, 31], reads=[B3], writes=[Btot])
                                    if d == 0:
                                        sch.op("dve", V.tensor_tensor, A2v, A3v, totb, ALU.subtract, reads=[Btot, B3, B2], writes=[B2])
                                        asg, rsg = 1.0, -1.0
                                    else:
                                        sch.op("dve", V.tensor_tensor, A2[:], A3[:], A2[:], ALU.subtract, reads=[B3, B2], writes=[B2])
                                        sch.op("dve", V.tensor_tensor, A3v, A2v, totb, ALU.subtract, reads=[Btot, B2, B3], writes=[B3])
                                        asg, rsg = -1.0, 1.0
                                    sch.op("act", A.activation, ch["dec"][:, c0 // 32:(c0 + HW) // 32], tot[:], AF.Exp, reads=[Btot], writes=[B["dec"]])
                                    sch.op("act", A.activation, A4[:], A3[:], AF.Exp, scale=asg, reads=[B3, B4], writes=[B4])
                                    for hh in range(2):
                                        hs_ = slice(hh * 64, (hh + 1) * 64)
                                        sch.op("dve", V.tensor_tensor, ch["qdm"][hs_, hh, cs], qh[hs_, :], A4[hs_, :], ALU.mult,
                                               reads=[Bq, B4, B["qdm"]], writes=[B["qdm"]])
                                    sch.op("act", A.activation, A4[:], A3[:], AF.Exp, scale=-asg, reads=[B3, B4], writes=[B4])
                                    sch.op("dve", V.tensor_tensor, ch["kd"][:, cs], A1[:], A4[:], ALU.mult, reads=[B1, B4], writes=[B["kd"]])
                                    sch.op("act", A.activation, A4[:], A2[:], AF.Exp, scale=rsg, reads=[B2, B4], writes=[B4])
                                    sch.op("dve", V.tensor_tensor, ksh[:], A1[:], A4[:], ALU.mult, reads=[B1, B4], writes=[Bks])
                                    nbh = HW // 128
                                    for b4 in range(0, nbh, 4):
                                        nn = min(4, nbh - b4)
                                        pt, Bp = ptk.next()
                                        for j in range(nn):
                                            sch.op("pe", P.transpose, pt[:, j, :], ksh[:, (b4 + j) * 128:(b4 + j + 1) * 128], ident[:],
                                                   reads=[Bks, B_const], writes=[Bp])
                                        ei += 1
                                        evac(ei, ch["kst"][:, h * nbh + b4:h * nbh + b4 + nn, :], pt[:, 0:nn, :], [Bp], [B["kst"]])
                            sch.barrier()
                        with ExitStack() as cs_:
                            vexr = Rot([sbt(cs_, f"vex{i}", [128, 4, 128], BF16) for i in range(4)])
                            for ch in chains:
                                d = ch["d"]
                                ch["psU"] = pst(cs_, f"psU{d}", [128, 4, 128]); ch["BpsU"] = Buf()
                                ch["psS"] = [pst(cs_, f"psS{d}{i}", [128, 128]) for i in range(2)]; ch["BpsS"] = [Buf(), Buf()]
                                ch["psO"] = pst(cs_, f"psO{d}", [128, 128]); ch["BpsO"] = Buf()
                                ch["Sf"] = Rot([sbt(cs_, f"Sf{d}{i}", [128, 128]) for i in range(2)])
                                ch["Sbd"] = Rot([sbt(cs_, f"Sbd{d}{i}", [128, 128], BF16) for i in range(12)])
                                ch["scb"] = Rot([sbt(cs_, f"scb{d}{i}", [128, 2, 128], BF16) for i in range(2)])
                                ch["order"] = ([32, 33] + list(range(32))) if d == 0 else ([33, 32] + list(range(31, -1, -1)))
                                ch["corder"] = [0, 1, 2, 3] if d == 0 else [3, 2, 1, 0]
                                ch["prev"] = None
                                ch["pending"] = None

                            def front(ch, step):
                                nb = ch["order"][step]
                                B = ch["B"]
                                cols = slice(nb * 128, (nb + 1) * 128)
                                vex, Bvex = vexr.next()
                                sch.op("dve", V.tensor_tensor, vex[:], cm3[:], vtm[:, nb:nb + 1, :].broadcast_to([128, 4, 128]), ALU.mult,
                                       reads=[Bmd, Bv], writes=[Bvex])
                                sch.op("pe", P.matmul, ch["psU"][:].rearrange("p c n -> p (c n)"), lhsT=ch["kst"][:, nb, :],
                                       rhs=vex[:].rearrange("p c n -> p (c n)"), start=True, stop=True,
                                       reads=[B["kst"], Bvex], writes=[ch["BpsU"]])
                                before = []
                                for c in ch["corder"]:
                                    n = nb * 4 + c
                                    before.append((c, ch["prev_bd"] if ch["prev"] is not None else None))
                                    sf, Bsf = ch["Sf"].next()
                                    if ch["prev"] is None:
                                        sch.op("dve", V.tensor_copy, sf[:], ch["psU"][:, c, :], reads=[ch["BpsU"]], writes=[Bsf])
                                    else:
                                        pv, Bpv = ch["prev"]
                                        sch.op("dve", V.scalar_tensor_tensor, sf[:], pv[:], ch["dec"][:, n:n + 1], ch["psU"][:, c, :],
                                               ALU.mult, ALU.add, reads=[Bpv, B["dec"], ch["BpsU"]], writes=[Bsf])
                                    ch["prev"] = (sf, Bsf)
                                    sbd, Bsbd = ch["Sbd"].next()
                                    sch.op("pool", G.tensor_tensor, sbd[:], sf[:], bdmask[:], ALU.mult, reads=[Bsf, B_const], writes=[Bsbd])
                                    ch["prev_bd"] = (sbd, Bsbd)
                                scb, Bscb = ch["scb"].next()
                                for hh in range(2):
                                    sch.op("pe", P.matmul, ch["psS"][hh][:], lhsT=ch["kd"][:, cols], rhs=ch["qdm"][:, hh, cols],
                                           start=True, stop=True, reads=[B["kd"], B["qdm"]], writes=[ch["BpsS"][hh]])
                                    sch.op("dve", V.tensor_tensor, scb[:, hh, :], ch["psS"][hh][:], mdir[:, ch["d"], :], ALU.mult,
                                           reads=[ch["BpsS"][hh], Bmd], writes=[Bscb])
                                return (nb, before, scb, Bscb)

                            def back(ch, pend):
                                nb, before, scb, Bscb = pend
                                B = ch["B"]
                                cols = slice(nb * 128, (nb + 1) * 128)
                                inter = [(c, s_) for (c, s_) in before if s_ is not None]
                                for hh in range(2):
                                    sch.op("pe", P.matmul, ch["psO"][:], lhsT=vm[:, hh, nb, :], rhs=scb[:, hh, :], start=(hh == 0),
                                           stop=(hh == 1 and not inter), reads=[Bvm, Bscb], writes=[ch["BpsO"]])
                                for i, (c, (sbd, Bsbd)) in enumerate(inter):
                                    for hh in range(2):
                                        sch.op("pe", P.matmul, ch["psO"][:, 32 * c:32 * c + 32], lhsT=sbd[:],
                                               rhs=ch["qdm"][:, hh, nb * 128 + 32 * c:nb * 128 + 32 * c + 32], start=False,
                                               stop=(i == len(inter) - 1 and hh == 1), reads=[Bsbd, B["qdm"]], writes=[ch["BpsO"]])
                                sch.op("act", A.copy, ch["o"][:, cols], ch["psO"][:], reads=[ch["BpsO"]], writes=[B["o"]])

                            for step in range(NB + 1):
                                newp = []
                                if step < NB:
                                    for ch in chains:
                                        newp.append(front(ch, step))
                                if step >= 1:
                                    for ch in chains:
                                        back(ch, ch["pending"])
                                if step < NB:
                                    for ch, p_ in zip(chains, newp):
                                        ch["pending"] = p_
                            sch.barrier()
                        with ExitStack() as rs:
                            of, ob = chains[0]["o"], chains[1]["o"]
                            Bof, Bob = chains[0]["B"]["o"], chains[1]["B"]["o"]
                            sch.op("dve", V.tensor_tensor, of[:], of[:], ob[:], ALU.add, reads=[Bof, Bob], writes=[Bof])
                            gt = sbt(rs, "hg", [128, T], BF16)
                            Bg = Buf()
                            sch.dma("sp", gt[:], fb[256 + hp * 128:256 + (hp + 1) * 128, :], writes=[Bg])
                            sqr = Rot([sbt(rs, f"hsq{i}", [128, 512]) for i in range(2)])
                            sgr = Rot([sbt(rs, f"hsg{i}", [128, 512]) for i in range(2)])
                            rr = Rot([sbt(rs, f"hr{i}", [128, 512]) for i in range(2)])
                            orr = Rot([sbt(rs, f"hob{i}", [128, 512], BF16) for i in range(2)])
                            pms = Rot([pst(rs, f"pms{i}", [128, 512]) for i in range(2)])
                            for t0 in range(0, T, 512):
                                n = min(512, T - t0)
                                cs = slice(t0, t0 + n)
                                sq, Bsq = sqr.next()
                                sch.op("act", A.activation, sq[:, 0:n], of[:, cs], AF.Square, reads=[Bof], writes=[Bsq])
                                pm_, Bpm = pms.next()
                                sch.op("pe", P.matmul, pm_[:, 0:n], lhsT=bd64[:], rhs=sq[:, 0:n], start=True, stop=True, pemode="f32",
                                       reads=[Bmd, Bsq], writes=[Bpm])
                                r_, Br = rr.next()
                                sch.op("act", A.activation, r_[:, 0:n], pm_[:, 0:n], AF.Sqrt, bias=EPS, scale=1.0, reads=[Bpm], writes=[Br])
                                sch.op("dve", V.reciprocal, r_[:, 0:n], r_[:, 0:n], reads=[Br], writes=[Br])
                                sch.op("dve", V.tensor_tensor, r_[:, 0:n], r_[:, 0:n], of[:, cs], ALU.mult, reads=[Br, Bof], writes=[Br])
                                sg, Bsg = sgr.next()
                                sch.op("act", A.activation, sg[:, 0:n], gt[:, cs], AF.Silu, reads=[Bg], writes=[Bsg])
                                ob_, Bo = orr.next()
                                sch.op("dve", V.scalar_tensor_tensor, ob_[:, 0:n], r_[:, 0:n], ngt[:, l:l + 1], sg[:, 0:n], ALU.mult, ALU.mult,
                                       reads=[Br, Bl, Bsg], writes=[Bo])
                                sch.dma("pool", catT[256 + hp * 128:256 + (hp + 1) * 128, cs], ob_[:, 0:n], reads=[Bo])
                            sch.barrier()

        def phase_e(l):
            with ExitStack() as es:
                qm = sbt(es, "naq", [128, 2, T], BF16)
                kT = sbt(es, "nak", [128, T], BF16)
                vt = sbt(es, "nav", [128, NB, 128], BF16)
                bias = [sbt(es, f"nab{i}", [128, 20, 512]) for i in range(2)]
                oT = sbt(es, "nao", [128, T], BF16)
                ones2 = sbt(es, "ones2", [128, 128], BF16)
                sbr = Rot([sbt(es, f"nas{i}", [128, 512]) for i in range(3)])
                pr = Rot([sbt(es, f"nap{i}", [128, 512], BF16) for i in range(4)])
                rcr = Rot([sbt(es, f"narc{i}", [128, 512]) for i in range(2)])
                psr = Rot([pst(es, f"naS{i}", [128, 512]) for i in range(4)])
                num = [pst(es, f"naN{i}", [128, 512]) for i in range(2)]
                den = [pst(es, f"naD{i}", [128, 512]) for i in range(2)]
                Bn = [Buf(), Buf()]
                Bd = [Buf(), Buf()]
                Bq, Bk, Bv, Bo, B1 = Buf(), Buf(), Buf(), Buf(), Buf()
                Bb = [Buf(), Buf()]
                sch.op("pool", G.memset, qm[:], 0.0, writes=[Bq])
                sch.op("pool", G.memset, ones2[:], 1.0, writes=[B1])
                LOOK = 2
                for hp in range(4):
                    for hh in range(2):
                        r0 = 512 + hp * 128 + hh * 64
                        sch.dma("sp", qm[hh * 64:(hh + 1) * 64, hh, :], fb[r0:r0 + 64, :], reads=[Bq], writes=[Bq])
                    sch.dma("sp", kT[:], fb[1024 + hp * 128:1024 + (hp + 1) * 128, :], writes=[Bk])
                    sch.dma("sp", vt[:], tm[:, 512 + hp * 128:512 + (hp + 1) * 128].rearrange("(k p) c -> p k c", p=128), writes=[Bv])
                    for hh in range(2):
                        sch.dma("sp", bias[hh][:], nabias[l, hp * 2 + hh].rearrange("p (t n) -> p t n", n=512), writes=[Bb[hh]])
                    items = []
                    for qb in range(8):
                        if qb == 0:
                            kts = [(2 * p * 64, 8 + p) for p in range(6)]
                        elif qb == 7:
                            kts = [((52 + 2 * p) * 64, 14 + p) for p in range(6)]
                        else:
                            kts = [((8 * qb - 4 + 2 * po) * 64, po) for po in range(8)]
                        kts += [(S, None), (S + 128, None)]
                        items.append((qb * 512, 512, kts))
                    if l == 0:
                        items.append((S, L, [(S, None), (S + 128, None)]))
                    for (q0, nq, kts) in items:
                        flat = [(hh, kt0, bi) for hh in range(2) for (kt0, bi) in kts]
                        prod = [None] * len(flat)
                        for i in range(len(flat) + LOOK):
                            if i < len(flat):
                                hh, kt0, bi = flat[i]
                                ps, Bps = psr.next()
                                sch.op("pe", P.matmul, ps[:, 0:nq], lhsT=kT[:, kt0:kt0 + 128], rhs=qm[:, hh, q0:q0 + nq], start=True, stop=True,
                                       reads=[Bk, Bq], writes=[Bps])
                                pt, Bpt = pr.next()
                                if bi is None:
                                    sch.op("act", A.activation, pt[:, 0:nq], ps[:, 0:nq], AF.Exp, reads=[Bps], writes=[Bpt])
                                else:
                                    st, Bst = sbr.next()
                                    sch.op("dve", V.tensor_tensor, st[:, 0:nq], ps[:, 0:nq], bias[hh][:, bi, 0:nq], ALU.add,
                                           reads=[Bps, Bb[hh]], writes=[Bst])
                                    sch.op("act", A.activation, pt[:, 0:nq], st[:, 0:nq], AF.Exp, reads=[Bst], writes=[Bpt])
                                prod[i] = (pt, Bpt)
                            j = i - LOOK
                            if j >= 0:
                                hh, kt0, bi = flat[j]
                                pt, Bpt = prod[j]
                                first = (j % len(kts) == 0)
                                last = (j % len(kts) == len(kts) - 1)
                                sch.op("pe", P.matmul, num[hh][:, 0:nq], lhsT=vt[:, kt0 // 128, :], rhs=pt[:, 0:nq],
                                       start=first, stop=last, reads=[Bv, Bpt], writes=[Bn[hh]])
                                sch.op("pe", P.matmul, den[hh][:, 0:nq], lhsT=ones2[:], rhs=pt[:, 0:nq],
                                       start=first, stop=last, reads=[B1, Bpt], writes=[Bd[hh]])
                                if last:
                                    hs_ = slice(hh * 64, (hh + 1) * 64)
                                    rc, Brc = rcr.next()
                                    sch.op("dve", V.reciprocal, rc[hs_, 0:nq], den[hh][hs_, 0:nq], reads=[Bd[hh]], writes=[Brc])
                                    sch.op("dve", V.tensor_tensor, oT[hs_, q0:q0 + nq], num[hh][hs_, 0:nq], rc[hs_, 0:nq], ALU.mult,
                                           reads=[Bn[hh], Brc], writes=[Bo])
                    ncol = T if l == 0 else S
                    sch.dma("pool", catT[512 + hp * 128:512 + (hp + 1) * 128, 0:ncol], oT[:, 0:ncol], reads=[Bo])
                sch.barrier()

        def phase_f1(l):
            xsrc = xin if l == 0 else xres
            with ExitStack() as es:
                wob, Bw = load_weight(es, "woutbf", w_out_b[l].rearrange("(k p) n -> p k n", p=128), 8, D)
                g1 = [sbt(es, f"g1_{v}", [128, D]) for v in range(2)]
                Bg = Buf()
                for v in range(2):
                    sch.dma("pool", g1[v][:], mod2d[2 * l + v, 2 * D:3 * D].partition_broadcast(128), writes=[Bg])
                cr = Rot([sbt(es, f"cat{i}", [128, 8, 512], BF16) for i in range(2)])
                xr = Rot([sbt(es, f"xt{i}", [128, D]) for i in range(3)])
                yr = Rot([sbt(es, f"yt{i}", [128, D]) for i in range(2)])
                pyr = Rot([pst(es, f"py{i}", [128, 512]) for i in range(4)])
                sbs = [(i * 512, 512, 0) for i in range(8)] + ([(S, L, 1)] if l == 0 else [])
                for (t0, ntok, v) in sbs:
                    ct, Bc = cr.next()
                    sch.dma("sp", ct[:, :, 0:ntok], catT[:, t0:t0 + ntok].rearrange("(k p) t -> p k t", p=128), writes=[Bc])
                    for tb in range(ntok // 128):
                        xt, Bx = xr.next()
                        rows = slice(t0 + tb * 128, t0 + (tb + 1) * 128)
                        sch.dma("sp", xt[:], xsrc[rows, :], writes=[Bx])
                        yt, By = yr.next()
                        for nb in range(2):
                            py, Bp = pyr.next()
                            for k in range(8):
                                sch.op("pe", P.matmul, py[:], lhsT=ct[:, k, tb * 128:(tb + 1) * 128], rhs=wob[:, k, nb * 512:(nb + 1) * 512],
                                       start=(k == 0), stop=(k == 7), reads=[Bc, Bw], writes=[Bp])
                            sch.op("dve", V.tensor_tensor, yt[:, nb * 512:(nb + 1) * 512], py[:], g1[v][:, nb * 512:(nb + 1) * 512], ALU.mult,
                                   reads=[Bp, Bg], writes=[By])
                        sch.op("pool", G.tensor_tensor, yt[:], yt[:], xt[:], ALU.add, reads=[By, Bx], writes=[By])
                        sch.dma("pool", xres[rows, :], yt[:], reads=[By])
                sch.barrier()

        def phase_f2(l):
            last = (l == DEPTH - 1)
            with ExitStack() as es:
                wg, Bwg = load_weight(es, "wgbf", w_gate_b[l].rearrange("(k p) n -> p k n", p=128), 8, DFF)
                wu, Bwu = load_weight(es, "wubf", w_up_b[l].rearrange("(k p) n -> p k n", p=128), 8, DFF)
                wd, Bwd = load_weight(es, "wdbf", w_down_b[l].rearrange("(j p) n -> p j n", p=128), NJ, D)
                junk = sbt(es, "junk", [128, D], BF16)
                nt = (junk, Buf(), Rot([sbt(es, f"ssq{i}", [128, 4]) for i in range(4)]),
                      Rot([sbt(es, f"ntmp{i}", [128, D]) for i in range(1)]))
                Gt = sbt(es, "G2", [128, D])
                SHt = sbt(es, "SH2", [128, D])
                g2 = sbt(es, "g2", [128, D])
                Bm = Buf()

                def load_mods(v):
                    sct, Bsc = nt[3].next()
                    sch.dma("pool", SHt[:], mod2d[2 * l + v, 3 * D:4 * D].partition_broadcast(128), writes=[Bm])
                    sch.dma("pool", Gt[:], norm2_g[l].partition_broadcast(128), writes=[Bm])
                    sch.dma("pool", g2[:], mod2d[2 * l + v, 5 * D:6 * D].partition_broadcast(128), writes=[Bm])
                    sch.dma("pool", sct[:], mod2d[2 * l + v, 4 * D:5 * D].partition_broadcast(128), writes=[Bsc])
                    sch.op("dve", V.scalar_tensor_tensor, Gt[:], sct[:], 1.0, Gt[:], ALU.add, ALU.mult, reads=[Bsc, Bm], writes=[Bm])
                fg = None
                if last:
                    fg = sbt(es, "fg", [128, D])
                    Bfg = Buf()
                    sch.dma("pool", fg[:], final_g.partition_broadcast(128), writes=[Bfg])
                x1 = [(sbt(es, f"x1_{i}", [128, D]), Buf()) for i in range(4)]
                ar = Rot([sbt(es, f"abf{i}", [128, D], BF16) for i in range(1)])
                aT = sbt(es, "aT", [128, 8, 512], BF16)
                BaT = Buf()
                hT = sbt(es, "hT", [128, NJ, 512], BF16)
                BhT = Buf()
                sgr = Rot([sbt(es, f"sg{i}", [128, 512]) for i in range(1)])
                ptrr = Rot([pst(es, f"ptr{i}", [128, 4, 128], BF16) for i in range(1)])
                pgr = Rot([pst(es, f"pg{i}", [128, 512]) for i in range(2)])
                pur = Rot([pst(es, f"pu{i}", [128, 512]) for i in range(2)])
                pdr = Rot([pst(es, f"pd{i}", [128, 512]) for i in range(3)])
                sbs = [(i * 512, 512, 0) for i in range(8)] + ([(S, L, 1)] if l == 0 else [])
                curv = None
                for (t0, ntok, v) in sbs:
                    if v != curv:
                        load_mods(v)
                        curv = v
                    ntb = ntok // 128
                    for tb in range(ntb):
                        xt, Bx = x1[tb]
                        sch.dma("sp", xt[:], xres[t0 + tb * 128:t0 + (tb + 1) * 128, :], writes=[Bx])
                        a_bf, Ba = ar.next()
                        norm_block(nt, xt, Bx, Gt, SHt, Bm, a_bf, Ba)
                        transpose_block(ptrr, a_bf, Ba, aT, BaT, tb, tb)
                    for j in range(NJ):
                        pg, Bpg = pgr.next()
                        pu, Bpu = pur.next()
                        for k in range(8):
                            sch.op("pe", P.matmul, pg[:, 0:ntok], lhsT=wg[:, k, j * 128:(j + 1) * 128], rhs=aT[:, k, 0:ntok],
                                   start=(k == 0), stop=(k == 7), reads=[Bwg, BaT], writes=[Bpg])
                        for k in range(8):
                            sch.op("pe", P.matmul, pu[:, 0:ntok], lhsT=wu[:, k, j * 128:(j + 1) * 128], rhs=aT[:, k, 0:ntok],
                                   start=(k == 0), stop=(k == 7), reads=[Bwu, BaT], writes=[Bpu])
                        sg, Bsg = sgr.next()
                        sch.op("act", A.activation, sg[:, 0:ntok], pg[:, 0:ntok], AF.Silu, reads=[Bpg], writes=[Bsg])
                        sch.op("dve", V.tensor_tensor, hT[:, j, 0:ntok], sg[:, 0:ntok], pu[:, 0:ntok], ALU.mult, reads=[Bsg, Bpu], writes=[BhT])
                    for tb in range(ntb):
                        xt, Bx = x1[tb]
                        rows = slice(t0 + tb * 128, t0 + (tb + 1) * 128)
                        tp, Bt = nt[3].next()
                        for nb in range(2):
                            pd, Bpd = pdr.next()
                            for j in range(NJ):
                                sch.op("pe", P.matmul, pd[:], lhsT=hT[:, j, tb * 128:(tb + 1) * 128], rhs=wd[:, j, nb * 512:(nb + 1) * 512],
                                       start=(j == 0), stop=(j == NJ - 1), reads=[BhT, Bwd], writes=[Bpd])
                            sch.op("dve", V.tensor_tensor, tp[:, nb * 512:(nb + 1) * 512], pd[:], g2[:, nb * 512:(nb + 1) * 512], ALU.mult,
                                   reads=[Bpd, Bm], writes=[Bt])
                        sch.op("pool", G.tensor_tensor, xt[:], xt[:], tp[:], ALU.add, reads=[Bx, Bt], writes=[Bx])
                        if not last:
                            sch.dma("pool", xres[rows, :], xt[:], reads=[Bx])
                        else:
                            sq, Bs = nt[2].next()
                            sch.op("act", A.activation, junk[:], xt[:], AF.Square, accum_out=sq[:, 0:1], reads=[Bx], writes=[nt[1], Bs])
                            sch.op("act", A.activation, sq[:, 1:2], sq[:, 0:1], AF.Sqrt, bias=EPS, scale=1.0 / D, reads=[Bs], writes=[Bs])
                            sch.op("dve", V.reciprocal, sq[:, 2:3], sq[:, 1:2], reads=[Bs], writes=[Bs])
                            sch.op("dve", V.scalar_tensor_tensor, xt[:], xt[:], sq[:, 2:3], fg[:], ALU.mult, ALU.mult,
                                   reads=[Bx, Bs, Bfg], writes=[Bx])
                            sch.dma("pool", out[rows, :], xt[:], reads=[Bx])
                sch.barrier()

        phases = [("wcast", phase_wcast, None), ("mod", phase_mod, None)]
        for l in range(DEPTH):
            phases += [(f"ab{l}", phase_ab, l), (f"c{l}", phase_c, l), (f"d{l}", phase_d, l), (f"e{l}", phase_e, l),
                       (f"f1{l}", phase_f1, l), (f"f2{l}", phase_f2, l)]
        for name, fn, arg in phases:
            if arg is None:
                fn()
            else:
                fn(arg)
            if stop is not None and name == stop:
                break
        sch.finish()
        nc._sched_stats = (sch.nops, sch.nwaits)
    return nc


def _prep(inputs):
    f32 = np.float32
    c = _consts()
    x = np.asarray(inputs["x"], f32)
    ctx = np.asarray(inputs["ctx"], f32)
    cvec = np.asarray(inputs["c"], f32)
    cctx = np.asarray(inputs["c_ctx"], f32)
    lb = np.asarray(inputs["hgrn_lb"], f32)
    lbT = np.ascontiguousarray(lb.reshape(2, 2, 2, 128).transpose(3, 0, 1, 2).reshape(128, 8))
    ng = np.asarray(inputs["hgrn_norm_g"], f32)
    ngT = np.ascontiguousarray(np.concatenate([ng, ng], axis=1).T)
    rpb = np.asarray(inputs["na_rpb"], f32)
    nabias = np.stack([_na_bias_tiles(rpb[l]) for l in range(DEPTH)])
    shared = {
        "w_mod": np.asarray(inputs["w_mod"], f32), "b_mod": np.asarray(inputs["b_mod"], f32),
        "norm1_g": np.asarray(inputs["norm1_g"], f32), "w_in": np.asarray(inputs["w_in"], f32),
        "fourier_w": np.asarray(inputs["fourier_w"], f32), "lbT": lbT, "ngT": ngT, "nabias": nabias,
        "w_out": np.asarray(inputs["w_out"], f32), "norm2_g": np.asarray(inputs["norm2_g"], f32),
        "w_gate": np.asarray(inputs["w_ffn_gate"], f32), "w_up": np.asarray(inputs["w_ffn_up"], f32),
        "w_down": np.asarray(inputs["w_ffn_down"], f32), "final_g": np.asarray(inputs["final_norm_g"], f32),
        "C4": c["C4"], "S4": c["S4"], "C2": c["C2"], "S2": c["S2"], "C64bd": c["C64bd"], "S64bdn": c["S64bdn"],
        "ident": c["ident"], "bdmask": c["bdmask"], "bd64": c["bd64"], "mfwd": c["mfwd"], "mbwd": c["mbwd"], "cm3": c["cm3"],
    }
    maps = []
    for b in range(8):
        m = dict(shared)
        m["xin"] = np.ascontiguousarray(np.concatenate([x[b], ctx[b]], axis=0))
        cv = np.stack([cvec[b], cctx], axis=0)
        m["cT"] = np.ascontiguousarray(cv.reshape(2, 8, 128).transpose(2, 1, 0).reshape(128, 16))
        maps.append(m)
    return maps


def kernel(**inputs):
    maps = _prep(inputs)
    nc = build()
    res = run_bass_kernel_spmd(nc, maps, core_ids=list(range(8)))
    return np.stack([np.asarray(r["out"], np.float32) for r in res.results], axis=0)
```

```python
import numpy as np
import ml_dtypes
from contextlib import ExitStack
import concourse.bass as bass
import concourse.mybir as mybir
from concourse.bass_utils import run_bass_kernel_spmd

F32 = mybir.dt.float32
BF16 = mybir.dt.bfloat16
AF = mybir.ActivationFunctionType
ALU = mybir.AluOpType

D = 1024
S = 4096
L = 256
T = S + L
NB = T // 128
DFF = 2816
NJ = DFF // 128
INW = 3072
EPS = 1e-6
NEG = -1e30
DEPTH = 2


class _Op:
    __slots__ = ("eng", "fn", "args", "kw", "deps", "signal", "sem", "val", "dma", "epoch")


class Buf:
    __slots__ = ("w", "r")

    def __init__(self):
        self.w = {}
        self.r = {}


class Sched:
    CENG = ("pe", "act", "dve", "pool")
    SEMCAP = 30000

    def __init__(self, nc, es):
        self.nc = nc
        self.eobj = dict(pe=nc.tensor, act=nc.scalar, dve=nc.vector, pool=nc.gpsimd, sp=nc.sync)
        self.pending = []
        self.lastc = {}
        self.dma_since = []
        self.bar_deps = {}
        ncs = dict(pe=3, act=3, dve=4, pool=2)
        self.csem = {e: [es.enter_context(nc.semaphore(f"c{e}{i}")) for i in range(ncs[e])] for e in self.CENG}
        self.ccount = {e: 0 for e in self.CENG}
        self.dsem = {q: [es.enter_context(nc.semaphore(f"d{q}{i}")) for i in range(8)] for q in ("sp", "pool")}
        self.dcount = {q: [0] * 8 for q in ("sp", "pool")}
        self.drr = {q: 0 for q in ("sp", "pool")}
        self.waited = {e: {} for e in self.eobj}
        self.nops = 0
        self.nwaits = 0
        self.epoch = 0
        self.pe_mode = "f"

    def op(self, eng, fn, *args, reads=(), writes=(), dma=False, force=False, **kw):
        o = _Op()
        o.eng, o.fn, o.args, o.kw, o.dma = eng, fn, args, kw, dma
        o.signal = dma or force
        o.sem = None
        o.val = 0
        o.epoch = self.epoch
        deps = {}
        if eng == "pe":
            mode = kw.pop("pemode", "f")
            if mode != self.pe_mode and "pe" in self.lastc:
                deps[id(self.lastc["pe"])] = self.lastc["pe"]
            self.pe_mode = mode

        def need(d, same_ok, bar=False):
            if d.epoch < self.epoch and not bar:
                return
            if same_ok and (not d.dma) and (not dma) and d.eng == eng:
                return
            deps[id(d)] = d

        for b in reads:
            for d in b.w.values():
                need(d, eng == "pe")
        for b in writes:
            if b.r:
                for d in b.r.values():
                    need(d, True)
                for d in b.w.values():
                    need(d, True)
                b.r = {}
                b.w = {}
        for d in self.bar_deps.pop(eng, ()):
            need(d, True, True)
        for d in deps.values():
            d.signal = True
        o.deps = list(deps.values())
        key = ("d", id(o)) if dma else eng
        for b in reads:
            b.r[key] = o
        for b in writes:
            b.w[key] = o
        if dma:
            self.dma_since.append(o)
        else:
            self.lastc[eng] = o
        self.pending.append(o)
        return o

    def dma(self, q, out, in_, reads=(), writes=()):
        fn = self.nc.sync.dma_start if q == "sp" else self.nc.gpsimd.dma_start
        return self.op(q, fn, reads=reads, writes=writes, dma=True, out=out, in_=in_)

    def flush(self):
        for o in self.pending:
            e = self.eobj[o.eng]
            w = self.waited[o.eng]
            for d in o.deps:
                assert d.sem is not None, "dependency on unsignalled op"
                k = id(d.sem)
                if w.get(k, 0) >= d.val:
                    continue
                e.wait_ge(d.sem, d.val)
                w[k] = d.val
                self.nwaits += 1
            ins = o.fn(*o.args, **o.kw)
            self.nops += 1
            if o.dma:
                q = o.eng
                i = self.drr[q]
                self.drr[q] = (i + 1) % 8
                self.dcount[q][i] += 16
                o.sem, o.val = self.dsem[q][i], self.dcount[q][i]
                ins.then_inc(o.sem, 16)
            elif o.signal:
                n = self.ccount[o.eng]
                self.ccount[o.eng] = n + 1
                sems = self.csem[o.eng]
                si = n // self.SEMCAP
                assert si < len(sems), "out of compute semaphores"
                o.sem, o.val = sems[si], n % self.SEMCAP + 1
                ins.then_inc(o.sem, 1)
            o.fn = o.args = o.kw = None
        self.pending = []

    def barrier(self):
        deps = [self.lastc[e] for e in self.CENG if e in self.lastc] + self.dma_since
        for d in deps:
            d.signal = True
        self.flush()
        old = self.bar_deps
        self.bar_deps = {e: list(deps) + list(old.get(e, ())) for e in self.eobj}
        self.dma_since = []
        self.epoch += 1

    def finish(self):
        self.barrier()
        sp = self.nc.sync
        for d in self.bar_deps["sp"]:
            k = id(d.sem)
            if self.waited["sp"].get(k, 0) >= d.val:
                continue
            sp.wait_ge(d.sem, d.val)
            self.waited["sp"][k] = d.val


class Rot:
    def __init__(self, tiles):
        self.t = [(t, Buf()) for t in tiles]
        self.i = 0

    def next(self):
        r = self.t[self.i]
        self.i = (self.i + 1) % len(self.t)
        return r


_CONST = {}


def _consts():
    if _CONST:
        return _CONST
    bf = ml_dtypes.bfloat16
    t = np.arange(S, dtype=np.int64)
    m = (t[:, None] * t[None, :]) % S
    ang = 2.0 * np.pi * m.astype(np.float64) / S
    _CONST["C4"] = (np.cos(ang) / 64.0).astype(np.float32).astype(bf)
    _CONST["S4"] = (np.sin(ang) / 64.0).astype(np.float32).astype(bf)
    del ang, m
    t = np.arange(L, dtype=np.int64)
    ang = 2.0 * np.pi * ((t[:, None] * t[None, :]) % L).astype(np.float64) / L
    _CONST["C2"] = (np.cos(ang) / 16.0).astype(np.float32).astype(bf)
    _CONST["S2"] = (np.sin(ang) / 16.0).astype(np.float32).astype(bf)
    t = np.arange(64, dtype=np.int64)
    ang = 2.0 * np.pi * ((t[:, None] * t[None, :]) % 64).astype(np.float64) / 64
    c64 = np.cos(ang) / 8.0
    s64 = np.sin(ang) / 8.0
    z = np.zeros((64, 64))
    _CONST["C64bd"] = np.block([[c64, z], [z, c64]]).astype(np.float32)
    _CONST["S64bdn"] = (-np.block([[s64, z], [z, s64]])).astype(np.float32)
    _CONST["ident"] = np.eye(128, dtype=np.float32).astype(bf)
    bd = np.zeros((128, 128), np.float32)
    bd[:64, :64] = 1.0
    bd[64:, 64:] = 1.0
    _CONST["bdmask"] = bd
    _CONST["bd64"] = (bd / 64.0).astype(np.float32)
    s_ = np.arange(128)[:, None]
    t_ = np.arange(128)[None, :]
    same = (s_ // 32) == (t_ // 32)
    _CONST["mfwd"] = (same & (s_ <= t_)).astype(np.float32)
    _CONST["mbwd"] = (same & (s_ >= t_)).astype(np.float32)
    cm = np.zeros((128, 4, 128), np.float32)
    for c_ in range(4):
        cm[32 * c_:32 * c_ + 32, c_, :] = 1.0
    _CONST["cm3"] = cm.reshape(128, 512)
    return _CONST


def _na_bias_tiles(rpb):
    H = rpb.shape[0]
    kc = np.arange(64)[:, None]
    qc = np.arange(64)[None, :]
    wc = np.clip(qc - 8, 0, 48)
    colvalid = (kc >= wc) & (kc < wc + 16)
    cidx = np.clip(kc - qc + 15, 0, 30)
    specs = [(8, 8 - 4 + 2 * po) for po in range(8)] + [(0, 2 * p) for p in range(6)] + [(56, 52 + 2 * p) for p in range(6)]
    out = np.full((H, 128, 20, 512), NEG, np.float32)
    for ti, (r0, kr0) in enumerate(specs):
        for kr2 in range(2):
            kr = kr0 + kr2
            for qr in range(8):
                r = r0 + qr
                rs = min(max(r - 4, 0), 56)
                if not (rs <= kr < rs + 8):
                    continue
                ridx = kr - r + 7
                blk = np.where(colvalid[None], rpb[:, ridx][:, cidx], NEG)
                out[:, kr2 * 64:(kr2 + 1) * 64, ti, qr * 64:(qr + 1) * 64] = blk
    return out.reshape(H, 128, 20 * 512)


def build(debug=False, stop=None):
    nc = bass.Bass("TRN2", target_bir_lowering=False)
    skind = "ExternalOutput" if debug else "Internal"

    def din(name, shape, dt=F32):
        return nc.dram_tensor(name, list(shape), dt, kind="ExternalInput").ap()

    def dscr(name, shape, dt=F32):
        return nc.dram_tensor(name, list(shape), dt, kind=skind).ap()

    xin = din("xin", [T, D])
    cT = din("cT", [128, 16])
    w_mod = din("w_mod", [DEPTH, D, 6 * D])
    b_mod = din("b_mod", [DEPTH, 6 * D])
    norm1_g = din("norm1_g", [DEPTH, D])
    w_in = din("w_in", [DEPTH, D, INW])
    fourier_w = din("fourier_w", [DEPTH, 4, 64, 64])
    lbT = din("lbT", [128, 8])
    ngT = din("ngT", [128, DEPTH])
    nabias = din("nabias", [DEPTH, 8, 128, 20 * 512])
    w_out = din("w_out", [DEPTH, D, D])
    norm2_g = din("norm2_g", [DEPTH, D])
    w_gate = din("w_gate", [DEPTH, D, DFF])
    w_up = din("w_up", [DEPTH, D, DFF])
    w_down = din("w_down", [DEPTH, DFF, D])
    final_g = din("final_g", [D])
    C4 = din("C4", [S, S], BF16)
    S4 = din("S4", [S, S], BF16)
    C2 = din("C2", [L, L], BF16)
    S2 = din("S2", [L, L], BF16)
    C64bd_d = din("C64bd", [128, 128])
    S64bdn_d = din("S64bdn", [128, 128])
    ident_d = din("ident", [128, 128], BF16)
    bdmask_d = din("bdmask", [128, 128])
    bd64_d = din("bd64", [128, 128])
    mfwd_d = din("mfwd", [128, 128])
    mbwd_d = din("mbwd", [128, 128])
    cm3_d = din("cm3", [128, 512])
    out = nc.dram_tensor("out", [S, D], F32, kind="ExternalOutput").ap()

    mod2d = dscr("mod2d", [2 * DEPTH, 6 * D])
    fz = dscr("fz", [512, T])
    fb = dscr("fb", [1536, T], BF16)
    tm = dscr("tm", [T, 1024], BF16)
    catT = dscr("catT", [1024, T], BF16)
    xres = dscr("xres", [T, D])
    w_in_b = dscr("w_in_b", [DEPTH, D, INW], BF16)
    w_out_b = dscr("w_out_b", [DEPTH, D, D], BF16)
    w_gate_b = dscr("w_gate_b", [DEPTH, D, DFF], BF16)
    w_up_b = dscr("w_up_b", [DEPTH, D, DFF], BF16)
    w_down_b = dscr("w_down_b", [DEPTH, DFF, D], BF16)

    with ExitStack() as ges:
        sch = Sched(nc, ges)
        V, A, P, G = nc.vector, nc.scalar, nc.tensor, nc.gpsimd

        uid = [0]

        def sbt(es, name, shape, dt=F32):
            uid[0] += 1
            return es.enter_context(nc.sbuf_tensor(f"s{uid[0]}_{name}", list(shape), dt))

        def pst(es, name, shape, dt=F32):
            uid[0] += 1
            return es.enter_context(nc.psum_tensor(f"p{uid[0]}_{name}", list(shape), dt))

        ident = sbt(ges, "ident", [128, 128], BF16)
        bdmask = sbt(ges, "bdmask", [128, 128])
        onesb = sbt(ges, "onesb", [128, 64], BF16)
        B_const = Buf()
        sch.dma("sp", ident[:], ident_d, writes=[B_const])
        sch.dma("sp", bdmask[:], bdmask_d, writes=[B_const])
        sch.op("pool", G.memset, onesb[:], 1.0, writes=[B_const], force=True)

        def evac(i, outap, inap, reads, writes, scale=None):
            if i % 2 == 0:
                if scale is None:
                    sch.op("act", A.copy, outap, inap, reads=reads, writes=writes)
                else:
                    sch.op("act", A.mul, outap, inap, scale, reads=reads, writes=writes)
            else:
                if scale is None:
                    sch.op("dve", V.tensor_copy, outap, inap, reads=reads, writes=writes)
                else:
                    sch.op("dve", V.tensor_scalar, outap, inap, scale, None, ALU.mult, reads=reads, writes=writes)

        def phase_mod():
            with ExitStack() as es:
                cTt = sbt(es, "cTt", [128, 16])
                sT = sbt(es, "sT", [128, 16])
                mod2s = sbt(es, "mod2s", [2, 6 * D])
                bt = sbt(es, "bt", [2, 6 * D])
                wm = Rot([sbt(es, f"wm{i}", [128, 8, 512]) for i in range(3)])
                pm = Rot([pst(es, f"pm{i}", [128, 512]) for i in range(2)])
                B_c, B_s, B_m, B_b = Buf(), Buf(), Buf(), Buf()
                sch.dma("sp", cTt[:], cT, writes=[B_c])
                sch.op("act", A.activation, sT[:], cTt[:], AF.Silu, reads=[B_c], writes=[B_s])
                for l in range(DEPTH):
                    sch.dma("sp", bt[:], b_mod[l].partition_broadcast(2), writes=[B_b])
                    for nb in range(12):
                        wt, Bw = wm.next()
                        sch.dma("sp", wt[:], w_mod[l][:, nb * 512:(nb + 1) * 512].rearrange("(k p) n -> p k n", p=128), writes=[Bw])
                        pt, Bp = pm.next()
                        for k in range(8):
                            sch.op("pe", P.matmul, pt[0:2, :], lhsT=sT[:, 2 * k:2 * k + 2], rhs=wt[:, k, :],
                                   start=(k == 0), stop=(k == 7), pemode="m2", reads=[B_s, Bw], writes=[Bp])
                        sch.op("dve", V.tensor_tensor, mod2s[:, nb * 512:(nb + 1) * 512], pt[0:2, :], bt[:, nb * 512:(nb + 1) * 512],
                               ALU.add, reads=[Bp, B_b], writes=[B_m])
                    sch.dma("pool", mod2d[2 * l:2 * l + 2, :], mod2s[:], reads=[B_m])
                sch.barrier()

        def norm_block(es_tiles, xt, Bx, Gt, SHt, B_mod, a_bf, Ba):
            junk, Bj, ssqr, tmpr = es_tiles
            sq, Bs = ssqr.next()
            sch.op("act", A.activation, junk[:], xt[:], AF.Square, accum_out=sq[:, 0:1], reads=[Bx], writes=[Bj, Bs])
            sch.op("act", A.activation, sq[:, 1:2], sq[:, 0:1], AF.Sqrt, bias=EPS, scale=1.0 / D, reads=[Bs], writes=[Bs])
            sch.op("dve", V.reciprocal, sq[:, 2:3], sq[:, 1:2], reads=[Bs], writes=[Bs])
            tp, Bt = tmpr.next()
            sch.op("dve", V.scalar_tensor_tensor, tp[:], xt[:], sq[:, 2:3], Gt[:], ALU.mult, ALU.mult,
                   reads=[Bx, Bs, B_mod], writes=[Bt])
            sch.op("dve", V.tensor_tensor, a_bf[:], tp[:], SHt[:], ALU.add, reads=[Bt, B_mod], writes=[Ba])

        def transpose_block(ptrr, a_bf, Ba, aT, BaT, tb, ei):
            for half in range(2):
                pt, Bp = ptrr.next()
                for j in range(4):
                    c = (half * 4 + j) * 128
                    sch.op("pe", P.transpose, pt[:, j, :], a_bf[:, c:c + 128], ident[:], reads=[Ba, B_const], writes=[Bp])
                evac(ei + half, aT[:, half * 4:(half + 1) * 4, tb * 128:(tb + 1) * 128], pt[:], [Bp], [BaT])

        def load_weight(es, name, src_kpn, nk, ncols):
            wb = sbt(es, name, [128, nk, ncols], BF16)
            Bw = Buf()
            for k in range(nk):
                sch.dma("sp", wb[:, k, :], src_kpn[:, k, :], writes=[Bw])
            return wb, Bw

        def phase_wcast():
            with ExitStack() as es:
                sin = Rot([sbt(es, f"wci{i}", [128, 2048]) for i in range(3)])
                sout = Rot([sbt(es, f"wco{i}", [128, 2048], BF16) for i in range(3)])
                i = 0
                for l in range(DEPTH):
                    for (src, dst, R, C) in ((w_in[l], w_in_b[l], D, INW), (w_out[l], w_out_b[l], D, D), (w_gate[l], w_gate_b[l], D, DFF),
                                             (w_up[l], w_up_b[l], D, DFF), (w_down[l], w_down_b[l], DFF, D)):
                        for r0 in range(0, R, 128):
                            for c0 in range(0, C, 2048):
                                c1 = min(C, c0 + 2048)
                                n = c1 - c0
                                ti, Bi = sin.next()
                                to, Bo = sout.next()
                                sch.dma("sp", ti[:, 0:n], src[r0:r0 + 128, c0:c1], writes=[Bi])
                                e = ("pool", "dve", "act")[i % 3]
                                i += 1
                                if e == "pool":
                                    sch.op("pool", G.tensor_copy, to[:, 0:n], ti[:, 0:n], reads=[Bi], writes=[Bo])
                                elif e == "dve":
                                    sch.op("dve", V.tensor_copy, to[:, 0:n], ti[:, 0:n], reads=[Bi], writes=[Bo])
                                else:
                                    sch.op("act", A.copy, to[:, 0:n], ti[:, 0:n], reads=[Bi], writes=[Bo])
                                sch.dma("pool", dst[r0:r0 + 128, c0:c1], to[:, 0:n], reads=[Bo])
                sch.barrier()

        def mod_tiles(es, l, v, gsrc, i_sh, i_sc, tag):
            Gt = sbt(es, f"G{tag}", [128, D])
            SHt = sbt(es, f"SH{tag}", [128, D])
            Bm = Buf()
            sch.dma("pool", SHt[:], mod2d[2 * l + v, i_sh * D:(i_sh + 1) * D].partition_broadcast(128), writes=[Bm])
            sch.dma("pool", Gt[:], gsrc.partition_broadcast(128), writes=[Bm])
            return Gt, SHt, Bm

        def phase_ab(l):
            xsrc = xin if l == 0 else xres
            with ExitStack() as es:
                wbf, Bw = load_weight(es, "winbf", w_in_b[l].rearrange("(k p) n -> p k n", p=128), 8, INW)
                mods = []
                sct = sbt(es, "sct", [128, D])
                Bsc = Buf()
                for v in range(2):
                    Gt, SHt, Bm = mod_tiles(es, l, v, norm1_g[l], 0, 1, f"1_{v}")
                    sch.dma("pool", sct[:], mod2d[2 * l + v, D:2 * D].partition_broadcast(128), writes=[Bsc])
                    sch.op("dve", V.scalar_tensor_tensor, Gt[:], sct[:], 1.0, Gt[:], ALU.add, ALU.mult, reads=[Bsc, Bm], writes=[Bm])
                    mods.append((Gt, SHt, Bm))
                junk = sbt(es, "junk", [128, D], BF16)
                nt = (junk, Buf(), Rot([sbt(es, f"ssq{i}", [128, 4]) for i in range(4)]),
                      Rot([sbt(es, f"ntmp{i}", [128, D]) for i in range(2)]))
                xr = Rot([sbt(es, f"xt{i}", [128, D]) for i in range(3)])
                ar = Rot([sbt(es, f"abf{i}", [128, D], BF16) for i in range(2)])
                aTr = Rot([sbt(es, f"aT{i}", [128, 8, 512], BF16) for i in range(2)])
                sfr = Rot([sbt(es, f"sf{i}", [128, 512]) for i in range(2)])
                sbr = Rot([sbt(es, f"sb{i}", [128, 512], BF16) for i in range(3)])
                tmr = Rot([sbt(es, f"tms{i}", [128, 1024], BF16) for i in range(2)])
                ptrr = Rot([pst(es, f"ptr{i}", [128, 4, 128], BF16) for i in range(2)])
                pfr = Rot([pst(es, f"pf{i}", [128, 512]) for i in range(3)])
                ptmr = Rot([pst(es, f"ptm{i}", [128, 512]) for i in range(2)])
                fm = []
                for c0, dst, r0, sc in ((256, fb, 0, None), (512, fz, 0, None), (768, fz, 256, None), (1280, fb, 256, None),
                                        (1536, fb, 512, 0.125), (2048, fb, 1024, None)):
                    wd = 512 if c0 >= 1536 else 256
                    for g in range(wd // 128):
                        fm.append((c0 + g * 128, dst, r0 + g * 128, sc))
                sbs = [(i * 512, 512, 0) for i in range(8)] + [(S, L, 1)]
                ei = 0
                for (t0, ntok, v) in sbs:
                    Gt, SHt, Bm = mods[v]
                    aT, BaT = aTr.next()
                    for tb in range(ntok // 128):
                        xt, Bx = xr.next()
                        sch.dma("sp", xt[:], xsrc[t0 + tb * 128:t0 + (tb + 1) * 128, :], writes=[Bx])
                        a_bf, Ba = ar.next()
                        norm_block(nt, xt, Bx, Gt, SHt, Bm, a_bf, Ba)
                        transpose_block(ptrr, a_bf, Ba, aT, BaT, tb, ei)
                    for (c0, dst, r0, sc) in fm:
                        pf, Bp = pfr.next()
                        for k in range(8):
                            sch.op("pe", P.matmul, pf[:, 0:ntok], lhsT=wbf[:, k, c0:c0 + 128], rhs=aT[:, k, 0:ntok],
                                   start=(k == 0), stop=(k == 7), reads=[Bw, BaT], writes=[Bp])
                        if dst is fz:
                            st, Bs = sfr.next()
                        else:
                            st, Bs = sbr.next()
                        ei += 1
                        evac(ei, st[:, 0:ntok], pf[:, 0:ntok], [Bp], [Bs], scale=sc)
                        sch.dma("pool", dst[r0:r0 + 128, t0:t0 + ntok], st[:, 0:ntok], reads=[Bs])
                    for tb in range(ntok // 128):
                        tms, Bts = tmr.next()
                        tsl = slice(tb * 128, (tb + 1) * 128)
                        p0, Bp0 = ptmr.next()
                        for (cc, oc) in ((0, 0), (1024, 256)):
                            for k in range(8):
                                sch.op("pe", P.matmul, p0[:, oc:oc + 256], lhsT=aT[:, k, tsl], rhs=wbf[:, k, cc:cc + 256],
                                       start=(k == 0), stop=(k == 7), reads=[Bw, BaT], writes=[Bp0])
                        ei += 1
                        evac(ei, tms[:, 0:512], p0[:], [Bp0], [Bts])
                        p1, Bp1 = ptmr.next()
                        for k in range(8):
                            sch.op("pe", P.matmul, p1[:], lhsT=aT[:, k, tsl], rhs=wbf[:, k, 2560:3072],
                                   start=(k == 0), stop=(k == 7), reads=[Bw, BaT], writes=[Bp1])
                        ei += 1
                        evac(ei, tms[:, 512:1024], p1[:], [Bp1], [Bts])
                        sch.dma("pool", tm[t0 + tb * 128:t0 + (tb + 1) * 128, :], tms[:], reads=[Bts])
                sch.barrier()

        def phase_c(l):
            with ExitStack() as es:
                U = sbt(es, "fU", [128, NB, 256], BF16)
                BU = Buf()
                sch.dma("sp", U[:], tm[:, 0:256].rearrange("(k p) c -> p k c", p=128), writes=[BU])
                c64 = sbt(es, "c64", [128, 2, 128])
                wf = sbt(es, "wf", [128, 2, 64])
                Mbd = sbt(es, "Mbd", [128, 2, 2, 128], BF16)
                Bc, Bwf, BM = Buf(), Buf(), Buf()
                sch.dma("sp", c64[:, 0, :], C64bd_d, writes=[Bc])
                sch.dma("sp", c64[:, 1, :], S64bdn_d, writes=[Bc])
                for cb in range(2):
                    sch.dma("sp", wf[:, cb, :], fourier_w[l][2 * cb:2 * cb + 2].rearrange("g c d -> (g c) d"), writes=[Bwf])
                sch.op("pool", G.memset, Mbd[:], 0.0, writes=[BM])
                pmr = Rot([pst(es, f"pM{i}", [128, 512]) for i in range(1)])
                pM, BpM = pmr.next()
                for w in range(2):
                    for cb in range(2):
                        o0 = (w * 2 + cb) * 64
                        sch.op("pe", P.matmul, pM[:, o0:o0 + 64], lhsT=c64[:, w, :], rhs=wf[:, cb, :], start=True, stop=True,
                               pemode="f32", reads=[Bc, Bwf], writes=[BpM])
                for w in range(2):
                    for cb in range(2):
                        o0 = (w * 2 + cb) * 64
                        sch.op("dve", V.tensor_copy, Mbd[0:64, w, cb, 0:64], pM[0:64, o0:o0 + 64], reads=[BpM, BM], writes=[BM])
                        sch.op("dve", V.tensor_copy, Mbd[64:128, w, cb, 64:128], pM[64:128, o0:o0 + 64], reads=[BpM, BM], writes=[BM])
                csr = Rot([sbt(es, f"cs{i}", [128, 2, 4, 512], BF16) for i in range(3)])
                pqr = Rot([sbt(es, f"pq{i}", [128, 4, 512], BF16) for i in range(2)])
                osr = Rot([sbt(es, f"fo{i}", [128, 512], BF16) for i in range(2)])
                pacc = [pst(es, f"pacc{i}", [128, 512]) for i in range(4)]
                Bacc = [Buf() for _ in range(4)]
                por = Rot([pst(es, f"po{i}", [128, 512]) for i in range(2)])

                def mix(pq, Bpq, n, tcol0):
                    for cb in range(2):
                        po, Bpo = por.next()
                        sch.op("pe", P.matmul, po[:, 0:n], lhsT=Mbd[:, 0, cb, :], rhs=pq[:, cb, 0:n], start=True, stop=False,
                               reads=[BM, Bpq], writes=[Bpo])
                        sch.op("pe", P.matmul, po[:, 0:n], lhsT=Mbd[:, 1, cb, :], rhs=pq[:, 2 + cb, 0:n], start=False, stop=True,
                               reads=[BM, Bpq], writes=[Bpo])
                        ot, Bo = osr.next()
                        evac(cb, ot[:, 0:n], po[:, 0:n], [Bpo], [Bo])
                        sch.dma("pool", catT[cb * 128:(cb + 1) * 128, tcol0:tcol0 + n], ot[:, 0:n], reads=[Bo])

                for nb in range(8):
                    for kg in range(8):
                        cs, Bcs = csr.next()
                        sch.dma("sp", cs[:, 0], C4[kg * 512:(kg + 1) * 512, nb * 512:(nb + 1) * 512].rearrange("(k p) n -> p k n", p=128), writes=[Bcs])
                        sch.dma("sp", cs[:, 1], S4[kg * 512:(kg + 1) * 512, nb * 512:(nb + 1) * 512].rearrange("(k p) n -> p k n", p=128), writes=[Bcs])
                        for kk in range(4):
                            k = kg * 4 + kk
                            for w in range(2):
                                for cb in range(2):
                                    i = w * 2 + cb
                                    sch.op("pe", P.matmul, pacc[i][:], lhsT=U[:, k, cb * 128:(cb + 1) * 128], rhs=cs[:, w, kk, :],
                                           start=(k == 0), stop=(k == 31), reads=[BU, Bcs], writes=[Bacc[i]])
                    pq, Bpq = pqr.next()
                    for i in range(4):
                        evac(i, pq[:, i, :], pacc[i][:], [Bacc[i]], [Bpq])
                    mix(pq, Bpq, 512, nb * 512)
                if l == 0:
                    c2 = sbt(es, "c2", [128, 2, 2, 256], BF16)
                    Bc2 = Buf()
                    sch.dma("sp", c2[:, 0], C2.rearrange("(k p) n -> p k n", p=128), writes=[Bc2])
                    sch.dma("sp", c2[:, 1], S2.rearrange("(k p) n -> p k n", p=128), writes=[Bc2])
                    for w in range(2):
                        for cb in range(2):
                            i = w * 2 + cb
                            for k in range(2):
                                sch.op("pe", P.matmul, pacc[i][:, 0:256], lhsT=U[:, 32 + k, cb * 128:(cb + 1) * 128], rhs=c2[:, w, k, :],
                                       start=(k == 0), stop=(k == 1), reads=[BU, Bc2], writes=[Bacc[i]])
                    pq, Bpq = pqr.next()
                    for i in range(4):
                        evac(i, pq[:, i, 0:256], pacc[i][:, 0:256], [Bacc[i]], [Bpq])
                    mix(pq, Bpq, 256, S)
                sch.barrier()

        def phase_d(l):
            HW = T // 2
            NCH = T // 32
            with ExitStack() as es:
                lbt = sbt(es, "lbt", [128, 8])
                lbv = sbt(es, "lbv", [128, 8])
                ngt = sbt(es, "ngt", [128, DEPTH])
                mdir = sbt(es, "mdir", [128, 2, 128])
                bd64 = sbt(es, "bd64", [128, 128])
                cm3 = sbt(es, "cm3", [128, 4, 128])
                epsc = sbt(es, "epsc", [128, 1])
                Bl, Bmd = Buf(), Buf()
                sch.op("pool", G.memset, epsc[:], EPS, writes=[Bmd])
                sch.dma("sp", lbt[:], lbT, writes=[Bl])
                sch.dma("sp", ngt[:], ngT, writes=[Bl])
                sch.dma("sp", mdir[:, 0, :], mfwd_d, writes=[Bmd])
                sch.dma("sp", mdir[:, 1, :], mbwd_d, writes=[Bmd])
                sch.dma("sp", bd64[:], bd64_d, writes=[Bmd])
                sch.dma("sp", cm3[:], cm3_d.rearrange("p (c n) -> p c n", n=128), writes=[Bmd])
                if l == 0:
                    sch.op("pool", G.memset, lbv[:, 0:4], 0.0, reads=[Bl], writes=[Bl])
                    sch.op("pool", G.memset, lbv[:, 4:8], 1.0, reads=[Bl], writes=[Bl])
                else:
                    ex = sbt(es, "lbex", [128, 8])
                    sch.op("act", A.activation, ex[:], lbt[:], AF.Exp, reads=[Bl], writes=[Bl])
                    sch.op("dve", V.tensor_tensor, lbv[:, 4:8], ex[:, 0:4], ex[:, 4:8], ALU.add, reads=[Bl], writes=[Bl])
                    sch.op("dve", V.reciprocal, lbv[:, 4:8], lbv[:, 4:8], reads=[Bl], writes=[Bl])
                    sch.op("dve", V.tensor_tensor, lbv[:, 0:4], ex[:, 4:8], lbv[:, 4:8], ALU.mult, reads=[Bl], writes=[Bl])
                    sch.op("dve", V.tensor_scalar, lbv[:, 4:8], lbv[:, 0:4], -1.0, 1.0, ALU.mult, ALU.add, reads=[Bl], writes=[Bl])
                for hp in range(2):
                    with ExitStack() as hs:
                        vtm = sbt(hs, "hv", [128, NB, 128], BF16)
                        vm = sbt(hs, "hvm", [128, 2, NB, 128], BF16)
                        Bv, Bvm = Buf(), Buf()
                        sch.dma("sp", vtm[:], tm[:, 256 + hp * 128:256 + (hp + 1) * 128].rearrange("(k p) c -> p k c", p=128), writes=[Bv])
                        sch.op("pool", G.memset, vm[:], 0.0, writes=[Bvm])
                        sch.op("pool", G.tensor_copy, vm[:, 0, :, 0:64], vtm[:, :, 0:64], reads=[Bv, Bvm], writes=[Bvm])
                        sch.op("pool", G.tensor_copy, vm[:, 1, :, 64:128], vtm[:, :, 64:128], reads=[Bv, Bvm], writes=[Bvm])
                        chains = []
                        for d in range(2):
                            ch = dict(d=d)
                            ch["qdm"] = sbt(hs, f"qdm{d}", [128, 2, T], BF16)
                            ch["kd"] = sbt(hs, f"kd{d}", [128, T], BF16)
                            ch["kst"] = sbt(hs, f"kst{d}", [128, NB, 128], BF16)
                            ch["dec"] = sbt(hs, f"dec{d}", [128, NCH])
                            ch["B"] = {k: Buf() for k in ("qdm", "kd", "kst", "dec", "o")}
                            sch.op("pool", G.memset, ch["qdm"][:], 0.0, writes=[ch["B"]["qdm"]])
                            chains.append(ch)
                        with ExitStack() as gs:
                            Asets = [([sbt(gs, f"gA{u}{i}", [128, HW]) for i in range(4)], [Buf() for _ in range(4)],
                                      sbt(gs, f"gq{u}", [128, HW], BF16), Buf(), sbt(gs, f"gks{u}", [128, HW], BF16), Buf(),
                                      sbt(gs, f"gtot{u}", [128, HW // 32]), Buf()) for u in range(2)]
                            unit = 0
                            msk = sbt(gs, "gmsk", [128, HW])
                            Bmsk = Buf()
                            sch.op("pool", G.memset, msk[:], 1.0, writes=[Bmsk])
                            sch.op("pool", G.memset, msk[:].rearrange("p (n c) -> p n c", c=32)[:, :, 0:1], 0.0, reads=[Bmsk], writes=[Bmsk])
                            ptk = Rot([pst(gs, f"ptk{i}", [128, 4, 128], BF16) for i in range(2)])
                            ei_ = [0]

                            def unit_gen(ch, h, uset):
                                d = ch["d"]
                                B = ch["B"]
                                lbc = lbv[:, d * 2 + hp:d * 2 + hp + 1]
                                omc = lbv[:, 4 + d * 2 + hp:4 + d * 2 + hp + 1]
                                Ar, BA, qh, Bq, ksh, Bks, tot, Btot = uset
                                c0 = h * HW
                                cs = slice(c0, c0 + HW)
                                A1, A2, A3, A4 = Ar
                                B1, B2, B3, B4 = BA
                                zr = 256 * d + hp * 128
                                sch.dma("sp", A1[:], fz[zr:zr + 128, cs], writes=[B1])
                                sch.dma("sp", qh[:], fb[hp * 128:(hp + 1) * 128, cs], writes=[Bq])
                                sch.op("act", A.activation, A1[:], A1[:], AF.Sigmoid, reads=[B1], writes=[B1])
                                yield
                                sch.op("act", A.activation, A1[:], A1[:], AF.Identity, bias=lbc, scale=omc, reads=[B1, Bl], writes=[B1])
                                yield
                                sch.op("act", A.activation, A2[:], A1[:], AF.Ln, reads=[B1], writes=[B2])
                                yield
                                sch.op("act", A.activation, A1[:], A1[:], AF.Identity, bias=1.0, scale=-1.0, reads=[B1, B2], writes=[B1])
                                sch.op("dve", V.tensor_tensor_scan, A3[:], msk[:], A2[:], 0.0, ALU.mult, ALU.add, reads=[Bmsk, B2], writes=[B3])
                                yield
                                A3v = A3[:].rearrange("p (n c) -> p n c", c=32)
                                A2v = A2[:].rearrange("p (n c) -> p n c", c=32)
                                totb = tot[:].rearrange("p (n o) -> p n o", o=1).broadcast_to([128, HW // 32, 32])
                                sch.op("act", A.copy, tot[:], A3v[:, :, 31], reads=[B3], writes=[Btot])
                                yield
                                if d == 0:
                                    sch.op("dve", V.tensor_tensor, A2v, A3v, totb, ALU.subtract, reads=[Btot, B3, B2], writes=[B2])
                                    asg, rsg = 1.0, -1.0
                                else:
                                    sch.op("dve", V.tensor_tensor, A2[:], A3[:], A2[:], ALU.subtract, reads=[B3, B2], writes=[B2])
                                    yield
                                    sch.op("dve", V.tensor_tensor, A3v, A2v, totb, ALU.subtract, reads=[Btot, B2, B3], writes=[B3])
                                    asg, rsg = -1.0, 1.0
                                yield
                                sch.op("act", A.activation, ch["dec"][:, c0 // 32:(c0 + HW) // 32], tot[:], AF.Exp, reads=[Btot], writes=[B["dec"]])
                                sch.op("act", A.activation, A4[:], A3[:], AF.Exp, scale=asg, reads=[B3, B4], writes=[B4])
                                yield
                                sch.op("dve", V.tensor_tensor, ch["qdm"][0:64, 0, cs], qh[0:64, :], A4[0:64, :], ALU.mult,
                                       reads=[Bq, B4, B["qdm"]], writes=[B["qdm"]])
                                sch.op("pool", G.tensor_tensor, ch["qdm"][64:128, 1, cs], qh[64:128, :], A4[64:128, :], ALU.mult,
                                       reads=[Bq, B4, B["qdm"]], writes=[B["qdm"]])
                                yield
                                sch.op("act", A.activation, A4[:], A3[:], AF.Exp, scale=-asg, reads=[B3, B4], writes=[B4])
                                yield
                                sch.op("dve", V.tensor_tensor, ch["kd"][:, cs], A1[:], A4[:], ALU.mult, reads=[B1, B4], writes=[B["kd"]])
                                yield
                                sch.op("act", A.activation, A4[:], A2[:], AF.Exp, scale=rsg, reads=[B2, B4], writes=[B4])
                                yield
                                sch.op("dve", V.tensor_tensor, ksh[:], A1[:], A4[:], ALU.mult, reads=[B1, B4], writes=[Bks])
                                yield
                                nbh = HW // 128
                                for b4 in range(0, nbh, 4):
                                    nn = min(4, nbh - b4)
                                    pt, Bp = ptk.next()
                                    for j in range(nn):
                                        sch.op("pe", P.transpose, pt[:, j, :], ksh[:, (b4 + j) * 128:(b4 + j + 1) * 128], ident[:],
                                               reads=[Bks, B_const], writes=[Bp])
                                    ei_[0] += 1
                                    evac(ei_[0], ch["kst"][:, h * nbh + b4:h * nbh + b4 + nn, :], pt[:, 0:nn, :], [Bp], [B["kst"]])
                                    yield

                            units = [(ch, h) for ch in chains for h in range(2)]
                            gens = []
                            SHIFT = 6
                            active = []
                            ui = 0
                            tick = 0
                            while ui < len(units) or active:
                                if ui < len(units) and len(active) < 2 and (not active or active[0][1] >= SHIFT):
                                    ch_, h_ = units[ui]
                                    active.append([unit_gen(ch_, h_, Asets[ui % 2]), 0])
                                    ui += 1
                                for a_ in list(active):
                                    try:
                                        next(a_[0])
                                        a_[1] += 1
                                    except StopIteration:
                                        active.remove(a_)
                            sch.barrier()
                        for ch in chains:
                            ch["o"] = sbt(hs, f"o{ch['d']}", [128, T])
                        with ExitStack() as cs_:
                            vexa = sbt(cs_, "vexa", [128, NB, 4, 128], BF16)
                            Bvex = Buf()
                            for nb in range(NB):
                                sch.op("dve", V.tensor_tensor, vexa[:, nb], cm3[:], vtm[:, nb:nb + 1, :].broadcast_to([128, 4, 128]), ALU.mult,
                                       reads=[Bmd, Bv], writes=[Bvex])
                            for ch in chains:
                                d = ch["d"]
                                ch["psU"] = pst(cs_, f"psU{d}", [128, 4, 128]); ch["BpsU"] = Buf()
                                ch["psS"] = [pst(cs_, f"psS{d}{i}", [128, 128]) for i in range(2)]; ch["BpsS"] = [Buf(), Buf()]
                                ch["psO"] = pst(cs_, f"psO{d}", [128, 128]); ch["BpsO"] = Buf()
                                ch["Sf"] = Rot([sbt(cs_, f"Sf{d}{i}", [128, 128]) for i in range(2)])
                                ch["Spad"] = Rot([sbt(cs_, f"Sp{d}{i}", [128, 2, 128], BF16) for i in range(10)])
                                for (tl, Bt_) in ch["Spad"].t:
                                    sch.op("pool", G.memset, tl[:], 0.0, writes=[Bt_])
                                ch["scb"] = Rot([sbt(cs_, f"scb{d}{i}", [128, 2, 128], BF16) for i in range(2)])
                                ch["order"] = ([32, 33] + list(range(32))) if d == 0 else ([33, 32] + list(range(31, -1, -1)))
                                ch["corder"] = [0, 1, 2, 3] if d == 0 else [3, 2, 1, 0]
                                ch["prev"] = None
                                ch["prev_bd"] = None
                                ch["pending"] = None

                            def front_all(step):
                                res = []
                                for ch in chains:
                                    nb = ch["order"][step]
                                    B = ch["B"]
                                    sch.op("pe", P.matmul, ch["psU"][:].rearrange("p c n -> p (c n)"), lhsT=ch["kst"][:, nb, :],
                                           rhs=vexa[:, nb].rearrange("p c n -> p (c n)"), start=True, stop=True,
                                           reads=[B["kst"], Bvex], writes=[ch["BpsU"]])
                                    ch["before"] = []
                                for ci in range(4):
                                    for k_, ch in enumerate(chains):
                                        nb = ch["order"][step]
                                        B = ch["B"]
                                        c = ch["corder"][ci]
                                        n = nb * 4 + c
                                        ch["before"].append((c, ch["prev_bd"]))
                                        sf, Bsf = ch["Sf"].next()
                                        if ch["prev"] is None:
                                            sch.op("dve", V.tensor_copy, sf[:], ch["psU"][:, c, :], reads=[ch["BpsU"]], writes=[Bsf])
                                        else:
                                            pv, Bpv = ch["prev"]
                                            sch.op("dve", V.scalar_tensor_tensor, sf[:], pv[:], ch["dec"][:, n:n + 1], ch["psU"][:, c, :],
                                                   ALU.mult, ALU.add, reads=[Bpv, B["dec"], ch["BpsU"]], writes=[Bsf])
                                        ch["prev"] = (sf, Bsf)
                                        sp_, Bsp = ch["Spad"].next()
                                        base = sp_[:]
                                        ov = bass.AP(tensor=base.tensor, offset=base.offset, ap=[[base.ap[0][0], 128], [192, 2], [1, 64]])
                                        iv = sf[:].rearrange("p (g c) -> p g c", g=2)
                                        sch.op("act", A.copy, ov, iv, reads=[Bsf, Bsp], writes=[Bsp])
                                        ch["prev_bd"] = (sp_, Bsp)
                                for ch in chains:
                                    nb = ch["order"][step]
                                    B = ch["B"]
                                    cols = slice(nb * 128, (nb + 1) * 128)
                                    scb, Bscb = ch["scb"].next()
                                    for hh in range(2):
                                        sch.op("pe", P.matmul, ch["psS"][hh][:], lhsT=ch["kd"][:, cols], rhs=ch["qdm"][:, hh, cols],
                                               start=True, stop=True, reads=[B["kd"], B["qdm"]], writes=[ch["BpsS"][hh]])
                                        sch.op("dve", V.tensor_tensor, scb[:, hh, :], ch["psS"][hh][:], mdir[:, ch["d"], :], ALU.mult,
                                               reads=[ch["BpsS"][hh], Bmd], writes=[Bscb])
                                    res.append((nb, ch["before"], scb, Bscb))
                                return res

                            def back(ch, pend):
                                nb, before, scb, Bscb = pend
                                B = ch["B"]
                                cols = slice(nb * 128, (nb + 1) * 128)
                                inter = [(c, s_) for (c, s_) in before if s_ is not None]
                                for hh in range(2):
                                    sch.op("pe", P.matmul, ch["psO"][:], lhsT=vm[:, hh, nb, :], rhs=scb[:, hh, :], start=(hh == 0),
                                           stop=(hh == 1 and not inter), reads=[Bvm, Bscb], writes=[ch["BpsO"]])
                                for i, (c, (sp_, Bsp)) in enumerate(inter):
                                    for hh in range(2):
                                        sch.op("pe", P.matmul, ch["psO"][:, 32 * c:32 * c + 32], lhsT=sp_[:, hh, :],
                                               rhs=ch["qdm"][:, hh, nb * 128 + 32 * c:nb * 128 + 32 * c + 32], start=False,
                                               stop=(i == len(inter) - 1 and hh == 1), reads=[Bsp, B["qdm"]], writes=[ch["BpsO"]])
                                sch.op("act", A.copy, ch["o"][:, cols], ch["psO"][:], reads=[ch["BpsO"]], writes=[B["o"]])

                            for step in range(NB + 1):
                                newp = None
                                if step < NB:
                                    newp = front_all(step)
                                if step >= 1:
                                    for ch in chains:
                                        back(ch, ch["pending"])
                                if step < NB:
                                    for ch, p_ in zip(chains, newp):
                                        ch["pending"] = p_
                            sch.barrier()
                        with ExitStack() as rs:
                            of, ob = chains[0]["o"], chains[1]["o"]
                            Bof, Bob = chains[0]["B"]["o"], chains[1]["B"]["o"]
                            sch.op("dve", V.tensor_tensor, of[:], of[:], ob[:], ALU.add, reads=[Bof, Bob], writes=[Bof])
                            gt = sbt(rs, "hg", [128, T], BF16)
                            Bg = Buf()
                            sch.dma("sp", gt[:], fb[256 + hp * 128:256 + (hp + 1) * 128, :], writes=[Bg])
                            sqr = Rot([sbt(rs, f"hsq{i}", [128, 512]) for i in range(2)])
                            sgr = Rot([sbt(rs, f"hsg{i}", [128, 512]) for i in range(2)])
                            rr = Rot([sbt(rs, f"hr{i}", [128, 512]) for i in range(2)])
                            orr = Rot([sbt(rs, f"hob{i}", [128, 512], BF16) for i in range(2)])
                            pms = Rot([pst(rs, f"pms{i}", [128, 512]) for i in range(2)])
                            for t0 in range(0, T, 512):
                                n = min(512, T - t0)
                                cs = slice(t0, t0 + n)
                                sq, Bsq = sqr.next()
                                sch.op("act", A.activation, sq[:, 0:n], of[:, cs], AF.Square, reads=[Bof], writes=[Bsq])
                                pm_, Bpm = pms.next()
                                sch.op("pe", P.matmul, pm_[:, 0:n], lhsT=bd64[:], rhs=sq[:, 0:n], start=True, stop=True, pemode="f32",
                                       reads=[Bmd, Bsq], writes=[Bpm])
                                r_, Br = rr.next()
                                sch.op("act", A.activation, r_[:, 0:n], pm_[:, 0:n], AF.Ln, bias=epsc[:, 0:1], scale=1.0, reads=[Bpm, Bmd], writes=[Br])
                                sch.op("act", A.activation, r_[:, 0:n], r_[:, 0:n], AF.Exp, scale=-0.5, reads=[Br], writes=[Br])
                                sch.op("dve", V.tensor_tensor, r_[:, 0:n], r_[:, 0:n], of[:, cs], ALU.mult, reads=[Br, Bof], writes=[Br])
                                sg, Bsg = sgr.next()
                                sch.op("act", A.activation, sg[:, 0:n], gt[:, cs], AF.Silu, reads=[Bg], writes=[Bsg])
                                ob_, Bo = orr.next()
                                sch.op("dve", V.scalar_tensor_tensor, ob_[:, 0:n], r_[:, 0:n], ngt[:, l:l + 1], sg[:, 0:n], ALU.mult, ALU.mult,
                                       reads=[Br, Bl, Bsg], writes=[Bo])
                                sch.dma("pool", catT[256 + hp * 128:256 + (hp + 1) * 128, cs], ob_[:, 0:n], reads=[Bo])
                            sch.barrier()

        def phase_e(l):
            with ExitStack() as es:
                sets = []
                for u in range(2):
                    qm = sbt(es, f"naq{u}", [128, 2, T], BF16)
                    kT = sbt(es, f"nak{u}", [128, T], BF16)
                    vm = sbt(es, f"nav{u}", [128, 2, NB, 128], BF16)
                    Bq, Bk, Bv = Buf(), Buf(), Buf()
                    sch.op("pool", G.memset, qm[:], 0.0, writes=[Bq])
                    sch.op("pool", G.memset, vm[:], 0.0, writes=[Bv])
                    sets.append((qm, kT, vm, Bq, Bk, Bv))
                bg_rng = ((8, 14), (0, 8), (14, 20))
                bias = [[sbt(es, f"nab{hh}{g}", [128, bg_rng[g][1] - bg_rng[g][0], 512]) for g in range(3)] for hh in range(2)]
                Bb = [[Buf() for g in range(3)] for hh in range(2)]
                oT = sbt(es, "nao", [128, T], BF16)
                oneh = sbt(es, "oneh", [128, 2, 128], BF16)
                sbr = Rot([sbt(es, f"nas{i}", [128, 512]) for i in range(4)])
                pr = Rot([sbt(es, f"nap{i}", [128, 512], BF16) for i in range(6)])
                rcr = Rot([sbt(es, f"narc{i}", [128, 512]) for i in range(2)])
                psr = Rot([pst(es, f"naS{i}", [128, 512]) for i in range(4)])
                numr = Rot([pst(es, f"naN{i}", [128, 512]) for i in range(2)])
                denr = Rot([pst(es, f"naD{i}", [128, 512]) for i in range(2)])
                Bo, B1 = Buf(), Buf()
                sch.op("pool", G.memset, oneh[:], 0.0, writes=[B1])
                for hh in range(2):
                    sch.op("pool", G.memset, oneh[:, hh, hh * 64:(hh + 1) * 64], 1.0, reads=[B1], writes=[B1])
                LOOK = 3

                def load_qkv(hp):
                    qm, kT, vm, Bq, Bk, Bv = sets[hp % 2]
                    for hh in range(2):
                        r0 = 512 + hp * 128 + hh * 64
                        sch.dma("sp", qm[hh * 64:(hh + 1) * 64, hh, :], fb[r0:r0 + 64, :], reads=[Bq], writes=[Bq])
                    sch.dma("sp", kT[:], fb[1024 + hp * 128:1024 + (hp + 1) * 128, :], writes=[Bk])
                    for hh in range(2):
                        c0 = 512 + hp * 128 + hh * 64
                        sch.dma("sp", vm[:, hh, :, hh * 64:(hh + 1) * 64], tm[:, c0:c0 + 64].rearrange("(k p) c -> p k c", p=128),
                                reads=[Bv], writes=[Bv])

                def load_bias(hp, g):
                    t0_, t1_ = bg_rng[g]
                    for hh in range(2):
                        sch.dma("sp", bias[hh][g][:], nabias[l, hp * 2 + hh].rearrange("p (t n) -> p t n", n=512)[:, t0_:t1_, :],
                                writes=[Bb[hh][g]])

                load_qkv(0)
                for g in range(3):
                    load_bias(0, g)
                for hp in range(4):
                    qm, kT, vm, Bq, Bk, Bv = sets[hp % 2]
                    if hp + 1 < 4:
                        load_qkv(hp + 1)
                    items = []
                    for qb in range(8):
                        if qb == 0:
                            kts = [(2 * p * 64, (0, p)) for p in range(6)]
                        elif qb == 7:
                            kts = [((52 + 2 * p) * 64, (2, p)) for p in range(6)]
                        else:
                            kts = [((8 * qb - 4 + 2 * po) * 64, (1, po)) for po in range(8)]
                        kts += [(S, None), (S + 128, None)]
                        items.append((qb * 512, 512, kts, qb))
                    if l == 0:
                        items.append((S, L, [(S, None), (S + 128, None)], 8))
                    for (q0, nq, kts, qb) in items:
                        flat = [(hh, kt0, bi) for hh in range(2) for (kt0, bi) in kts]
                        prod = [None] * len(flat)
                        num, Bn = numr.next()
                        den, Bd = denr.next()
                        for i in range(len(flat) + LOOK):
                            if i < len(flat):
                                hh, kt0, bi = flat[i]
                                ps, Bps = psr.next()
                                sch.op("pe", P.matmul, ps[:, 0:nq], lhsT=kT[:, kt0:kt0 + 128], rhs=qm[:, hh, q0:q0 + nq], start=True, stop=True,
                                       reads=[Bk, Bq], writes=[Bps])
                                pt, Bpt = pr.next()
                                if bi is None:
                                    sch.op("act", A.activation, pt[:, 0:nq], ps[:, 0:nq], AF.Exp, reads=[Bps], writes=[Bpt])
                                else:
                                    st, Bst = sbr.next()
                                    sch.op("dve", V.tensor_tensor, st[:, 0:nq], ps[:, 0:nq], bias[hh][bi[0]][:, bi[1], 0:nq], ALU.add,
                                           reads=[Bps, Bb[hh][bi[0]]], writes=[Bst])
                                    sch.op("act", A.activation, pt[:, 0:nq], st[:, 0:nq], AF.Exp, reads=[Bst], writes=[Bpt])
                                prod[i] = (pt, Bpt)
                            j = i - LOOK
                            if j >= 0:
                                hh, kt0, bi = flat[j]
                                pt, Bpt = prod[j]
                                first = (j == 0)
                                last = (j == len(flat) - 1)
                                sch.op("pe", P.matmul, num[:, 0:nq], lhsT=vm[:, hh, kt0 // 128, :], rhs=pt[:, 0:nq],
                                       start=first, stop=last, reads=[Bv, Bpt], writes=[Bn])
                                sch.op("pe", P.matmul, den[:, 0:nq], lhsT=oneh[:, hh, :], rhs=pt[:, 0:nq],
                                       start=first, stop=last, reads=[B1, Bpt], writes=[Bd])
                        rc, Brc = rcr.next()
                        sch.op("dve", V.reciprocal, rc[:, 0:nq], den[:, 0:nq], reads=[Bd], writes=[Brc])
                        sch.op("dve", V.tensor_tensor, oT[:, q0:q0 + nq], num[:, 0:nq], rc[:, 0:nq], ALU.mult, reads=[Bn, Brc], writes=[Bo])
                        if hp + 1 < 4:
                            if qb == 0:
                                load_bias(hp + 1, 0)
                            elif qb == 6:
                                load_bias(hp + 1, 1)
                            elif qb == 7:
                                load_bias(hp + 1, 2)
                    ncol = T if l == 0 else S
                    sch.dma("pool", catT[512 + hp * 128:512 + (hp + 1) * 128, 0:ncol], oT[:, 0:ncol], reads=[Bo])
                sch.barrier()

        def phase_f1(l):
            xsrc = xin if l == 0 else xres
            with ExitStack() as es:
                wob, Bw = load_weight(es, "woutbf", w_out_b[l].rearrange("(k p) n -> p k n", p=128), 8, D)
                g1 = [sbt(es, f"g1_{v}", [128, D]) for v in range(2)]
                Bg = Buf()
                for v in range(2):
                    sch.dma("pool", g1[v][:], mod2d[2 * l + v, 2 * D:3 * D].partition_broadcast(128), writes=[Bg])
                cr = Rot([sbt(es, f"cat{i}", [128, 8, 512], BF16) for i in range(2)])
                xr = Rot([sbt(es, f"xt{i}", [128, D]) for i in range(3)])
                yr = Rot([sbt(es, f"yt{i}", [128, D]) for i in range(2)])
                pyr = Rot([pst(es, f"py{i}", [128, 512]) for i in range(4)])
                sbs = [(i * 512, 512, 0) for i in range(8)] + ([(S, L, 1)] if l == 0 else [])
                for (t0, ntok, v) in sbs:
                    ct, Bc = cr.next()
                    sch.dma("sp", ct[:, :, 0:ntok], catT[:, t0:t0 + ntok].rearrange("(k p) t -> p k t", p=128), writes=[Bc])
                    for tb in range(ntok // 128):
                        xt, Bx = xr.next()
                        rows = slice(t0 + tb * 128, t0 + (tb + 1) * 128)
                        sch.dma("sp", xt[:], xsrc[rows, :], writes=[Bx])
                        yt, By = yr.next()
                        for nb in range(2):
                            py, Bp = pyr.next()
                            for k in range(8):
                                sch.op("pe", P.matmul, py[:], lhsT=ct[:, k, tb * 128:(tb + 1) * 128], rhs=wob[:, k, nb * 512:(nb + 1) * 512],
                                       start=(k == 0), stop=(k == 7), reads=[Bc, Bw], writes=[Bp])
                            sch.op("dve", V.tensor_tensor, yt[:, nb * 512:(nb + 1) * 512], py[:], g1[v][:, nb * 512:(nb + 1) * 512], ALU.mult,
                                   reads=[Bp, Bg], writes=[By])
                        sch.op("pool", G.tensor_tensor, yt[:], yt[:], xt[:], ALU.add, reads=[By, Bx], writes=[By])
                        sch.dma("pool", xres[rows, :], yt[:], reads=[By])
                sch.barrier()

        def phase_f2(l):
            last = (l == DEPTH - 1)
            with ExitStack() as es:
                wg, Bwg = load_weight(es, "wgbf", w_gate_b[l].rearrange("(k p) n -> p k n", p=128), 8, DFF)
                wu, Bwu = load_weight(es, "wubf", w_up_b[l].rearrange("(k p) n -> p k n", p=128), 8, DFF)
                wd, Bwd = load_weight(es, "wdbf", w_down_b[l].rearrange("(j p) n -> p j n", p=128), NJ, D)
                junk = sbt(es, "junk", [128, D], BF16)
                nt = (junk, Buf(), Rot([sbt(es, f"ssq{i}", [128, 4]) for i in range(4)]),
                      Rot([sbt(es, f"ntmp{i}", [128, D]) for i in range(1)]))
                Gt = sbt(es, "G2", [128, D])
                SHt = sbt(es, "SH2", [128, D])
                g2 = sbt(es, "g2", [128, D])
                Bm = Buf()

                def load_mods(v):
                    sct, Bsc = nt[3].next()
                    sch.dma("pool", SHt[:], mod2d[2 * l + v, 3 * D:4 * D].partition_broadcast(128), writes=[Bm])
                    sch.dma("pool", Gt[:], norm2_g[l].partition_broadcast(128), writes=[Bm])
                    sch.dma("pool", g2[:], mod2d[2 * l + v, 5 * D:6 * D].partition_broadcast(128), writes=[Bm])
                    sch.dma("pool", sct[:], mod2d[2 * l + v, 4 * D:5 * D].partition_broadcast(128), writes=[Bsc])
                    sch.op("dve", V.scalar_tensor_tensor, Gt[:], sct[:], 1.0, Gt[:], ALU.add, ALU.mult, reads=[Bsc, Bm], writes=[Bm])
                fg = None
                if last:
                    fg = sbt(es, "fg", [128, D])
                    Bfg = Buf()
                    sch.dma("pool", fg[:], final_g.partition_broadcast(128), writes=[Bfg])
                x1 = [(sbt(es, f"x1_{i}", [128, D]), Buf()) for i in range(4)]
                ar = Rot([sbt(es, f"abf{i}", [128, D], BF16) for i in range(1)])
                aT = sbt(es, "aT", [128, 8, 512], BF16)
                BaT = Buf()
                hT = sbt(es, "hT", [128, NJ, 512], BF16)
                BhT = Buf()
                sgr = Rot([sbt(es, f"sg{i}", [128, 512]) for i in range(1)])
                ptrr = Rot([pst(es, f"ptr{i}", [128, 4, 128], BF16) for i in range(1)])
                pgr = Rot([pst(es, f"pg{i}", [128, 512]) for i in range(2)])
                pur = Rot([pst(es, f"pu{i}", [128, 512]) for i in range(2)])
                pdr = Rot([pst(es, f"pd{i}", [128, 512]) for i in range(3)])
                sbs = [(i * 512, 512, 0) for i in range(8)] + ([(S, L, 1)] if l == 0 else [])
                curv = None
                for (t0, ntok, v) in sbs:
                    if v != curv:
                        load_mods(v)
                        curv = v
                    ntb = ntok // 128
                    for tb in range(ntb):
                        xt, Bx = x1[tb]
                        sch.dma("sp", xt[:], xres[t0 + tb * 128:t0 + (tb + 1) * 128, :], writes=[Bx])
                        a_bf, Ba = ar.next()
                        norm_block(nt, xt, Bx, Gt, SHt, Bm, a_bf, Ba)
                        transpose_block(ptrr, a_bf, Ba, aT, BaT, tb, tb)
                    for j in range(NJ):
                        pg, Bpg = pgr.next()
                        pu, Bpu = pur.next()
                        for k in range(8):
                            sch.op("pe", P.matmul, pg[:, 0:ntok], lhsT=wg[:, k, j * 128:(j + 1) * 128], rhs=aT[:, k, 0:ntok],
                                   start=(k == 0), stop=(k == 7), reads=[Bwg, BaT], writes=[Bpg])
                        for k in range(8):
                            sch.op("pe", P.matmul, pu[:, 0:ntok], lhsT=wu[:, k, j * 128:(j + 1) * 128], rhs=aT[:, k, 0:ntok],
                                   start=(k == 0), stop=(k == 7), reads=[Bwu, BaT], writes=[Bpu])
                        sg, Bsg = sgr.next()
                        sch.op("act", A.activation, sg[:, 0:ntok], pg[:, 0:ntok], AF.Silu, reads=[Bpg], writes=[Bsg])
                        sch.op("dve", V.tensor_tensor, hT[:, j, 0:ntok], sg[:, 0:ntok], pu[:, 0:ntok], ALU.mult, reads=[Bsg, Bpu], writes=[BhT])
                    for tb in range(ntb):
                        xt, Bx = x1[tb]
                        rows = slice(t0 + tb * 128, t0 + (tb + 1) * 128)
                        tp, Bt = nt[3].next()
                        for nb in range(2):
                            pd, Bpd = pdr.next()
                            for j in range(NJ):
                                sch.op("pe", P.matmul, pd[:], lhsT=hT[:, j, tb * 128:(tb + 1) * 128], rhs=wd[:, j, nb * 512:(nb + 1) * 512],
                                       start=(j == 0), stop=(j == NJ - 1), reads=[BhT, Bwd], writes=[Bpd])
                            sch.op("dve", V.tensor_tensor, tp[:, nb * 512:(nb + 1) * 512], pd[:], g2[:, nb * 512:(nb + 1) * 512], ALU.mult,
                                   reads=[Bpd, Bm], writes=[Bt])
                        sch.op("pool", G.tensor_tensor, xt[:], xt[:], tp[:], ALU.add, reads=[Bx, Bt], writes=[Bx])
                        if not last:
                            sch.dma("pool", xres[rows, :], xt[:], reads=[Bx])
                        else:
                            sq, Bs = nt[2].next()
                            sch.op("act", A.activation, junk[:], xt[:], AF.Square, accum_out=sq[:, 0:1], reads=[Bx], writes=[nt[1], Bs])
                            sch.op("act", A.activation, sq[:, 1:2], sq[:, 0:1], AF.Sqrt, bias=EPS, scale=1.0 / D, reads=[Bs], writes=[Bs])
                            sch.op("dve", V.reciprocal, sq[:, 2:3], sq[:, 1:2], reads=[Bs], writes=[Bs])
                            sch.op("dve", V.scalar_tensor_tensor, xt[:], xt[:], sq[:, 2:3], fg[:], ALU.mult, ALU.mult,
                                   reads=[Bx, Bs, Bfg], writes=[Bx])
                            sch.dma("pool", out[rows, :], xt[:], reads=[Bx])
                sch.barrier()

        phases = [("wcast", phase_wcast, None), ("mod", phase_mod, None)]
        for l in range(DEPTH):
            phases += [(f"ab{l}", phase_ab, l), (f"c{l}", phase_c, l), (f"d{l}", phase_d, l), (f"e{l}", phase_e, l),
                       (f"f1{l}", phase_f1, l), (f"f2{l}", phase_f2, l)]
        for name, fn, arg in phases:
            if arg is None:
                fn()
            else:
                fn(arg)
            if stop is not None and name == stop:
                break
        sch.finish()
        nc._sched_stats = (sch.nops, sch.nwaits)
    return nc


def _prep(inputs):
    f32 = np.float32
    c = _consts()
    x = np.asarray(inputs["x"], f32)
    ctx = np.asarray(inputs["ctx"], f32)
    cvec = np.asarray(inputs["c"], f32)
    cctx = np.asarray(inputs["c_ctx"], f32)
    lb = np.asarray(inputs["hgrn_lb"], f32)
    lbT = np.ascontiguousarray(lb.reshape(2, 2, 2, 128).transpose(3, 0, 1, 2).reshape(128, 8))
    ng = np.asarray(inputs["hgrn_norm_g"], f32)
    ngT = np.ascontiguousarray(np.concatenate([ng, ng], axis=1).T)
    rpb = np.asarray(inputs["na_rpb"], f32)
    nabias = np.stack([_na_bias_tiles(rpb[l]) for l in range(DEPTH)])
    shared = {
        "w_mod": np.asarray(inputs["w_mod"], f32), "b_mod": np.asarray(inputs["b_mod"], f32),
        "norm1_g": np.asarray(inputs["norm1_g"], f32), "w_in": np.asarray(inputs["w_in"], f32),
        "fourier_w": np.asarray(inputs["fourier_w"], f32), "lbT": lbT, "ngT": ngT, "nabias": nabias,
        "w_out": np.asarray(inputs["w_out"], f32), "norm2_g": np.asarray(inputs["norm2_g"], f32),
        "w_gate": np.asarray(inputs["w_ffn_gate"], f32), "w_up": np.asarray(inputs["w_ffn_up"], f32),
        "w_down": np.asarray(inputs["w_ffn_down"], f32), "final_g": np.asarray(inputs["final_norm_g"], f32),
        "C4": c["C4"], "S4": c["S4"], "C2": c["C2"], "S2": c["S2"], "C64bd": c["C64bd"], "S64bdn": c["S64bdn"],
        "ident": c["ident"], "bdmask": c["bdmask"], "bd64": c["bd64"], "mfwd": c["mfwd"], "mbwd": c["mbwd"], "cm3": c["cm3"],
    }
    maps = []
    for b in range(8):
        m = dict(shared)
        m["xin"] = np.ascontiguousarray(np.concatenate([x[b], ctx[b]], axis=0))
        cv = np.stack([cvec[b], cctx], axis=0)
        m["cT"] = np.ascontiguousarray(cv.reshape(2, 8, 128).transpose(2, 1, 0).reshape(128, 16))
        maps.append(m)
    return maps


def kernel(**inputs):
    maps = _prep(inputs)
    nc = build()
    res = run_bass_kernel_spmd(nc, maps, core_ids=list(range(8)))
    return np.stack([np.asarray(r["out"], np.float32) for r in res.results], axis=0)
```

```python
import numpy as np
import ml_dtypes
from contextlib import ExitStack
import concourse.bass as bass
import concourse.mybir as mybir
from concourse.bass_utils import run_bass_kernel_spmd

F32 = mybir.dt.float32
BF16 = mybir.dt.bfloat16
AF = mybir.ActivationFunctionType
ALU = mybir.AluOpType

D = 1024
S = 4096
L = 256
T = S + L
NB = T // 128
DFF = 2816
NJ = DFF // 128
INW = 3072
EPS = 1e-6
NEG = -1e30
DEPTH = 2


class _Op:
    __slots__ = ("eng", "fn", "args", "kw", "deps", "signal", "sem", "val", "dma", "epoch")


class Buf:
    __slots__ = ("w", "r", "g")

    def __init__(self):
        self.w = {}
        self.r = {}
        self.g = ()


class Sched:
    CENG = ("pe", "act", "dve", "pool")
    SEMCAP = 30000

    def __init__(self, nc, es):
        self.nc = nc
        self.eobj = dict(pe=nc.tensor, act=nc.scalar, dve=nc.vector, pool=nc.gpsimd, sp=nc.sync)
        self.pending = []
        self.lastc = {}
        self.dma_since = []
        self.bar_deps = {}
        ncs = dict(pe=3, act=3, dve=4, pool=2)
        self.csem = {e: [es.enter_context(nc.semaphore(f"c{e}{i}")) for i in range(ncs[e])] for e in self.CENG}
        self.ccount = {e: 0 for e in self.CENG}
        self.dsem = {q: [es.enter_context(nc.semaphore(f"d{q}{i}")) for i in range(8)] for q in ("sp", "pool")}
        self.dcount = {q: [0] * 8 for q in ("sp", "pool")}
        self.drr = {q: 0 for q in ("sp", "pool")}
        self.waited = {e: {} for e in self.eobj}
        self.nops = 0
        self.nwaits = 0
        self.epoch = 0
        self.pe_mode = "f"

    def op(self, eng, fn, *args, reads=(), writes=(), dma=False, force=False, **kw):
        o = _Op()
        o.eng, o.fn, o.args, o.kw, o.dma = eng, fn, args, kw, dma
        o.signal = dma or force
        o.sem = None
        o.val = 0
        o.epoch = self.epoch
        deps = {}
        if eng == "pe":
            mode = kw.pop("pemode", "f")
            if mode != self.pe_mode and "pe" in self.lastc:
                deps[id(self.lastc["pe"])] = self.lastc["pe"]
            self.pe_mode = mode

        def need(d, same_ok, bar=False):
            if d.epoch < self.epoch and not bar:
                return
            if same_ok and (not d.dma) and (not dma) and d.eng == eng:
                return
            deps[id(d)] = d

        for b in reads:
            for d in b.w.values():
                need(d, eng == "pe")
        for b in writes:
            if b.r:
                b.g = tuple(b.r.values()) + tuple(b.w.values())
                b.r = {}
                b.w = {}
            for d in b.g:
                need(d, True)
        for d in self.bar_deps.pop(eng, ()):
            need(d, True, True)
        for d in deps.values():
            d.signal = True
        o.deps = list(deps.values())
        key = ("d", id(o)) if dma else eng
        for b in reads:
            b.r[key] = o
        for b in writes:
            b.w[key] = o
        if dma:
            self.dma_since.append(o)
        else:
            self.lastc[eng] = o
        self.pending.append(o)
        return o

    def dma(self, q, out, in_, reads=(), writes=()):
        fn = self.nc.sync.dma_start if q == "sp" else self.nc.gpsimd.dma_start
        return self.op(q, fn, reads=reads, writes=writes, dma=True, out=out, in_=in_)

    def flush(self):
        for o in self.pending:
            e = self.eobj[o.eng]
            w = self.waited[o.eng]
            for d in o.deps:
                assert d.sem is not None, "dependency on unsignalled op"
                k = id(d.sem)
                if w.get(k, 0) >= d.val:
                    continue
                e.wait_ge(d.sem, d.val)
                w[k] = d.val
                self.nwaits += 1
            ins = o.fn(*o.args, **o.kw)
            self.nops += 1
            if o.dma:
                q = o.eng
                i = self.drr[q]
                self.drr[q] = (i + 1) % 8
                self.dcount[q][i] += 16
                o.sem, o.val = self.dsem[q][i], self.dcount[q][i]
                ins.then_inc(o.sem, 16)
            elif o.signal:
                n = self.ccount[o.eng]
                self.ccount[o.eng] = n + 1
                sems = self.csem[o.eng]
                si = n // self.SEMCAP
                assert si < len(sems), "out of compute semaphores"
                o.sem, o.val = sems[si], n % self.SEMCAP + 1
                ins.then_inc(o.sem, 1)
            o.fn = o.args = o.kw = None
        self.pending = []

    def barrier(self):
        deps = [self.lastc[e] for e in self.CENG if e in self.lastc] + self.dma_since
        for d in deps:
            d.signal = True
        self.flush()
        old = self.bar_deps
        self.bar_deps = {e: list(deps) + list(old.get(e, ())) for e in self.eobj}
        self.dma_since = []
        self.epoch += 1

    def finish(self):
        self.barrier()
        sp = self.nc.sync
        for d in self.bar_deps["sp"]:
            k = id(d.sem)
            if self.waited["sp"].get(k, 0) >= d.val:
                continue
            sp.wait_ge(d.sem, d.val)
            self.waited["sp"][k] = d.val


class Rot:
    def __init__(self, tiles):
        self.t = [(t, Buf()) for t in tiles]
        self.i = 0

    def next(self):
        r = self.t[self.i]
        self.i = (self.i + 1) % len(self.t)
        return r


_CONST = {}


def _consts():
    if _CONST:
        return _CONST
    bf = ml_dtypes.bfloat16
    t = np.arange(S, dtype=np.int64)
    m = (t[:, None] * t[None, :]) % S
    ang = 2.0 * np.pi * m.astype(np.float64) / S
    _CONST["C4"] = (np.cos(ang) / 64.0).astype(np.float32).astype(bf)
    _CONST["S4"] = (np.sin(ang) / 64.0).astype(np.float32).astype(bf)
    del ang, m
    t = np.arange(L, dtype=np.int64)
    ang = 2.0 * np.pi * ((t[:, None] * t[None, :]) % L).astype(np.float64) / L
    _CONST["C2"] = (np.cos(ang) / 16.0).astype(np.float32).astype(bf)
    _CONST["S2"] = (np.sin(ang) / 16.0).astype(np.float32).astype(bf)
    t = np.arange(64, dtype=np.int64)
    ang = 2.0 * np.pi * ((t[:, None] * t[None, :]) % 64).astype(np.float64) / 64
    c64 = np.cos(ang) / 8.0
    s64 = np.sin(ang) / 8.0
    z = np.zeros((64, 64))
    _CONST["C64bd"] = np.block([[c64, z], [z, c64]]).astype(np.float32)
    _CONST["S64bdn"] = (-np.block([[s64, z], [z, s64]])).astype(np.float32)
    _CONST["ident"] = np.eye(128, dtype=np.float32).astype(bf)
    bd = np.zeros((128, 128), np.float32)
    bd[:64, :64] = 1.0
    bd[64:, 64:] = 1.0
    _CONST["bdmask"] = bd
    _CONST["bd64"] = (bd / 64.0).astype(np.float32)
    s_ = np.arange(128)[:, None]
    t_ = np.arange(128)[None, :]
    same = (s_ // 32) == (t_ // 32)
    _CONST["mfwd"] = (same & (s_ <= t_)).astype(np.float32)
    _CONST["mbwd"] = (same & (s_ >= t_)).astype(np.float32)
    cm = np.zeros((128, 4, 128), np.float32)
    for c_ in range(4):
        cm[32 * c_:32 * c_ + 32, c_, :] = 1.0
    _CONST["cm3"] = cm.reshape(128, 512)
    return _CONST


def _na_bias_tiles(rpb):
    H = rpb.shape[0]
    kc = np.arange(64)[:, None]
    qc = np.arange(64)[None, :]
    wc = np.clip(qc - 8, 0, 48)
    colvalid = (kc >= wc) & (kc < wc + 16)
    cidx = np.clip(kc - qc + 15, 0, 30)
    specs = [(8, 8 - 4 + 2 * po) for po in range(8)] + [(0, 2 * p) for p in range(6)] + [(56, 52 + 2 * p) for p in range(6)]
    out = np.full((H, 128, 20, 512), NEG, np.float32)
    for ti, (r0, kr0) in enumerate(specs):
        for kr2 in range(2):
            kr = kr0 + kr2
            for qr in range(8):
                r = r0 + qr
                rs = min(max(r - 4, 0), 56)
                if not (rs <= kr < rs + 8):
                    continue
                ridx = kr - r + 7
                blk = np.where(colvalid[None], rpb[:, ridx][:, cidx], NEG)
                out[:, kr2 * 64:(kr2 + 1) * 64, ti, qr * 64:(qr + 1) * 64] = blk
    return out.reshape(H, 128, 20 * 512)


def build(debug=False, stop=None):
    nc = bass.Bass("TRN2", target_bir_lowering=False)
    skind = "ExternalOutput" if debug else "Internal"

    def din(name, shape, dt=F32):
        return nc.dram_tensor(name, list(shape), dt, kind="ExternalInput").ap()

    def dscr(name, shape, dt=F32):
        return nc.dram_tensor(name, list(shape), dt, kind=skind).ap()

    xin = din("xin", [T, D])
    cT = din("cT", [128, 16])
    w_mod = din("w_mod", [DEPTH, D, 6 * D])
    b_mod = din("b_mod", [DEPTH, 6 * D])
    norm1_g = din("norm1_g", [DEPTH, D])
    w_in = din("w_in", [DEPTH, D, INW])
    fourier_w = din("fourier_w", [DEPTH, 4, 64, 64])
    lbT = din("lbT", [128, 8])
    ngT = din("ngT", [128, DEPTH])
    nabias = din("nabias", [DEPTH, 8, 128, 20 * 512])
    w_out = din("w_out", [DEPTH, D, D])
    norm2_g = din("norm2_g", [DEPTH, D])
    w_gate = din("w_gate", [DEPTH, D, DFF])
    w_up = din("w_up", [DEPTH, D, DFF])
    w_down = din("w_down", [DEPTH, DFF, D])
    final_g = din("final_g", [D])
    C4 = din("C4", [S, S], BF16)
    S4 = din("S4", [S, S], BF16)
    C2 = din("C2", [L, L], BF16)
    S2 = din("S2", [L, L], BF16)
    C64bd_d = din("C64bd", [128, 128])
    S64bdn_d = din("S64bdn", [128, 128])
    ident_d = din("ident", [128, 128], BF16)
    bdmask_d = din("bdmask", [128, 128])
    bd64_d = din("bd64", [128, 128])
    mfwd_d = din("mfwd", [128, 128])
    mbwd_d = din("mbwd", [128, 128])
    cm3_d = din("cm3", [128, 512])
    out = nc.dram_tensor("out", [S, D], F32, kind="ExternalOutput").ap()

    mod2d = dscr("mod2d", [2 * DEPTH, 6 * D])
    fz = dscr("fz", [512, T])
    fb = dscr("fb", [1536, T], BF16)
    tm = dscr("tm", [T, 1024], BF16)
    catT = dscr("catT", [1024, T], BF16)
    xres = dscr("xres", [T, D])
    w_in_b = dscr("w_in_b", [DEPTH, D, INW], BF16)
    w_out_b = dscr("w_out_b", [DEPTH, D, D], BF16)
    w_gate_b = dscr("w_gate_b", [DEPTH, D, DFF], BF16)
    w_up_b = dscr("w_up_b", [DEPTH, D, DFF], BF16)
    w_down_b = dscr("w_down_b", [DEPTH, DFF, D], BF16)

    with ExitStack() as ges:
        sch = Sched(nc, ges)
        V, A, P, G = nc.vector, nc.scalar, nc.tensor, nc.gpsimd

        uid = [0]

        def sbt(es, name, shape, dt=F32):
            uid[0] += 1
            return es.enter_context(nc.sbuf_tensor(f"s{uid[0]}_{name}", list(shape), dt))

        def pst(es, name, shape, dt=F32):
            uid[0] += 1
            return es.enter_context(nc.psum_tensor(f"p{uid[0]}_{name}", list(shape), dt))

        ident = sbt(ges, "ident", [128, 128], BF16)
        bdmask = sbt(ges, "bdmask", [128, 128])
        onesb = sbt(ges, "onesb", [128, 64], BF16)
        B_const = Buf()
        sch.dma("sp", ident[:], ident_d, writes=[B_const])
        sch.dma("sp", bdmask[:], bdmask_d, writes=[B_const])
        sch.op("pool", G.memset, onesb[:], 1.0, writes=[B_const], force=True)

        def evac(i, outap, inap, reads, writes, scale=None):
            if i % 2 == 0:
                if scale is None:
                    sch.op("act", A.copy, outap, inap, reads=reads, writes=writes)
                else:
                    sch.op("act", A.mul, outap, inap, scale, reads=reads, writes=writes)
            else:
                if scale is None:
                    sch.op("dve", V.tensor_copy, outap, inap, reads=reads, writes=writes)
                else:
                    sch.op("dve", V.tensor_scalar, outap, inap, scale, None, ALU.mult, reads=reads, writes=writes)

        def phase_mod():
            with ExitStack() as es:
                cTt = sbt(es, "cTt", [128, 16])
                sT = sbt(es, "sT", [128, 16])
                mod2s = sbt(es, "mod2s", [2, 6 * D])
                bt = sbt(es, "bt", [2, 6 * D])
                wm = Rot([sbt(es, f"wm{i}", [128, 8, 512]) for i in range(3)])
                pm = Rot([pst(es, f"pm{i}", [128, 512]) for i in range(2)])
                B_c, B_s, B_m, B_b = Buf(), Buf(), Buf(), Buf()
                sch.dma("sp", cTt[:], cT, writes=[B_c])
                sch.op("act", A.activation, sT[:], cTt[:], AF.Silu, reads=[B_c], writes=[B_s])
                for l in range(DEPTH):
                    sch.dma("sp", bt[:], b_mod[l].partition_broadcast(2), writes=[B_b])
                    for nb in range(12):
                        wt, Bw = wm.next()
                        sch.dma("sp", wt[:], w_mod[l][:, nb * 512:(nb + 1) * 512].rearrange("(k p) n -> p k n", p=128), writes=[Bw])
                        pt, Bp = pm.next()
                        for k in range(8):
                            sch.op("pe", P.matmul, pt[0:2, :], lhsT=sT[:, 2 * k:2 * k + 2], rhs=wt[:, k, :],
                                   start=(k == 0), stop=(k == 7), pemode="m2", reads=[B_s, Bw], writes=[Bp])
                        sch.op("dve", V.tensor_tensor, mod2s[:, nb * 512:(nb + 1) * 512], pt[0:2, :], bt[:, nb * 512:(nb + 1) * 512],
                               ALU.add, reads=[Bp, B_b], writes=[B_m])
                    sch.dma("pool", mod2d[2 * l:2 * l + 2, :], mod2s[:], reads=[B_m])
                sch.barrier()

        def norm_block(es_tiles, xt, Bx, Gt, SHt, B_mod, a_bf, Ba):
            junk, Bj, ssqr, tmpr = es_tiles
            sq, Bs = ssqr.next()
            sch.op("act", A.activation, junk[:], xt[:], AF.Square, accum_out=sq[:, 0:1], reads=[Bx], writes=[Bj, Bs])
            sch.op("act", A.activation, sq[:, 1:2], sq[:, 0:1], AF.Sqrt, bias=EPS, scale=1.0 / D, reads=[Bs], writes=[Bs])
            sch.op("dve", V.reciprocal, sq[:, 2:3], sq[:, 1:2], reads=[Bs], writes=[Bs])
            tp, Bt = tmpr.next()
            sch.op("dve", V.scalar_tensor_tensor, tp[:], xt[:], sq[:, 2:3], Gt[:], ALU.mult, ALU.mult,
                   reads=[Bx, Bs, B_mod], writes=[Bt])
            sch.op("dve", V.tensor_tensor, a_bf[:], tp[:], SHt[:], ALU.add, reads=[Bt, B_mod], writes=[Ba])

        def transpose_block(ptrr, a_bf, Ba, aT, BaT, tb, ei):
            for half in range(2):
                pt, Bp = ptrr.next()
                for j in range(4):
                    c = (half * 4 + j) * 128
                    sch.op("pe", P.transpose, pt[:, j, :], a_bf[:, c:c + 128], ident[:], reads=[Ba, B_const], writes=[Bp])
                evac(ei + half, aT[:, half * 4:(half + 1) * 4, tb * 128:(tb + 1) * 128], pt[:], [Bp], [BaT])

        def load_weight(es, name, src_kpn, nk, ncols):
            wb = sbt(es, name, [128, nk, ncols], BF16)
            Bw = Buf()
            for k in range(nk):
                sch.dma("sp", wb[:, k, :], src_kpn[:, k, :], writes=[Bw])
            return wb, Bw

        bgq = []
        for l_ in range(DEPTH):
            for (src_, dst_, R_, C_) in ((w_in[l_], w_in_b[l_], D, INW), (w_out[l_], w_out_b[l_], D, D), (w_gate[l_], w_gate_b[l_], D, DFF),
                                         (w_up[l_], w_up_b[l_], D, DFF), (w_down[l_], w_down_b[l_], DFF, D)):
                if l_ == 0 and src_ is w_in[0]:
                    continue
                for r0_ in range(0, R_, 128):
                    for c0_ in range(0, C_, 512):
                        c1_ = min(C_, c0_ + 512)
                        bgq.append((src_[r0_:r0_ + 128, c0_:c1_], dst_[r0_:r0_ + 128, c0_:c1_], c1_ - c0_))
        NBG0 = 16 + 48 + 48 + 44
        bgdone = [0]

        def make_bg(es, nsets):
            return (Rot([sbt(es, f"bgi{i}", [128, 512]) for i in range(nsets)]),
                    Rot([sbt(es, f"bgo{i}", [128, 512], BF16) for i in range(nsets)]))

        inflight = []

        def _bg_finish(k):
            while len(inflight) > k:
                (ti, Bi, to, Bo, dst_, nc_) = inflight.pop(0)
                sch.op("pool", G.tensor_copy, to[:, 0:nc_], ti[:, 0:nc_], reads=[Bi], writes=[Bo])
                sch.dma("pool", dst_, to[:, 0:nc_], reads=[Bo])

        def pump(n, bgs, upto=None, drain=False):
            bgi, bgo = bgs
            for _ in range(n):
                if not bgq or (upto is not None and bgdone[0] >= upto):
                    break
                src_, dst_, nc_ = bgq.pop(0)
                bgdone[0] += 1
                ti, Bi = bgi.next()
                to, Bo = bgo.next()
                sch.dma("pool", ti[:, 0:nc_], src_, writes=[Bi])
                inflight.append((ti, Bi, to, Bo, dst_, nc_))
                _bg_finish(2)
            if drain:
                _bg_finish(0)

        def phase_wcast():
            with ExitStack() as es:
                sin = Rot([sbt(es, f"wci{i}", [128, 2048]) for i in range(3)])
                sout = Rot([sbt(es, f"wco{i}", [128, 2048], BF16) for i in range(3)])
                i = 0
                for l in range(1):
                    for (src, dst, R, C) in ((w_in[l], w_in_b[l], D, INW),):
                        for r0 in range(0, R, 128):
                            for c0 in range(0, C, 2048):
                                c1 = min(C, c0 + 2048)
                                n = c1 - c0
                                ti, Bi = sin.next()
                                to, Bo = sout.next()
                                sch.dma("sp", ti[:, 0:n], src[r0:r0 + 128, c0:c1], writes=[Bi])
                                e = ("pool", "dve", "act")[i % 3]
                                i += 1
                                if e == "pool":
                                    sch.op("pool", G.tensor_copy, to[:, 0:n], ti[:, 0:n], reads=[Bi], writes=[Bo])
                                elif e == "dve":
                                    sch.op("dve", V.tensor_copy, to[:, 0:n], ti[:, 0:n], reads=[Bi], writes=[Bo])
                                else:
                                    sch.op("act", A.copy, to[:, 0:n], ti[:, 0:n], reads=[Bi], writes=[Bo])
                                sch.dma("pool", dst[r0:r0 + 128, c0:c1], to[:, 0:n], reads=[Bo])
                sch.barrier()

        def mod_tiles(es, l, v, gsrc, i_sh, i_sc, tag):
            Gt = sbt(es, f"G{tag}", [128, D])
            SHt = sbt(es, f"SH{tag}", [128, D])
            Bm = Buf()
            sch.dma("pool", SHt[:], mod2d[2 * l + v, i_sh * D:(i_sh + 1) * D].partition_broadcast(128), writes=[Bm])
            sch.dma("pool", Gt[:], gsrc.partition_broadcast(128), writes=[Bm])
            return Gt, SHt, Bm

        def phase_ab(l):
            xsrc = xin if l == 0 else xres
            if l >= 1 and bgq:
                with ExitStack() as es0:
                    pump(100000, make_bg(es0, 4), drain=True)
                    sch.barrier()
            with ExitStack() as es:
                wbf, Bw = load_weight(es, "winbf", w_in_b[l].rearrange("(k p) n -> p k n", p=128), 8, INW)
                mods = []
                sct = sbt(es, "sct", [128, D])
                Bsc = Buf()
                for v in range(2):
                    Gt, SHt, Bm = mod_tiles(es, l, v, norm1_g[l], 0, 1, f"1_{v}")
                    sch.dma("pool", sct[:], mod2d[2 * l + v, D:2 * D].partition_broadcast(128), writes=[Bsc])
                    sch.op("dve", V.scalar_tensor_tensor, Gt[:], sct[:], 1.0, Gt[:], ALU.add, ALU.mult, reads=[Bsc, Bm], writes=[Bm])
                    mods.append((Gt, SHt, Bm))
                junk = sbt(es, "junk", [128, D], BF16)
                nt = (junk, Buf(), Rot([sbt(es, f"ssq{i}", [128, 4]) for i in range(4)]),
                      Rot([sbt(es, f"ntmp{i}", [128, D]) for i in range(2)]))
                xr = Rot([sbt(es, f"xt{i}", [128, D]) for i in range(3)])
                ar = Rot([sbt(es, f"abf{i}", [128, D], BF16) for i in range(2)])
                aTr = Rot([sbt(es, f"aT{i}", [128, 8, 512], BF16) for i in range(2)])
                sfr = Rot([sbt(es, f"sf{i}", [128, 512]) for i in range(2)])
                sbr = Rot([sbt(es, f"sb{i}", [128, 512], BF16) for i in range(3)])
                tmr = Rot([sbt(es, f"tms{i}", [128, 1024], BF16) for i in range(2)])
                ptrr = Rot([pst(es, f"ptr{i}", [128, 4, 128], BF16) for i in range(2)])
                pfr = Rot([pst(es, f"pf{i}", [128, 512]) for i in range(3)])
                ptmr = Rot([pst(es, f"ptm{i}", [128, 512]) for i in range(2)])
                fm = []
                for c0, dst, r0, sc in ((256, fb, 0, None), (512, fz, 0, None), (768, fz, 256, None), (1280, fb, 256, None),
                                        (1536, fb, 512, 0.125), (2048, fb, 1024, None)):
                    wd = 512 if c0 >= 1536 else 256
                    for g in range(wd // 128):
                        fm.append((c0 + g * 128, dst, r0 + g * 128, sc))
                sbs = [(i * 512, 512, 0) for i in range(8)] + [(S, L, 1)]
                bgs = make_bg(es, 4)
                eic = [0]
                aTs = {}
                abfs = {}

                def norm_part(i, tb):
                    t0, ntok, v = sbs[i]
                    Gt, SHt, Bm = mods[v]
                    if tb == 0:
                        aTs[i] = aTr.next()
                    xt, Bx = xr.next()
                    sch.dma("sp", xt[:], xsrc[t0 + tb * 128:t0 + (tb + 1) * 128, :], writes=[Bx])
                    a_bf, Ba = ar.next()
                    norm_block(nt, xt, Bx, Gt, SHt, Bm, a_bf, Ba)
                    abfs[(i, tb)] = (a_bf, Ba)

                def tr_part(i, tb):
                    aT, BaT = aTs[i]
                    a_bf, Ba = abfs.pop((i, tb))
                    eic[0] += 2
                    transpose_block(ptrr, a_bf, Ba, aT, BaT, tb, eic[0])

                for tb in range(sbs[0][1] // 128):
                    norm_part(0, tb)
                    tr_part(0, tb)
                for i, (t0, ntok, v) in enumerate(sbs):
                    aT, BaT = aTs[i]
                    groups = []
                    for (c0, dst, r0, sc) in fm:
                        groups.append(("fm", c0, dst, r0, sc))
                    for tb in range(ntok // 128):
                        groups.append(("tm", tb))
                    nxt = {}
                    if i + 1 < len(sbs):
                        ntb2 = sbs[i + 1][1] // 128
                        for tb in range(ntb2):
                            nxt.setdefault(1 + 5 * tb, []).append(("n", tb))
                            nxt.setdefault(4 + 5 * tb, []).append(("t", tb))
                    for gi, g in enumerate(groups):
                        for (kind, tb) in nxt.pop(gi, []):
                            (norm_part if kind == "n" else tr_part)(i + 1, tb)
                        if g[0] == "fm":
                            _, c0, dst, r0, sc = g
                            pf, Bp = pfr.next()
                            for k in range(8):
                                sch.op("pe", P.matmul, pf[:, 0:ntok], lhsT=wbf[:, k, c0:c0 + 128], rhs=aT[:, k, 0:ntok],
                                       start=(k == 0), stop=(k == 7), reads=[Bw, BaT], writes=[Bp])
                            if dst is fz:
                                st, Bs = sfr.next()
                            else:
                                st, Bs = sbr.next()
                            eic[0] += 1
                            evac(eic[0], st[:, 0:ntok], pf[:, 0:ntok], [Bp], [Bs], scale=sc)
                            sch.dma("sp", dst[r0:r0 + 128, t0:t0 + ntok], st[:, 0:ntok], reads=[Bs])
                        else:
                            tb = g[1]
                            tms, Bts = tmr.next()
                            tsl = slice(tb * 128, (tb + 1) * 128)
                            p0, Bp0 = ptmr.next()
                            for (cc, oc) in ((0, 0), (1024, 256)):
                                for k in range(8):
                                    sch.op("pe", P.matmul, p0[:, oc:oc + 256], lhsT=aT[:, k, tsl], rhs=wbf[:, k, cc:cc + 256],
                                           start=(k == 0), stop=(k == 7), reads=[Bw, BaT], writes=[Bp0])
                            eic[0] += 1
                            evac(eic[0], tms[:, 0:512], p0[:], [Bp0], [Bts])
                            p1, Bp1 = ptmr.next()
                            for k in range(8):
                                sch.op("pe", P.matmul, p1[:], lhsT=aT[:, k, tsl], rhs=wbf[:, k, 2560:3072],
                                       start=(k == 0), stop=(k == 7), reads=[Bw, BaT], writes=[Bp1])
                            eic[0] += 1
                            evac(eic[0], tms[:, 512:1024], p1[:], [Bp1], [Bts])
                            sch.dma("sp", tm[t0 + tb * 128:t0 + (tb + 1) * 128, :], tms[:], reads=[Bts])
                        if l == 0:
                            pump(1, bgs, upto=NBG0)
                    for lst in nxt.values():
                        for (kind, tb) in lst:
                            (norm_part if kind == "n" else tr_part)(i + 1, tb)
                if l == 0:
                    pump(1000, bgs, upto=NBG0, drain=True)
                else:
                    pump(1000, bgs, drain=True)
                sch.barrier()

        def phase_c(l):
            with ExitStack() as es:
                U = sbt(es, "fU", [128, NB, 256], BF16)
                BU = Buf()
                sch.dma("sp", U[:], tm[:, 0:256].rearrange("(k p) c -> p k c", p=128), writes=[BU])
                c64 = sbt(es, "c64", [128, 2, 128])
                wf = sbt(es, "wf", [128, 2, 64])
                Mbd = sbt(es, "Mbd", [128, 2, 2, 128], BF16)
                Bc, Bwf, BM = Buf(), Buf(), Buf()
                sch.dma("sp", c64[:, 0, :], C64bd_d, writes=[Bc])
                sch.dma("sp", c64[:, 1, :], S64bdn_d, writes=[Bc])
                for cb in range(2):
                    sch.dma("sp", wf[:, cb, :], fourier_w[l][2 * cb:2 * cb + 2].rearrange("g c d -> (g c) d"), writes=[Bwf])
                sch.op("pool", G.memset, Mbd[:], 0.0, writes=[BM])
                pmr = Rot([pst(es, f"pM{i}", [128, 512]) for i in range(1)])
                pM, BpM = pmr.next()
                for w in range(2):
                    for cb in range(2):
                        o0 = (w * 2 + cb) * 64
                        sch.op("pe", P.matmul, pM[:, o0:o0 + 64], lhsT=c64[:, w, :], rhs=wf[:, cb, :], start=True, stop=True,
                               pemode="f32", reads=[Bc, Bwf], writes=[BpM])
                for w in range(2):
                    for cb in range(2):
                        o0 = (w * 2 + cb) * 64
                        sch.op("dve", V.tensor_copy, Mbd[0:64, w, cb, 0:64], pM[0:64, o0:o0 + 64], reads=[BpM, BM], writes=[BM])
                        sch.op("dve", V.tensor_copy, Mbd[64:128, w, cb, 64:128], pM[64:128, o0:o0 + 64], reads=[BpM, BM], writes=[BM])
                csr = Rot([sbt(es, f"cs{i}", [128, 2, 4, 512], BF16) for i in range(3)])
                pqr = Rot([sbt(es, f"pq{i}", [128, 4, 512], BF16) for i in range(2)])
                osr = Rot([sbt(es, f"fo{i}", [128, 512], BF16) for i in range(2)])
                pacc = [pst(es, f"pacc{i}", [128, 512]) for i in range(4)]
                Bacc = [Buf() for _ in range(4)]
                por = Rot([pst(es, f"po{i}", [128, 512]) for i in range(2)])

                def mix(pq, Bpq, n, tcol0):
                    for cb in range(2):
                        po, Bpo = por.next()
                        sch.op("pe", P.matmul, po[:, 0:n], lhsT=Mbd[:, 0, cb, :], rhs=pq[:, cb, 0:n], start=True, stop=False,
                               reads=[BM, Bpq], writes=[Bpo])
                        sch.op("pe", P.matmul, po[:, 0:n], lhsT=Mbd[:, 1, cb, :], rhs=pq[:, 2 + cb, 0:n], start=False, stop=True,
                               reads=[BM, Bpq], writes=[Bpo])
                        ot, Bo = osr.next()
                        evac(cb, ot[:, 0:n], po[:, 0:n], [Bpo], [Bo])
                        sch.dma("pool", catT[cb * 128:(cb + 1) * 128, tcol0:tcol0 + n], ot[:, 0:n], reads=[Bo])

                for nb in range(8):
                    for kg in range(8):
                        cs, Bcs = csr.next()
                        sch.dma("sp", cs[:, 0], C4[kg * 512:(kg + 1) * 512, nb * 512:(nb + 1) * 512].rearrange("(k p) n -> p k n", p=128), writes=[Bcs])
                        sch.dma("sp", cs[:, 1], S4[kg * 512:(kg + 1) * 512, nb * 512:(nb + 1) * 512].rearrange("(k p) n -> p k n", p=128), writes=[Bcs])
                        for kk in range(4):
                            k = kg * 4 + kk
                            for w in range(2):
                                for cb in range(2):
                                    i = w * 2 + cb
                                    sch.op("pe", P.matmul, pacc[i][:], lhsT=U[:, k, cb * 128:(cb + 1) * 128], rhs=cs[:, w, kk, :],
                                           start=(k == 0), stop=(k == 31), reads=[BU, Bcs], writes=[Bacc[i]])
                    pq, Bpq = pqr.next()
                    for i in range(4):
                        evac(i, pq[:, i, :], pacc[i][:], [Bacc[i]], [Bpq])
                    mix(pq, Bpq, 512, nb * 512)
                if l == 0:
                    c2 = sbt(es, "c2", [128, 2, 2, 256], BF16)
                    Bc2 = Buf()
                    sch.dma("sp", c2[:, 0], C2.rearrange("(k p) n -> p k n", p=128), writes=[Bc2])
                    sch.dma("sp", c2[:, 1], S2.rearrange("(k p) n -> p k n", p=128), writes=[Bc2])
                    for w in range(2):
                        for cb in range(2):
                            i = w * 2 + cb
                            for k in range(2):
                                sch.op("pe", P.matmul, pacc[i][:, 0:256], lhsT=U[:, 32 + k, cb * 128:(cb + 1) * 128], rhs=c2[:, w, k, :],
                                       start=(k == 0), stop=(k == 1), reads=[BU, Bc2], writes=[Bacc[i]])
                    pq, Bpq = pqr.next()
                    for i in range(4):
                        evac(i, pq[:, i, 0:256], pacc[i][:, 0:256], [Bacc[i]], [Bpq])
                    mix(pq, Bpq, 256, S)
                sch.barrier()

        def phase_d(l):
            HW = T // 2
            NCH = T // 32
            with ExitStack() as es:
                lbt = sbt(es, "lbt", [128, 8])
                lbv = sbt(es, "lbv", [128, 8])
                ngt = sbt(es, "ngt", [128, DEPTH])
                mdir = sbt(es, "mdir", [128, 2, 128])
                bd64 = sbt(es, "bd64", [128, 128])
                cm3 = sbt(es, "cm3", [128, 4, 128])
                epsc = sbt(es, "epsc", [128, 1])
                Bl, Bmd = Buf(), Buf()
                sch.op("pool", G.memset, epsc[:], EPS, writes=[Bmd])
                sch.dma("sp", lbt[:], lbT, writes=[Bl])
                sch.dma("sp", ngt[:], ngT, writes=[Bl])
                sch.dma("sp", mdir[:, 0, :], mfwd_d, writes=[Bmd])
                sch.dma("sp", mdir[:, 1, :], mbwd_d, writes=[Bmd])
                sch.dma("sp", bd64[:], bd64_d, writes=[Bmd])
                sch.dma("sp", cm3[:], cm3_d.rearrange("p (c n) -> p c n", n=128), writes=[Bmd])
                if l == 0:
                    sch.op("pool", G.memset, lbv[:, 0:4], 0.0, reads=[Bl], writes=[Bl])
                    sch.op("pool", G.memset, lbv[:, 4:8], 1.0, reads=[Bl], writes=[Bl])
                else:
                    ex = sbt(es, "lbex", [128, 8])
                    sch.op("act", A.activation, ex[:], lbt[:], AF.Exp, reads=[Bl], writes=[Bl])
                    sch.op("dve", V.tensor_tensor, lbv[:, 4:8], ex[:, 0:4], ex[:, 4:8], ALU.add, reads=[Bl], writes=[Bl])
                    sch.op("dve", V.reciprocal, lbv[:, 4:8], lbv[:, 4:8], reads=[Bl], writes=[Bl])
                    sch.op("dve", V.tensor_tensor, lbv[:, 0:4], ex[:, 4:8], lbv[:, 4:8], ALU.mult, reads=[Bl], writes=[Bl])
                    sch.op("dve", V.tensor_scalar, lbv[:, 4:8], lbv[:, 0:4], -1.0, 1.0, ALU.mult, ALU.add, reads=[Bl], writes=[Bl])
                for hp in range(2):
                    with ExitStack() as hs:
                        vtm = sbt(hs, "hv", [128, NB, 128], BF16)
                        vm = sbt(hs, "hvm", [128, 2, NB, 128], BF16)
                        Bv, Bvm = Buf(), Buf()
                        sch.dma("sp", vtm[:], tm[:, 256 + hp * 128:256 + (hp + 1) * 128].rearrange("(k p) c -> p k c", p=128), writes=[Bv])
                        sch.op("pool", G.memset, vm[:], 0.0, writes=[Bvm])
                        sch.op("pool", G.tensor_copy, vm[:, 0, :, 0:64], vtm[:, :, 0:64], reads=[Bv, Bvm], writes=[Bvm])
                        sch.op("pool", G.tensor_copy, vm[:, 1, :, 64:128], vtm[:, :, 64:128], reads=[Bv, Bvm], writes=[Bvm])
                        chains = []
                        for d in range(2):
                            ch = dict(d=d)
                            ch["qdm"] = sbt(hs, f"qdm{d}", [128, 2, T], BF16)
                            ch["kd"] = sbt(hs, f"kd{d}", [128, T], BF16)
                            ch["kst"] = sbt(hs, f"kst{d}", [128, NB, 128], BF16)
                            ch["dec"] = sbt(hs, f"dec{d}", [128, NCH])
                            ch["B"] = {k: Buf() for k in ("qdm", "kd", "kst", "dec", "o")}
                            sch.op("pool", G.memset, ch["qdm"][:], 0.0, writes=[ch["B"]["qdm"]])
                            chains.append(ch)
                        with ExitStack() as gs:
                            Asets = [([sbt(gs, f"gA{u}{i}", [128, HW]) for i in range(4)], [Buf() for _ in range(4)],
                                      sbt(gs, f"gq{u}", [128, HW], BF16), Buf(), sbt(gs, f"gks{u}", [128, HW], BF16), Buf(),
                                      sbt(gs, f"gtot{u}", [128, HW // 32]), Buf()) for u in range(2)]
                            unit = 0
                            msk = sbt(gs, "gmsk", [128, HW])
                            Bmsk = Buf()
                            sch.op("pool", G.memset, msk[:], 1.0, writes=[Bmsk])
                            sch.op("pool", G.memset, msk[:].rearrange("p (n c) -> p n c", c=32)[:, :, 0:1], 0.0, reads=[Bmsk], writes=[Bmsk])
                            ptk = Rot([pst(gs, f"ptk{i}", [128, 4, 128], BF16) for i in range(2)])
                            ei_ = [0]

                            def unit_gen(ch, h, uset):
                                d = ch["d"]
                                B = ch["B"]
                                lbc = lbv[:, d * 2 + hp:d * 2 + hp + 1]
                                omc = lbv[:, 4 + d * 2 + hp:4 + d * 2 + hp + 1]
                                Ar, BA, qh, Bq, ksh, Bks, tot, Btot = uset
                                c0 = h * HW
                                cs = slice(c0, c0 + HW)
                                A1, A2, A3, A4 = Ar
                                B1, B2, B3, B4 = BA
                                zr = 256 * d + hp * 128
                                sch.dma("sp", A1[:], fz[zr:zr + 128, cs], writes=[B1])
                                sch.dma("sp", qh[:], fb[hp * 128:(hp + 1) * 128, cs], writes=[Bq])
                                sch.op("act", A.activation, A1[:], A1[:], AF.Sigmoid, reads=[B1], writes=[B1])
                                yield
                                sch.op("act", A.activation, A1[:], A1[:], AF.Identity, bias=lbc, scale=omc, reads=[B1, Bl], writes=[B1])
                                yield
                                sch.op("act", A.activation, A2[:], A1[:], AF.Ln, reads=[B1], writes=[B2])
                                yield
                                sch.op("act", A.activation, A1[:], A1[:], AF.Identity, bias=1.0, scale=-1.0, reads=[B1, B2], writes=[B1])
                                sch.op("dve", V.tensor_tensor_scan, A3[:], msk[:], A2[:], 0.0, ALU.mult, ALU.add, reads=[Bmsk, B2], writes=[B3])
                                yield
                                A3v = A3[:].rearrange("p (n c) -> p n c", c=32)
                                A2v = A2[:].rearrange("p (n c) -> p n c", c=32)
                                totb = tot[:].rearrange("p (n o) -> p n o", o=1).broadcast_to([128, HW // 32, 32])
                                sch.op("act", A.copy, tot[:], A3v[:, :, 31], reads=[B3], writes=[Btot])
                                yield
                                if d == 0:
                                    sch.op("dve", V.tensor_tensor, A2v, A3v, totb, ALU.subtract, reads=[Btot, B3, B2], writes=[B2])
                                    asg, rsg = 1.0, -1.0
                                else:
                                    sch.op("dve", V.tensor_tensor, A2[:], A3[:], A2[:], ALU.subtract, reads=[B3, B2], writes=[B2])
                                    yield
                                    sch.op("dve", V.tensor_tensor, A3v, A2v, totb, ALU.subtract, reads=[Btot, B2, B3], writes=[B3])
                                    asg, rsg = -1.0, 1.0
                                yield
                                sch.op("act", A.activation, ch["dec"][:, c0 // 32:(c0 + HW) // 32], tot[:], AF.Exp, reads=[Btot], writes=[B["dec"]])
                                sch.op("act", A.activation, A4[:], A3[:], AF.Exp, scale=asg, reads=[B3, B4], writes=[B4])
                                yield
                                sch.op("dve", V.tensor_tensor, ch["qdm"][0:64, 0, cs], qh[0:64, :], A4[0:64, :], ALU.mult,
                                       reads=[Bq, B4, B["qdm"]], writes=[B["qdm"]])
                                sch.op("pool", G.tensor_tensor, ch["qdm"][64:128, 1, cs], qh[64:128, :], A4[64:128, :], ALU.mult,
                                       reads=[Bq, B4, B["qdm"]], writes=[B["qdm"]])
                                yield
                                sch.op("act", A.activation, A4[:], A3[:], AF.Exp, scale=-asg, reads=[B3, B4], writes=[B4])
                                yield
                                sch.op("dve", V.tensor_tensor, ch["kd"][:, cs], A1[:], A4[:], ALU.mult, reads=[B1, B4], writes=[B["kd"]])
                                yield
                                sch.op("act", A.activation, A4[:], A2[:], AF.Exp, scale=rsg, reads=[B2, B4], writes=[B4])
                                yield
                                sch.op("dve", V.tensor_tensor, ksh[:], A1[:], A4[:], ALU.mult, reads=[B1, B4], writes=[Bks])
                                yield
                                nbh = HW // 128
                                for b4 in range(0, nbh, 4):
                                    nn = min(4, nbh - b4)
                                    pt, Bp = ptk.next()
                                    for j in range(nn):
                                        sch.op("pe", P.transpose, pt[:, j, :], ksh[:, (b4 + j) * 128:(b4 + j + 1) * 128], ident[:],
                                               reads=[Bks, B_const], writes=[Bp])
                                    ei_[0] += 1
                                    evac(ei_[0], ch["kst"][:, h * nbh + b4:h * nbh + b4 + nn, :], pt[:, 0:nn, :], [Bp], [B["kst"]])
                                    yield

                            gbgs = make_bg(gs, 4)
                            units = [(ch, h) for ch in chains for h in range(2)]
                            gens = []
                            SHIFT = 6
                            active = []
                            ui = 0
                            tick = 0
                            while ui < len(units) or active:
                                if ui < len(units) and len(active) < 2 and (not active or active[0][1] >= SHIFT):
                                    ch_, h_ = units[ui]
                                    active.append([unit_gen(ch_, h_, Asets[ui % 2]), 0])
                                    ui += 1
                                tick += 1
                                if l == 0 and tick % 2 == 0:
                                    pump(1, gbgs)
                                for a_ in list(active):
                                    try:
                                        next(a_[0])
                                        a_[1] += 1
                                    except StopIteration:
                                        active.remove(a_)
                            pump(0, gbgs, drain=True)
                            sch.barrier()
                        for ch in chains:
                            ch["o"] = sbt(hs, f"o{ch['d']}", [128, T])
                        with ExitStack() as cs_:
                            vexa = sbt(cs_, "vexa", [128, NB, 4, 128], BF16)
                            Bvex = Buf()
                            for nb in range(NB):
                                sch.op("dve", V.tensor_tensor, vexa[:, nb], cm3[:], vtm[:, nb:nb + 1, :].broadcast_to([128, 4, 128]), ALU.mult,
                                       reads=[Bmd, Bv], writes=[Bvex])
                            for ch in chains:
                                d = ch["d"]
                                ch["psU"] = pst(cs_, f"psU{d}", [128, 4, 128]); ch["BpsU"] = Buf()
                                ch["psS"] = [pst(cs_, f"psS{d}{i}", [128, 128]) for i in range(2)]; ch["BpsS"] = [Buf(), Buf()]
                                ch["psO"] = pst(cs_, f"psO{d}", [128, 128]); ch["BpsO"] = Buf()
                                ch["Sf"] = Rot([sbt(cs_, f"Sf{d}{i}", [128, 128]) for i in range(2)])
                                ch["Spad"] = Rot([sbt(cs_, f"Sp{d}{i}", [128, 2, 128], BF16) for i in range(10)])
                                for (tl, Bt_) in ch["Spad"].t:
                                    sch.op("pool", G.memset, tl[:], 0.0, writes=[Bt_])
                                ch["scb"] = Rot([sbt(cs_, f"scb{d}{i}", [128, 2, 128], BF16) for i in range(2)])
                                ch["order"] = ([32, 33] + list(range(32))) if d == 0 else ([33, 32] + list(range(31, -1, -1)))
                                ch["corder"] = [0, 1, 2, 3] if d == 0 else [3, 2, 1, 0]
                                ch["prev"] = None
                                ch["prev_bd"] = None
                                ch["pending"] = None

                            def front_all(step):
                                res = []
                                for ch in chains:
                                    nb = ch["order"][step]
                                    B = ch["B"]
                                    sch.op("pe", P.matmul, ch["psU"][:].rearrange("p c n -> p (c n)"), lhsT=ch["kst"][:, nb, :],
                                           rhs=vexa[:, nb].rearrange("p c n -> p (c n)"), start=True, stop=True,
                                           reads=[B["kst"], Bvex], writes=[ch["BpsU"]])
                                    ch["before"] = []
                                for ci in range(4):
                                    for k_, ch in enumerate(chains):
                                        nb = ch["order"][step]
                                        B = ch["B"]
                                        c = ch["corder"][ci]
                                        n = nb * 4 + c
                                        ch["before"].append((c, ch["prev_bd"]))
                                        sf, Bsf = ch["Sf"].next()
                                        if ch["prev"] is None:
                                            sch.op("dve", V.tensor_copy, sf[:], ch["psU"][:, c, :], reads=[ch["BpsU"]], writes=[Bsf])
                                        else:
                                            pv, Bpv = ch["prev"]
                                            sch.op("dve", V.scalar_tensor_tensor, sf[:], pv[:], ch["dec"][:, n:n + 1], ch["psU"][:, c, :],
                                                   ALU.mult, ALU.add, reads=[Bpv, B["dec"], ch["BpsU"]], writes=[Bsf])
                                        ch["prev"] = (sf, Bsf)
                                        sp_, Bsp = ch["Spad"].next()
                                        base = sp_[:]
                                        ov = bass.AP(tensor=base.tensor, offset=base.offset, ap=[[base.ap[0][0], 128], [192, 2], [1, 64]])
                                        iv = sf[:].rearrange("p (g c) -> p g c", g=2)
                                        sch.op("act", A.copy, ov, iv, reads=[Bsf, Bsp], writes=[Bsp])
                                        ch["prev_bd"] = (sp_, Bsp)
                                for ch in chains:
                                    nb = ch["order"][step]
                                    B = ch["B"]
                                    cols = slice(nb * 128, (nb + 1) * 128)
                                    scb, Bscb = ch["scb"].next()
                                    for hh in range(2):
                                        sch.op("pe", P.matmul, ch["psS"][hh][:], lhsT=ch["kd"][:, cols], rhs=ch["qdm"][:, hh, cols],
                                               start=True, stop=True, reads=[B["kd"], B["qdm"]], writes=[ch["BpsS"][hh]])
                                        sch.op("dve", V.tensor_tensor, scb[:, hh, :], ch["psS"][hh][:], mdir[:, ch["d"], :], ALU.mult,
                                               reads=[ch["BpsS"][hh], Bmd], writes=[Bscb])
                                    res.append((nb, ch["before"], scb, Bscb))
                                return res

                            def back(ch, pend):
                                nb, before, scb, Bscb = pend
                                B = ch["B"]
                                cols = slice(nb * 128, (nb + 1) * 128)
                                inter = [(c, s_) for (c, s_) in before if s_ is not None]
                                for hh in range(2):
                                    sch.op("pe", P.matmul, ch["psO"][:], lhsT=vm[:, hh, nb, :], rhs=scb[:, hh, :], start=(hh == 0),
                                           stop=(hh == 1 and not inter), reads=[Bvm, Bscb], writes=[ch["BpsO"]])
                                for i, (c, (sp_, Bsp)) in enumerate(inter):
                                    for hh in range(2):
                                        sch.op("pe", P.matmul, ch["psO"][:, 32 * c:32 * c + 32], lhsT=sp_[:, hh, :],
                                               rhs=ch["qdm"][:, hh, nb * 128 + 32 * c:nb * 128 + 32 * c + 32], start=False,
                                               stop=(i == len(inter) - 1 and hh == 1), reads=[Bsp, B["qdm"]], writes=[ch["BpsO"]])
                                sch.op("act", A.copy, ch["o"][:, cols], ch["psO"][:], reads=[ch["BpsO"]], writes=[B["o"]])

                            bgs = make_bg(cs_, 4)
                            for step in range(NB + 1):
                                if l == 0:
                                    pump(1, bgs)
                                newp = None
                                if step < NB:
                                    newp = front_all(step)
                                if step >= 1:
                                    for ch in chains:
                                        back(ch, ch["pending"])
                                if step < NB:
                                    for ch, p_ in zip(chains, newp):
                                        ch["pending"] = p_
                            pump(0, bgs, drain=True)
                            sch.barrier()
                        with ExitStack() as rs:
                            of, ob = chains[0]["o"], chains[1]["o"]
                            Bof, Bob = chains[0]["B"]["o"], chains[1]["B"]["o"]
                            sch.op("dve", V.tensor_tensor, of[:], of[:], ob[:], ALU.add, reads=[Bof, Bob], writes=[Bof])
                            gt = sbt(rs, "hg", [128, T], BF16)
                            Bg = Buf()
                            sch.dma("sp", gt[:], fb[256 + hp * 128:256 + (hp + 1) * 128, :], writes=[Bg])
                            sqr = Rot([sbt(rs, f"hsq{i}", [128, 512]) for i in range(2)])
                            sgr = Rot([sbt(rs, f"hsg{i}", [128, 512]) for i in range(2)])
                            rr = Rot([sbt(rs, f"hr{i}", [128, 512]) for i in range(2)])
                            orr = Rot([sbt(rs, f"hob{i}", [128, 512], BF16) for i in range(2)])
                            pms = Rot([pst(rs, f"pms{i}", [128, 512]) for i in range(2)])
                            for t0 in range(0, T, 512):
                                n = min(512, T - t0)
                                cs = slice(t0, t0 + n)
                                sq, Bsq = sqr.next()
                                sch.op("act", A.activation, sq[:, 0:n], of[:, cs], AF.Square, reads=[Bof], writes=[Bsq])
                                pm_, Bpm = pms.next()
                                sch.op("pe", P.matmul, pm_[:, 0:n], lhsT=bd64[:], rhs=sq[:, 0:n], start=True, stop=True, pemode="f32",
                                       reads=[Bmd, Bsq], writes=[Bpm])
                                r_, Br = rr.next()
                                sch.op("act", A.activation, r_[:, 0:n], pm_[:, 0:n], AF.Ln, bias=epsc[:, 0:1], scale=1.0, reads=[Bpm, Bmd], writes=[Br])
                                sch.op("act", A.activation, r_[:, 0:n], r_[:, 0:n], AF.Exp, scale=-0.5, reads=[Br], writes=[Br])
                                sch.op("dve", V.tensor_tensor, r_[:, 0:n], r_[:, 0:n], of[:, cs], ALU.mult, reads=[Br, Bof], writes=[Br])
                                sg, Bsg = sgr.next()
                                sch.op("act", A.activation, sg[:, 0:n], gt[:, cs], AF.Silu, reads=[Bg], writes=[Bsg])
                                ob_, Bo = orr.next()
                                sch.op("dve", V.scalar_tensor_tensor, ob_[:, 0:n], r_[:, 0:n], ngt[:, l:l + 1], sg[:, 0:n], ALU.mult, ALU.mult,
                                       reads=[Br, Bl, Bsg], writes=[Bo])
                                sch.dma("pool", catT[256 + hp * 128:256 + (hp + 1) * 128, cs], ob_[:, 0:n], reads=[Bo])
                            sch.barrier()

        def phase_e(l):
            with ExitStack() as es:
                sets = []
                for u in range(2):
                    qm = sbt(es, f"naq{u}", [128, 2, T], BF16)
                    kT = sbt(es, f"nak{u}", [128, T], BF16)
                    vm = sbt(es, f"nav{u}", [128, 2, NB, 128], BF16)
                    Bq, Bk, Bv = Buf(), Buf(), Buf()
                    sch.op("pool", G.memset, qm[:], 0.0, writes=[Bq])
                    sch.op("pool", G.memset, vm[:], 0.0, writes=[Bv])
                    sets.append((qm, kT, vm, Bq, Bk, Bv))
                bg_rng = ((8, 14), (0, 8), (14, 20))
                bias = [[sbt(es, f"nab{hh}{g}", [128, bg_rng[g][1] - bg_rng[g][0], 512]) for g in range(3)] for hh in range(2)]
                Bb = [[Buf() for g in range(3)] for hh in range(2)]
                oT = sbt(es, "nao", [128, T], BF16)
                oneh = sbt(es, "oneh", [128, 2, 128], BF16)
                sbr = Rot([sbt(es, f"nas{i}", [128, 512]) for i in range(4)])
                pr = Rot([sbt(es, f"nap{i}", [128, 512], BF16) for i in range(6)])
                rcr = Rot([sbt(es, f"narc{i}", [128, 512]) for i in range(2)])
                psr = Rot([pst(es, f"naS{i}", [128, 512]) for i in range(4)])
                numr = Rot([pst(es, f"naN{i}", [128, 512]) for i in range(2)])
                denr = Rot([pst(es, f"naD{i}", [128, 512]) for i in range(2)])
                Bo, B1 = Buf(), Buf()
                sch.op("pool", G.memset, oneh[:], 0.0, writes=[B1])
                for hh in range(2):
                    sch.op("pool", G.memset, oneh[:, hh, hh * 64:(hh + 1) * 64], 1.0, reads=[B1], writes=[B1])
                LOOK = 3
                ebgs = make_bg(es, 4)

                def load_qkv(hp):
                    qm, kT, vm, Bq, Bk, Bv = sets[hp % 2]
                    for hh in range(2):
                        r0 = 512 + hp * 128 + hh * 64
                        sch.dma("sp", qm[hh * 64:(hh + 1) * 64, hh, :], fb[r0:r0 + 64, :], reads=[Bq], writes=[Bq])
                    sch.dma("sp", kT[:], fb[1024 + hp * 128:1024 + (hp + 1) * 128, :], writes=[Bk])
                    for hh in range(2):
                        c0 = 512 + hp * 128 + hh * 64
                        sch.dma("sp", vm[:, hh, :, hh * 64:(hh + 1) * 64], tm[:, c0:c0 + 64].rearrange("(k p) c -> p k c", p=128),
                                reads=[Bv], writes=[Bv])

                def load_bias(hp, g):
                    t0_, t1_ = bg_rng[g]
                    for hh in range(2):
                        sch.dma("sp", bias[hh][g][:], nabias[l, hp * 2 + hh].rearrange("p (t n) -> p t n", n=512)[:, t0_:t1_, :],
                                writes=[Bb[hh][g]])

                load_qkv(0)
                for g in range(3):
                    load_bias(0, g)
                for hp in range(4):
                    qm, kT, vm, Bq, Bk, Bv = sets[hp % 2]
                    if hp + 1 < 4:
                        load_qkv(hp + 1)
                    items = []
                    for qb in range(8):
                        if qb == 0:
                            kts = [(2 * p * 64, (0, p)) for p in range(6)]
                        elif qb == 7:
                            kts = [((52 + 2 * p) * 64, (2, p)) for p in range(6)]
                        else:
                            kts = [((8 * qb - 4 + 2 * po) * 64, (1, po)) for po in range(8)]
                        kts += [(S, None), (S + 128, None)]
                        items.append((qb * 512, 512, kts, qb))
                    if l == 0:
                        items.append((S, L, [(S, None), (S + 128, None)], 8))
                    for (q0, nq, kts, qb) in items:
                        flat = [(hh, kt0, bi) for hh in range(2) for (kt0, bi) in kts]
                        prod = [None] * len(flat)
                        num, Bn = numr.next()
                        den, Bd = denr.next()
                        for i in range(len(flat) + LOOK):
                            if i < len(flat):
                                hh, kt0, bi = flat[i]
                                ps, Bps = psr.next()
                                sch.op("pe", P.matmul, ps[:, 0:nq], lhsT=kT[:, kt0:kt0 + 128], rhs=qm[:, hh, q0:q0 + nq], start=True, stop=True,
                                       reads=[Bk, Bq], writes=[Bps])
                                pt, Bpt = pr.next()
                                if bi is None:
                                    sch.op("act", A.activation, pt[:, 0:nq], ps[:, 0:nq], AF.Exp, reads=[Bps], writes=[Bpt])
                                else:
                                    st, Bst = sbr.next()
                                    sch.op("dve", V.tensor_tensor, st[:, 0:nq], ps[:, 0:nq], bias[hh][bi[0]][:, bi[1], 0:nq], ALU.add,
                                           reads=[Bps, Bb[hh][bi[0]]], writes=[Bst])
                                    sch.op("act", A.activation, pt[:, 0:nq], st[:, 0:nq], AF.Exp, reads=[Bst], writes=[Bpt])
                                prod[i] = (pt, Bpt)
                            j = i - LOOK
                            if j >= 0:
                                hh, kt0, bi = flat[j]
                                pt, Bpt = prod[j]
                                first = (j == 0)
                                last = (j == len(flat) - 1)
                                sch.op("pe", P.matmul, num[:, 0:nq], lhsT=vm[:, hh, kt0 // 128, :], rhs=pt[:, 0:nq],
                                       start=first, stop=last, reads=[Bv, Bpt], writes=[Bn])
                                sch.op("pe", P.matmul, den[:, 0:nq], lhsT=oneh[:, hh, :], rhs=pt[:, 0:nq],
                                       start=first, stop=last, reads=[B1, Bpt], writes=[Bd])
                        rc, Brc = rcr.next()
                        sch.op("dve", V.reciprocal, rc[:, 0:nq], den[:, 0:nq], reads=[Bd], writes=[Brc])
                        sch.op("dve", V.tensor_tensor, oT[:, q0:q0 + nq], num[:, 0:nq], rc[:, 0:nq], ALU.mult, reads=[Bn, Brc], writes=[Bo])
                        if l == 0:
                            pump(2, ebgs)
                        if hp + 1 < 4:
                            if qb == 0:
                                load_bias(hp + 1, 0)
                            elif qb == 6:
                                load_bias(hp + 1, 1)
                            elif qb == 7:
                                load_bias(hp + 1, 2)
                    ncol = T if l == 0 else S
                    sch.dma("pool", catT[512 + hp * 128:512 + (hp + 1) * 128, 0:ncol], oT[:, 0:ncol], reads=[Bo])
                pump(0, ebgs, drain=True)
                sch.barrier()

        def phase_f1(l):
            xsrc = xin if l == 0 else xres
            with ExitStack() as es:
                wob, Bw = load_weight(es, "woutbf", w_out_b[l].rearrange("(k p) n -> p k n", p=128), 8, D)
                g1 = [sbt(es, f"g1_{v}", [128, D]) for v in range(2)]
                Bg = Buf()
                for v in range(2):
                    sch.dma("pool", g1[v][:], mod2d[2 * l + v, 2 * D:3 * D].partition_broadcast(128), writes=[Bg])
                cr = Rot([sbt(es, f"cat{i}", [128, 8, 512], BF16) for i in range(2)])
                xr = Rot([sbt(es, f"xt{i}", [128, D]) for i in range(3)])
                yr = Rot([sbt(es, f"yt{i}", [128, D]) for i in range(2)])
                pyr = Rot([pst(es, f"py{i}", [128, 512]) for i in range(4)])
                sbs = [(i * 512, 512, 0) for i in range(8)] + ([(S, L, 1)] if l == 0 else [])
                for (t0, ntok, v) in sbs:
                    ct, Bc = cr.next()
                    sch.dma("sp", ct[:, :, 0:ntok], catT[:, t0:t0 + ntok].rearrange("(k p) t -> p k t", p=128), writes=[Bc])
                    for tb in range(ntok // 128):
                        xt, Bx = xr.next()
                        rows = slice(t0 + tb * 128, t0 + (tb + 1) * 128)
                        sch.dma("sp", xt[:], xsrc[rows, :], writes=[Bx])
                        yt, By = yr.next()
                        for nb in range(2):
                            py, Bp = pyr.next()
                            for k in range(8):
                                sch.op("pe", P.matmul, py[:], lhsT=ct[:, k, tb * 128:(tb + 1) * 128], rhs=wob[:, k, nb * 512:(nb + 1) * 512],
                                       start=(k == 0), stop=(k == 7), reads=[Bc, Bw], writes=[Bp])
                            sch.op("dve", V.tensor_tensor, yt[:, nb * 512:(nb + 1) * 512], py[:], g1[v][:, nb * 512:(nb + 1) * 512], ALU.mult,
                                   reads=[Bp, Bg], writes=[By])
                        sch.op("dve", V.tensor_tensor, yt[:], yt[:], xt[:], ALU.add, reads=[By, Bx], writes=[By])
                        sch.dma("pool", xres[rows, :], yt[:], reads=[By])
                sch.barrier()

        def phase_f2(l):
            last = (l == DEPTH - 1)
            with ExitStack() as es:
                wg, Bwg = load_weight(es, "wgbf", w_gate_b[l].rearrange("(k p) n -> p k n", p=128), 8, DFF)
                wu, Bwu = load_weight(es, "wubf", w_up_b[l].rearrange("(k p) n -> p k n", p=128), 8, DFF)
                wd, Bwd = load_weight(es, "wdbf", w_down_b[l].rearrange("(j p) n -> p j n", p=128), NJ, D)
                junk = sbt(es, "junk", [128, D], BF16)
                nt = (junk, Buf(), Rot([sbt(es, f"ssq{i}", [128, 4]) for i in range(4)]),
                      Rot([sbt(es, f"ntmp{i}", [128, D]) for i in range(1)]))
                Gt = sbt(es, "G2", [128, D])
                SHt = sbt(es, "SH2", [128, D])
                g2 = sbt(es, "g2", [128, D])
                Bm = Buf()

                def load_mods(v):
                    sct, Bsc = nt[3].next()
                    sch.dma("pool", SHt[:], mod2d[2 * l + v, 3 * D:4 * D].partition_broadcast(128), writes=[Bm])
                    sch.dma("pool", Gt[:], norm2_g[l].partition_broadcast(128), writes=[Bm])
                    sch.dma("pool", g2[:], mod2d[2 * l + v, 5 * D:6 * D].partition_broadcast(128), writes=[Bm])
                    sch.dma("pool", sct[:], mod2d[2 * l + v, 4 * D:5 * D].partition_broadcast(128), writes=[Bsc])
                    sch.op("dve", V.scalar_tensor_tensor, Gt[:], sct[:], 1.0, Gt[:], ALU.add, ALU.mult, reads=[Bsc, Bm], writes=[Bm])
                fg = None
                if last:
                    fg = sbt(es, "fg", [128, D])
                    Bfg = Buf()
                    sch.dma("pool", fg[:], final_g.partition_broadcast(128), writes=[Bfg])
                x1 = [(sbt(es, f"x1_{i}", [128, D]), Buf()) for i in range(4)]
                ar = Rot([sbt(es, f"abf{i}", [128, D], BF16) for i in range(1)])
                aT = sbt(es, "aT", [128, 8, 512], BF16)
                BaT = Buf()
                hT = sbt(es, "hT", [128, NJ, 512], BF16)
                BhT = Buf()
                sgr = Rot([sbt(es, f"sg{i}", [128, 512]) for i in range(1)])
                ptrr = Rot([pst(es, f"ptr{i}", [128, 4, 128], BF16) for i in range(1)])
                pgr = Rot([pst(es, f"pg{i}", [128, 512]) for i in range(2)])
                pur = Rot([pst(es, f"pu{i}", [128, 512]) for i in range(2)])
                pdr = Rot([pst(es, f"pd{i}", [128, 512]) for i in range(3)])
                sbs = [(i * 512, 512, 0) for i in range(8)] + ([(S, L, 1)] if l == 0 else [])
                curv = None
                for (t0, ntok, v) in sbs:
                    if v != curv:
                        load_mods(v)
                        curv = v
                    ntb = ntok // 128
                    for tb in range(ntb):
                        xt, Bx = x1[tb]
                        sch.dma("sp", xt[:], xres[t0 + tb * 128:t0 + (tb + 1) * 128, :], writes=[Bx])
                        a_bf, Ba = ar.next()
                        norm_block(nt, xt, Bx, Gt, SHt, Bm, a_bf, Ba)
                        transpose_block(ptrr, a_bf, Ba, aT, BaT, tb, tb)
                    for j in range(NJ):
                        pg, Bpg = pgr.next()
                        pu, Bpu = pur.next()
                        for k in range(8):
                            sch.op("pe", P.matmul, pg[:, 0:ntok], lhsT=wg[:, k, j * 128:(j + 1) * 128], rhs=aT[:, k, 0:ntok],
                                   start=(k == 0), stop=(k == 7), reads=[Bwg, BaT], writes=[Bpg])
                        for k in range(8):
                            sch.op("pe", P.matmul, pu[:, 0:ntok], lhsT=wu[:, k, j * 128:(j + 1) * 128], rhs=aT[:, k, 0:ntok],
                                   start=(k == 0), stop=(k == 7), reads=[Bwu, BaT], writes=[Bpu])
                        sg, Bsg = sgr.next()
                        sch.op("act", A.activation, sg[:, 0:ntok], pg[:, 0:ntok], AF.Silu, reads=[Bpg], writes=[Bsg])
                        sch.op("dve", V.tensor_tensor, hT[:, j, 0:ntok], sg[:, 0:ntok], pu[:, 0:ntok], ALU.mult, reads=[Bsg, Bpu], writes=[BhT])
                    for tb in range(ntb):
                        xt, Bx = x1[tb]
                        rows = slice(t0 + tb * 128, t0 + (tb + 1) * 128)
                        tp, Bt = nt[3].next()
                        for nb in range(2):
                            pd, Bpd = pdr.next()
                            for j in range(NJ):
                                sch.op("pe", P.matmul, pd[:], lhsT=hT[:, j, tb * 128:(tb + 1) * 128], rhs=wd[:, j, nb * 512:(nb + 1) * 512],
                                       start=(j == 0), stop=(j == NJ - 1), reads=[BhT, Bwd], writes=[Bpd])
                            sch.op("dve", V.tensor_tensor, tp[:, nb * 512:(nb + 1) * 512], pd[:], g2[:, nb * 512:(nb + 1) * 512], ALU.mult,
                                   reads=[Bpd, Bm], writes=[Bt])
                        sch.op("dve", V.tensor_tensor, xt[:], xt[:], tp[:], ALU.add, reads=[Bx, Bt], writes=[Bx])
                        if not last:
                            sch.dma("pool", xres[rows, :], xt[:], reads=[Bx])
                        else:
                            sq, Bs = nt[2].next()
                            sch.op("act", A.activation, junk[:], xt[:], AF.Square, accum_out=sq[:, 0:1], reads=[Bx], writes=[nt[1], Bs])
                            sch.op("act", A.activation, sq[:, 1:2], sq[:, 0:1], AF.Sqrt, bias=EPS, scale=1.0 / D, reads=[Bs], writes=[Bs])
                            sch.op("dve", V.reciprocal, sq[:, 2:3], sq[:, 1:2], reads=[Bs], writes=[Bs])
                            sch.op("dve", V.scalar_tensor_tensor, xt[:], xt[:], sq[:, 2:3], fg[:], ALU.mult, ALU.mult,
                                   reads=[Bx, Bs, Bfg], writes=[Bx])
                            sch.dma("pool", out[rows, :], xt[:], reads=[Bx])
                sch.barrier()

        phases = [("wcast", phase_wcast, None), ("mod", phase_mod, None)]
        for l in range(DEPTH):
            phases += [(f"ab{l}", phase_ab, l), (f"c{l}", phase_c, l), (f"d{l}", phase_d, l), (f"e{l}", phase_e, l),
                       (f"f1{l}", phase_f1, l), (f"f2{l}", phase_f2, l)]
        for name, fn, arg in phases:
            if arg is None:
                fn()
            else:
                fn(arg)
            if stop is not None and name == stop:
                break
        sch.finish()
        nc._sched_stats = (sch.nops, sch.nwaits)
    return nc


def _prep(inputs):
    f32 = np.float32
    c = _consts()
    x = np.asarray(inputs["x"], f32)
    ctx = np.asarray(inputs["ctx"], f32)
    cvec = np.asarray(inputs["c"], f32)
    cctx = np.asarray(inputs["c_ctx"], f32)
    lb = np.asarray(inputs["hgrn_lb"], f32)
    lbT = np.ascontiguousarray(lb.reshape(2, 2, 2, 128).transpose(3, 0, 1, 2).reshape(128, 8))
    ng = np.asarray(inputs["hgrn_norm_g"], f32)
    ngT = np.ascontiguousarray(np.concatenate([ng, ng], axis=1).T)
    rpb = np.asarray(inputs["na_rpb"], f32)
    nabias = np.stack([_na_bias_tiles(rpb[l]) for l in range(DEPTH)])
    shared = {
        "w_mod": np.asarray(inputs["w_mod"], f32), "b_mod": np.asarray(inputs["b_mod"], f32),
        "norm1_g": np.asarray(inputs["norm1_g"], f32), "w_in": np.asarray(inputs["w_in"], f32),
        "fourier_w": np.asarray(inputs["fourier_w"], f32), "lbT": lbT, "ngT": ngT, "nabias": nabias,
        "w_out": np.asarray(inputs["w_out"], f32), "norm2_g": np.asarray(inputs["norm2_g"], f32),
        "w_gate": np.asarray(inputs["w_ffn_gate"], f32), "w_up": np.asarray(inputs["w_ffn_up"], f32),
        "w_down": np.asarray(inputs["w_ffn_down"], f32), "final_g": np.asarray(inputs["final_norm_g"], f32),
        "C4": c["C4"], "S4": c["S4"], "C2": c["C2"], "S2": c["S2"], "C64bd": c["C64bd"], "S64bdn": c["S64bdn"],
        "ident": c["ident"], "bdmask": c["bdmask"], "bd64": c["bd64"], "mfwd": c["mfwd"], "mbwd": c["mbwd"], "cm3": c["cm3"],
    }
    maps = []
    for b in range(8):
        m = dict(shared)
        m["xin"] = np.ascontiguousarray(np.concatenate([x[b], ctx[b]], axis=0))
        cv = np.stack([cvec[b], cctx], axis=0)
        m["cT"] = np.ascontiguousarray(cv.reshape(2, 8, 128).transpose(2, 1, 0).reshape(128, 16))
        maps.append(m)
    return maps


def kernel(**inputs):
    maps = _prep(inputs)
    nc = build()
    res = run_bass_kernel_spmd(nc, maps, core_ids=list(range(8)))
    return np.stack([np.asarray(r["out"], np.float32) for r in res.results], axis=0)
```

```python
import numpy as np
import ml_dtypes
from contextlib import ExitStack
import concourse.bass as bass
import concourse.mybir as mybir
from concourse.bass_utils import run_bass_kernel_spmd

F32 = mybir.dt.float32
BF16 = mybir.dt.bfloat16
AF = mybir.ActivationFunctionType
ALU = mybir.AluOpType

D = 1024
S = 4096
L = 256
T = S + L
NB = T // 128
DFF = 2816
NJ = DFF // 128
INW = 3072
EPS = 1e-6
NEG = -1e30
DEPTH = 2


class _Op:
    __slots__ = ("eng", "fn", "args", "kw", "deps", "signal", "sem", "val", "dma", "epoch")


class Buf:
    __slots__ = ("w", "r", "g")

    def __init__(self):
        self.w = {}
        self.r = {}
        self.g = ()


class Sched:
    CENG = ("pe", "act", "dve", "pool")
    SEMCAP = 30000

    def __init__(self, nc, es):
        self.nc = nc
        self.eobj = dict(pe=nc.tensor, act=nc.scalar, dve=nc.vector, pool=nc.gpsimd, sp=nc.sync)
        self.pending = []
        self.lastc = {}
        self.dma_since = []
        self.bar_deps = {}
        ncs = dict(pe=3, act=3, dve=4, pool=2)
        self.csem = {e: [es.enter_context(nc.semaphore(f"c{e}{i}")) for i in range(ncs[e])] for e in self.CENG}
        self.ccount = {e: 0 for e in self.CENG}
        self.dsem = {q: [es.enter_context(nc.semaphore(f"d{q}{i}")) for i in range(8)] for q in ("sp", "pool")}
        self.dcount = {q: [0] * 8 for q in ("sp", "pool")}
        self.drr = {q: 0 for q in ("sp", "pool")}
        self.waited = {e: {} for e in self.eobj}
        self.nops = 0
        self.nwaits = 0
        self.epoch = 0
        self.pe_mode = "f"

    def op(self, eng, fn, *args, reads=(), writes=(), dma=False, force=False, **kw):
        o = _Op()
        o.eng, o.fn, o.args, o.kw, o.dma = eng, fn, args, kw, dma
        o.signal = dma or force
        o.sem = None
        o.val = 0
        o.epoch = self.epoch
        deps = {}
        if eng == "pe":
            mode = kw.pop("pemode", "f")
            if mode != self.pe_mode and "pe" in self.lastc:
                deps[id(self.lastc["pe"])] = self.lastc["pe"]
            self.pe_mode = mode

        def need(d, same_ok, bar=False):
            if d.epoch < self.epoch and not bar:
                return
            if same_ok and (not d.dma) and (not dma) and d.eng == eng:
                return
            deps[id(d)] = d

        for b in reads:
            for d in b.w.values():
                need(d, eng == "pe")
        for b in writes:
            if b.r:
                b.g = tuple(b.r.values()) + tuple(b.w.values())
                b.r = {}
                b.w = {}
            for d in b.g:
                need(d, True)
        for d in self.bar_deps.pop(eng, ()):
            need(d, True, True)
        for d in deps.values():
            d.signal = True
        o.deps = list(deps.values())
        key = ("d", id(o)) if dma else eng
        for b in reads:
            b.r[key] = o
        for b in writes:
            b.w[key] = o
        if dma:
            self.dma_since.append(o)
        else:
            self.lastc[eng] = o
        self.pending.append(o)
        return o

    def dma(self, q, out, in_, reads=(), writes=(), **kw):
        fn = self.nc.sync.dma_start if q == "sp" else self.nc.gpsimd.dma_start
        return self.op(q, fn, reads=reads, writes=writes, dma=True, out=out, in_=in_, **kw)

    def flush(self):
        for o in self.pending:
            e = self.eobj[o.eng]
            w = self.waited[o.eng]
            for d in o.deps:
                assert d.sem is not None, "dependency on unsignalled op"
                k = id(d.sem)
                if w.get(k, 0) >= d.val:
                    continue
                e.wait_ge(d.sem, d.val)
                w[k] = d.val
                self.nwaits += 1
            ins = o.fn(*o.args, **o.kw)
            self.nops += 1
            if o.dma:
                q = o.eng
                i = self.drr[q]
                self.drr[q] = (i + 1) % 8
                self.dcount[q][i] += 16
                o.sem, o.val = self.dsem[q][i], self.dcount[q][i]
                ins.then_inc(o.sem, 16)
            elif o.signal:
                n = self.ccount[o.eng]
                self.ccount[o.eng] = n + 1
                sems = self.csem[o.eng]
                si = n // self.SEMCAP
                assert si < len(sems), "out of compute semaphores"
                o.sem, o.val = sems[si], n % self.SEMCAP + 1
                ins.then_inc(o.sem, 1)
            o.fn = o.args = o.kw = None
        self.pending = []

    def barrier(self):
        deps = [self.lastc[e] for e in self.CENG if e in self.lastc] + self.dma_since
        for d in deps:
            d.signal = True
        self.flush()
        old = self.bar_deps
        self.bar_deps = {e: list(deps) + list(old.get(e, ())) for e in self.eobj}
        self.dma_since = []
        self.epoch += 1

    def finish(self):
        self.barrier()
        sp = self.nc.sync
        for d in self.bar_deps["sp"]:
            k = id(d.sem)
            if self.waited["sp"].get(k, 0) >= d.val:
                continue
            sp.wait_ge(d.sem, d.val)
            self.waited["sp"][k] = d.val


class Rot:
    def __init__(self, tiles):
        self.t = [(t, Buf()) for t in tiles]
        self.i = 0

    def next(self):
        r = self.t[self.i]
        self.i = (self.i + 1) % len(self.t)
        return r


_CONST = {}


def _consts():
    if _CONST:
        return _CONST
    bf = ml_dtypes.bfloat16
    t = np.arange(S, dtype=np.int64)
    m = (t[:, None] * t[None, :]) % S
    ang = 2.0 * np.pi * m.astype(np.float64) / S
    _CONST["C4"] = (np.cos(ang) / 64.0).astype(np.float32).astype(bf)
    _CONST["S4"] = (np.sin(ang) / 64.0).astype(np.float32).astype(bf)
    del ang, m
    t = np.arange(L, dtype=np.int64)
    ang = 2.0 * np.pi * ((t[:, None] * t[None, :]) % L).astype(np.float64) / L
    _CONST["C2"] = (np.cos(ang) / 16.0).astype(np.float32).astype(bf)
    _CONST["S2"] = (np.sin(ang) / 16.0).astype(np.float32).astype(bf)
    t = np.arange(64, dtype=np.int64)
    ang = 2.0 * np.pi * ((t[:, None] * t[None, :]) % 64).astype(np.float64) / 64
    c64 = np.cos(ang) / 8.0
    s64 = np.sin(ang) / 8.0
    z = np.zeros((64, 64))
    _CONST["C64bd"] = np.block([[c64, z], [z, c64]]).astype(np.float32)
    _CONST["S64bdn"] = (-np.block([[s64, z], [z, s64]])).astype(np.float32)
    _CONST["ident"] = np.eye(128, dtype=np.float32).astype(bf)
    bd = np.zeros((128, 128), np.float32)
    bd[:64, :64] = 1.0
    bd[64:, 64:] = 1.0
    _CONST["bdmask"] = bd
    _CONST["bd64"] = (bd / 64.0).astype(np.float32)
    s_ = np.arange(128)[:, None]
    t_ = np.arange(128)[None, :]
    same = (s_ // 32) == (t_ // 32)
    _CONST["mfwd"] = (same & (s_ <= t_)).astype(np.float32)
    _CONST["mbwd"] = (same & (s_ >= t_)).astype(np.float32)
    cm = np.zeros((128, 4, 128), np.float32)
    for c_ in range(4):
        cm[32 * c_:32 * c_ + 32, c_, :] = 1.0
    _CONST["cm3"] = cm.reshape(128, 512)
    return _CONST


def _na_bias_tiles(rpb):
    H = rpb.shape[0]
    kc = np.arange(64)[:, None]
    qc = np.arange(64)[None, :]
    wc = np.clip(qc - 8, 0, 48)
    colvalid = (kc >= wc) & (kc < wc + 16)
    cidx = np.clip(kc - qc + 15, 0, 30)
    specs = [(8, 8 - 4 + 2 * po) for po in range(8)] + [(0, 2 * p) for p in range(6)] + [(56, 52 + 2 * p) for p in range(6)]
    out = np.full((H, 128, 20, 512), NEG, np.float32)
    for ti, (r0, kr0) in enumerate(specs):
        for kr2 in range(2):
            kr = kr0 + kr2
            for qr in range(8):
                r = r0 + qr
                rs = min(max(r - 4, 0), 56)
                if not (rs <= kr < rs + 8):
                    continue
                ridx = kr - r + 7
                blk = np.where(colvalid[None], rpb[:, ridx][:, cidx], NEG)
                out[:, kr2 * 64:(kr2 + 1) * 64, ti, qr * 64:(qr + 1) * 64] = blk
    return out.reshape(H, 128, 20 * 512)


def build(debug=False, stop=None):
    nc = bass.Bass("TRN2", target_bir_lowering=False)
    skind = "ExternalOutput" if debug else "Internal"

    def din(name, shape, dt=F32):
        return nc.dram_tensor(name, list(shape), dt, kind="ExternalInput").ap()

    def dscr(name, shape, dt=F32):
        return nc.dram_tensor(name, list(shape), dt, kind=skind).ap()

    xin = din("xin", [T, D])
    cT = din("cT", [128, 16])
    w_mod = din("w_mod", [DEPTH, D, 6 * D])
    b_mod = din("b_mod", [DEPTH, 6 * D])
    norm1_g = din("norm1_g", [DEPTH, D])
    w_in = din("w_in", [DEPTH, D, INW])
    fourier_w = din("fourier_w", [DEPTH, 4, 64, 64])
    lbT = din("lbT", [128, 8])
    ngT = din("ngT", [128, DEPTH])
    nabias = din("nabias", [DEPTH, 8, 128, 20 * 512])
    w_out = din("w_out", [DEPTH, D, D])
    norm2_g = din("norm2_g", [DEPTH, D])
    w_gate = din("w_gate", [DEPTH, D, DFF])
    w_up = din("w_up", [DEPTH, D, DFF])
    w_down = din("w_down", [DEPTH, DFF, D])
    final_g = din("final_g", [D])
    C4 = din("C4", [S, S], BF16)
    S4 = din("S4", [S, S], BF16)
    C2 = din("C2", [L, L], BF16)
    S2 = din("S2", [L, L], BF16)
    C64bd_d = din("C64bd", [128, 128])
    S64bdn_d = din("S64bdn", [128, 128])
    ident_d = din("ident", [128, 128], BF16)
    bdmask_d = din("bdmask", [128, 128])
    bd64_d = din("bd64", [128, 128])
    mfwd_d = din("mfwd", [128, 128])
    mbwd_d = din("mbwd", [128, 128])
    cm3_d = din("cm3", [128, 512])
    out = nc.dram_tensor("out", [S, D], F32, kind="ExternalOutput").ap()

    mod2d = dscr("mod2d", [2 * DEPTH, 6 * D])
    fz = dscr("fz", [512, T])
    fb = dscr("fb", [1536, T], BF16)
    tm = dscr("tm", [T, 1024], BF16)
    catT = dscr("catT", [1024, T], BF16)
    xres = dscr("xres", [T, D])
    w_in_b = dscr("w_in_b", [DEPTH, D, INW], BF16)
    w_out_b = dscr("w_out_b", [DEPTH, D, D], BF16)
    w_gate_b = dscr("w_gate_b", [DEPTH, D, DFF], BF16)
    w_up_b = dscr("w_up_b", [DEPTH, D, DFF], BF16)
    w_down_b = dscr("w_down_b", [DEPTH, DFF, D], BF16)

    with ExitStack() as ges:
        sch = Sched(nc, ges)
        V, A, P, G = nc.vector, nc.scalar, nc.tensor, nc.gpsimd

        uid = [0]

        def sbt(es, name, shape, dt=F32):
            uid[0] += 1
            return es.enter_context(nc.sbuf_tensor(f"s{uid[0]}_{name}", list(shape), dt))

        def pst(es, name, shape, dt=F32):
            uid[0] += 1
            return es.enter_context(nc.psum_tensor(f"p{uid[0]}_{name}", list(shape), dt))

        ident = sbt(ges, "ident", [128, 128], BF16)
        bdmask = sbt(ges, "bdmask", [128, 128])
        onesb = sbt(ges, "onesb", [128, 64], BF16)
        B_const = Buf()
        sch.dma("sp", ident[:], ident_d, writes=[B_const])
        sch.dma("sp", bdmask[:], bdmask_d, writes=[B_const])
        sch.op("pool", G.memset, onesb[:], 1.0, writes=[B_const], force=True)

        def evac(i, outap, inap, reads, writes, scale=None):
            if i % 2 == 0:
                if scale is None:
                    sch.op("act", A.copy, outap, inap, reads=reads, writes=writes)
                else:
                    sch.op("act", A.mul, outap, inap, scale, reads=reads, writes=writes)
            else:
                if scale is None:
                    sch.op("dve", V.tensor_copy, outap, inap, reads=reads, writes=writes)
                else:
                    sch.op("dve", V.tensor_scalar, outap, inap, scale, None, ALU.mult, reads=reads, writes=writes)

        def phase_mod():
            with ExitStack() as es:
                cTt = sbt(es, "cTt", [128, 16])
                sT = sbt(es, "sT", [128, 16])
                mod2s = sbt(es, "mod2s", [2, 6 * D])
                bt = sbt(es, "bt", [2, 6 * D])
                wm = Rot([sbt(es, f"wm{i}", [128, 8, 512]) for i in range(3)])
                pm = Rot([pst(es, f"pm{i}", [128, 512]) for i in range(2)])
                B_c, B_s, B_m, B_b = Buf(), Buf(), Buf(), Buf()
                sch.dma("sp", cTt[:], cT, writes=[B_c])
                sch.op("act", A.activation, sT[:], cTt[:], AF.Silu, reads=[B_c], writes=[B_s])
                for l in range(DEPTH):
                    sch.dma("sp", bt[:], b_mod[l].partition_broadcast(2), writes=[B_b])
                    for nb in range(12):
                        wt, Bw = wm.next()
                        sch.dma("sp", wt[:], w_mod[l][:, nb * 512:(nb + 1) * 512].rearrange("(k p) n -> p k n", p=128), writes=[Bw])
                        pt, Bp = pm.next()
                        for k in range(8):
                            sch.op("pe", P.matmul, pt[0:2, :], lhsT=sT[:, 2 * k:2 * k + 2], rhs=wt[:, k, :],
                                   start=(k == 0), stop=(k == 7), pemode="m2", reads=[B_s, Bw], writes=[Bp])
                        sch.op("dve", V.tensor_tensor, mod2s[:, nb * 512:(nb + 1) * 512], pt[0:2, :], bt[:, nb * 512:(nb + 1) * 512],
                               ALU.add, reads=[Bp, B_b], writes=[B_m])
                    sch.dma("pool", mod2d[2 * l:2 * l + 2, :], mod2s[:], reads=[B_m])
                sch.barrier()

        def norm_block(es_tiles, xt, Bx, Gt, SHt, B_mod, a_bf, Ba):
            junk, Bj, ssqr, tmpr = es_tiles
            sq, Bs = ssqr.next()
            sch.op("act", A.activation, junk[:], xt[:], AF.Square, accum_out=sq[:, 0:1], reads=[Bx], writes=[Bj, Bs])
            sch.op("act", A.activation, sq[:, 1:2], sq[:, 0:1], AF.Sqrt, bias=EPS, scale=1.0 / D, reads=[Bs], writes=[Bs])
            sch.op("dve", V.reciprocal, sq[:, 2:3], sq[:, 1:2], reads=[Bs], writes=[Bs])
            tp, Bt = tmpr.next()
            sch.op("dve", V.scalar_tensor_tensor, tp[:], xt[:], sq[:, 2:3], Gt[:], ALU.mult, ALU.mult,
                   reads=[Bx, Bs, B_mod], writes=[Bt])
            sch.op("dve", V.tensor_tensor, a_bf[:], tp[:], SHt[:], ALU.add, reads=[Bt, B_mod], writes=[Ba])

        def transpose_block(ptrr, a_bf, Ba, aT, BaT, tb, ei):
            for half in range(2):
                pt, Bp = ptrr.next()
                for j in range(4):
                    c = (half * 4 + j) * 128
                    sch.op("pe", P.transpose, pt[:, j, :], a_bf[:, c:c + 128], ident[:], reads=[Ba, B_const], writes=[Bp])
                evac(ei + half, aT[:, half * 4:(half + 1) * 4, tb * 128:(tb + 1) * 128], pt[:], [Bp], [BaT])

        def load_weight(es, name, src_kpn, nk, ncols):
            wb = sbt(es, name, [128, nk, ncols], BF16)
            Bw = Buf()
            for k in range(nk):
                sch.dma("sp", wb[:, k, :], src_kpn[:, k, :], writes=[Bw])
            return wb, Bw

        bgq = []
        for l_ in range(DEPTH):
            for (src_, dst_, R_, C_) in ((w_in[l_], w_in_b[l_], D, INW), (w_out[l_], w_out_b[l_], D, D), (w_gate[l_], w_gate_b[l_], D, DFF),
                                         (w_up[l_], w_up_b[l_], D, DFF), (w_down[l_], w_down_b[l_], DFF, D)):
                if l_ == 0 and src_ is w_in[0]:
                    continue
                for r0_ in range(0, R_, 128):
                    for c0_ in range(0, C_, 512):
                        c1_ = min(C_, c0_ + 512)
                        bgq.append((src_[r0_:r0_ + 128, c0_:c1_], dst_[r0_:r0_ + 128, c0_:c1_], c1_ - c0_))
        NBG0 = 16 + 48 + 48 + 44
        bgdone = [0]

        def make_bg(es, nsets):
            return (Rot([sbt(es, f"bgi{i}", [128, 512]) for i in range(nsets)]),
                    Rot([sbt(es, f"bgo{i}", [128, 512], BF16) for i in range(nsets)]))

        inflight = []

        def _bg_finish(k):
            while len(inflight) > k:
                (ti, Bi, to, Bo, dst_, nc_) = inflight.pop(0)
                sch.op("pool", G.tensor_copy, to[:, 0:nc_], ti[:, 0:nc_], reads=[Bi], writes=[Bo])
                sch.dma("pool", dst_, to[:, 0:nc_], reads=[Bo])

        def pump(n, bgs, upto=None, drain=False):
            bgi, bgo = bgs
            for _ in range(n):
                if not bgq or (upto is not None and bgdone[0] >= upto):
                    break
                src_, dst_, nc_ = bgq.pop(0)
                bgdone[0] += 1
                ti, Bi = bgi.next()
                to, Bo = bgo.next()
                sch.dma("pool", ti[:, 0:nc_], src_, writes=[Bi])
                inflight.append((ti, Bi, to, Bo, dst_, nc_))
                _bg_finish(2)
            if drain:
                _bg_finish(0)

        def phase_wcast():
            with ExitStack() as es:
                sin = Rot([sbt(es, f"wci{i}", [128, 2048]) for i in range(3)])
                sout = Rot([sbt(es, f"wco{i}", [128, 2048], BF16) for i in range(3)])
                i = 0
                for l in range(1):
                    for (src, dst, R, C) in ((w_in[l], w_in_b[l], D, INW),):
                        for r0 in range(0, R, 128):
                            for c0 in range(0, C, 2048):
                                c1 = min(C, c0 + 2048)
                                n = c1 - c0
                                ti, Bi = sin.next()
                                to, Bo = sout.next()
                                sch.dma("sp", ti[:, 0:n], src[r0:r0 + 128, c0:c1], writes=[Bi])
                                e = ("pool", "dve", "act")[i % 3]
                                i += 1
                                if e == "pool":
                                    sch.op("pool", G.tensor_copy, to[:, 0:n], ti[:, 0:n], reads=[Bi], writes=[Bo])
                                elif e == "dve":
                                    sch.op("dve", V.tensor_copy, to[:, 0:n], ti[:, 0:n], reads=[Bi], writes=[Bo])
                                else:
                                    sch.op("act", A.copy, to[:, 0:n], ti[:, 0:n], reads=[Bi], writes=[Bo])
                                sch.dma("pool", dst[r0:r0 + 128, c0:c1], to[:, 0:n], reads=[Bo])
                sch.barrier()

        def mod_tiles(es, l, v, gsrc, i_sh, i_sc, tag):
            Gt = sbt(es, f"G{tag}", [128, D])
            SHt = sbt(es, f"SH{tag}", [128, D])
            Bm = Buf()
            sch.dma("pool", SHt[:], mod2d[2 * l + v, i_sh * D:(i_sh + 1) * D].partition_broadcast(128), writes=[Bm])
            sch.dma("pool", Gt[:], gsrc.partition_broadcast(128), writes=[Bm])
            return Gt, SHt, Bm

        def phase_ab(l):
            xsrc = xin if l == 0 else xres
            if l >= 1 and bgq:
                with ExitStack() as es0:
                    pump(100000, make_bg(es0, 4), drain=True)
                    sch.barrier()
            with ExitStack() as es:
                wbf, Bw = load_weight(es, "winbf", w_in_b[l].rearrange("(k p) n -> p k n", p=128), 8, INW)
                mods = []
                sct = sbt(es, "sct", [128, D])
                Bsc = Buf()
                for v in range(2):
                    Gt, SHt, Bm = mod_tiles(es, l, v, norm1_g[l], 0, 1, f"1_{v}")
                    sch.dma("pool", sct[:], mod2d[2 * l + v, D:2 * D].partition_broadcast(128), writes=[Bsc])
                    sch.op("dve", V.scalar_tensor_tensor, Gt[:], sct[:], 1.0, Gt[:], ALU.add, ALU.mult, reads=[Bsc, Bm], writes=[Bm])
                    mods.append((Gt, SHt, Bm))
                junk = sbt(es, "junk", [128, D], BF16)
                nt = (junk, Buf(), Rot([sbt(es, f"ssq{i}", [128, 4]) for i in range(4)]),
                      Rot([sbt(es, f"ntmp{i}", [128, D]) for i in range(2)]))
                xr = Rot([sbt(es, f"xt{i}", [128, D]) for i in range(3)])
                ar = Rot([sbt(es, f"abf{i}", [128, D], BF16) for i in range(2)])
                aTr = Rot([sbt(es, f"aT{i}", [128, 8, 512], BF16) for i in range(2)])
                sfr = Rot([sbt(es, f"sf{i}", [128, 512]) for i in range(2)])
                sbr = Rot([sbt(es, f"sb{i}", [128, 512], BF16) for i in range(3)])
                tmr = Rot([sbt(es, f"tms{i}", [128, 1024], BF16) for i in range(2)])
                ptrr = Rot([pst(es, f"ptr{i}", [128, 4, 128], BF16) for i in range(2)])
                pfr = Rot([pst(es, f"pf{i}", [128, 512]) for i in range(3)])
                ptmr = Rot([pst(es, f"ptm{i}", [128, 512]) for i in range(2)])
                fm = []
                for c0, dst, r0, sc in ((256, fb, 0, None), (512, fz, 0, None), (768, fz, 256, None), (1280, fb, 256, None),
                                        (1536, fb, 512, 0.125), (2048, fb, 1024, None)):
                    wd = 512 if c0 >= 1536 else 256
                    for g in range(wd // 128):
                        fm.append((c0 + g * 128, dst, r0 + g * 128, sc))
                sbs = [(i * 512, 512, 0) for i in range(8)] + [(S, L, 1)]
                bgs = make_bg(es, 4)
                eic = [0]
                aTs = {}
                abfs = {}

                def norm_part(i, tb):
                    t0, ntok, v = sbs[i]
                    Gt, SHt, Bm = mods[v]
                    if tb == 0:
                        aTs[i] = aTr.next()
                    xt, Bx = xr.next()
                    sch.dma("sp", xt[:], xsrc[t0 + tb * 128:t0 + (tb + 1) * 128, :], writes=[Bx])
                    a_bf, Ba = ar.next()
                    norm_block(nt, xt, Bx, Gt, SHt, Bm, a_bf, Ba)
                    abfs[(i, tb)] = (a_bf, Ba)

                def tr_part(i, tb):
                    aT, BaT = aTs[i]
                    a_bf, Ba = abfs.pop((i, tb))
                    eic[0] += 2
                    transpose_block(ptrr, a_bf, Ba, aT, BaT, tb, eic[0])

                for tb in range(sbs[0][1] // 128):
                    norm_part(0, tb)
                    tr_part(0, tb)
                for i, (t0, ntok, v) in enumerate(sbs):
                    aT, BaT = aTs[i]
                    groups = []
                    for (c0, dst, r0, sc) in fm:
                        groups.append(("fm", c0, dst, r0, sc))
                    for tb in range(ntok // 128):
                        groups.append(("tm", tb))
                    nxt = {}
                    if i + 1 < len(sbs):
                        ntb2 = sbs[i + 1][1] // 128
                        for tb in range(ntb2):
                            nxt.setdefault(1 + 5 * tb, []).append(("n", tb))
                            nxt.setdefault(4 + 5 * tb, []).append(("t", tb))
                    for gi, g in enumerate(groups):
                        for (kind, tb) in nxt.pop(gi, []):
                            (norm_part if kind == "n" else tr_part)(i + 1, tb)
                        if g[0] == "fm":
                            _, c0, dst, r0, sc = g
                            pf, Bp = pfr.next()
                            for k in range(8):
                                sch.op("pe", P.matmul, pf[:, 0:ntok], lhsT=wbf[:, k, c0:c0 + 128], rhs=aT[:, k, 0:ntok],
                                       start=(k == 0), stop=(k == 7), reads=[Bw, BaT], writes=[Bp])
                            if dst is fz:
                                st, Bs = sfr.next()
                            else:
                                st, Bs = sbr.next()
                            evac(0, st[:, 0:ntok], pf[:, 0:ntok], [Bp], [Bs], scale=sc)
                            sch.dma("sp", dst[r0:r0 + 128, t0:t0 + ntok], st[:, 0:ntok], reads=[Bs])
                        else:
                            tb = g[1]
                            tms, Bts = tmr.next()
                            tsl = slice(tb * 128, (tb + 1) * 128)
                            p0, Bp0 = ptmr.next()
                            for (cc, oc) in ((0, 0), (1024, 256)):
                                for k in range(8):
                                    sch.op("pe", P.matmul, p0[:, oc:oc + 256], lhsT=aT[:, k, tsl], rhs=wbf[:, k, cc:cc + 256],
                                           start=(k == 0), stop=(k == 7), reads=[Bw, BaT], writes=[Bp0])
                            evac(0, tms[:, 0:512], p0[:], [Bp0], [Bts])
                            p1, Bp1 = ptmr.next()
                            for k in range(8):
                                sch.op("pe", P.matmul, p1[:], lhsT=aT[:, k, tsl], rhs=wbf[:, k, 2560:3072],
                                       start=(k == 0), stop=(k == 7), reads=[Bw, BaT], writes=[Bp1])
                            evac(0, tms[:, 512:1024], p1[:], [Bp1], [Bts])
                            sch.dma("sp", tm[t0 + tb * 128:t0 + (tb + 1) * 128, :], tms[:], reads=[Bts])
                        if l == 0:
                            pump(1, bgs, upto=NBG0)
                    for lst in nxt.values():
                        for (kind, tb) in lst:
                            (norm_part if kind == "n" else tr_part)(i + 1, tb)
                if l == 0:
                    pump(1000, bgs, upto=NBG0, drain=True)
                else:
                    pump(1000, bgs, drain=True)
                sch.barrier()

        def phase_c(l):
            with ExitStack() as es:
                U = sbt(es, "fU", [128, NB, 256], BF16)
                BU = Buf()
                sch.dma("sp", U[:], tm[:, 0:256].rearrange("(k p) c -> p k c", p=128), writes=[BU])
                c64 = sbt(es, "c64", [128, 2, 128])
                wf = sbt(es, "wf", [128, 2, 64])
                Mbd = sbt(es, "Mbd", [128, 3, 2, 128], BF16)
                Bc, Bwf, BM = Buf(), Buf(), Buf()
                sch.dma("sp", c64[:, 0, :], C64bd_d, writes=[Bc])
                sch.dma("sp", c64[:, 1, :], S64bdn_d, writes=[Bc])
                for cb in range(2):
                    sch.dma("sp", wf[:, cb, :], fourier_w[l][2 * cb:2 * cb + 2].rearrange("g c d -> (g c) d"), writes=[Bwf])
                sch.op("pool", G.memset, Mbd[:], 0.0, writes=[BM])
                pmr = Rot([pst(es, f"pM{i}", [128, 512]) for i in range(1)])
                pM, BpM = pmr.next()
                for w in range(2):
                    for cb in range(2):
                        o0 = (w * 2 + cb) * 64
                        sch.op("pe", P.matmul, pM[:, o0:o0 + 64], lhsT=c64[:, w, :], rhs=wf[:, cb, :], start=True, stop=True,
                               pemode="f32", reads=[Bc, Bwf], writes=[BpM])
                for w in range(2):
                    for cb in range(2):
                        o0 = (w * 2 + cb) * 64
                        sch.op("dve", V.tensor_copy, Mbd[0:64, w, cb, 0:64], pM[0:64, o0:o0 + 64], reads=[BpM, BM], writes=[BM])
                        sch.op("dve", V.tensor_copy, Mbd[64:128, w, cb, 64:128], pM[64:128, o0:o0 + 64], reads=[BpM, BM], writes=[BM])
                sch.op("dve", V.tensor_scalar, Mbd[:, 2].rearrange("p c n -> p (c n)"), Mbd[:, 1].rearrange("p c n -> p (c n)"), -1.0, None,
                       ALU.mult, reads=[BM], writes=[BM])
                csr = Rot([sbt(es, f"cs{i}", [128, 2, 4, 512], BF16) for i in range(3)])
                pqr = Rot([sbt(es, f"pq{i}", [128, 4, 512], BF16) for i in range(2)])
                osr = Rot([sbt(es, f"fo{i}", [128, 512], BF16) for i in range(3)])
                pacc = [pst(es, f"pacc{i}", [128, 512]) for i in range(4)]
                Bacc = [Buf() for _ in range(4)]
                por = Rot([pst(es, f"po{i}", [128, 512]) for i in range(2)])
                c0col = sbt(es, "c0col", [128, 2], BF16)
                Bc0 = Buf()
                sch.op("pool", G.memset, c0col[:], 1.0 / 64.0, writes=[Bc0])

                def mix(pq, Bpq, n, tcol0, mirror_col0=None, ncols=None):
                    for cb in range(2):
                        po, Bpo = por.next()
                        sch.op("pe", P.matmul, po[:, 0:n], lhsT=Mbd[:, 0, cb, :], rhs=pq[:, cb, 0:n], start=True, stop=False,
                               reads=[BM, Bpq], writes=[Bpo])
                        sch.op("pe", P.matmul, po[:, 0:n], lhsT=Mbd[:, 1, cb, :], rhs=pq[:, 2 + cb, 0:n], start=False, stop=True,
                               reads=[BM, Bpq], writes=[Bpo])
                        ot, Bo = osr.next()
                        evac(cb, ot[:, 0:n], po[:, 0:n], [Bpo], [Bo])
                        nst = n if ncols is None else ncols
                        if nst == 1:
                            sch.dma("pool", catT[cb * 128:(cb + 1) * 128, tcol0:tcol0 + 1], ot[:, 0:1], reads=[Bo],
                                    allow_slow_non_contiguous=True)
                        else:
                            sch.dma("pool", catT[cb * 128:(cb + 1) * 128, tcol0:tcol0 + nst], ot[:, 0:nst], reads=[Bo])
                        if mirror_col0 is not None:
                            po, Bpo = por.next()
                            for (wm, qi) in ((0, cb), (2, 2 + cb)):
                                bq_ = pq[:, qi, 0:n]
                                rev = bass.AP(tensor=bq_.tensor, offset=bq_.offset + n - 1, ap=[[bq_.ap[0][0], 128], [-1, n]])
                                sch.op("pe", P.matmul, po[:, 0:n], lhsT=Mbd[:, wm, cb, :], rhs=rev, start=(wm == 0), stop=(wm == 2),
                                       reads=[BM, Bpq], writes=[Bpo])
                            ot, Bo = osr.next()
                            evac(cb + 1, ot[:, 0:n], po[:, 0:n], [Bpo], [Bo])
                            sch.dma("pool", catT[cb * 128:(cb + 1) * 128, mirror_col0:mirror_col0 + n], ot[:, 0:n], reads=[Bo])

                for cb in range(2):
                    for k in range(32):
                        sch.op("pe", P.matmul, pacc[cb][:, 0:2], lhsT=U[:, k, cb * 128:(cb + 1) * 128], rhs=c0col[:],
                               start=(k == 0), stop=(k == 31), reads=[BU, Bc0], writes=[Bacc[cb]])
                pq, Bpq = pqr.next()
                sch.op("pool", G.memset, pq[:, 2:4, 0:2], 0.0, writes=[Bpq])
                for cb in range(2):
                    evac(cb, pq[:, cb, 0:2], pacc[cb][:, 0:2], [Bacc[cb], Bpq], [Bpq])
                mix(pq, Bpq, 2, 0, ncols=1)
                for nb in range(4):
                    c_lo = 512 * nb + 1
                    for kg in range(8):
                        cs, Bcs = csr.next()
                        sch.dma("sp", cs[:, 0], C4[kg * 512:(kg + 1) * 512, c_lo:c_lo + 512].rearrange("(k p) n -> p k n", p=128), writes=[Bcs])
                        sch.dma("sp", cs[:, 1], S4[kg * 512:(kg + 1) * 512, c_lo:c_lo + 512].rearrange("(k p) n -> p k n", p=128), writes=[Bcs])
                        for kk in range(4):
                            k = kg * 4 + kk
                            for w in range(2):
                                for cb in range(2):
                                    i = w * 2 + cb
                                    sch.op("pe", P.matmul, pacc[i][:], lhsT=U[:, k, cb * 128:(cb + 1) * 128], rhs=cs[:, w, kk, :],
                                           start=(k == 0), stop=(k == 31), reads=[BU, Bcs], writes=[Bacc[i]])
                    pq, Bpq = pqr.next()
                    for i in range(4):
                        evac(i, pq[:, i, :], pacc[i][:], [Bacc[i]], [Bpq])
                    mix(pq, Bpq, 512, c_lo, mirror_col0=3584 - 512 * nb)
                if l == 0:
                    c2 = sbt(es, "c2", [128, 2, 2, 256], BF16)
                    Bc2 = Buf()
                    sch.dma("sp", c2[:, 0], C2.rearrange("(k p) n -> p k n", p=128), writes=[Bc2])
                    sch.dma("sp", c2[:, 1], S2.rearrange("(k p) n -> p k n", p=128), writes=[Bc2])
                    for w in range(2):
                        for cb in range(2):
                            i = w * 2 + cb
                            for k in range(2):
                                sch.op("pe", P.matmul, pacc[i][:, 0:256], lhsT=U[:, 32 + k, cb * 128:(cb + 1) * 128], rhs=c2[:, w, k, :],
                                       start=(k == 0), stop=(k == 1), reads=[BU, Bc2], writes=[Bacc[i]])
                    pq, Bpq = pqr.next()
                    for i in range(4):
                        evac(i, pq[:, i, 0:256], pacc[i][:, 0:256], [Bacc[i]], [Bpq])
                    mix(pq, Bpq, 256, S)
                sch.barrier()

        def phase_d(l):
            HW = T // 2
            NCH = T // 32
            with ExitStack() as es:
                lbt = sbt(es, "lbt", [128, 8])
                lbv = sbt(es, "lbv", [128, 8])
                ngt = sbt(es, "ngt", [128, DEPTH])
                mdir = sbt(es, "mdir", [128, 2, 128])
                bd64 = sbt(es, "bd64", [128, 128])
                cm3 = sbt(es, "cm3", [128, 4, 128])
                epsc = sbt(es, "epsc", [128, 1])
                Bl, Bmd = Buf(), Buf()
                sch.op("pool", G.memset, epsc[:], EPS, writes=[Bmd])
                sch.dma("sp", lbt[:], lbT, writes=[Bl])
                sch.dma("sp", ngt[:], ngT, writes=[Bl])
                sch.dma("sp", mdir[:, 0, :], mfwd_d, writes=[Bmd])
                sch.dma("sp", mdir[:, 1, :], mbwd_d, writes=[Bmd])
                sch.dma("sp", bd64[:], bd64_d, writes=[Bmd])
                sch.dma("sp", cm3[:], cm3_d.rearrange("p (c n) -> p c n", n=128), writes=[Bmd])
                if l == 0:
                    sch.op("pool", G.memset, lbv[:, 0:4], 0.0, reads=[Bl], writes=[Bl])
                    sch.op("pool", G.memset, lbv[:, 4:8], 1.0, reads=[Bl], writes=[Bl])
                else:
                    ex = sbt(es, "lbex", [128, 8])
                    sch.op("act", A.activation, ex[:], lbt[:], AF.Exp, reads=[Bl], writes=[Bl])
                    sch.op("dve", V.tensor_tensor, lbv[:, 4:8], ex[:, 0:4], ex[:, 4:8], ALU.add, reads=[Bl], writes=[Bl])
                    sch.op("dve", V.reciprocal, lbv[:, 4:8], lbv[:, 4:8], reads=[Bl], writes=[Bl])
                    sch.op("dve", V.tensor_tensor, lbv[:, 0:4], ex[:, 4:8], lbv[:, 4:8], ALU.mult, reads=[Bl], writes=[Bl])
                    sch.op("dve", V.tensor_scalar, lbv[:, 4:8], lbv[:, 0:4], -1.0, 1.0, ALU.mult, ALU.add, reads=[Bl], writes=[Bl])
                for hp in range(2):
                    with ExitStack() as hs:
                        vtm = sbt(hs, "hv", [128, NB, 128], BF16)
                        vm = sbt(hs, "hvm", [128, 2, NB, 128], BF16)
                        Bv, Bvm = Buf(), Buf()
                        sch.dma("sp", vtm[:], tm[:, 256 + hp * 128:256 + (hp + 1) * 128].rearrange("(k p) c -> p k c", p=128), writes=[Bv])
                        sch.op("pool", G.memset, vm[:], 0.0, writes=[Bvm])
                        sch.op("pool", G.tensor_copy, vm[:, 0, :, 0:64], vtm[:, :, 0:64], reads=[Bv, Bvm], writes=[Bvm])
                        sch.op("pool", G.tensor_copy, vm[:, 1, :, 64:128], vtm[:, :, 64:128], reads=[Bv, Bvm], writes=[Bvm])
                        chains = []
                        for d in range(2):
                            ch = dict(d=d)
                            ch["qdm"] = sbt(hs, f"qdm{d}", [128, 2, T], BF16)
                            ch["kd"] = sbt(hs, f"kd{d}", [128, T], BF16)
                            ch["kst"] = sbt(hs, f"kst{d}", [128, NB, 128], BF16)
                            ch["dec"] = sbt(hs, f"dec{d}", [128, NCH])
                            ch["B"] = {k: Buf() for k in ("qdm", "kd", "kst", "dec", "o")}
                            sch.op("pool", G.memset, ch["qdm"][:], 0.0, writes=[ch["B"]["qdm"]])
                            chains.append(ch)
                        with ExitStack() as gs:
                            Asets = [([sbt(gs, f"gA{u}{i}", [128, HW]) for i in range(4)], [Buf() for _ in range(4)],
                                      sbt(gs, f"gq{u}", [128, HW], BF16), Buf(), sbt(gs, f"gks{u}", [128, HW], BF16), Buf(),
                                      sbt(gs, f"gtot{u}", [128, HW // 32]), Buf()) for u in range(2)]
                            unit = 0
                            msk = sbt(gs, "gmsk", [128, HW])
                            Bmsk = Buf()
                            sch.op("pool", G.memset, msk[:], 1.0, writes=[Bmsk])
                            sch.op("pool", G.memset, msk[:].rearrange("p (n c) -> p n c", c=32)[:, :, 0:1], 0.0, reads=[Bmsk], writes=[Bmsk])
                            ptk = Rot([pst(gs, f"ptk{i}", [128, 4, 128], BF16) for i in range(2)])
                            ei_ = [0]

                            def unit_gen(ch, h, uset):
                                d = ch["d"]
                                B = ch["B"]
                                lbc = lbv[:, d * 2 + hp:d * 2 + hp + 1]
                                omc = lbv[:, 4 + d * 2 + hp:4 + d * 2 + hp + 1]
                                Ar, BA, qh, Bq, ksh, Bks, tot, Btot = uset
                                c0 = h * HW
                                cs = slice(c0, c0 + HW)
                                A1, A2, A3, A4 = Ar
                                B1, B2, B3, B4 = BA
                                zr = 256 * d + hp * 128
                                sch.dma("sp", A1[:], fz[zr:zr + 128, cs], writes=[B1])
                                sch.dma("sp", qh[:], fb[hp * 128:(hp + 1) * 128, cs], writes=[Bq])
                                sch.op("act", A.activation, A1[:], A1[:], AF.Sigmoid, reads=[B1], writes=[B1])
                                yield
                                sch.op("act", A.activation, A1[:], A1[:], AF.Identity, bias=lbc, scale=omc, reads=[B1, Bl], writes=[B1])
                                yield
                                sch.op("act", A.activation, A2[:], A1[:], AF.Ln, reads=[B1], writes=[B2])
                                yield
                                sch.op("act", A.activation, A1[:], A1[:], AF.Identity, bias=1.0, scale=-1.0, reads=[B1, B2], writes=[B1])
                                sch.op("dve", V.tensor_tensor_scan, A3[:], msk[:], A2[:], 0.0, ALU.mult, ALU.add, reads=[Bmsk, B2], writes=[B3])
                                yield
                                A3v = A3[:].rearrange("p (n c) -> p n c", c=32)
                                A2v = A2[:].rearrange("p (n c) -> p n c", c=32)
                                totb = tot[:].rearrange("p (n o) -> p n o", o=1).broadcast_to([128, HW // 32, 32])
                                sch.op("act", A.copy, tot[:], A3v[:, :, 31], reads=[B3], writes=[Btot])
                                yield
                                if d == 0:
                                    sch.op("dve", V.tensor_tensor, A2v, A3v, totb, ALU.subtract, reads=[Btot, B3, B2], writes=[B2])
                                    asg, rsg = 1.0, -1.0
                                else:
                                    sch.op("dve", V.tensor_tensor, A2[:], A3[:], A2[:], ALU.subtract, reads=[B3, B2], writes=[B2])
                                    yield
                                    sch.op("dve", V.tensor_tensor, A3v, A2v, totb, ALU.subtract, reads=[Btot, B2, B3], writes=[B3])
                                    asg, rsg = -1.0, 1.0
                                yield
                                sch.op("act", A.activation, ch["dec"][:, c0 // 32:(c0 + HW) // 32], tot[:], AF.Exp, reads=[Btot], writes=[B["dec"]])
                                sch.op("act", A.activation, A4[:], A3[:], AF.Exp, scale=asg, reads=[B3, B4], writes=[B4])
                                yield
                                sch.op("dve", V.tensor_tensor, ch["qdm"][0:64, 0, cs], qh[0:64, :], A4[0:64, :], ALU.mult,
                                       reads=[Bq, B4, B["qdm"]], writes=[B["qdm"]])
                                sch.op("pool", G.tensor_tensor, ch["qdm"][64:128, 1, cs], qh[64:128, :], A4[64:128, :], ALU.mult,
                                       reads=[Bq, B4, B["qdm"]], writes=[B["qdm"]])
                                yield
                                sch.op("act", A.activation, A4[:], A3[:], AF.Exp, scale=-asg, reads=[B3, B4], writes=[B4])
                                yield
                                sch.op("dve", V.tensor_tensor, ch["kd"][:, cs], A1[:], A4[:], ALU.mult, reads=[B1, B4], writes=[B["kd"]])
                                yield
                                sch.op("act", A.activation, A4[:], A2[:], AF.Exp, scale=rsg, reads=[B2, B4], writes=[B4])
                                yield
                                sch.op("dve", V.tensor_tensor, ksh[:], A1[:], A4[:], ALU.mult, reads=[B1, B4], writes=[Bks])
                                yield
                                nbh = HW // 128
                                for b4 in range(0, nbh, 4):
                                    nn = min(4, nbh - b4)
                                    pt, Bp = ptk.next()
                                    for j in range(nn):
                                        sch.op("pe", P.transpose, pt[:, j, :], ksh[:, (b4 + j) * 128:(b4 + j + 1) * 128], ident[:],
                                               reads=[Bks, B_const], writes=[Bp])
                                    ei_[0] += 1
                                    evac(ei_[0], ch["kst"][:, h * nbh + b4:h * nbh + b4 + nn, :], pt[:, 0:nn, :], [Bp], [B["kst"]])
                                    yield

                            gbgs = make_bg(gs, 4)
                            units = [(ch, h) for ch in chains for h in range(2)]
                            gens = []
                            SHIFT = 6
                            active = []
                            ui = 0
                            tick = 0
                            while ui < len(units) or active:
                                if ui < len(units) and len(active) < 2 and (not active or active[0][1] >= SHIFT):
                                    ch_, h_ = units[ui]
                                    active.append([unit_gen(ch_, h_, Asets[ui % 2]), 0])
                                    ui += 1
                                tick += 1
                                if l == 0 and tick % 2 == 0:
                                    pump(1, gbgs)
                                for a_ in list(active):
                                    try:
                                        next(a_[0])
                                        a_[1] += 1
                                    except StopIteration:
                                        active.remove(a_)
                            pump(0, gbgs, drain=True)
                            sch.barrier()
                        for ch in chains:
                            ch["o"] = sbt(hs, f"o{ch['d']}", [128, T])
                        with ExitStack() as cs_:
                            vexa = sbt(cs_, "vexa", [128, NB, 4, 128], BF16)
                            Bvex = Buf()
                            for nb in range(NB):
                                sch.op("dve", V.tensor_tensor, vexa[:, nb], cm3[:], vtm[:, nb:nb + 1, :].broadcast_to([128, 4, 128]), ALU.mult,
                                       reads=[Bmd, Bv], writes=[Bvex])
                            for ch in chains:
                                d = ch["d"]
                                ch["psU"] = pst(cs_, f"psU{d}", [128, 4, 128]); ch["BpsU"] = Buf()
                                ch["psS"] = [pst(cs_, f"psS{d}{i}", [128, 128]) for i in range(2)]; ch["BpsS"] = [Buf(), Buf()]
                                ch["psO"] = pst(cs_, f"psO{d}", [128, 128]); ch["BpsO"] = Buf()
                                ch["Sf"] = Rot([sbt(cs_, f"Sf{d}{i}", [128, 128]) for i in range(2)])
                                ch["Spad"] = Rot([sbt(cs_, f"Sp{d}{i}", [128, 2, 128], BF16) for i in range(10)])
                                for (tl, Bt_) in ch["Spad"].t:
                                    sch.op("pool", G.memset, tl[:], 0.0, writes=[Bt_])
                                ch["scb"] = Rot([sbt(cs_, f"scb{d}{i}", [128, 2, 128], BF16) for i in range(2)])
                                ch["order"] = ([32, 33] + list(range(32))) if d == 0 else ([33, 32] + list(range(31, -1, -1)))
                                ch["corder"] = [0, 1, 2, 3] if d == 0 else [3, 2, 1, 0]
                                ch["prev"] = None
                                ch["prev_bd"] = None
                                ch["pending"] = None

                            def front_all(step):
                                res = []
                                for ch in chains:
                                    nb = ch["order"][step]
                                    B = ch["B"]
                                    sch.op("pe", P.matmul, ch["psU"][:].rearrange("p c n -> p (c n)"), lhsT=ch["kst"][:, nb, :],
                                           rhs=vexa[:, nb].rearrange("p c n -> p (c n)"), start=True, stop=True,
                                           reads=[B["kst"], Bvex], writes=[ch["BpsU"]])
                                    ch["before"] = []
                                for ci in range(4):
                                    for k_, ch in enumerate(chains):
                                        nb = ch["order"][step]
                                        B = ch["B"]
                                        c = ch["corder"][ci]
                                        n = nb * 4 + c
                                        ch["before"].append((c, ch["prev_bd"]))
                                        sf, Bsf = ch["Sf"].next()
                                        if ch["prev"] is None:
                                            sch.op("dve", V.tensor_copy, sf[:], ch["psU"][:, c, :], reads=[ch["BpsU"]], writes=[Bsf])
                                        else:
                                            pv, Bpv = ch["prev"]
                                            sch.op("dve", V.scalar_tensor_tensor, sf[:], pv[:], ch["dec"][:, n:n + 1], ch["psU"][:, c, :],
                                                   ALU.mult, ALU.add, reads=[Bpv, B["dec"], ch["BpsU"]], writes=[Bsf])
                                        ch["prev"] = (sf, Bsf)
                                        sp_, Bsp = ch["Spad"].next()
                                        base = sp_[:]
                                        ov = bass.AP(tensor=base.tensor, offset=base.offset, ap=[[base.ap[0][0], 128], [192, 2], [1, 64]])
                                        iv = sf[:].rearrange("p (g c) -> p g c", g=2)
                                        sch.op("act", A.copy, ov, iv, reads=[Bsf, Bsp], writes=[Bsp])
                                        ch["prev_bd"] = (sp_, Bsp)
                                for ch in chains:
                                    nb = ch["order"][step]
                                    B = ch["B"]
                                    cols = slice(nb * 128, (nb + 1) * 128)
                                    scb, Bscb = ch["scb"].next()
                                    for hh in range(2):
                                        sch.op("pe", P.matmul, ch["psS"][hh][:], lhsT=ch["kd"][:, cols], rhs=ch["qdm"][:, hh, cols],
                                               start=True, stop=True, reads=[B["kd"], B["qdm"]], writes=[ch["BpsS"][hh]])
                                        sch.op("dve", V.tensor_tensor, scb[:, hh, :], ch["psS"][hh][:], mdir[:, ch["d"], :], ALU.mult,
                                               reads=[ch["BpsS"][hh], Bmd], writes=[Bscb])
                                    res.append((nb, ch["before"], scb, Bscb))
                                return res

                            def back(ch, pend):
                                nb, before, scb, Bscb = pend
                                B = ch["B"]
                                cols = slice(nb * 128, (nb + 1) * 128)
                                inter = [(c, s_) for (c, s_) in before if s_ is not None]
                                for hh in range(2):
                                    sch.op("pe", P.matmul, ch["psO"][:], lhsT=vm[:, hh, nb, :], rhs=scb[:, hh, :], start=(hh == 0),
                                           stop=(hh == 1 and not inter), reads=[Bvm, Bscb], writes=[ch["BpsO"]])
                                for i, (c, (sp_, Bsp)) in enumerate(inter):
                                    for hh in range(2):
                                        sch.op("pe", P.matmul, ch["psO"][:, 32 * c:32 * c + 32], lhsT=sp_[:, hh, :],
                                               rhs=ch["qdm"][:, hh, nb * 128 + 32 * c:nb * 128 + 32 * c + 32], start=False,
                                               stop=(i == len(inter) - 1 and hh == 1), reads=[Bsp, B["qdm"]], writes=[ch["BpsO"]])
                                sch.op("act", A.copy, ch["o"][:, cols], ch["psO"][:], reads=[ch["BpsO"]], writes=[B["o"]])

                            bgs = make_bg(cs_, 4)
                            for step in range(NB + 1):
                                if l == 0:
                                    pump(1, bgs)
                                newp = None
                                if step < NB:
                                    newp = front_all(step)
                                if step >= 1:
                                    for ch in chains:
                                        back(ch, ch["pending"])
                                if step < NB:
                                    for ch, p_ in zip(chains, newp):
                                        ch["pending"] = p_
                            pump(0, bgs, drain=True)
                            sch.barrier()
                        with ExitStack() as rs:
                            of, ob = chains[0]["o"], chains[1]["o"]
                            Bof, Bob = chains[0]["B"]["o"], chains[1]["B"]["o"]
                            sch.op("dve", V.tensor_tensor, of[:], of[:], ob[:], ALU.add, reads=[Bof, Bob], writes=[Bof])
                            gt = sbt(rs, "hg", [128, T], BF16)
                            Bg = Buf()
                            sch.dma("sp", gt[:], fb[256 + hp * 128:256 + (hp + 1) * 128, :], writes=[Bg])
                            sqr = Rot([sbt(rs, f"hsq{i}", [128, 512]) for i in range(2)])
                            sgr = Rot([sbt(rs, f"hsg{i}", [128, 512]) for i in range(2)])
                            rr = Rot([sbt(rs, f"hr{i}", [128, 512]) for i in range(2)])
                            orr = Rot([sbt(rs, f"hob{i}", [128, 512], BF16) for i in range(2)])
                            pms = Rot([pst(rs, f"pms{i}", [128, 512]) for i in range(2)])
                            for t0 in range(0, T, 512):
                                n = min(512, T - t0)
                                cs = slice(t0, t0 + n)
                                sq, Bsq = sqr.next()
                                sch.op("act", A.activation, sq[:, 0:n], of[:, cs], AF.Square, reads=[Bof], writes=[Bsq])
                                pm_, Bpm = pms.next()
                                sch.op("pe", P.matmul, pm_[:, 0:n], lhsT=bd64[:], rhs=sq[:, 0:n], start=True, stop=True, pemode="f32",
                                       reads=[Bmd, Bsq], writes=[Bpm])
                                r_, Br = rr.next()
                                sch.op("act", A.activation, r_[:, 0:n], pm_[:, 0:n], AF.Ln, bias=epsc[:, 0:1], scale=1.0, reads=[Bpm, Bmd], writes=[Br])
                                sch.op("act", A.activation, r_[:, 0:n], r_[:, 0:n], AF.Exp, scale=-0.5, reads=[Br], writes=[Br])
                                sch.op("dve", V.tensor_tensor, r_[:, 0:n], r_[:, 0:n], of[:, cs], ALU.mult, reads=[Br, Bof], writes=[Br])
                                sg, Bsg = sgr.next()
                                sch.op("act", A.activation, sg[:, 0:n], gt[:, cs], AF.Silu, reads=[Bg], writes=[Bsg])
                                ob_, Bo = orr.next()
                                sch.op("dve", V.scalar_tensor_tensor, ob_[:, 0:n], r_[:, 0:n], ngt[:, l:l + 1], sg[:, 0:n], ALU.mult, ALU.mult,
                                       reads=[Br, Bl, Bsg], writes=[Bo])
                                sch.dma("pool", catT[256 + hp * 128:256 + (hp + 1) * 128, cs], ob_[:, 0:n], reads=[Bo])
                            sch.barrier()

        def phase_e(l):
            with ExitStack() as es:
                sets = []
                for u in range(2):
                    qm = sbt(es, f"naq{u}", [128, 2, T], BF16)
                    kT = sbt(es, f"nak{u}", [128, T], BF16)
                    vm = sbt(es, f"nav{u}", [128, 2, NB, 128], BF16)
                    Bq, Bk, Bv = Buf(), Buf(), Buf()
                    sch.op("pool", G.memset, qm[:], 0.0, writes=[Bq])
                    sch.op("pool", G.memset, vm[:], 0.0, writes=[Bv])
                    sets.append((qm, kT, vm, Bq, Bk, Bv))
                bg_rng = ((8, 14), (0, 8), (14, 20))
                bias = [[sbt(es, f"nab{hh}{g}", [128, bg_rng[g][1] - bg_rng[g][0], 512]) for g in range(3)] for hh in range(2)]
                Bb = [[Buf() for g in range(3)] for hh in range(2)]
                oT = sbt(es, "nao", [128, T], BF16)
                oneh = sbt(es, "oneh", [128, 2, 128], BF16)
                sbr = Rot([sbt(es, f"nas{i}", [128, 512]) for i in range(4)])
                pr = Rot([sbt(es, f"nap{i}", [128, 512], BF16) for i in range(6)])
                rcr = Rot([sbt(es, f"narc{i}", [128, 512]) for i in range(2)])
                psr = Rot([pst(es, f"naS{i}", [128, 512]) for i in range(4)])
                numr = Rot([pst(es, f"naN{i}", [128, 512]) for i in range(2)])
                denr = Rot([pst(es, f"naD{i}", [128, 512]) for i in range(2)])
                Bo, B1 = Buf(), Buf()
                sch.op("pool", G.memset, oneh[:], 0.0, writes=[B1])
                for hh in range(2):
                    sch.op("pool", G.memset, oneh[:, hh, hh * 64:(hh + 1) * 64], 1.0, reads=[B1], writes=[B1])
                LOOK = 3
                ebgs = make_bg(es, 4)

                def load_qkv(hp):
                    qm, kT, vm, Bq, Bk, Bv = sets[hp % 2]
                    for hh in range(2):
                        r0 = 512 + hp * 128 + hh * 64
                        sch.dma("sp", qm[hh * 64:(hh + 1) * 64, hh, :], fb[r0:r0 + 64, :], reads=[Bq], writes=[Bq])
                    sch.dma("sp", kT[:], fb[1024 + hp * 128:1024 + (hp + 1) * 128, :], writes=[Bk])
                    for hh in range(2):
                        c0 = 512 + hp * 128 + hh * 64
                        sch.dma("sp", vm[:, hh, :, hh * 64:(hh + 1) * 64], tm[:, c0:c0 + 64].rearrange("(k p) c -> p k c", p=128),
                                reads=[Bv], writes=[Bv])

                def load_bias(hp, g):
                    t0_, t1_ = bg_rng[g]
                    for hh in range(2):
                        sch.dma("sp", bias[hh][g][:], nabias[l, hp * 2 + hh].rearrange("p (t n) -> p t n", n=512)[:, t0_:t1_, :],
                                writes=[Bb[hh][g]])

                load_qkv(0)
                for g in range(3):
                    load_bias(0, g)
                for hp in range(4):
                    qm, kT, vm, Bq, Bk, Bv = sets[hp % 2]
                    if hp + 1 < 4:
                        load_qkv(hp + 1)
                    items = []
                    for qb in range(8):
                        if qb == 0:
                            kts = [(2 * p * 64, (0, p)) for p in range(6)]
                        elif qb == 7:
                            kts = [((52 + 2 * p) * 64, (2, p)) for p in range(6)]
                        else:
                            kts = [((8 * qb - 4 + 2 * po) * 64, (1, po)) for po in range(8)]
                        kts += [(S, None), (S + 128, None)]
                        items.append((qb * 512, 512, kts, qb))
                    if l == 0:
                        items.append((S, L, [(S, None), (S + 128, None)], 8))
                    for (q0, nq, kts, qb) in items:
                        flat = [(hh, kt0, bi) for hh in range(2) for (kt0, bi) in kts]
                        prod = [None] * len(flat)
                        num, Bn = numr.next()
                        den, Bd = denr.next()
                        for i in range(len(flat) + LOOK):
                            if i < len(flat):
                                hh, kt0, bi = flat[i]
                                ps, Bps = psr.next()
                                sch.op("pe", P.matmul, ps[:, 0:nq], lhsT=kT[:, kt0:kt0 + 128], rhs=qm[:, hh, q0:q0 + nq], start=True, stop=True,
                                       reads=[Bk, Bq], writes=[Bps])
                                pt, Bpt = pr.next()
                                if bi is None:
                                    sch.op("act", A.activation, pt[:, 0:nq], ps[:, 0:nq], AF.Exp, reads=[Bps], writes=[Bpt])
                                else:
                                    st, Bst = sbr.next()
                                    sch.op("dve", V.tensor_tensor, st[:, 0:nq], ps[:, 0:nq], bias[hh][bi[0]][:, bi[1], 0:nq], ALU.add,
                                           reads=[Bps, Bb[hh][bi[0]]], writes=[Bst])
                                    sch.op("act", A.activation, pt[:, 0:nq], st[:, 0:nq], AF.Exp, reads=[Bst], writes=[Bpt])
                                prod[i] = (pt, Bpt)
                            j = i - LOOK
                            if j >= 0:
                                hh, kt0, bi = flat[j]
                                pt, Bpt = prod[j]
                                first = (j == 0)
                                last = (j == len(flat) - 1)
                                sch.op("pe", P.matmul, num[:, 0:nq], lhsT=vm[:, hh, kt0 // 128, :], rhs=pt[:, 0:nq],
                                       start=first, stop=last, reads=[Bv, Bpt], writes=[Bn])
                                sch.op("pe", P.matmul, den[:, 0:nq], lhsT=oneh[:, hh, :], rhs=pt[:, 0:nq],
                                       start=first, stop=last, reads=[B1, Bpt], writes=[Bd])
                        rc, Brc = rcr.next()
                        sch.op("dve", V.reciprocal, rc[:, 0:nq], den[:, 0:nq], reads=[Bd], writes=[Brc])
                        sch.op("dve", V.tensor_tensor, oT[:, q0:q0 + nq], num[:, 0:nq], rc[:, 0:nq], ALU.mult, reads=[Bn, Brc], writes=[Bo])
                        if l == 0:
                            pump(4, ebgs)
                        if hp + 1 < 4:
                            if qb == 0:
                                load_bias(hp + 1, 0)
                            elif qb == 6:
                                load_bias(hp + 1, 1)
                            elif qb == 7:
                                load_bias(hp + 1, 2)
                    ncol = T if l == 0 else S
                    sch.dma("pool", catT[512 + hp * 128:512 + (hp + 1) * 128, 0:ncol], oT[:, 0:ncol], reads=[Bo])
                pump(0, ebgs, drain=True)
                sch.barrier()

        def phase_f1(l):
            xsrc = xin if l == 0 else xres
            with ExitStack() as es:
                wob, Bw = load_weight(es, "woutbf", w_out_b[l].rearrange("(k p) n -> p k n", p=128), 8, D)
                g1 = [sbt(es, f"g1_{v}", [128, D]) for v in range(2)]
                Bg = Buf()
                for v in range(2):
                    sch.dma("pool", g1[v][:], mod2d[2 * l + v, 2 * D:3 * D].partition_broadcast(128), writes=[Bg])
                fbgs = make_bg(es, 4)
                cr = Rot([sbt(es, f"cat{i}", [128, 8, 512], BF16) for i in range(2)])
                xr = Rot([sbt(es, f"xt{i}", [128, D]) for i in range(3)])
                yr = Rot([sbt(es, f"yt{i}", [128, D]) for i in range(2)])
                pyr = Rot([pst(es, f"py{i}", [128, 512]) for i in range(4)])
                sbs = [(i * 512, 512, 0) for i in range(8)] + ([(S, L, 1)] if l == 0 else [])
                for (t0, ntok, v) in sbs:
                    ct, Bc = cr.next()
                    sch.dma("sp", ct[:, :, 0:ntok], catT[:, t0:t0 + ntok].rearrange("(k p) t -> p k t", p=128), writes=[Bc])
                    for tb in range(ntok // 128):
                        xt, Bx = xr.next()
                        rows = slice(t0 + tb * 128, t0 + (tb + 1) * 128)
                        sch.dma("sp", xt[:], xsrc[rows, :], writes=[Bx])
                        yt, By = yr.next()
                        for nb in range(2):
                            py, Bp = pyr.next()
                            for k in range(8):
                                sch.op("pe", P.matmul, py[:], lhsT=ct[:, k, tb * 128:(tb + 1) * 128], rhs=wob[:, k, nb * 512:(nb + 1) * 512],
                                       start=(k == 0), stop=(k == 7), reads=[Bc, Bw], writes=[Bp])
                            sch.op("dve", V.tensor_tensor, yt[:, nb * 512:(nb + 1) * 512], py[:], g1[v][:, nb * 512:(nb + 1) * 512], ALU.mult,
                                   reads=[Bp, Bg], writes=[By])
                        sch.op("dve", V.tensor_tensor, yt[:], yt[:], xt[:], ALU.add, reads=[By, Bx], writes=[By])
                        sch.dma("pool", xres[rows, :], yt[:], reads=[By])
                        if l == 0:
                            pump(2, fbgs)
                if l == 0:
                    pump(0, fbgs, drain=True)
                sch.barrier()

        def phase_f2(l):
            last = (l == DEPTH - 1)
            with ExitStack() as es:
                wg, Bwg = load_weight(es, "wgbf", w_gate_b[l].rearrange("(k p) n -> p k n", p=128), 8, DFF)
                wu, Bwu = load_weight(es, "wubf", w_up_b[l].rearrange("(k p) n -> p k n", p=128), 8, DFF)
                wd, Bwd = load_weight(es, "wdbf", w_down_b[l].rearrange("(j p) n -> p j n", p=128), NJ, D)
                junk = sbt(es, "junk", [128, D], BF16)
                nt = (junk, Buf(), Rot([sbt(es, f"ssq{i}", [128, 4]) for i in range(4)]),
                      Rot([sbt(es, f"ntmp{i}", [128, D]) for i in range(1)]))
                Gt = sbt(es, "G2", [128, D])
                SHt = sbt(es, "SH2", [128, D])
                g2 = sbt(es, "g2", [128, D])
                Bm = Buf()

                def load_mods(v):
                    sct, Bsc = nt[3].next()
                    sch.dma("pool", SHt[:], mod2d[2 * l + v, 3 * D:4 * D].partition_broadcast(128), writes=[Bm])
                    sch.dma("pool", Gt[:], norm2_g[l].partition_broadcast(128), writes=[Bm])
                    sch.dma("pool", g2[:], mod2d[2 * l + v, 5 * D:6 * D].partition_broadcast(128), writes=[Bm])
                    sch.dma("pool", sct[:], mod2d[2 * l + v, 4 * D:5 * D].partition_broadcast(128), writes=[Bsc])
                    sch.op("dve", V.scalar_tensor_tensor, Gt[:], sct[:], 1.0, Gt[:], ALU.add, ALU.mult, reads=[Bsc, Bm], writes=[Bm])
                fg = None
                if last:
                    fg = sbt(es, "fg", [128, D])
                    Bfg = Buf()
                    sch.dma("pool", fg[:], final_g.partition_broadcast(128), writes=[Bfg])
                x1 = [(sbt(es, f"x1_{i}", [128, D]), Buf()) for i in range(4)]
                ar = Rot([sbt(es, f"abf{i}", [128, D], BF16) for i in range(1)])
                aT = sbt(es, "aT", [128, 8, 512], BF16)
                BaT = Buf()
                hT = sbt(es, "hT", [128, NJ, 512], BF16)
                BhT = Buf()
                sgr = Rot([sbt(es, f"sg{i}", [128, 512]) for i in range(1)])
                ptrr = Rot([pst(es, f"ptr{i}", [128, 4, 128], BF16) for i in range(1)])
                pgr = Rot([pst(es, f"pg{i}", [128, 512]) for i in range(2)])
                pur = Rot([pst(es, f"pu{i}", [128, 512]) for i in range(2)])
                pdr = Rot([pst(es, f"pd{i}", [128, 512]) for i in range(3)])
                sbs = [(i * 512, 512, 0) for i in range(8)] + ([(S, L, 1)] if l == 0 else [])
                curv = None
                for (t0, ntok, v) in sbs:
                    if v != curv:
                        load_mods(v)
                        curv = v
                    ntb = ntok // 128
                    for tb in range(ntb):
                        xt, Bx = x1[tb]
                        sch.dma("sp", xt[:], xres[t0 + tb * 128:t0 + (tb + 1) * 128, :], writes=[Bx])
                        a_bf, Ba = ar.next()
                        norm_block(nt, xt, Bx, Gt, SHt, Bm, a_bf, Ba)
                        transpose_block(ptrr, a_bf, Ba, aT, BaT, tb, tb)
                    for j in range(NJ):
                        pg, Bpg = pgr.next()
                        pu, Bpu = pur.next()
                        for k in range(8):
                            sch.op("pe", P.matmul, pg[:, 0:ntok], lhsT=wg[:, k, j * 128:(j + 1) * 128], rhs=aT[:, k, 0:ntok],
                                   start=(k == 0), stop=(k == 7), reads=[Bwg, BaT], writes=[Bpg])
                        for k in range(8):
                            sch.op("pe", P.matmul, pu[:, 0:ntok], lhsT=wu[:, k, j * 128:(j + 1) * 128], rhs=aT[:, k, 0:ntok],
                                   start=(k == 0), stop=(k == 7), reads=[Bwu, BaT], writes=[Bpu])
                        sg, Bsg = sgr.next()
                        sch.op("act", A.activation, sg[:, 0:ntok], pg[:, 0:ntok], AF.Silu, reads=[Bpg], writes=[Bsg])
                        sch.op("dve", V.tensor_tensor, hT[:, j, 0:ntok], sg[:, 0:ntok], pu[:, 0:ntok], ALU.mult, reads=[Bsg, Bpu], writes=[BhT])
                    for tb in range(ntb):
                        xt, Bx = x1[tb]
                        rows = slice(t0 + tb * 128, t0 + (tb + 1) * 128)
                        tp, Bt = nt[3].next()
                        for nb in range(2):
                            pd, Bpd = pdr.next()
                            for j in range(NJ):
                                sch.op("pe", P.matmul, pd[:], lhsT=hT[:, j, tb * 128:(tb + 1) * 128], rhs=wd[:, j, nb * 512:(nb + 1) * 512],
                                       start=(j == 0), stop=(j == NJ - 1), reads=[BhT, Bwd], writes=[Bpd])
                            sch.op("dve", V.tensor_tensor, tp[:, nb * 512:(nb + 1) * 512], pd[:], g2[:, nb * 512:(nb + 1) * 512], ALU.mult,
                                   reads=[Bpd, Bm], writes=[Bt])
                        sch.op("dve", V.tensor_tensor, xt[:], xt[:], tp[:], ALU.add, reads=[Bx, Bt], writes=[Bx])
                        if not last:
                            sch.dma("pool", xres[rows, :], xt[:], reads=[Bx])
                        else:
                            sq, Bs = nt[2].next()
                            sch.op("act", A.activation, junk[:], xt[:], AF.Square, accum_out=sq[:, 0:1], reads=[Bx], writes=[nt[1], Bs])
                            sch.op("act", A.activation, sq[:, 1:2], sq[:, 0:1], AF.Sqrt, bias=EPS, scale=1.0 / D, reads=[Bs], writes=[Bs])
                            sch.op("dve", V.reciprocal, sq[:, 2:3], sq[:, 1:2], reads=[Bs], writes=[Bs])
                            sch.op("dve", V.scalar_tensor_tensor, xt[:], xt[:], sq[:, 2:3], fg[:], ALU.mult, ALU.mult,
                                   reads=[Bx, Bs, Bfg], writes=[Bx])
                            sch.dma("pool", out[rows, :], xt[:], reads=[Bx])
                sch.barrier()

        phases = [("wcast", phase_wcast, None), ("mod", phase_mod, None)]
        for l in range(DEPTH):
            phases += [(f"ab{l}", phase_ab, l), (f"c{l}", phase_c, l), (f"d{l}", phase_d, l), (f"e{l}", phase_e, l),
                       (f"f1{l}", phase_f1, l), (f"f2{l}", phase_f2, l)]
        for name, fn, arg in phases:
            if arg is None:
                fn()
            else:
                fn(arg)
            if stop is not None and name == stop:
                break
        sch.finish()
        nc._sched_stats = (sch.nops, sch.nwaits)
    return nc


def _prep(inputs):
    f32 = np.float32
    c = _consts()
    x = np.asarray(inputs["x"], f32)
    ctx = np.asarray(inputs["ctx"], f32)
    cvec = np.asarray(inputs["c"], f32)
    cctx = np.asarray(inputs["c_ctx"], f32)
    lb = np.asarray(inputs["hgrn_lb"], f32)
    lbT = np.ascontiguousarray(lb.reshape(2, 2, 2, 128).transpose(3, 0, 1, 2).reshape(128, 8))
    ng = np.asarray(inputs["hgrn_norm_g"], f32)
    ngT = np.ascontiguousarray(np.concatenate([ng, ng], axis=1).T)
    rpb = np.asarray(inputs["na_rpb"], f32)
    nabias = np.stack([_na_bias_tiles(rpb[l]) for l in range(DEPTH)])
    shared = {
        "w_mod": np.asarray(inputs["w_mod"], f32), "b_mod": np.asarray(inputs["b_mod"], f32),
        "norm1_g": np.asarray(inputs["norm1_g"], f32), "w_in": np.asarray(inputs["w_in"], f32),
        "fourier_w": np.asarray(inputs["fourier_w"], f32), "lbT": lbT, "ngT": ngT, "nabias": nabias,
        "w_out": np.asarray(inputs["w_out"], f32), "norm2_g": np.asarray(inputs["norm2_g"], f32),
        "w_gate": np.asarray(inputs["w_ffn_gate"], f32), "w_up": np.asarray(inputs["w_ffn_up"], f32),
        "w_down": np.asarray(inputs["w_ffn_down"], f32), "final_g": np.asarray(inputs["final_norm_g"], f32),
        "C4": c["C4"], "S4": c["S4"], "C2": c["C2"], "S2": c["S2"], "C64bd": c["C64bd"], "S64bdn": c["S64bdn"],
        "ident": c["ident"], "bdmask": c["bdmask"], "bd64": c["bd64"], "mfwd": c["mfwd"], "mbwd": c["mbwd"], "cm3": c["cm3"],
    }
    maps = []
    for b in range(8):
        m = dict(shared)
        m["xin"] = np.ascontiguousarray(np.concatenate([x[b], ctx[b]], axis=0))
        cv = np.stack([cvec[b], cctx], axis=0)
        m["cT"] = np.ascontiguousarray(cv.reshape(2, 8, 128).transpose(2, 1, 0).reshape(128, 16))
        maps.append(m)
    return maps


def kernel(**inputs):
    maps = _prep(inputs)
    nc = build()
    res = run_bass_kernel_spmd(nc, maps, core_ids=list(range(8)))
    return np.stack([np.asarray(r["out"], np.float32) for r in res.results], axis=0)
```

```python
import numpy as np
import ml_dtypes
from contextlib import ExitStack
import concourse.bass as bass
import concourse.mybir as mybir
from concourse.bass_utils import run_bass_kernel_spmd

F32 = mybir.dt.float32
BF16 = mybir.dt.bfloat16
AF = mybir.ActivationFunctionType
ALU = mybir.AluOpType

D = 1024
S = 4096
L = 256
T = S + L
NB = T // 128
DFF = 2816
NJ = DFF // 128
INW = 3072
EPS = 1e-6
NEG = -1e30
DEPTH = 2


class _Op:
    __slots__ = ("eng", "fn", "args", "kw", "deps", "signal", "sem", "val", "dma", "epoch")


class Buf:
    __slots__ = ("w", "r", "g")

    def __init__(self):
        self.w = {}
        self.r = {}
        self.g = ()


class Sched:
    CENG = ("pe", "act", "dve", "pool")
    SEMCAP = 30000

    def __init__(self, nc, es):
        self.nc = nc
        self.eobj = dict(pe=nc.tensor, act=nc.scalar, dve=nc.vector, pool=nc.gpsimd, sp=nc.sync)
        self.pending = []
        self.lastc = {}
        self.dma_since = []
        self.bar_deps = {}
        ncs = dict(pe=3, act=3, dve=4, pool=2)
        self.csem = {e: [es.enter_context(nc.semaphore(f"c{e}{i}")) for i in range(ncs[e])] for e in self.CENG}
        self.ccount = {e: 0 for e in self.CENG}
        self.dsem = {q: [es.enter_context(nc.semaphore(f"d{q}{i}")) for i in range(8)] for q in ("sp", "pool")}
        self.dcount = {q: [0] * 8 for q in ("sp", "pool")}
        self.drr = {q: 0 for q in ("sp", "pool")}
        self.waited = {e: {} for e in self.eobj}
        self.nops = 0
        self.nwaits = 0
        self.epoch = 0
        self.pe_mode = "f"

    def op(self, eng, fn, *args, reads=(), writes=(), dma=False, force=False, **kw):
        o = _Op()
        o.eng, o.fn, o.args, o.kw, o.dma = eng, fn, args, kw, dma
        o.signal = dma or force
        o.sem = None
        o.val = 0
        o.epoch = self.epoch
        deps = {}
        if eng == "pe":
            mode = kw.pop("pemode", "f")
            if mode != self.pe_mode and "pe" in self.lastc:
                deps[id(self.lastc["pe"])] = self.lastc["pe"]
            self.pe_mode = mode

        def need(d, same_ok, bar=False):
            if d.epoch < self.epoch and not bar:
                return
            if same_ok and (not d.dma) and (not dma) and d.eng == eng:
                return
            deps[id(d)] = d

        for b in reads:
            for d in b.w.values():
                need(d, eng == "pe")
        for b in writes:
            if b.r:
                b.g = tuple(b.r.values()) + tuple(b.w.values())
                b.r = {}
                b.w = {}
            for d in b.g:
                need(d, True)
        for d in self.bar_deps.pop(eng, ()):
            need(d, True, True)
        for d in deps.values():
            d.signal = True
        o.deps = list(deps.values())
        key = ("d", id(o)) if dma else eng
        for b in reads:
            b.r[key] = o
        for b in writes:
            b.w[key] = o
        if dma:
            self.dma_since.append(o)
        else:
            self.lastc[eng] = o
        self.pending.append(o)
        return o

    def dma(self, q, out, in_, reads=(), writes=(), **kw):
        fn = self.nc.sync.dma_start if q == "sp" else self.nc.gpsimd.dma_start
        return self.op(q, fn, reads=reads, writes=writes, dma=True, out=out, in_=in_, **kw)

    def flush(self):
        for o in self.pending:
            e = self.eobj[o.eng]
            w = self.waited[o.eng]
            for d in o.deps:
                assert d.sem is not None, "dependency on unsignalled op"
                k = id(d.sem)
                if w.get(k, 0) >= d.val:
                    continue
                e.wait_ge(d.sem, d.val)
                w[k] = d.val
                self.nwaits += 1
            ins = o.fn(*o.args, **o.kw)
            self.nops += 1
            if o.dma:
                q = o.eng
                i = self.drr[q]
                self.drr[q] = (i + 1) % 8
                self.dcount[q][i] += 16
                o.sem, o.val = self.dsem[q][i], self.dcount[q][i]
                ins.then_inc(o.sem, 16)
            elif o.signal:
                n = self.ccount[o.eng]
                self.ccount[o.eng] = n + 1
                sems = self.csem[o.eng]
                si = n // self.SEMCAP
                assert si < len(sems), "out of compute semaphores"
                o.sem, o.val = sems[si], n % self.SEMCAP + 1
                ins.then_inc(o.sem, 1)
            o.fn = o.args = o.kw = None
        self.pending = []

    def barrier(self):
        deps = [self.lastc[e] for e in self.CENG if e in self.lastc] + self.dma_since
        for d in deps:
            d.signal = True
        self.flush()
        old = self.bar_deps
        self.bar_deps = {e: list(deps) + list(old.get(e, ())) for e in self.eobj}
        self.dma_since = []
        self.epoch += 1

    def finish(self):
        self.barrier()
        sp = self.nc.sync
        for d in self.bar_deps["sp"]:
            k = id(d.sem)
            if self.waited["sp"].get(k, 0) >= d.val:
                continue
            sp.wait_ge(d.sem, d.val)
            self.waited["sp"][k] = d.val


class Rot:
    def __init__(self, tiles):
        self.t = [(t, Buf()) for t in tiles]
        self.i = 0

    def next(self):
        r = self.t[self.i]
        self.i = (self.i + 1) % len(self.t)
        return r


_CONST = {}


def _consts():
    if _CONST:
        return _CONST
    bf = ml_dtypes.bfloat16
    t = np.arange(S, dtype=np.int64)
    m = (t[:, None] * t[None, :]) % S
    ang = 2.0 * np.pi * m.astype(np.float64) / S
    _CONST["C4"] = (np.cos(ang) / 64.0).astype(np.float32).astype(bf)
    _CONST["S4"] = (np.sin(ang) / 64.0).astype(np.float32).astype(bf)
    del ang, m
    t = np.arange(L, dtype=np.int64)
    ang = 2.0 * np.pi * ((t[:, None] * t[None, :]) % L).astype(np.float64) / L
    _CONST["C2"] = (np.cos(ang) / 16.0).astype(np.float32).astype(bf)
    _CONST["S2"] = (np.sin(ang) / 16.0).astype(np.float32).astype(bf)
    t = np.arange(64, dtype=np.int64)
    ang = 2.0 * np.pi * ((t[:, None] * t[None, :]) % 64).astype(np.float64) / 64
    c64 = np.cos(ang) / 8.0
    s64 = np.sin(ang) / 8.0
    z = np.zeros((64, 64))
    _CONST["C64bd"] = np.block([[c64, z], [z, c64]]).astype(np.float32)
    _CONST["S64bdn"] = (-np.block([[s64, z], [z, s64]])).astype(np.float32)
    _CONST["ident"] = np.eye(128, dtype=np.float32).astype(bf)
    bd = np.zeros((128, 128), np.float32)
    bd[:64, :64] = 1.0
    bd[64:, 64:] = 1.0
    _CONST["bdmask"] = bd
    _CONST["bd64"] = (bd / 64.0).astype(np.float32)
    s_ = np.arange(128)[:, None]
    t_ = np.arange(128)[None, :]
    same = (s_ // 32) == (t_ // 32)
    _CONST["mfwd"] = (same & (s_ <= t_)).astype(np.float32)
    _CONST["mbwd"] = (same & (s_ >= t_)).astype(np.float32)
    cm = np.zeros((128, 4, 128), np.float32)
    for c_ in range(4):
        cm[32 * c_:32 * c_ + 32, c_, :] = 1.0
    _CONST["cm3"] = cm.reshape(128, 512)
    return _CONST


def _na_bias_tiles(rpb):
    H = rpb.shape[0]
    kc = np.arange(64)[:, None]
    qc = np.arange(64)[None, :]
    wc = np.clip(qc - 8, 0, 48)
    colvalid = (kc >= wc) & (kc < wc + 16)
    cidx = np.clip(kc - qc + 15, 0, 30)
    specs = [(8, 8 - 4 + 2 * po) for po in range(8)] + [(0, 2 * p) for p in range(6)] + [(56, 52 + 2 * p) for p in range(6)]
    out = np.full((H, 128, 20, 512), NEG, np.float32)
    for ti, (r0, kr0) in enumerate(specs):
        for kr2 in range(2):
            kr = kr0 + kr2
            for qr in range(8):
                r = r0 + qr
                rs = min(max(r - 4, 0), 56)
                if not (rs <= kr < rs + 8):
                    continue
                ridx = kr - r + 7
                blk = np.where(colvalid[None], rpb[:, ridx][:, cidx], NEG)
                out[:, kr2 * 64:(kr2 + 1) * 64, ti, qr * 64:(qr + 1) * 64] = blk
    return out.reshape(H, 128, 20 * 512)


def build(debug=False, stop=None):
    nc = bass.Bass("TRN2", target_bir_lowering=False)
    skind = "ExternalOutput" if debug else "Internal"

    def din(name, shape, dt=F32):
        return nc.dram_tensor(name, list(shape), dt, kind="ExternalInput").ap()

    def dscr(name, shape, dt=F32):
        return nc.dram_tensor(name, list(shape), dt, kind=skind).ap()

    xin = din("xin", [T, D])
    cT = din("cT", [128, 16])
    w_mod = din("w_mod", [DEPTH, D, 6 * D])
    b_mod = din("b_mod", [DEPTH, 6 * D])
    norm1_g = din("norm1_g", [DEPTH, D])
    w_in = din("w_in", [DEPTH, D, INW])
    fourier_w = din("fourier_w", [DEPTH, 4, 64, 64])
    lbT = din("lbT", [128, 8])
    ngT = din("ngT", [128, DEPTH])
    nabias = din("nabias", [DEPTH, 8, 128, 20 * 512])
    w_out = din("w_out", [DEPTH, D, D])
    norm2_g = din("norm2_g", [DEPTH, D])
    w_gate = din("w_gate", [DEPTH, D, DFF])
    w_up = din("w_up", [DEPTH, D, DFF])
    w_down = din("w_down", [DEPTH, DFF, D])
    final_g = din("final_g", [D])
    C4 = din("C4", [S, S], BF16)
    S4 = din("S4", [S, S], BF16)
    C2 = din("C2", [L, L], BF16)
    S2 = din("S2", [L, L], BF16)
    C64bd_d = din("C64bd", [128, 128])
    S64bdn_d = din("S64bdn", [128, 128])
    ident_d = din("ident", [128, 128], BF16)
    bdmask_d = din("bdmask", [128, 128])
    bd64_d = din("bd64", [128, 128])
    mfwd_d = din("mfwd", [128, 128])
    mbwd_d = din("mbwd", [128, 128])
    cm3_d = din("cm3", [128, 512])
    out = nc.dram_tensor("out", [S, D], F32, kind="ExternalOutput").ap()

    mod2d = dscr("mod2d", [2 * DEPTH, 6 * D])
    fz = dscr("fz", [512, T])
    fb = dscr("fb", [1536, T], BF16)
    tm = dscr("tm", [T, 1024], BF16)
    catT = dscr("catT", [1024, T], BF16)
    xres = dscr("xres", [T, D])
    w_in_b = dscr("w_in_b", [DEPTH, D, INW], BF16)
    w_out_b = dscr("w_out_b", [DEPTH, D, D], BF16)
    w_gate_b = dscr("w_gate_b", [DEPTH, D, DFF], BF16)
    w_up_b = dscr("w_up_b", [DEPTH, D, DFF], BF16)
    w_down_b = dscr("w_down_b", [DEPTH, DFF, D], BF16)

    with ExitStack() as ges:
        sch = Sched(nc, ges)
        V, A, P, G = nc.vector, nc.scalar, nc.tensor, nc.gpsimd

        uid = [0]

        def sbt(es, name, shape, dt=F32):
            uid[0] += 1
            return es.enter_context(nc.sbuf_tensor(f"s{uid[0]}_{name}", list(shape), dt))

        def pst(es, name, shape, dt=F32):
            uid[0] += 1
            return es.enter_context(nc.psum_tensor(f"p{uid[0]}_{name}", list(shape), dt))

        ident = sbt(ges, "ident", [128, 128], BF16)
        bdmask = sbt(ges, "bdmask", [128, 128])
        onesb = sbt(ges, "onesb", [128, 64], BF16)
        B_const = Buf()
        sch.dma("sp", ident[:], ident_d, writes=[B_const])
        sch.dma("sp", bdmask[:], bdmask_d, writes=[B_const])
        sch.op("pool", G.memset, onesb[:], 1.0, writes=[B_const], force=True)

        def evac(i, outap, inap, reads, writes, scale=None):
            if i % 2 == 0:
                if scale is None:
                    sch.op("act", A.copy, outap, inap, reads=reads, writes=writes)
                else:
                    sch.op("act", A.mul, outap, inap, scale, reads=reads, writes=writes)
            else:
                if scale is None:
                    sch.op("dve", V.tensor_copy, outap, inap, reads=reads, writes=writes)
                else:
                    sch.op("dve", V.tensor_scalar, outap, inap, scale, None, ALU.mult, reads=reads, writes=writes)

        def phase_mod():
            with ExitStack() as es:
                cTt = sbt(es, "cTt", [128, 16])
                sT = sbt(es, "sT", [128, 16])
                mod2s = sbt(es, "mod2s", [2, 6 * D])
                bt = sbt(es, "bt", [2, 6 * D])
                wm = Rot([sbt(es, f"wm{i}", [128, 8, 512]) for i in range(3)])
                pm = Rot([pst(es, f"pm{i}", [128, 512]) for i in range(2)])
                B_c, B_s, B_m, B_b = Buf(), Buf(), Buf(), Buf()
                sch.dma("sp", cTt[:], cT, writes=[B_c])
                sch.op("act", A.activation, sT[:], cTt[:], AF.Silu, reads=[B_c], writes=[B_s])
                for l in range(DEPTH):
                    sch.dma("sp", bt[:], b_mod[l].partition_broadcast(2), writes=[B_b])
                    for nb in range(12):
                        wt, Bw = wm.next()
                        sch.dma("sp", wt[:], w_mod[l][:, nb * 512:(nb + 1) * 512].rearrange("(k p) n -> p k n", p=128), writes=[Bw])
                        pt, Bp = pm.next()
                        for k in range(8):
                            sch.op("pe", P.matmul, pt[0:2, :], lhsT=sT[:, 2 * k:2 * k + 2], rhs=wt[:, k, :],
                                   start=(k == 0), stop=(k == 7), pemode="m2", reads=[B_s, Bw], writes=[Bp])
                        sch.op("dve", V.tensor_tensor, mod2s[:, nb * 512:(nb + 1) * 512], pt[0:2, :], bt[:, nb * 512:(nb + 1) * 512],
                               ALU.add, reads=[Bp, B_b], writes=[B_m])
                    sch.dma("pool", mod2d[2 * l:2 * l + 2, :], mod2s[:], reads=[B_m])
                sch.barrier()

        def norm_block(es_tiles, xt, Bx, Gt, SHt, B_mod, a_bf, Ba):
            junk, Bj, ssqr, tmpr = es_tiles
            sq, Bs = ssqr.next()
            sch.op("act", A.activation, junk[:], xt[:], AF.Square, accum_out=sq[:, 0:1], reads=[Bx], writes=[Bj, Bs])
            sch.op("act", A.activation, sq[:, 1:2], sq[:, 0:1], AF.Sqrt, bias=EPS, scale=1.0 / D, reads=[Bs], writes=[Bs])
            sch.op("dve", V.reciprocal, sq[:, 2:3], sq[:, 1:2], reads=[Bs], writes=[Bs])
            tp, Bt = tmpr.next()
            sch.op("dve", V.scalar_tensor_tensor, tp[:], xt[:], sq[:, 2:3], Gt[:], ALU.mult, ALU.mult,
                   reads=[Bx, Bs, B_mod], writes=[Bt])
            sch.op("dve", V.tensor_tensor, a_bf[:], tp[:], SHt[:], ALU.add, reads=[Bt, B_mod], writes=[Ba])

        def transpose_block(ptrr, a_bf, Ba, aT, BaT, tb, ei):
            for half in range(2):
                pt, Bp = ptrr.next()
                for j in range(4):
                    c = (half * 4 + j) * 128
                    sch.op("pe", P.transpose, pt[:, j, :], a_bf[:, c:c + 128], ident[:], reads=[Ba, B_const], writes=[Bp])
                evac(ei + half, aT[:, half * 4:(half + 1) * 4, tb * 128:(tb + 1) * 128], pt[:], [Bp], [BaT])

        def load_weight(es, name, src_kpn, nk, ncols):
            wb = sbt(es, name, [128, nk, ncols], BF16)
            Bw = Buf()
            for k in range(nk):
                sch.dma("sp", wb[:, k, :], src_kpn[:, k, :], writes=[Bw])
            return wb, Bw

        bgq = []
        for l_ in range(DEPTH):
            for (src_, dst_, R_, C_) in ((w_in[l_], w_in_b[l_], D, INW), (w_out[l_], w_out_b[l_], D, D), (w_gate[l_], w_gate_b[l_], D, DFF),
                                         (w_up[l_], w_up_b[l_], D, DFF), (w_down[l_], w_down_b[l_], DFF, D)):
                if l_ == 0 and src_ is w_in[0]:
                    continue
                for r0_ in range(0, R_, 128):
                    for c0_ in range(0, C_, 512):
                        c1_ = min(C_, c0_ + 512)
                        bgq.append((src_[r0_:r0_ + 128, c0_:c1_], dst_[r0_:r0_ + 128, c0_:c1_], c1_ - c0_))
        NBG0 = 16 + 48 + 48 + 44
        bgdone = [0]

        def make_bg(es, nsets):
            return (Rot([sbt(es, f"bgi{i}", [128, 512]) for i in range(nsets)]),
                    Rot([sbt(es, f"bgo{i}", [128, 512], BF16) for i in range(nsets)]))

        inflight = []

        def _bg_finish(k):
            while len(inflight) > k:
                (ti, Bi, to, Bo, dst_, nc_) = inflight.pop(0)
                sch.op("pool", G.tensor_copy, to[:, 0:nc_], ti[:, 0:nc_], reads=[Bi], writes=[Bo])
                sch.dma("pool", dst_, to[:, 0:nc_], reads=[Bo])

        def pump(n, bgs, upto=None, drain=False):
            bgi, bgo = bgs
            for _ in range(n):
                if not bgq or (upto is not None and bgdone[0] >= upto):
                    break
                src_, dst_, nc_ = bgq.pop(0)
                bgdone[0] += 1
                ti, Bi = bgi.next()
                to, Bo = bgo.next()
                sch.dma("pool", ti[:, 0:nc_], src_, writes=[Bi])
                inflight.append((ti, Bi, to, Bo, dst_, nc_))
                _bg_finish(2)
            if drain:
                _bg_finish(0)

        def phase_wcast():
            with ExitStack() as es:
                sin = Rot([sbt(es, f"wci{i}", [128, 2048]) for i in range(3)])
                sout = Rot([sbt(es, f"wco{i}", [128, 2048], BF16) for i in range(3)])
                i = 0
                for l in range(1):
                    for (src, dst, R, C) in ((w_in[l], w_in_b[l], D, INW),):
                        for r0 in range(0, R, 128):
                            for c0 in range(0, C, 2048):
                                c1 = min(C, c0 + 2048)
                                n = c1 - c0
                                ti, Bi = sin.next()
                                to, Bo = sout.next()
                                sch.dma("sp", ti[:, 0:n], src[r0:r0 + 128, c0:c1], writes=[Bi])
                                e = ("pool", "dve", "act")[i % 3]
                                i += 1
                                if e == "pool":
                                    sch.op("pool", G.tensor_copy, to[:, 0:n], ti[:, 0:n], reads=[Bi], writes=[Bo])
                                elif e == "dve":
                                    sch.op("dve", V.tensor_copy, to[:, 0:n], ti[:, 0:n], reads=[Bi], writes=[Bo])
                                else:
                                    sch.op("act", A.copy, to[:, 0:n], ti[:, 0:n], reads=[Bi], writes=[Bo])
                                sch.dma("pool", dst[r0:r0 + 128, c0:c1], to[:, 0:n], reads=[Bo])
                sch.barrier()

        def mod_tiles(es, l, v, gsrc, i_sh, i_sc, tag):
            Gt = sbt(es, f"G{tag}", [128, D])
            SHt = sbt(es, f"SH{tag}", [128, D])
            Bm = Buf()
            sch.dma("pool", SHt[:], mod2d[2 * l + v, i_sh * D:(i_sh + 1) * D].partition_broadcast(128), writes=[Bm])
            sch.dma("pool", Gt[:], gsrc.partition_broadcast(128), writes=[Bm])
            return Gt, SHt, Bm

        def phase_ab(l):
            xsrc = xin if l == 0 else xres
            if l >= 1 and bgq:
                with ExitStack() as es0:
                    pump(100000, make_bg(es0, 4), drain=True)
                    sch.barrier()
            with ExitStack() as es:
                wbf, Bw = load_weight(es, "winbf", w_in_b[l].rearrange("(k p) n -> p k n", p=128), 8, INW)
                mods = []
                sct = sbt(es, "sct", [128, D])
                Bsc = Buf()
                for v in range(2):
                    Gt, SHt, Bm = mod_tiles(es, l, v, norm1_g[l], 0, 1, f"1_{v}")
                    sch.dma("pool", sct[:], mod2d[2 * l + v, D:2 * D].partition_broadcast(128), writes=[Bsc])
                    sch.op("dve", V.scalar_tensor_tensor, Gt[:], sct[:], 1.0, Gt[:], ALU.add, ALU.mult, reads=[Bsc, Bm], writes=[Bm])
                    mods.append((Gt, SHt, Bm))
                junk = sbt(es, "junk", [128, D], BF16)
                nt = (junk, Buf(), Rot([sbt(es, f"ssq{i}", [128, 4]) for i in range(4)]),
                      Rot([sbt(es, f"ntmp{i}", [128, D]) for i in range(2)]))
                xr = Rot([sbt(es, f"xt{i}", [128, D]) for i in range(8)])
                ar = Rot([sbt(es, f"abf{i}", [128, D], BF16) for i in range(2)])
                aTr = Rot([sbt(es, f"aT{i}", [128, 8, 512], BF16) for i in range(2)])
                sfr = Rot([sbt(es, f"sf{i}", [128, 512]) for i in range(2)])
                sbr = Rot([sbt(es, f"sb{i}", [128, 512], BF16) for i in range(3)])
                tmr = Rot([sbt(es, f"tms{i}", [128, 1024], BF16) for i in range(2)])
                ptrr = Rot([pst(es, f"ptr{i}", [128, 4, 128], BF16) for i in range(2)])
                pfr = Rot([pst(es, f"pf{i}", [128, 512]) for i in range(3)])
                ptmr = Rot([pst(es, f"ptm{i}", [128, 512]) for i in range(2)])
                fm = []
                for c0, dst, r0, sc in ((256, fb, 0, None), (512, fz, 0, None), (768, fz, 256, None), (1280, fb, 256, None),
                                        (1536, fb, 512, 0.125), (2048, fb, 1024, None)):
                    wd = 512 if c0 >= 1536 else 256
                    for g in range(wd // 128):
                        fm.append((c0 + g * 128, dst, r0 + g * 128, sc))
                sbs = [(i * 512, 512, 0) for i in range(8)] + [(S, L, 1)]
                if l == 0:
                    pump(100, make_bg(es, 4), drain=True)
                eic = [0]
                aTs = {}
                abfs = {}

                xts = {}

                def load_part(i):
                    t0, ntok, v = sbs[i]
                    for tb in range(ntok // 128):
                        xt, Bx = xr.next()
                        sch.dma("sp", xt[:], xsrc[t0 + tb * 128:t0 + (tb + 1) * 128, :], writes=[Bx])
                        xts[(i, tb)] = (xt, Bx)

                def norm_part(i, tb):
                    t0, ntok, v = sbs[i]
                    Gt, SHt, Bm = mods[v]
                    if tb == 0:
                        aTs[i] = aTr.next()
                    xt, Bx = xts.pop((i, tb))
                    a_bf, Ba = ar.next()
                    norm_block(nt, xt, Bx, Gt, SHt, Bm, a_bf, Ba)
                    abfs[(i, tb)] = (a_bf, Ba)

                def tr_part(i, tb):
                    aT, BaT = aTs[i]
                    a_bf, Ba = abfs.pop((i, tb))
                    eic[0] += 2
                    transpose_block(ptrr, a_bf, Ba, aT, BaT, tb, eic[0])

                load_part(0)
                for tb in range(sbs[0][1] // 128):
                    norm_part(0, tb)
                    tr_part(0, tb)
                for i, (t0, ntok, v) in enumerate(sbs):
                    aT, BaT = aTs[i]
                    if i + 1 < len(sbs):
                        load_part(i + 1)
                    groups = []
                    for (c0, dst, r0, sc) in fm:
                        groups.append(("fm", c0, dst, r0, sc))
                    for tb in range(ntok // 128):
                        groups.append(("tm", tb))
                    nxt = {}
                    if i + 1 < len(sbs):
                        ntb2 = sbs[i + 1][1] // 128
                        for tb in range(ntb2):
                            nxt.setdefault(1 + 5 * tb, []).append(("n", tb))
                            nxt.setdefault(4 + 5 * tb, []).append(("t", tb))
                    for gi, g in enumerate(groups):
                        for (kind, tb) in nxt.pop(gi, []):
                            (norm_part if kind == "n" else tr_part)(i + 1, tb)
                        if g[0] == "fm":
                            _, c0, dst, r0, sc = g
                            pf, Bp = pfr.next()
                            for k in range(8):
                                sch.op("pe", P.matmul, pf[:, 0:ntok], lhsT=wbf[:, k, c0:c0 + 128], rhs=aT[:, k, 0:ntok],
                                       start=(k == 0), stop=(k == 7), reads=[Bw, BaT], writes=[Bp])
                            if dst is fz:
                                st, Bs = sfr.next()
                            else:
                                st, Bs = sbr.next()
                            evac(0, st[:, 0:ntok], pf[:, 0:ntok], [Bp], [Bs], scale=sc)
                            sch.dma("sp", dst[r0:r0 + 128, t0:t0 + ntok], st[:, 0:ntok], reads=[Bs])
                        else:
                            tb = g[1]
                            tms, Bts = tmr.next()
                            tsl = slice(tb * 128, (tb + 1) * 128)
                            p0, Bp0 = ptmr.next()
                            for (cc, oc) in ((0, 0), (1024, 256)):
                                for k in range(8):
                                    sch.op("pe", P.matmul, p0[:, oc:oc + 256], lhsT=aT[:, k, tsl], rhs=wbf[:, k, cc:cc + 256],
                                           start=(k == 0), stop=(k == 7), reads=[Bw, BaT], writes=[Bp0])
                            evac(0, tms[:, 0:512], p0[:], [Bp0], [Bts])
                            p1, Bp1 = ptmr.next()
                            for k in range(8):
                                sch.op("pe", P.matmul, p1[:], lhsT=aT[:, k, tsl], rhs=wbf[:, k, 2560:3072],
                                       start=(k == 0), stop=(k == 7), reads=[Bw, BaT], writes=[Bp1])
                            evac(0, tms[:, 512:1024], p1[:], [Bp1], [Bts])
                            sch.dma("sp", tm[t0 + tb * 128:t0 + (tb + 1) * 128, :], tms[:], reads=[Bts])
                    for lst in nxt.values():
                        for (kind, tb) in lst:
                            (norm_part if kind == "n" else tr_part)(i + 1, tb)
                sch.barrier()

        def phase_c(l):
            with ExitStack() as es:
                cbgs = make_bg(es, 4)
                U = sbt(es, "fU", [128, NB, 256], BF16)
                BU = Buf()
                sch.dma("sp", U[:], tm[:, 0:256].rearrange("(k p) c -> p k c", p=128), writes=[BU])
                c64 = sbt(es, "c64", [128, 2, 128])
                wf = sbt(es, "wf", [128, 2, 64])
                Mbd = sbt(es, "Mbd", [128, 3, 2, 128], BF16)
                Bc, Bwf, BM = Buf(), Buf(), Buf()
                sch.dma("sp", c64[:, 0, :], C64bd_d, writes=[Bc])
                sch.dma("sp", c64[:, 1, :], S64bdn_d, writes=[Bc])
                for cb in range(2):
                    sch.dma("sp", wf[:, cb, :], fourier_w[l][2 * cb:2 * cb + 2].rearrange("g c d -> (g c) d"), writes=[Bwf])
                sch.op("pool", G.memset, Mbd[:], 0.0, writes=[BM])
                pmr = Rot([pst(es, f"pM{i}", [128, 512]) for i in range(1)])
                pM, BpM = pmr.next()
                for w in range(2):
                    for cb in range(2):
                        o0 = (w * 2 + cb) * 64
                        sch.op("pe", P.matmul, pM[:, o0:o0 + 64], lhsT=c64[:, w, :], rhs=wf[:, cb, :], start=True, stop=True,
                               pemode="f32", reads=[Bc, Bwf], writes=[BpM])
                for w in range(2):
                    for cb in range(2):
                        o0 = (w * 2 + cb) * 64
                        sch.op("dve", V.tensor_copy, Mbd[0:64, w, cb, 0:64], pM[0:64, o0:o0 + 64], reads=[BpM, BM], writes=[BM])
                        sch.op("dve", V.tensor_copy, Mbd[64:128, w, cb, 64:128], pM[64:128, o0:o0 + 64], reads=[BpM, BM], writes=[BM])
                sch.op("dve", V.tensor_scalar, Mbd[:, 2].rearrange("p c n -> p (c n)"), Mbd[:, 1].rearrange("p c n -> p (c n)"), -1.0, None,
                       ALU.mult, reads=[BM], writes=[BM])
                csr = Rot([sbt(es, f"cs{i}", [128, 2, 4, 512], BF16) for i in range(3)])
                pqr = Rot([sbt(es, f"pq{i}", [128, 4, 512], BF16) for i in range(2)])
                osr = Rot([sbt(es, f"fo{i}", [128, 512], BF16) for i in range(3)])
                pacc = [pst(es, f"pacc{i}", [128, 512]) for i in range(4)]
                Bacc = [Buf() for _ in range(4)]
                por = Rot([pst(es, f"po{i}", [128, 512]) for i in range(2)])
                c0col = sbt(es, "c0col", [128, 2], BF16)
                Bc0 = Buf()
                sch.op("pool", G.memset, c0col[:], 1.0 / 64.0, writes=[Bc0])

                def mix(pq, Bpq, n, tcol0, mirror_col0=None, ncols=None):
                    for cb in range(2):
                        po, Bpo = por.next()
                        sch.op("pe", P.matmul, po[:, 0:n], lhsT=Mbd[:, 0, cb, :], rhs=pq[:, cb, 0:n], start=True, stop=False,
                               reads=[BM, Bpq], writes=[Bpo])
                        sch.op("pe", P.matmul, po[:, 0:n], lhsT=Mbd[:, 1, cb, :], rhs=pq[:, 2 + cb, 0:n], start=False, stop=True,
                               reads=[BM, Bpq], writes=[Bpo])
                        ot, Bo = osr.next()
                        evac(cb, ot[:, 0:n], po[:, 0:n], [Bpo], [Bo])
                        nst = n if ncols is None else ncols
                        if nst == 1:
                            sch.dma("pool", catT[cb * 128:(cb + 1) * 128, tcol0:tcol0 + 1], ot[:, 0:1], reads=[Bo],
                                    allow_slow_non_contiguous=True)
                        else:
                            sch.dma("pool", catT[cb * 128:(cb + 1) * 128, tcol0:tcol0 + nst], ot[:, 0:nst], reads=[Bo])
                        if mirror_col0 is not None:
                            po, Bpo = por.next()
                            for (wm, qi) in ((0, cb), (2, 2 + cb)):
                                bq_ = pq[:, qi, 0:n]
                                rev = bass.AP(tensor=bq_.tensor, offset=bq_.offset + n - 1, ap=[[bq_.ap[0][0], 128], [-1, n]])
                                sch.op("pe", P.matmul, po[:, 0:n], lhsT=Mbd[:, wm, cb, :], rhs=rev, start=(wm == 0), stop=(wm == 2),
                                       reads=[BM, Bpq], writes=[Bpo])
                            ot, Bo = osr.next()
                            evac(cb + 1, ot[:, 0:n], po[:, 0:n], [Bpo], [Bo])
                            sch.dma("pool", catT[cb * 128:(cb + 1) * 128, mirror_col0:mirror_col0 + n], ot[:, 0:n], reads=[Bo])

                for cb in range(2):
                    for k in range(32):
                        sch.op("pe", P.matmul, pacc[cb][:, 0:2], lhsT=U[:, k, cb * 128:(cb + 1) * 128], rhs=c0col[:],
                               start=(k == 0), stop=(k == 31), reads=[BU, Bc0], writes=[Bacc[cb]])
                pq, Bpq = pqr.next()
                sch.op("pool", G.memset, pq[:, 2:4, 0:2], 0.0, writes=[Bpq])
                for cb in range(2):
                    evac(cb, pq[:, cb, 0:2], pacc[cb][:, 0:2], [Bacc[cb], Bpq], [Bpq])
                mix(pq, Bpq, 2, 0, ncols=1)
                for nb in range(4):
                    c_lo = 512 * nb + 1
                    for kg in range(8):
                        cs, Bcs = csr.next()
                        sch.dma("sp", cs[:, 0], C4[kg * 512:(kg + 1) * 512, c_lo:c_lo + 512].rearrange("(k p) n -> p k n", p=128), writes=[Bcs])
                        sch.dma("sp", cs[:, 1], S4[kg * 512:(kg + 1) * 512, c_lo:c_lo + 512].rearrange("(k p) n -> p k n", p=128), writes=[Bcs])
                        for kk in range(4):
                            k = kg * 4 + kk
                            for w in range(2):
                                for cb in range(2):
                                    i = w * 2 + cb
                                    sch.op("pe", P.matmul, pacc[i][:], lhsT=U[:, k, cb * 128:(cb + 1) * 128], rhs=cs[:, w, kk, :],
                                           start=(k == 0), stop=(k == 31), reads=[BU, Bcs], writes=[Bacc[i]])
                    pq, Bpq = pqr.next()
                    for i in range(4):
                        evac(i, pq[:, i, :], pacc[i][:], [Bacc[i]], [Bpq])
                    mix(pq, Bpq, 512, c_lo, mirror_col0=3584 - 512 * nb)
                    if l == 0:
                        pump(5, cbgs, drain=True)
                if l == 0:
                    c2 = sbt(es, "c2", [128, 2, 2, 256], BF16)
                    Bc2 = Buf()
                    sch.dma("sp", c2[:, 0], C2.rearrange("(k p) n -> p k n", p=128), writes=[Bc2])
                    sch.dma("sp", c2[:, 1], S2.rearrange("(k p) n -> p k n", p=128), writes=[Bc2])
                    for w in range(2):
                        for cb in range(2):
                            i = w * 2 + cb
                            for k in range(2):
                                sch.op("pe", P.matmul, pacc[i][:, 0:256], lhsT=U[:, 32 + k, cb * 128:(cb + 1) * 128], rhs=c2[:, w, k, :],
                                       start=(k == 0), stop=(k == 1), reads=[BU, Bc2], writes=[Bacc[i]])
                    pq, Bpq = pqr.next()
                    for i in range(4):
                        evac(i, pq[:, i, 0:256], pacc[i][:, 0:256], [Bacc[i]], [Bpq])
                    mix(pq, Bpq, 256, S)
                sch.barrier()

        def phase_d(l):
            HW = T // 2
            NCH = T // 32
            with ExitStack() as es:
                lbt = sbt(es, "lbt", [128, 8])
                lbv = sbt(es, "lbv", [128, 8])
                ngt = sbt(es, "ngt", [128, DEPTH])
                mdir = sbt(es, "mdir", [128, 2, 128])
                bd64 = sbt(es, "bd64", [128, 128])
                cm3 = sbt(es, "cm3", [128, 4, 128])
                epsc = sbt(es, "epsc", [128, 1])
                Bl, Bmd = Buf(), Buf()
                sch.op("pool", G.memset, epsc[:], EPS, writes=[Bmd])
                sch.dma("sp", lbt[:], lbT, writes=[Bl])
                sch.dma("sp", ngt[:], ngT, writes=[Bl])
                sch.dma("sp", mdir[:, 0, :], mfwd_d, writes=[Bmd])
                sch.dma("sp", mdir[:, 1, :], mbwd_d, writes=[Bmd])
                sch.dma("sp", bd64[:], bd64_d, writes=[Bmd])
                sch.dma("sp", cm3[:], cm3_d.rearrange("p (c n) -> p c n", n=128), writes=[Bmd])
                if l == 0:
                    sch.op("pool", G.memset, lbv[:, 0:4], 0.0, reads=[Bl], writes=[Bl])
                    sch.op("pool", G.memset, lbv[:, 4:8], 1.0, reads=[Bl], writes=[Bl])
                else:
                    ex = sbt(es, "lbex", [128, 8])
                    sch.op("act", A.activation, ex[:], lbt[:], AF.Exp, reads=[Bl], writes=[Bl])
                    sch.op("dve", V.tensor_tensor, lbv[:, 4:8], ex[:, 0:4], ex[:, 4:8], ALU.add, reads=[Bl], writes=[Bl])
                    sch.op("dve", V.reciprocal, lbv[:, 4:8], lbv[:, 4:8], reads=[Bl], writes=[Bl])
                    sch.op("dve", V.tensor_tensor, lbv[:, 0:4], ex[:, 4:8], lbv[:, 4:8], ALU.mult, reads=[Bl], writes=[Bl])
                    sch.op("dve", V.tensor_scalar, lbv[:, 4:8], lbv[:, 0:4], -1.0, 1.0, ALU.mult, ALU.add, reads=[Bl], writes=[Bl])
                for hp in range(2):
                    with ExitStack() as hs:
                        vtm = sbt(hs, "hv", [128, NB, 128], BF16)
                        vm = sbt(hs, "hvm", [128, 2, NB, 128], BF16)
                        Bv, Bvm = Buf(), Buf()
                        sch.dma("sp", vtm[:], tm[:, 256 + hp * 128:256 + (hp + 1) * 128].rearrange("(k p) c -> p k c", p=128), writes=[Bv])
                        sch.op("pool", G.memset, vm[:], 0.0, writes=[Bvm])
                        sch.op("pool", G.tensor_copy, vm[:, 0, :, 0:64], vtm[:, :, 0:64], reads=[Bv, Bvm], writes=[Bvm])
                        sch.op("pool", G.tensor_copy, vm[:, 1, :, 64:128], vtm[:, :, 64:128], reads=[Bv, Bvm], writes=[Bvm])
                        chains = []
                        for d in range(2):
                            ch = dict(d=d)
                            ch["qdm"] = sbt(hs, f"qdm{d}", [128, 2, T], BF16)
                            ch["kd"] = sbt(hs, f"kd{d}", [128, T], BF16)
                            ch["kst"] = sbt(hs, f"kst{d}", [128, NB, 128], BF16)
                            ch["dec"] = sbt(hs, f"dec{d}", [128, NCH])
                            ch["B"] = {k: Buf() for k in ("qdm", "kd", "kst", "dec", "o")}
                            sch.op("pool", G.memset, ch["qdm"][:], 0.0, writes=[ch["B"]["qdm"]])
                            chains.append(ch)
                        with ExitStack() as gs:
                            Asets = [([sbt(gs, f"gA{u}{i}", [128, HW]) for i in range(4)], [Buf() for _ in range(4)],
                                      sbt(gs, f"gq{u}", [128, HW], BF16), Buf(), sbt(gs, f"gks{u}", [128, HW], BF16), Buf(),
                                      sbt(gs, f"gtot{u}", [128, HW // 32]), Buf()) for u in range(2)]
                            unit = 0
                            msk = sbt(gs, "gmsk", [128, HW])
                            Bmsk = Buf()
                            sch.op("pool", G.memset, msk[:], 1.0, writes=[Bmsk])
                            sch.op("pool", G.memset, msk[:].rearrange("p (n c) -> p n c", c=32)[:, :, 0:1], 0.0, reads=[Bmsk], writes=[Bmsk])
                            ptk = Rot([pst(gs, f"ptk{i}", [128, 4, 128], BF16) for i in range(2)])
                            ei_ = [0]

                            def unit_gen(ch, h, uset):
                                d = ch["d"]
                                B = ch["B"]
                                lbc = lbv[:, d * 2 + hp:d * 2 + hp + 1]
                                omc = lbv[:, 4 + d * 2 + hp:4 + d * 2 + hp + 1]
                                Ar, BA, qh, Bq, ksh, Bks, tot, Btot = uset
                                c0 = h * HW
                                cs = slice(c0, c0 + HW)
                                A1, A2, A3, A4 = Ar
                                B1, B2, B3, B4 = BA
                                zr = 256 * d + hp * 128
                                sch.dma("sp", A1[:], fz[zr:zr + 128, cs], writes=[B1])
                                sch.dma("sp", qh[:], fb[hp * 128:(hp + 1) * 128, cs], writes=[Bq])
                                sch.op("act", A.activation, A1[:], A1[:], AF.Sigmoid, reads=[B1], writes=[B1])
                                yield
                                sch.op("act", A.activation, A1[:], A1[:], AF.Identity, bias=lbc, scale=omc, reads=[B1, Bl], writes=[B1])
                                yield
                                sch.op("act", A.activation, A2[:], A1[:], AF.Ln, reads=[B1], writes=[B2])
                                yield
                                sch.op("act", A.activation, A1[:], A1[:], AF.Identity, bias=1.0, scale=-1.0, reads=[B1, B2], writes=[B1])
                                sch.op("dve", V.tensor_tensor_scan, A3[:], msk[:], A2[:], 0.0, ALU.mult, ALU.add, reads=[Bmsk, B2], writes=[B3])
                                yield
                                A3v = A3[:].rearrange("p (n c) -> p n c", c=32)
                                A2v = A2[:].rearrange("p (n c) -> p n c", c=32)
                                totb = tot[:].rearrange("p (n o) -> p n o", o=1).broadcast_to([128, HW // 32, 32])
                                sch.op("act", A.copy, tot[:], A3v[:, :, 31], reads=[B3], writes=[Btot])
                                yield
                                if d == 0:
                                    sch.op("dve", V.tensor_tensor, A2v, A3v, totb, ALU.subtract, reads=[Btot, B3, B2], writes=[B2])
                                    asg, rsg = 1.0, -1.0
                                else:
                                    sch.op("dve", V.tensor_tensor, A2[:], A3[:], A2[:], ALU.subtract, reads=[B3, B2], writes=[B2])
                                    yield
                                    sch.op("dve", V.tensor_tensor, A3v, A2v, totb, ALU.subtract, reads=[Btot, B2, B3], writes=[B3])
                                    asg, rsg = -1.0, 1.0
                                yield
                                sch.op("act", A.activation, ch["dec"][:, c0 // 32:(c0 + HW) // 32], tot[:], AF.Exp, reads=[Btot], writes=[B["dec"]])
                                sch.op("act", A.activation, A4[:], A3[:], AF.Exp, scale=asg, reads=[B3, B4], writes=[B4])
                                yield
                                sch.op("dve", V.tensor_tensor, ch["qdm"][0:64, 0, cs], qh[0:64, :], A4[0:64, :], ALU.mult,
                                       reads=[Bq, B4, B["qdm"]], writes=[B["qdm"]])
                                sch.op("dve", V.tensor_tensor, ch["qdm"][64:128, 1, cs], qh[64:128, :], A4[64:128, :], ALU.mult,
                                       reads=[Bq, B4, B["qdm"]], writes=[B["qdm"]])
                                yield
                                sch.op("act", A.activation, A4[:], A3[:], AF.Exp, scale=-asg, reads=[B3, B4], writes=[B4])
                                yield
                                sch.op("dve", V.tensor_tensor, ch["kd"][:, cs], A1[:], A4[:], ALU.mult, reads=[B1, B4], writes=[B["kd"]])
                                yield
                                sch.op("act", A.activation, A4[:], A2[:], AF.Exp, scale=rsg, reads=[B2, B4], writes=[B4])
                                yield
                                sch.op("dve", V.tensor_tensor, ksh[:], A1[:], A4[:], ALU.mult, reads=[B1, B4], writes=[Bks])
                                yield
                                nbh = HW // 128
                                for b4 in range(0, nbh, 4):
                                    nn = min(4, nbh - b4)
                                    pt, Bp = ptk.next()
                                    for j in range(nn):
                                        sch.op("pe", P.transpose, pt[:, j, :], ksh[:, (b4 + j) * 128:(b4 + j + 1) * 128], ident[:],
                                               reads=[Bks, B_const], writes=[Bp])
                                    ei_[0] += 1
                                    evac(ei_[0], ch["kst"][:, h * nbh + b4:h * nbh + b4 + nn, :], pt[:, 0:nn, :], [Bp], [B["kst"]])
                                    yield

                            gbgs = make_bg(gs, 4)
                            units = [(ch, h) for ch in chains for h in range(2)]
                            gens = []
                            SHIFT = 6
                            active = []
                            ui = 0
                            tick = 0
                            while ui < len(units) or active:
                                if ui < len(units) and len(active) < 2 and (not active or active[0][1] >= SHIFT):
                                    ch_, h_ = units[ui]
                                    active.append([unit_gen(ch_, h_, Asets[ui % 2]), 0])
                                    ui += 1
                                tick += 1
                                if l == 0 and tick % 2 == 0:
                                    pump(1, gbgs)
                                for a_ in list(active):
                                    try:
                                        next(a_[0])
                                        a_[1] += 1
                                    except StopIteration:
                                        active.remove(a_)
                            pump(0, gbgs, drain=True)
                            sch.barrier()
                        for ch in chains:
                            ch["o"] = sbt(hs, f"o{ch['d']}", [128, T])
                        with ExitStack() as cs_:
                            vexa = sbt(cs_, "vexa", [128, NB, 4, 128], BF16)
                            Bvex = Buf()
                            for nb in range(NB):
                                sch.op("dve", V.tensor_tensor, vexa[:, nb], cm3[:], vtm[:, nb:nb + 1, :].broadcast_to([128, 4, 128]), ALU.mult,
                                       reads=[Bmd, Bv], writes=[Bvex])
                            for ch in chains:
                                d = ch["d"]
                                ch["psU"] = pst(cs_, f"psU{d}", [128, 4, 128]); ch["BpsU"] = Buf()
                                ch["psS"] = [pst(cs_, f"psS{d}{i}", [128, 128]) for i in range(2)]; ch["BpsS"] = [Buf(), Buf()]
                                ch["psO"] = pst(cs_, f"psO{d}", [128, 128]); ch["BpsO"] = Buf()
                                ch["Sf"] = Rot([sbt(cs_, f"Sf{d}{i}", [128, 128]) for i in range(2)])
                                ch["Spad"] = Rot([sbt(cs_, f"Sp{d}{i}", [128, 2, 128], BF16) for i in range(10)])
                                for (tl, Bt_) in ch["Spad"].t:
                                    sch.op("pool", G.memset, tl[:], 0.0, writes=[Bt_])
                                ch["scb"] = Rot([sbt(cs_, f"scb{d}{i}", [128, 2, 128], BF16) for i in range(2)])
                                ch["order"] = ([32, 33] + list(range(32))) if d == 0 else ([33, 32] + list(range(31, -1, -1)))
                                ch["corder"] = [0, 1, 2, 3] if d == 0 else [3, 2, 1, 0]
                                ch["prev"] = None
                                ch["prev_bd"] = None
                                ch["pending"] = None

                            def front_all(step):
                                res = []
                                for ch in chains:
                                    nb = ch["order"][step]
                                    B = ch["B"]
                                    sch.op("pe", P.matmul, ch["psU"][:].rearrange("p c n -> p (c n)"), lhsT=ch["kst"][:, nb, :],
                                           rhs=vexa[:, nb].rearrange("p c n -> p (c n)"), start=True, stop=True,
                                           reads=[B["kst"], Bvex], writes=[ch["BpsU"]])
                                    ch["before"] = []
                                for ci in range(4):
                                    for k_, ch in enumerate(chains):
                                        nb = ch["order"][step]
                                        B = ch["B"]
                                        c = ch["corder"][ci]
                                        n = nb * 4 + c
                                        ch["before"].append((c, ch["prev_bd"]))
                                        sf, Bsf = ch["Sf"].next()
                                        if ch["prev"] is None:
                                            sch.op("dve", V.tensor_copy, sf[:], ch["psU"][:, c, :], reads=[ch["BpsU"]], writes=[Bsf])
                                        else:
                                            pv, Bpv = ch["prev"]
                                            sch.op("dve", V.scalar_tensor_tensor, sf[:], pv[:], ch["dec"][:, n:n + 1], ch["psU"][:, c, :],
                                                   ALU.mult, ALU.add, reads=[Bpv, B["dec"], ch["BpsU"]], writes=[Bsf])
                                        ch["prev"] = (sf, Bsf)
                                        sp_, Bsp = ch["Spad"].next()
                                        base = sp_[:]
                                        ov = bass.AP(tensor=base.tensor, offset=base.offset, ap=[[base.ap[0][0], 128], [192, 2], [1, 64]])
                                        iv = sf[:].rearrange("p (g c) -> p g c", g=2)
                                        sch.op("act", A.copy, ov, iv, reads=[Bsf, Bsp], writes=[Bsp])
                                        ch["prev_bd"] = (sp_, Bsp)
                                for ch in chains:
                                    nb = ch["order"][step]
                                    B = ch["B"]
                                    cols = slice(nb * 128, (nb + 1) * 128)
                                    scb, Bscb = ch["scb"].next()
                                    for hh in range(2):
                                        sch.op("pe", P.matmul, ch["psS"][hh][:], lhsT=ch["kd"][:, cols], rhs=ch["qdm"][:, hh, cols],
                                               start=True, stop=True, reads=[B["kd"], B["qdm"]], writes=[ch["BpsS"][hh]])
                                        sch.op("dve", V.tensor_tensor, scb[:, hh, :], ch["psS"][hh][:], mdir[:, ch["d"], :], ALU.mult,
                                               reads=[ch["BpsS"][hh], Bmd], writes=[Bscb])
                                    res.append((nb, ch["before"], scb, Bscb))
                                return res

                            def back(ch, pend):
                                nb, before, scb, Bscb = pend
                                B = ch["B"]
                                cols = slice(nb * 128, (nb + 1) * 128)
                                inter = [(c, s_) for (c, s_) in before if s_ is not None]
                                for hh in range(2):
                                    sch.op("pe", P.matmul, ch["psO"][:], lhsT=vm[:, hh, nb, :], rhs=scb[:, hh, :], start=(hh == 0),
                                           stop=(hh == 1 and not inter), reads=[Bvm, Bscb], writes=[ch["BpsO"]])
                                for i, (c, (sp_, Bsp)) in enumerate(inter):
                                    for hh in range(2):
                                        sch.op("pe", P.matmul, ch["psO"][:, 32 * c:32 * c + 32], lhsT=sp_[:, hh, :],
                                               rhs=ch["qdm"][:, hh, nb * 128 + 32 * c:nb * 128 + 32 * c + 32], start=False,
                                               stop=(i == len(inter) - 1 and hh == 1), reads=[Bsp, B["qdm"]], writes=[ch["BpsO"]])
                                sch.op("act", A.copy, ch["o"][:, cols], ch["psO"][:], reads=[ch["BpsO"]], writes=[B["o"]])

                            for step in range(NB + 1):
                                if l == 0 and step == 1:
                                    pump(35, make_bg(cs_, 4), drain=True)
                                newp = None
                                if step < NB:
                                    newp = front_all(step)
                                if step >= 1:
                                    for ch in chains:
                                        back(ch, ch["pending"])
                                if step < NB:
                                    for ch, p_ in zip(chains, newp):
                                        ch["pending"] = p_
                            sch.barrier()
                        with ExitStack() as rs:
                            of, ob = chains[0]["o"], chains[1]["o"]
                            Bof, Bob = chains[0]["B"]["o"], chains[1]["B"]["o"]
                            sch.op("dve", V.tensor_tensor, of[:], of[:], ob[:], ALU.add, reads=[Bof, Bob], writes=[Bof])
                            gt = sbt(rs, "hg", [128, T], BF16)
                            Bg = Buf()
                            sch.dma("sp", gt[:], fb[256 + hp * 128:256 + (hp + 1) * 128, :], writes=[Bg])
                            sqr = Rot([sbt(rs, f"hsq{i}", [128, 512]) for i in range(2)])
                            sgr = Rot([sbt(rs, f"hsg{i}", [128, 512]) for i in range(2)])
                            rr = Rot([sbt(rs, f"hr{i}", [128, 512]) for i in range(2)])
                            orr = Rot([sbt(rs, f"hob{i}", [128, 512], BF16) for i in range(2)])
                            pms = Rot([pst(rs, f"pms{i}", [128, 512]) for i in range(2)])
                            for t0 in range(0, T, 512):
                                n = min(512, T - t0)
                                cs = slice(t0, t0 + n)
                                sq, Bsq = sqr.next()
                                sch.op("act", A.activation, sq[:, 0:n], of[:, cs], AF.Square, reads=[Bof], writes=[Bsq])
                                pm_, Bpm = pms.next()
                                sch.op("pe", P.matmul, pm_[:, 0:n], lhsT=bd64[:], rhs=sq[:, 0:n], start=True, stop=True, pemode="f32",
                                       reads=[Bmd, Bsq], writes=[Bpm])
                                r_, Br = rr.next()
                                sch.op("act", A.activation, r_[:, 0:n], pm_[:, 0:n], AF.Ln, bias=epsc[:, 0:1], scale=1.0, reads=[Bpm, Bmd], writes=[Br])
                                sch.op("act", A.activation, r_[:, 0:n], r_[:, 0:n], AF.Exp, scale=-0.5, reads=[Br], writes=[Br])
                                sch.op("dve", V.tensor_tensor, r_[:, 0:n], r_[:, 0:n], of[:, cs], ALU.mult, reads=[Br, Bof], writes=[Br])
                                sg, Bsg = sgr.next()
                                sch.op("act", A.activation, sg[:, 0:n], gt[:, cs], AF.Silu, reads=[Bg], writes=[Bsg])
                                ob_, Bo = orr.next()
                                sch.op("dve", V.scalar_tensor_tensor, ob_[:, 0:n], r_[:, 0:n], ngt[:, l:l + 1], sg[:, 0:n], ALU.mult, ALU.mult,
                                       reads=[Br, Bl, Bsg], writes=[Bo])
                                sch.dma("pool", catT[256 + hp * 128:256 + (hp + 1) * 128, cs], ob_[:, 0:n], reads=[Bo])
                            sch.barrier()

        def phase_e(l):
            with ExitStack() as es:
                sets = []
                for u in range(2):
                    qm = sbt(es, f"naq{u}", [128, 2, T], BF16)
                    kT = sbt(es, f"nak{u}", [128, T], BF16)
                    vm = sbt(es, f"nav{u}", [128, 2, NB, 128], BF16)
                    Bq, Bk, Bv = Buf(), Buf(), Buf()
                    sch.op("pool", G.memset, qm[:], 0.0, writes=[Bq])
                    sch.op("pool", G.memset, vm[:], 0.0, writes=[Bv])
                    sets.append((qm, kT, vm, Bq, Bk, Bv))
                bg_rng = ((8, 14), (0, 8), (14, 20))
                bias = [[sbt(es, f"nab{hh}{g}", [128, bg_rng[g][1] - bg_rng[g][0], 512]) for g in range(3)] for hh in range(2)]
                Bb = [[Buf() for g in range(3)] for hh in range(2)]
                oT = sbt(es, "nao", [128, T], BF16)
                oneh = sbt(es, "oneh", [128, 2, 128], BF16)
                sbr = Rot([sbt(es, f"nas{i}", [128, 512]) for i in range(4)])
                pr = Rot([sbt(es, f"nap{i}", [128, 512], BF16) for i in range(6)])
                rcr = Rot([sbt(es, f"narc{i}", [128, 512]) for i in range(2)])
                psr = Rot([pst(es, f"naS{i}", [128, 512]) for i in range(4)])
                numr = Rot([pst(es, f"naN{i}", [128, 512]) for i in range(2)])
                denr = Rot([pst(es, f"naD{i}", [128, 512]) for i in range(2)])
                Bo, B1 = Buf(), Buf()
                sch.op("pool", G.memset, oneh[:], 0.0, writes=[B1])
                for hh in range(2):
                    sch.op("pool", G.memset, oneh[:, hh, hh * 64:(hh + 1) * 64], 1.0, reads=[B1], writes=[B1])
                LOOK = 3
                ebgs = make_bg(es, 4)
                if l == 0:
                    pump(30, ebgs, drain=True)

                def load_qkv(hp):
                    qm, kT, vm, Bq, Bk, Bv = sets[hp % 2]
                    for hh in range(2):
                        r0 = 512 + hp * 128 + hh * 64
                        sch.dma("sp", qm[hh * 64:(hh + 1) * 64, hh, :], fb[r0:r0 + 64, :], reads=[Bq], writes=[Bq])
                    sch.dma("sp", kT[:], fb[1024 + hp * 128:1024 + (hp + 1) * 128, :], writes=[Bk])
                    for hh in range(2):
                        c0 = 512 + hp * 128 + hh * 64
                        sch.dma("sp", vm[:, hh, :, hh * 64:(hh + 1) * 64], tm[:, c0:c0 + 64].rearrange("(k p) c -> p k c", p=128),
                                reads=[Bv], writes=[Bv])

                def load_bias(hp, g):
                    t0_, t1_ = bg_rng[g]
                    for hh in range(2):
                        sch.dma("sp", bias[hh][g][:], nabias[l, hp * 2 + hh].rearrange("p (t n) -> p t n", n=512)[:, t0_:t1_, :],
                                writes=[Bb[hh][g]])

                load_qkv(0)
                for g in range(3):
                    load_bias(0, g)
                for hp in range(4):
                    qm, kT, vm, Bq, Bk, Bv = sets[hp % 2]
                    if hp + 1 < 4:
                        load_qkv(hp + 1)
                    items = []
                    for qb in range(8):
                        if qb == 0:
                            kts = [(2 * p * 64, (0, p)) for p in range(6)]
                        elif qb == 7:
                            kts = [((52 + 2 * p) * 64, (2, p)) for p in range(6)]
                        else:
                            kts = [((8 * qb - 4 + 2 * po) * 64, (1, po)) for po in range(8)]
                        kts += [(S, None), (S + 128, None)]
                        items.append((qb * 512, 512, kts, qb))
                    if l == 0:
                        items.append((S, L, [(S, None), (S + 128, None)], 8))
                    for (q0, nq, kts, qb) in items:
                        flat = [(hh, kt0, bi) for hh in range(2) for (kt0, bi) in kts]
                        prod = [None] * len(flat)
                        num, Bn = numr.next()
                        den, Bd = denr.next()
                        for i in range(len(flat) + LOOK):
                            if i < len(flat):
                                hh, kt0, bi = flat[i]
                                ps, Bps = psr.next()
                                sch.op("pe", P.matmul, ps[:, 0:nq], lhsT=kT[:, kt0:kt0 + 128], rhs=qm[:, hh, q0:q0 + nq], start=True, stop=True,
                                       reads=[Bk, Bq], writes=[Bps])
                                pt, Bpt = pr.next()
                                if bi is None:
                                    sch.op("act", A.activation, pt[:, 0:nq], ps[:, 0:nq], AF.Exp, reads=[Bps], writes=[Bpt])
                                else:
                                    st, Bst = sbr.next()
                                    sch.op("dve", V.tensor_tensor, st[:, 0:nq], ps[:, 0:nq], bias[hh][bi[0]][:, bi[1], 0:nq], ALU.add,
                                           reads=[Bps, Bb[hh][bi[0]]], writes=[Bst])
                                    sch.op("act", A.activation, pt[:, 0:nq], st[:, 0:nq], AF.Exp, reads=[Bst], writes=[Bpt])
                                prod[i] = (pt, Bpt)
                            j = i - LOOK
                            if j >= 0:
                                hh, kt0, bi = flat[j]
                                pt, Bpt = prod[j]
                                first = (j == 0)
                                last = (j == len(flat) - 1)
                                sch.op("pe", P.matmul, num[:, 0:nq], lhsT=vm[:, hh, kt0 // 128, :], rhs=pt[:, 0:nq],
                                       start=first, stop=last, reads=[Bv, Bpt], writes=[Bn])
                                sch.op("pe", P.matmul, den[:, 0:nq], lhsT=oneh[:, hh, :], rhs=pt[:, 0:nq],
                                       start=first, stop=last, reads=[B1, Bpt], writes=[Bd])
                        rc, Brc = rcr.next()
                        sch.op("dve", V.reciprocal, rc[:, 0:nq], den[:, 0:nq], reads=[Bd], writes=[Brc])
                        sch.op("dve", V.tensor_tensor, oT[:, q0:q0 + nq], num[:, 0:nq], rc[:, 0:nq], ALU.mult, reads=[Bn, Brc], writes=[Bo])
                        if hp + 1 < 4:
                            if qb == 0:
                                load_bias(hp + 1, 0)
                            elif qb == 6:
                                load_bias(hp + 1, 1)
                            elif qb == 7:
                                load_bias(hp + 1, 2)
                    ncol = T if l == 0 else S
                    sch.dma("pool", catT[512 + hp * 128:512 + (hp + 1) * 128, 0:ncol], oT[:, 0:ncol], reads=[Bo])
                    if l == 0 and hp < 3:
                        pump(30, ebgs, drain=True)
                sch.barrier()

        def phase_f1(l):
            xsrc = xin if l == 0 else xres
            with ExitStack() as es:
                wob, Bw = load_weight(es, "woutbf", w_out_b[l].rearrange("(k p) n -> p k n", p=128), 8, D)
                g1 = [sbt(es, f"g1_{v}", [128, D]) for v in range(2)]
                Bg = Buf()
                for v in range(2):
                    sch.dma("pool", g1[v][:], mod2d[2 * l + v, 2 * D:3 * D].partition_broadcast(128), writes=[Bg])
                fbgs = make_bg(es, 4)
                cr = Rot([sbt(es, f"cat{i}", [128, 8, 512], BF16) for i in range(2)])
                xr = Rot([sbt(es, f"xt{i}", [128, D]) for i in range(3)])
                yr = Rot([sbt(es, f"yt{i}", [128, D]) for i in range(2)])
                pyr = Rot([pst(es, f"py{i}", [128, 512]) for i in range(4)])
                sbs = [(i * 512, 512, 0) for i in range(8)] + ([(S, L, 1)] if l == 0 else [])
                for (t0, ntok, v) in sbs:
                    ct, Bc = cr.next()
                    sch.dma("sp", ct[:, :, 0:ntok], catT[:, t0:t0 + ntok].rearrange("(k p) t -> p k t", p=128), writes=[Bc])
                    for tb in range(ntok // 128):
                        xt, Bx = xr.next()
                        rows = slice(t0 + tb * 128, t0 + (tb + 1) * 128)
                        sch.dma("sp", xt[:], xsrc[rows, :], writes=[Bx])
                        yt, By = yr.next()
                        for nb in range(2):
                            py, Bp = pyr.next()
                            for k in range(8):
                                sch.op("pe", P.matmul, py[:], lhsT=ct[:, k, tb * 128:(tb + 1) * 128], rhs=wob[:, k, nb * 512:(nb + 1) * 512],
                                       start=(k == 0), stop=(k == 7), reads=[Bc, Bw], writes=[Bp])
                            sch.op("dve", V.tensor_tensor, yt[:, nb * 512:(nb + 1) * 512], py[:], g1[v][:, nb * 512:(nb + 1) * 512], ALU.mult,
                                   reads=[Bp, Bg], writes=[By])
                        sch.op("dve", V.tensor_tensor, yt[:], yt[:], xt[:], ALU.add, reads=[By, Bx], writes=[By])
                        sch.dma("pool", xres[rows, :], yt[:], reads=[By])
                        if l == 0:
                            pump(1, fbgs)
                pump(0, fbgs, drain=True)
                sch.barrier()

        def phase_f2(l):
            last = (l == DEPTH - 1)
            with ExitStack() as es:
                wg, Bwg = load_weight(es, "wgbf", w_gate_b[l].rearrange("(k p) n -> p k n", p=128), 8, DFF)
                wu, Bwu = load_weight(es, "wubf", w_up_b[l].rearrange("(k p) n -> p k n", p=128), 8, DFF)
                wd, Bwd = load_weight(es, "wdbf", w_down_b[l].rearrange("(j p) n -> p j n", p=128), NJ, D)
                junk = sbt(es, "junk", [128, D], BF16)
                nt = (junk, Buf(), Rot([sbt(es, f"ssq{i}", [128, 4]) for i in range(4)]),
                      Rot([sbt(es, f"ntmp{i}", [128, D]) for i in range(1)]))
                Gt = sbt(es, "G2", [128, D])
                SHt = sbt(es, "SH2", [128, D])
                g2 = sbt(es, "g2", [128, D])
                Bm = Buf()

                def load_mods(v):
                    sct, Bsc = nt[3].next()
                    sch.dma("pool", SHt[:], mod2d[2 * l + v, 3 * D:4 * D].partition_broadcast(128), writes=[Bm])
                    sch.dma("pool", Gt[:], norm2_g[l].partition_broadcast(128), writes=[Bm])
                    sch.dma("pool", g2[:], mod2d[2 * l + v, 5 * D:6 * D].partition_broadcast(128), writes=[Bm])
                    sch.dma("pool", sct[:], mod2d[2 * l + v, 4 * D:5 * D].partition_broadcast(128), writes=[Bsc])
                    sch.op("dve", V.scalar_tensor_tensor, Gt[:], sct[:], 1.0, Gt[:], ALU.add, ALU.mult, reads=[Bsc, Bm], writes=[Bm])
                fg = None
                if last:
                    fg = sbt(es, "fg", [128, D])
                    Bfg = Buf()
                    sch.dma("pool", fg[:], final_g.partition_broadcast(128), writes=[Bfg])
                x1 = [(sbt(es, f"x1_{i}", [128, D]), Buf()) for i in range(4)]
                ar = Rot([sbt(es, f"abf{i}", [128, D], BF16) for i in range(1)])
                aT = sbt(es, "aT", [128, 8, 512], BF16)
                BaT = Buf()
                hT = sbt(es, "hT", [128, NJ, 512], BF16)
                BhT = Buf()
                sgr = Rot([sbt(es, f"sg{i}", [128, 512]) for i in range(1)])
                ptrr = Rot([pst(es, f"ptr{i}", [128, 4, 128], BF16) for i in range(1)])
                pgr = Rot([pst(es, f"pg{i}", [128, 512]) for i in range(2)])
                pur = Rot([pst(es, f"pu{i}", [128, 512]) for i in range(2)])
                pdr = Rot([pst(es, f"pd{i}", [128, 512]) for i in range(3)])
                sbs = [(i * 512, 512, 0) for i in range(8)] + ([(S, L, 1)] if l == 0 else [])
                curv = None
                for (t0, ntok, v) in sbs:
                    if v != curv:
                        load_mods(v)
                        curv = v
                    ntb = ntok // 128
                    for tb in range(ntb):
                        xt, Bx = x1[tb]
                        sch.dma("sp", xt[:], xres[t0 + tb * 128:t0 + (tb + 1) * 128, :], writes=[Bx])
                        a_bf, Ba = ar.next()
                        norm_block(nt, xt, Bx, Gt, SHt, Bm, a_bf, Ba)
                        transpose_block(ptrr, a_bf, Ba, aT, BaT, tb, tb)
                    for j in range(NJ):
                        pg, Bpg = pgr.next()
                        pu, Bpu = pur.next()
                        for k in range(8):
                            sch.op("pe", P.matmul, pg[:, 0:ntok], lhsT=wg[:, k, j * 128:(j + 1) * 128], rhs=aT[:, k, 0:ntok],
                                   start=(k == 0), stop=(k == 7), reads=[Bwg, BaT], writes=[Bpg])
                        for k in range(8):
                            sch.op("pe", P.matmul, pu[:, 0:ntok], lhsT=wu[:, k, j * 128:(j + 1) * 128], rhs=aT[:, k, 0:ntok],
                                   start=(k == 0), stop=(k == 7), reads=[Bwu, BaT], writes=[Bpu])
                        sg, Bsg = sgr.next()
                        sch.op("act", A.activation, sg[:, 0:ntok], pg[:, 0:ntok], AF.Silu, reads=[Bpg], writes=[Bsg])
                        sch.op("dve", V.tensor_tensor, hT[:, j, 0:ntok], sg[:, 0:ntok], pu[:, 0:ntok], ALU.mult, reads=[Bsg, Bpu], writes=[BhT])
                    for tb in range(ntb):
                        xt, Bx = x1[tb]
                        rows = slice(t0 + tb * 128, t0 + (tb + 1) * 128)
                        tp, Bt = nt[3].next()
                        for nb in range(2):
                            pd, Bpd = pdr.next()
                            for j in range(NJ):
                                sch.op("pe", P.matmul, pd[:], lhsT=hT[:, j, tb * 128:(tb + 1) * 128], rhs=wd[:, j, nb * 512:(nb + 1) * 512],
                                       start=(j == 0), stop=(j == NJ - 1), reads=[BhT, Bwd], writes=[Bpd])
                            sch.op("dve", V.tensor_tensor, tp[:, nb * 512:(nb + 1) * 512], pd[:], g2[:, nb * 512:(nb + 1) * 512], ALU.mult,
                                   reads=[Bpd, Bm], writes=[Bt])
                        sch.op("dve", V.tensor_tensor, xt[:], xt[:], tp[:], ALU.add, reads=[Bx, Bt], writes=[Bx])
                        if not last:
                            sch.dma("pool", xres[rows, :], xt[:], reads=[Bx])
                        else:
                            sq, Bs = nt[2].next()
                            sch.op("act", A.activation, junk[:], xt[:], AF.Square, accum_out=sq[:, 0:1], reads=[Bx], writes=[nt[1], Bs])
                            sch.op("act", A.activation, sq[:, 1:2], sq[:, 0:1], AF.Sqrt, bias=EPS, scale=1.0 / D, reads=[Bs], writes=[Bs])
                            sch.op("dve", V.reciprocal, sq[:, 2:3], sq[:, 1:2], reads=[Bs], writes=[Bs])
                            sch.op("dve", V.scalar_tensor_tensor, xt[:], xt[:], sq[:, 2:3], fg[:], ALU.mult, ALU.mult,
                                   reads=[Bx, Bs, Bfg], writes=[Bx])
                            sch.dma("pool", out[rows, :], xt[:], reads=[Bx])
                sch.barrier()

        phases = [("wcast", phase_wcast, None), ("mod", phase_mod, None)]
        for l in range(DEPTH):
            phases += [(f"ab{l}", phase_ab, l), (f"c{l}", phase_c, l), (f"d{l}", phase_d, l), (f"e{l}", phase_e, l),
                       (f"f1{l}", phase_f1, l), (f"f2{l}", phase_f2, l)]
        for name, fn, arg in phases:
            if arg is None:
                fn()
            else:
                fn(arg)
            if stop is not None and name == stop:
                break
        sch.finish()
        nc._sched_stats = (sch.nops, sch.nwaits)
    return nc


def _prep(inputs):
    f32 = np.float32
    c = _consts()
    x = np.asarray(inputs["x"], f32)
    ctx = np.asarray(inputs["ctx"], f32)
    cvec = np.asarray(inputs["c"], f32)
    cctx = np.asarray(inputs["c_ctx"], f32)
    lb = np.asarray(inputs["hgrn_lb"], f32)
    lbT = np.ascontiguousarray(lb.reshape(2, 2, 2, 128).transpose(3, 0, 1, 2).reshape(128, 8))
    ng = np.asarray(inputs["hgrn_norm_g"], f32)
    ngT = np.ascontiguousarray(np.concatenate([ng, ng], axis=1).T)
    rpb = np.asarray(inputs["na_rpb"], f32)
    nabias = np.stack([_na_bias_tiles(rpb[l]) for l in range(DEPTH)])
    shared = {
        "w_mod": np.asarray(inputs["w_mod"], f32), "b_mod": np.asarray(inputs["b_mod"], f32),
        "norm1_g": np.asarray(inputs["norm1_g"], f32), "w_in": np.asarray(inputs["w_in"], f32),
        "fourier_w": np.asarray(inputs["fourier_w"], f32), "lbT": lbT, "ngT": ngT, "nabias": nabias,
        "w_out": np.asarray(inputs["w_out"], f32), "norm2_g": np.asarray(inputs["norm2_g"], f32),
        "w_gate": np.asarray(inputs["w_ffn_gate"], f32), "w_up": np.asarray(inputs["w_ffn_up"], f32),
        "w_down": np.asarray(inputs["w_ffn_down"], f32), "final_g": np.asarray(inputs["final_norm_g"], f32),
        "C4": c["C4"], "S4": c["S4"], "C2": c["C2"], "S2": c["S2"], "C64bd": c["C64bd"], "S64bdn": c["S64bdn"],
        "ident": c["ident"], "bdmask": c["bdmask"], "bd64": c["bd64"], "mfwd": c["mfwd"], "mbwd": c["mbwd"], "cm3": c["cm3"],
    }
    maps = []
    for b in range(8):
        m = dict(shared)
        m["xin"] = np.ascontiguousarray(np.concatenate([x[b], ctx[b]], axis=0))
        cv = np.stack([cvec[b], cctx], axis=0)
        m["cT"] = np.ascontiguousarray(cv.reshape(2, 8, 128).transpose(2, 1, 0).reshape(128, 16))
        maps.append(m)
    return maps


def kernel(**inputs):
    maps = _prep(inputs)
    nc = build()
    res = run_bass_kernel_spmd(nc, maps, core_ids=list(range(8)))
    return np.stack([np.asarray(r["out"], np.float32) for r in res.results], axis=0)
```

```python
import numpy as np
import ml_dtypes
from contextlib import ExitStack
import concourse.bass as bass
import concourse.mybir as mybir
from concourse.bass_utils import run_bass_kernel_spmd

F32 = mybir.dt.float32
BF16 = mybir.dt.bfloat16
AF = mybir.ActivationFunctionType
ALU = mybir.AluOpType

D = 1024
S = 4096
L = 256
T = S + L
NB = T // 128
DFF = 2816
NJ = DFF // 128
INW = 3072
EPS = 1e-6
NEG = -1e30
DEPTH = 2


class _Op:
    __slots__ = ("eng", "fn", "args", "kw", "deps", "signal", "sem", "val", "dma", "epoch")


class Buf:
    __slots__ = ("w", "r", "g")

    def __init__(self):
        self.w = {}
        self.r = {}
        self.g = ()


class Sched:
    CENG = ("pe", "act", "dve", "pool")
    SEMCAP = 30000

    def __init__(self, nc, es):
        self.nc = nc
        self.eobj = dict(pe=nc.tensor, act=nc.scalar, dve=nc.vector, pool=nc.gpsimd, sp=nc.sync)
        self.pending = []
        self.lastc = {}
        self.dma_since = []
        self.bar_deps = {}
        ncs = dict(pe=3, act=3, dve=4, pool=2)
        self.csem = {e: [es.enter_context(nc.semaphore(f"c{e}{i}")) for i in range(ncs[e])] for e in self.CENG}
        self.ccount = {e: 0 for e in self.CENG}
        self.dsem = {q: [es.enter_context(nc.semaphore(f"d{q}{i}")) for i in range(8)] for q in ("sp", "pool")}
        self.dcount = {q: [0] * 8 for q in ("sp", "pool")}
        self.drr = {q: 0 for q in ("sp", "pool")}
        self.waited = {e: {} for e in self.eobj}
        self.nops = 0
        self.nwaits = 0
        self.epoch = 0
        self.pe_mode = "f"
        self.bg_ops = []

    def op(self, eng, fn, *args, reads=(), writes=(), dma=False, force=False, bg=False, **kw):
        o = _Op()
        o.eng, o.fn, o.args, o.kw, o.dma = eng, fn, args, kw, dma
        o.signal = dma or force
        o.sem = None
        o.val = 0
        o.epoch = self.epoch
        deps = {}
        if eng == "pe":
            mode = kw.pop("pemode", "f")
            if mode != self.pe_mode and "pe" in self.lastc:
                deps[id(self.lastc["pe"])] = self.lastc["pe"]
            self.pe_mode = mode

        def need(d, same_ok, bar=False):
            if d.epoch < self.epoch and not bar:
                return
            if same_ok and (not d.dma) and (not dma) and d.eng == eng:
                return
            deps[id(d)] = d

        for b in reads:
            for d in b.w.values():
                need(d, eng == "pe")
        for b in writes:
            if b.r:
                b.g = tuple(b.r.values()) + tuple(b.w.values())
                b.r = {}
                b.w = {}
            for d in b.g:
                need(d, True)
        for d in self.bar_deps.pop(eng, ()):
            need(d, True, True)
        for d in deps.values():
            d.signal = True
        o.deps = list(deps.values())
        key = ("d", id(o)) if dma else eng
        for b in reads:
            b.r[key] = o
        for b in writes:
            b.w[key] = o
        if dma and bg:
            o.epoch = 1 << 60
            self.bg_ops.append(o)
        elif dma:
            self.dma_since.append(o)
        else:
            self.lastc[eng] = o
        self.pending.append(o)
        return o

    def dma(self, q, out, in_, reads=(), writes=(), **kw):
        fn = self.nc.sync.dma_start if q == "sp" else self.nc.gpsimd.dma_start
        return self.op(q, fn, reads=reads, writes=writes, dma=True, out=out, in_=in_, **kw)

    def flush(self):
        for o in self.pending:
            e = self.eobj[o.eng]
            w = self.waited[o.eng]
            for d in o.deps:
                assert d.sem is not None, "dependency on unsignalled op"
                k = id(d.sem)
                if w.get(k, 0) >= d.val:
                    continue
                e.wait_ge(d.sem, d.val)
                w[k] = d.val
                self.nwaits += 1
            ins = o.fn(*o.args, **o.kw)
            self.nops += 1
            if o.dma:
                q = o.eng
                i = self.drr[q]
                self.drr[q] = (i + 1) % 8
                self.dcount[q][i] += 16
                o.sem, o.val = self.dsem[q][i], self.dcount[q][i]
                ins.then_inc(o.sem, 16)
            elif o.signal:
                n = self.ccount[o.eng]
                self.ccount[o.eng] = n + 1
                sems = self.csem[o.eng]
                si = n // self.SEMCAP
                assert si < len(sems), "out of compute semaphores"
                o.sem, o.val = sems[si], n % self.SEMCAP + 1
                ins.then_inc(o.sem, 1)
            o.fn = o.args = o.kw = None
        self.pending = []

    def barrier(self):
        deps = [self.lastc[e] for e in self.CENG if e in self.lastc] + self.dma_since
        for d in deps:
            d.signal = True
        self.flush()
        old = self.bar_deps
        self.bar_deps = {e: list(deps) + list(old.get(e, ())) for e in self.eobj}
        self.dma_since = []
        self.epoch += 1

    def finish(self):
        self.barrier()
        sp = self.nc.sync
        for d in list(self.bar_deps["sp"]) + self.bg_ops:
            k = id(d.sem)
            if self.waited["sp"].get(k, 0) >= d.val:
                continue
            sp.wait_ge(d.sem, d.val)
            self.waited["sp"][k] = d.val


class Rot:
    def __init__(self, tiles):
        self.t = [(t, Buf()) for t in tiles]
        self.i = 0

    def next(self):
        r = self.t[self.i]
        self.i = (self.i + 1) % len(self.t)
        return r


_CONST = {}


def _consts():
    if _CONST:
        return _CONST
    bf = ml_dtypes.bfloat16
    t = np.arange(S, dtype=np.int64)
    m = (t[:, None] * t[None, :]) % S
    ang = 2.0 * np.pi * m.astype(np.float64) / S
    _CONST["C4"] = (np.cos(ang) / 64.0).astype(np.float32).astype(bf)
    _CONST["S4"] = (np.sin(ang) / 64.0).astype(np.float32).astype(bf)
    del ang, m
    t = np.arange(L, dtype=np.int64)
    ang = 2.0 * np.pi * ((t[:, None] * t[None, :]) % L).astype(np.float64) / L
    _CONST["C2"] = (np.cos(ang) / 16.0).astype(np.float32).astype(bf)
    _CONST["S2"] = (np.sin(ang) / 16.0).astype(np.float32).astype(bf)
    t = np.arange(64, dtype=np.int64)
    ang = 2.0 * np.pi * ((t[:, None] * t[None, :]) % 64).astype(np.float64) / 64
    c64 = np.cos(ang) / 8.0
    s64 = np.sin(ang) / 8.0
    z = np.zeros((64, 64))
    _CONST["C64bd"] = np.block([[c64, z], [z, c64]]).astype(np.float32)
    _CONST["S64bdn"] = (-np.block([[s64, z], [z, s64]])).astype(np.float32)
    _CONST["ident"] = np.eye(128, dtype=np.float32).astype(bf)
    bd = np.zeros((128, 128), np.float32)
    bd[:64, :64] = 1.0
    bd[64:, 64:] = 1.0
    _CONST["bdmask"] = bd
    _CONST["bd64"] = (bd / 64.0).astype(np.float32)
    s_ = np.arange(128)[:, None]
    t_ = np.arange(128)[None, :]
    same = (s_ // 32) == (t_ // 32)
    _CONST["mfwd"] = (same & (s_ <= t_)).astype(np.float32)
    _CONST["mbwd"] = (same & (s_ >= t_)).astype(np.float32)
    cm = np.zeros((128, 4, 128), np.float32)
    for c_ in range(4):
        cm[32 * c_:32 * c_ + 32, c_, :] = 1.0
    _CONST["cm3"] = cm.reshape(128, 512)
    return _CONST


def _na_bias_tiles(rpb):
    H = rpb.shape[0]
    kc = np.arange(64)[:, None]
    qc = np.arange(64)[None, :]
    wc = np.clip(qc - 8, 0, 48)
    colvalid = (kc >= wc) & (kc < wc + 16)
    cidx = np.clip(kc - qc + 15, 0, 30)
    specs = [(8, 8 - 4 + 2 * po) for po in range(8)] + [(0, 2 * p) for p in range(6)] + [(56, 52 + 2 * p) for p in range(6)]
    out = np.full((H, 128, 20, 512), NEG, np.float32)
    for ti, (r0, kr0) in enumerate(specs):
        for kr2 in range(2):
            kr = kr0 + kr2
            for qr in range(8):
                r = r0 + qr
                rs = min(max(r - 4, 0), 56)
                if not (rs <= kr < rs + 8):
                    continue
                ridx = kr - r + 7
                blk = np.where(colvalid[None], rpb[:, ridx][:, cidx], NEG)
                out[:, kr2 * 64:(kr2 + 1) * 64, ti, qr * 64:(qr + 1) * 64] = blk
    return out.reshape(H, 128, 20 * 512)


def build(debug=False, stop=None):
    nc = bass.Bass("TRN2", target_bir_lowering=False)
    skind = "ExternalOutput" if debug else "Internal"

    def din(name, shape, dt=F32):
        return nc.dram_tensor(name, list(shape), dt, kind="ExternalInput").ap()

    def dscr(name, shape, dt=F32):
        return nc.dram_tensor(name, list(shape), dt, kind=skind).ap()

    xin = din("xin", [T, D])
    cT = din("cT", [128, 16])
    w_mod = din("w_mod", [DEPTH, D, 6 * D])
    b_mod = din("b_mod", [DEPTH, 6 * D])
    norm1_g = din("norm1_g", [DEPTH, D])
    w_in = din("w_in", [DEPTH, D, INW])
    fourier_w = din("fourier_w", [DEPTH, 4, 64, 64])
    lbT = din("lbT", [128, 8])
    ngT = din("ngT", [128, DEPTH])
    nabias = din("nabias", [DEPTH, 8, 128, 20 * 512])
    w_out = din("w_out", [DEPTH, D, D])
    norm2_g = din("norm2_g", [DEPTH, D])
    w_gate = din("w_gate", [DEPTH, D, DFF])
    w_up = din("w_up", [DEPTH, D, DFF])
    w_down = din("w_down", [DEPTH, DFF, D])
    final_g = din("final_g", [D])
    C4 = din("C4", [S, S], BF16)
    S4 = din("S4", [S, S], BF16)
    C2 = din("C2", [L, L], BF16)
    S2 = din("S2", [L, L], BF16)
    C64bd_d = din("C64bd", [128, 128])
    S64bdn_d = din("S64bdn", [128, 128])
    ident_d = din("ident", [128, 128], BF16)
    bdmask_d = din("bdmask", [128, 128])
    bd64_d = din("bd64", [128, 128])
    mfwd_d = din("mfwd", [128, 128])
    mbwd_d = din("mbwd", [128, 128])
    cm3_d = din("cm3", [128, 512])
    out = nc.dram_tensor("out", [S, D], F32, kind="ExternalOutput").ap()

    mod2d = dscr("mod2d", [2 * DEPTH, 6 * D])
    fz = dscr("fz", [512, T])
    fb = dscr("fb", [1536, T], BF16)
    tm = dscr("tm", [T, 1024], BF16)
    catT = dscr("catT", [1024, T], BF16)
    xres = dscr("xres", [T, D])
    w_in_b = dscr("w_in_b", [DEPTH, D, INW], BF16)
    w_out_b = dscr("w_out_b", [DEPTH, D, D], BF16)
    w_gate_b = dscr("w_gate_b", [DEPTH, D, DFF], BF16)
    w_up_b = dscr("w_up_b", [DEPTH, D, DFF], BF16)
    w_down_b = dscr("w_down_b", [DEPTH, DFF, D], BF16)

    with ExitStack() as ges:
        sch = Sched(nc, ges)
        V, A, P, G = nc.vector, nc.scalar, nc.tensor, nc.gpsimd

        uid = [0]

        def sbt(es, name, shape, dt=F32):
            uid[0] += 1
            return es.enter_context(nc.sbuf_tensor(f"s{uid[0]}_{name}", list(shape), dt))

        def pst(es, name, shape, dt=F32):
            uid[0] += 1
            return es.enter_context(nc.psum_tensor(f"p{uid[0]}_{name}", list(shape), dt))

        ident = sbt(ges, "ident", [128, 128], BF16)
        bdmask = sbt(ges, "bdmask", [128, 128])
        onesb = sbt(ges, "onesb", [128, 64], BF16)
        B_const = Buf()
        sch.dma("sp", ident[:], ident_d, writes=[B_const])
        sch.dma("sp", bdmask[:], bdmask_d, writes=[B_const])
        sch.op("pool", G.memset, onesb[:], 1.0, writes=[B_const], force=True)

        def evac(i, outap, inap, reads, writes, scale=None):
            if i % 2 == 0:
                if scale is None:
                    sch.op("act", A.copy, outap, inap, reads=reads, writes=writes)
                else:
                    sch.op("act", A.mul, outap, inap, scale, reads=reads, writes=writes)
            else:
                if scale is None:
                    sch.op("dve", V.tensor_copy, outap, inap, reads=reads, writes=writes)
                else:
                    sch.op("dve", V.tensor_scalar, outap, inap, scale, None, ALU.mult, reads=reads, writes=writes)

        def phase_mod():
            with ExitStack() as es:
                cTt = sbt(es, "cTt", [128, 16])
                sT = sbt(es, "sT", [128, 16])
                mod2s = sbt(es, "mod2s", [2, 6 * D])
                bt = sbt(es, "bt", [2, 6 * D])
                wm = Rot([sbt(es, f"wm{i}", [128, 8, 512]) for i in range(3)])
                pm = Rot([pst(es, f"pm{i}", [128, 512]) for i in range(2)])
                B_c, B_s, B_m, B_b = Buf(), Buf(), Buf(), Buf()
                sch.dma("sp", cTt[:], cT, writes=[B_c])
                sch.op("act", A.activation, sT[:], cTt[:], AF.Silu, reads=[B_c], writes=[B_s])
                for l in range(DEPTH):
                    sch.dma("sp", bt[:], b_mod[l].partition_broadcast(2), writes=[B_b])
                    for nb in range(12):
                        wt, Bw = wm.next()
                        sch.dma("sp", wt[:], w_mod[l][:, nb * 512:(nb + 1) * 512].rearrange("(k p) n -> p k n", p=128), writes=[Bw])
                        pt, Bp = pm.next()
                        for k in range(8):
                            sch.op("pe", P.matmul, pt[0:2, :], lhsT=sT[:, 2 * k:2 * k + 2], rhs=wt[:, k, :],
                                   start=(k == 0), stop=(k == 7), pemode="m2", reads=[B_s, Bw], writes=[Bp])
                        sch.op("dve", V.tensor_tensor, mod2s[:, nb * 512:(nb + 1) * 512], pt[0:2, :], bt[:, nb * 512:(nb + 1) * 512],
                               ALU.add, reads=[Bp, B_b], writes=[B_m])
                    sch.dma("pool", mod2d[2 * l:2 * l + 2, :], mod2s[:], reads=[B_m])
                sch.barrier()

        def norm_block(es_tiles, xt, Bx, Gt, SHt, B_mod, a_bf, Ba):
            junk, Bj, ssqr, tmpr = es_tiles
            sq, Bs = ssqr.next()
            sch.op("act", A.activation, junk[:], xt[:], AF.Square, accum_out=sq[:, 0:1], reads=[Bx], writes=[Bj, Bs])
            sch.op("act", A.activation, sq[:, 1:2], sq[:, 0:1], AF.Sqrt, bias=EPS, scale=1.0 / D, reads=[Bs], writes=[Bs])
            sch.op("dve", V.reciprocal, sq[:, 2:3], sq[:, 1:2], reads=[Bs], writes=[Bs])
            tp, Bt = tmpr.next()
            sch.op("dve", V.scalar_tensor_tensor, tp[:], xt[:], sq[:, 2:3], Gt[:], ALU.mult, ALU.mult,
                   reads=[Bx, Bs, B_mod], writes=[Bt])
            sch.op("dve", V.tensor_tensor, a_bf[:], tp[:], SHt[:], ALU.add, reads=[Bt, B_mod], writes=[Ba])

        def transpose_block(ptrr, a_bf, Ba, aT, BaT, tb, ei):
            for half in range(2):
                pt, Bp = ptrr.next()
                for j in range(4):
                    c = (half * 4 + j) * 128
                    sch.op("pe", P.transpose, pt[:, j, :], a_bf[:, c:c + 128], ident[:], reads=[Ba, B_const], writes=[Bp])
                evac(ei + half, aT[:, half * 4:(half + 1) * 4, tb * 128:(tb + 1) * 128], pt[:], [Bp], [BaT])

        def load_weight(es, name, src_kpn, nk, ncols, Bsrc):
            wb = sbt(es, name, [128, nk, ncols], BF16)
            Bw = Buf()
            for k in range(nk):
                sch.dma("sp", wb[:, k, :], src_kpn[:, k, :], reads=[Bsrc], writes=[Bw])
            return wb, Bw

        wcast_buf = {}

        def cast_weights(l, names):
            table = {"in": (w_in, w_in_b, D), "out": (w_out, w_out_b, D), "gate": (w_gate, w_gate_b, D), "up": (w_up, w_up_b, D),
                     "down": (w_down, w_down_b, DFF)}
            for nm in names:
                src, dst, R_ = table[nm]
                Bw = Buf()
                wcast_buf[(nm, l)] = Bw
                for r0 in range(0, R_, 128):
                    sch.dma("pool", dst[l][r0:r0 + 128, :], src[l][r0:r0 + 128, :], writes=[Bw], bg=True)

        def pump(*a_, **k_):
            return

        def make_bg(es, n):
            return None

        bgq = []

        def mod_tiles(es, l, v, gsrc, i_sh, i_sc, tag):
            Gt = sbt(es, f"G{tag}", [128, D])
            SHt = sbt(es, f"SH{tag}", [128, D])
            Bm = Buf()
            sch.dma("pool", SHt[:], mod2d[2 * l + v, i_sh * D:(i_sh + 1) * D].partition_broadcast(128), writes=[Bm])
            sch.dma("pool", Gt[:], gsrc.partition_broadcast(128), writes=[Bm])
            return Gt, SHt, Bm

        def phase_ab(l):
            xsrc = xin if l == 0 else xres
            if l >= 1 and bgq:
                with ExitStack() as es0:
                    pump(100000, make_bg(es0, 4), drain=True)
                    sch.barrier()
            with ExitStack() as es:
                wbf, Bw = load_weight(es, "winbf", w_in_b[l].rearrange("(k p) n -> p k n", p=128), 8, INW, wcast_buf[("in", l)])
                mods = []
                sct = sbt(es, "sct", [128, D])
                Bsc = Buf()
                for v in range(2):
                    Gt, SHt, Bm = mod_tiles(es, l, v, norm1_g[l], 0, 1, f"1_{v}")
                    sch.dma("pool", sct[:], mod2d[2 * l + v, D:2 * D].partition_broadcast(128), writes=[Bsc])
                    sch.op("dve", V.scalar_tensor_tensor, Gt[:], sct[:], 1.0, Gt[:], ALU.add, ALU.mult, reads=[Bsc, Bm], writes=[Bm])
                    mods.append((Gt, SHt, Bm))
                junk = sbt(es, "junk", [128, D], BF16)
                nt = (junk, Buf(), Rot([sbt(es, f"ssq{i}", [128, 4]) for i in range(4)]),
                      Rot([sbt(es, f"ntmp{i}", [128, D]) for i in range(2)]))
                xr = Rot([sbt(es, f"xt{i}", [128, D]) for i in range(8)])
                ar = Rot([sbt(es, f"abf{i}", [128, D], BF16) for i in range(2)])
                aTr = Rot([sbt(es, f"aT{i}", [128, 8, 512], BF16) for i in range(2)])
                sfr = Rot([sbt(es, f"sf{i}", [128, 512]) for i in range(2)])
                sbr = Rot([sbt(es, f"sb{i}", [128, 512], BF16) for i in range(3)])
                tmr = Rot([sbt(es, f"tms{i}", [128, 1024], BF16) for i in range(2)])
                ptrr = Rot([pst(es, f"ptr{i}", [128, 4, 128], BF16) for i in range(2)])
                pfr = Rot([pst(es, f"pf{i}", [128, 512]) for i in range(3)])
                ptmr = Rot([pst(es, f"ptm{i}", [128, 512]) for i in range(2)])
                fm = []
                for c0, dst, r0, sc in ((256, fb, 0, None), (512, fz, 0, None), (768, fz, 256, None), (1280, fb, 256, None),
                                        (1536, fb, 512, 0.125), (2048, fb, 1024, None)):
                    wd = 512 if c0 >= 1536 else 256
                    for g in range(wd // 128):
                        fm.append((c0 + g * 128, dst, r0 + g * 128, sc))
                sbs = [(i * 512, 512, 0) for i in range(8)] + [(S, L, 1)]
                if l == 0:
                    cast_weights(0, ["out", "gate", "up", "down"])
                eic = [0]
                aTs = {}
                abfs = {}

                xts = {}

                def load_part(i):
                    t0, ntok, v = sbs[i]
                    for tb in range(ntok // 128):
                        xt, Bx = xr.next()
                        sch.dma("sp", xt[:], xsrc[t0 + tb * 128:t0 + (tb + 1) * 128, :], writes=[Bx])
                        xts[(i, tb)] = (xt, Bx)

                def norm_part(i, tb):
                    t0, ntok, v = sbs[i]
                    Gt, SHt, Bm = mods[v]
                    if tb == 0:
                        aTs[i] = aTr.next()
                    xt, Bx = xts.pop((i, tb))
                    a_bf, Ba = ar.next()
                    norm_block(nt, xt, Bx, Gt, SHt, Bm, a_bf, Ba)
                    abfs[(i, tb)] = (a_bf, Ba)

                def tr_part(i, tb):
                    aT, BaT = aTs[i]
                    a_bf, Ba = abfs.pop((i, tb))
                    eic[0] += 2
                    transpose_block(ptrr, a_bf, Ba, aT, BaT, tb, eic[0])

                load_part(0)
                for tb in range(sbs[0][1] // 128):
                    norm_part(0, tb)
                    tr_part(0, tb)
                for i, (t0, ntok, v) in enumerate(sbs):
                    aT, BaT = aTs[i]
                    if i + 1 < len(sbs):
                        load_part(i + 1)
                    groups = []
                    for (c0, dst, r0, sc) in fm:
                        groups.append(("fm", c0, dst, r0, sc))
                    for tb in range(ntok // 128):
                        groups.append(("tm", tb))
                    nxt = {}
                    if i + 1 < len(sbs):
                        ntb2 = sbs[i + 1][1] // 128
                        for tb in range(ntb2):
                            nxt.setdefault(1 + 5 * tb, []).append(("n", tb))
                            nxt.setdefault(4 + 5 * tb, []).append(("t", tb))
                    for gi, g in enumerate(groups):
                        for (kind, tb) in nxt.pop(gi, []):
                            (norm_part if kind == "n" else tr_part)(i + 1, tb)
                        if g[0] == "fm":
                            _, c0, dst, r0, sc = g
                            pf, Bp = pfr.next()
                            for k in range(8):
                                sch.op("pe", P.matmul, pf[:, 0:ntok], lhsT=wbf[:, k, c0:c0 + 128], rhs=aT[:, k, 0:ntok],
                                       start=(k == 0), stop=(k == 7), reads=[Bw, BaT], writes=[Bp])
                            if dst is fz:
                                st, Bs = sfr.next()
                            else:
                                st, Bs = sbr.next()
                            evac(0, st[:, 0:ntok], pf[:, 0:ntok], [Bp], [Bs], scale=sc)
                            sch.dma("sp", dst[r0:r0 + 128, t0:t0 + ntok], st[:, 0:ntok], reads=[Bs])
                        else:
                            tb = g[1]
                            tms, Bts = tmr.next()
                            tsl = slice(tb * 128, (tb + 1) * 128)
                            p0, Bp0 = ptmr.next()
                            for (cc, oc) in ((0, 0), (1024, 256)):
                                for k in range(8):
                                    sch.op("pe", P.matmul, p0[:, oc:oc + 256], lhsT=aT[:, k, tsl], rhs=wbf[:, k, cc:cc + 256],
                                           start=(k == 0), stop=(k == 7), reads=[Bw, BaT], writes=[Bp0])
                            evac(0, tms[:, 0:512], p0[:], [Bp0], [Bts])
                            p1, Bp1 = ptmr.next()
                            for k in range(8):
                                sch.op("pe", P.matmul, p1[:], lhsT=aT[:, k, tsl], rhs=wbf[:, k, 2560:3072],
                                       start=(k == 0), stop=(k == 7), reads=[Bw, BaT], writes=[Bp1])
                            evac(0, tms[:, 512:1024], p1[:], [Bp1], [Bts])
                            sch.dma("sp", tm[t0 + tb * 128:t0 + (tb + 1) * 128, :], tms[:], reads=[Bts])
                    for lst in nxt.values():
                        for (kind, tb) in lst:
                            (norm_part if kind == "n" else tr_part)(i + 1, tb)
                sch.barrier()

        def phase_c(l):
            with ExitStack() as es:
                cbgs = make_bg(es, 4)
                U = sbt(es, "fU", [128, NB, 256], BF16)
                BU = Buf()
                sch.dma("sp", U[:], tm[:, 0:256].rearrange("(k p) c -> p k c", p=128), writes=[BU])
                c64 = sbt(es, "c64", [128, 2, 128])
                wf = sbt(es, "wf", [128, 2, 64])
                Mbd = sbt(es, "Mbd", [128, 3, 2, 128], BF16)
                Bc, Bwf, BM = Buf(), Buf(), Buf()
                sch.dma("sp", c64[:, 0, :], C64bd_d, writes=[Bc])
                sch.dma("sp", c64[:, 1, :], S64bdn_d, writes=[Bc])
                for cb in range(2):
                    sch.dma("sp", wf[:, cb, :], fourier_w[l][2 * cb:2 * cb + 2].rearrange("g c d -> (g c) d"), writes=[Bwf])
                sch.op("pool", G.memset, Mbd[:], 0.0, writes=[BM])
                pmr = Rot([pst(es, f"pM{i}", [128, 512]) for i in range(1)])
                pM, BpM = pmr.next()
                for w in range(2):
                    for cb in range(2):
                        o0 = (w * 2 + cb) * 64
                        sch.op("pe", P.matmul, pM[:, o0:o0 + 64], lhsT=c64[:, w, :], rhs=wf[:, cb, :], start=True, stop=True,
                               pemode="f32", reads=[Bc, Bwf], writes=[BpM])
                for w in range(2):
                    for cb in range(2):
                        o0 = (w * 2 + cb) * 64
                        sch.op("dve", V.tensor_copy, Mbd[0:64, w, cb, 0:64], pM[0:64, o0:o0 + 64], reads=[BpM, BM], writes=[BM])
                        sch.op("dve", V.tensor_copy, Mbd[64:128, w, cb, 64:128], pM[64:128, o0:o0 + 64], reads=[BpM, BM], writes=[BM])
                sch.op("dve", V.tensor_scalar, Mbd[:, 2].rearrange("p c n -> p (c n)"), Mbd[:, 1].rearrange("p c n -> p (c n)"), -1.0, None,
                       ALU.mult, reads=[BM], writes=[BM])
                csr = Rot([sbt(es, f"cs{i}", [128, 2, 4, 512], BF16) for i in range(3)])
                pqr = Rot([sbt(es, f"pq{i}", [128, 4, 512], BF16) for i in range(2)])
                osr = Rot([sbt(es, f"fo{i}", [128, 512], BF16) for i in range(3)])
                pacc = [pst(es, f"pacc{i}", [128, 512]) for i in range(4)]
                Bacc = [Buf() for _ in range(4)]
                por = Rot([pst(es, f"po{i}", [128, 512]) for i in range(2)])
                c0col = sbt(es, "c0col", [128, 2], BF16)
                Bc0 = Buf()
                sch.op("pool", G.memset, c0col[:], 1.0 / 64.0, writes=[Bc0])

                def mix(pq, Bpq, n, tcol0, mirror_col0=None, ncols=None):
                    for cb in range(2):
                        po, Bpo = por.next()
                        sch.op("pe", P.matmul, po[:, 0:n], lhsT=Mbd[:, 0, cb, :], rhs=pq[:, cb, 0:n], start=True, stop=False,
                               reads=[BM, Bpq], writes=[Bpo])
                        sch.op("pe", P.matmul, po[:, 0:n], lhsT=Mbd[:, 1, cb, :], rhs=pq[:, 2 + cb, 0:n], start=False, stop=True,
                               reads=[BM, Bpq], writes=[Bpo])
                        ot, Bo = osr.next()
                        evac(cb, ot[:, 0:n], po[:, 0:n], [Bpo], [Bo])
                        nst = n if ncols is None else ncols
                        if nst == 1:
                            sch.dma("pool", catT[cb * 128:(cb + 1) * 128, tcol0:tcol0 + 1], ot[:, 0:1], reads=[Bo],
                                    allow_slow_non_contiguous=True)
                        else:
                            sch.dma("pool", catT[cb * 128:(cb + 1) * 128, tcol0:tcol0 + nst], ot[:, 0:nst], reads=[Bo])
                        if mirror_col0 is not None:
                            po, Bpo = por.next()
                            for (wm, qi) in ((0, cb), (2, 2 + cb)):
                                bq_ = pq[:, qi, 0:n]
                                rev = bass.AP(tensor=bq_.tensor, offset=bq_.offset + n - 1, ap=[[bq_.ap[0][0], 128], [-1, n]])
                                sch.op("pe", P.matmul, po[:, 0:n], lhsT=Mbd[:, wm, cb, :], rhs=rev, start=(wm == 0), stop=(wm == 2),
                                       reads=[BM, Bpq], writes=[Bpo])
                            ot, Bo = osr.next()
                            evac(cb + 1, ot[:, 0:n], po[:, 0:n], [Bpo], [Bo])
                            sch.dma("pool", catT[cb * 128:(cb + 1) * 128, mirror_col0:mirror_col0 + n], ot[:, 0:n], reads=[Bo])

                for cb in range(2):
                    for k in range(32):
                        sch.op("pe", P.matmul, pacc[cb][:, 0:2], lhsT=U[:, k, cb * 128:(cb + 1) * 128], rhs=c0col[:],
                               start=(k == 0), stop=(k == 31), reads=[BU, Bc0], writes=[Bacc[cb]])
                pq, Bpq = pqr.next()
                sch.op("pool", G.memset, pq[:, 2:4, 0:2], 0.0, writes=[Bpq])
                for cb in range(2):
                    evac(cb, pq[:, cb, 0:2], pacc[cb][:, 0:2], [Bacc[cb], Bpq], [Bpq])
                mix(pq, Bpq, 2, 0, ncols=1)
                for nb in range(4):
                    c_lo = 512 * nb + 1
                    for kg in range(8):
                        cs, Bcs = csr.next()
                        sch.dma("sp", cs[:, 0], C4[kg * 512:(kg + 1) * 512, c_lo:c_lo + 512].rearrange("(k p) n -> p k n", p=128), writes=[Bcs])
                        sch.dma("sp", cs[:, 1], S4[kg * 512:(kg + 1) * 512, c_lo:c_lo + 512].rearrange("(k p) n -> p k n", p=128), writes=[Bcs])
                        for kk in range(4):
                            k = kg * 4 + kk
                            for w in range(2):
                                for cb in range(2):
                                    i = w * 2 + cb
                                    sch.op("pe", P.matmul, pacc[i][:], lhsT=U[:, k, cb * 128:(cb + 1) * 128], rhs=cs[:, w, kk, :],
                                           start=(k == 0), stop=(k == 31), reads=[BU, Bcs], writes=[Bacc[i]])
                    pq, Bpq = pqr.next()
                    for i in range(4):
                        evac(i, pq[:, i, :], pacc[i][:], [Bacc[i]], [Bpq])
                    mix(pq, Bpq, 512, c_lo, mirror_col0=3584 - 512 * nb)
                    if l == 0:
                        pump(5, cbgs, drain=True)
                if l == 0:
                    c2 = sbt(es, "c2", [128, 2, 2, 256], BF16)
                    Bc2 = Buf()
                    sch.dma("sp", c2[:, 0], C2.rearrange("(k p) n -> p k n", p=128), writes=[Bc2])
                    sch.dma("sp", c2[:, 1], S2.rearrange("(k p) n -> p k n", p=128), writes=[Bc2])
                    for w in range(2):
                        for cb in range(2):
                            i = w * 2 + cb
                            for k in range(2):
                                sch.op("pe", P.matmul, pacc[i][:, 0:256], lhsT=U[:, 32 + k, cb * 128:(cb + 1) * 128], rhs=c2[:, w, k, :],
                                       start=(k == 0), stop=(k == 1), reads=[BU, Bc2], writes=[Bacc[i]])
                    pq, Bpq = pqr.next()
                    for i in range(4):
                        evac(i, pq[:, i, 0:256], pacc[i][:, 0:256], [Bacc[i]], [Bpq])
                    mix(pq, Bpq, 256, S)
                sch.barrier()

        def phase_d(l):
            HW = 9 * 128
            NCH = T // 32
            PIECES = [(0, 9), (9, 8), (17, 9), (26, 8)]
            with ExitStack() as es:
                lbt = sbt(es, "lbt", [128, 8])
                lbv = sbt(es, "lbv", [128, 8])
                ngt = sbt(es, "ngt", [128, DEPTH])
                mdir = sbt(es, "mdir", [128, 2, 128])
                bd64 = sbt(es, "bd64", [128, 128])
                cm3 = sbt(es, "cm3", [128, 4, 128])
                epsc = sbt(es, "epsc", [128, 1])
                Bl, Bmd = Buf(), Buf()
                sch.op("pool", G.memset, epsc[:], EPS, writes=[Bmd])
                sch.dma("sp", lbt[:], lbT, writes=[Bl])
                sch.dma("sp", ngt[:], ngT, writes=[Bl])
                sch.dma("sp", mdir[:, 0, :], mfwd_d, writes=[Bmd])
                sch.dma("sp", mdir[:, 1, :], mbwd_d, writes=[Bmd])
                sch.dma("sp", bd64[:], bd64_d, writes=[Bmd])
                sch.dma("sp", cm3[:], cm3_d.rearrange("p (c n) -> p c n", n=128), writes=[Bmd])
                if l == 0:
                    cast_weights(1, ["in", "out", "gate", "up", "down"])
                if l == 0:
                    sch.op("pool", G.memset, lbv[:, 0:4], 0.0, reads=[Bl], writes=[Bl])
                    sch.op("pool", G.memset, lbv[:, 4:8], 1.0, reads=[Bl], writes=[Bl])
                else:
                    ex = sbt(es, "lbex", [128, 8])
                    sch.op("act", A.activation, ex[:], lbt[:], AF.Exp, reads=[Bl], writes=[Bl])
                    sch.op("dve", V.tensor_tensor, lbv[:, 4:8], ex[:, 0:4], ex[:, 4:8], ALU.add, reads=[Bl], writes=[Bl])
                    sch.op("dve", V.reciprocal, lbv[:, 4:8], lbv[:, 4:8], reads=[Bl], writes=[Bl])
                    sch.op("dve", V.tensor_tensor, lbv[:, 0:4], ex[:, 4:8], lbv[:, 4:8], ALU.mult, reads=[Bl], writes=[Bl])
                    sch.op("dve", V.tensor_scalar, lbv[:, 4:8], lbv[:, 0:4], -1.0, 1.0, ALU.mult, ALU.add, reads=[Bl], writes=[Bl])
                for hp in range(2):
                    with ExitStack() as hs:
                        vtm = sbt(hs, "hv", [128, NB, 128], BF16)
                        vm = sbt(hs, "hvm", [128, 2, NB, 128], BF16)
                        Bv, Bvm = Buf(), Buf()
                        sch.dma("sp", vtm[:], tm[:, 256 + hp * 128:256 + (hp + 1) * 128].rearrange("(k p) c -> p k c", p=128), writes=[Bv])
                        sch.op("pool", G.memset, vm[:], 0.0, writes=[Bvm])
                        sch.op("pool", G.tensor_copy, vm[:, 0, :, 0:64], vtm[:, :, 0:64], reads=[Bv, Bvm], writes=[Bvm])
                        sch.op("pool", G.tensor_copy, vm[:, 1, :, 64:128], vtm[:, :, 64:128], reads=[Bv, Bvm], writes=[Bvm])
                        chains = []
                        for d in range(2):
                            ch = dict(d=d)
                            ch["qdm"] = sbt(hs, f"qdm{d}", [128, 2, T], BF16)
                            ch["kd"] = sbt(hs, f"kd{d}", [128, T], BF16)
                            ch["kst"] = sbt(hs, f"kst{d}", [128, NB, 128], BF16)
                            ch["dec"] = sbt(hs, f"dec{d}", [128, NCH])
                            ch["B"] = {k: Buf() for k in ("qdm", "kd", "kst", "dec", "o")}
                            ch["Bz"] = Buf()
                            sch.op("pool", G.memset, ch["qdm"][:], 0.0, writes=[ch["Bz"]])
                            chains.append(ch)
                        with ExitStack() as gs:
                            Asets = [([sbt(gs, f"gA{u}{i}", [128, HW]) for i in range(4)], [Buf() for _ in range(4)],
                                      sbt(gs, f"gq{u}", [128, HW], BF16), Buf(), sbt(gs, f"gks{u}", [128, HW], BF16), Buf(),
                                      sbt(gs, f"gtot{u}", [128, HW // 32]), Buf()) for u in range(4)]
                            unit = 0
                            msk = sbt(gs, "gmsk", [128, HW])
                            Bmsk = Buf()
                            sch.op("pool", G.memset, msk[:], 1.0, writes=[Bmsk])
                            sch.op("pool", G.memset, msk[:].rearrange("p (n c) -> p n c", c=32)[:, :, 0:1], 0.0, reads=[Bmsk], writes=[Bmsk])
                            ptk = Rot([pst(gs, f"ptk{i}", [128, 4, 128], BF16) for i in range(2)])
                            ei_ = [0]

                            def unit_gen(ch, h, uset):
                                d = ch["d"]
                                B = ch["B"]
                                lbc = lbv[:, d * 2 + hp:d * 2 + hp + 1]
                                omc = lbv[:, 4 + d * 2 + hp:4 + d * 2 + hp + 1]
                                Ar_, BA, qh_, Bq, ksh_, Bks, tot_, Btot = uset
                                b0, nbh = PIECES[h]
                                W_ = nbh * 128
                                c0 = b0 * 128
                                cs = slice(c0, c0 + W_)
                                A1, A2, A3, A4 = [a_[:, 0:W_] for a_ in Ar_]
                                qh = qh_[:, 0:W_]
                                ksh = ksh_[:, 0:W_]
                                tot = tot_[:, 0:W_ // 32]
                                mskv = msk[:, 0:W_]
                                B1, B2, B3, B4 = BA
                                zr = 256 * d + hp * 128
                                sch.dma("sp", A1, fz[zr:zr + 128, cs], writes=[B1])
                                sch.dma("sp", qh, fb[hp * 128:(hp + 1) * 128, cs], writes=[Bq])
                                sch.op("act", A.activation, A1, A1, AF.Sigmoid, reads=[B1], writes=[B1])
                                yield
                                sch.op("act", A.activation, A1, A1, AF.Identity, bias=lbc, scale=omc, reads=[B1, Bl], writes=[B1])
                                yield
                                sch.op("act", A.activation, A2, A1, AF.Ln, reads=[B1], writes=[B2])
                                yield
                                sch.op("act", A.activation, A1, A1, AF.Identity, bias=1.0, scale=-1.0, reads=[B1, B2], writes=[B1])
                                sch.op("dve", V.tensor_tensor_scan, A3, mskv, A2, 0.0, ALU.mult, ALU.add, reads=[Bmsk, B2], writes=[B3])
                                yield
                                A3v = A3.rearrange("p (n c) -> p n c", c=32)
                                A2v = A2.rearrange("p (n c) -> p n c", c=32)
                                totb = tot.rearrange("p (n o) -> p n o", o=1).broadcast_to([128, W_ // 32, 32])
                                sch.op("act", A.copy, tot, A3v[:, :, 31], reads=[B3], writes=[Btot])
                                yield
                                if d == 0:
                                    sch.op("dve", V.tensor_tensor, A2v, A3v, totb, ALU.subtract, reads=[Btot, B3, B2], writes=[B2])
                                    asg, rsg = 1.0, -1.0
                                else:
                                    sch.op("dve", V.tensor_tensor, A2, A3, A2, ALU.subtract, reads=[B3, B2], writes=[B2])
                                    yield
                                    sch.op("dve", V.tensor_tensor, A3v, A2v, totb, ALU.subtract, reads=[Btot, B2, B3], writes=[B3])
                                    asg, rsg = -1.0, 1.0
                                yield
                                sch.op("act", A.activation, ch["dec"][:, c0 // 32:(c0 + W_) // 32], tot, AF.Exp, reads=[Btot], writes=[B["dec"]])
                                sch.op("act", A.activation, A4, A3, AF.Exp, scale=asg, reads=[B3, B4], writes=[B4])
                                yield
                                sch.op("dve", V.tensor_tensor, ch["qdm"][0:64, 0, cs], qh[0:64], A4[0:64], ALU.mult,
                                       reads=[Bq, B4, ch["Bz"]], writes=[B["qdm"]])
                                sch.op("dve", V.tensor_tensor, ch["qdm"][64:128, 1, cs], qh[64:128], A4[64:128], ALU.mult,
                                       reads=[Bq, B4, ch["Bz"]], writes=[B["qdm"]])
                                yield
                                sch.op("act", A.activation, A4, A3, AF.Exp, scale=-asg, reads=[B3, B4], writes=[B4])
                                yield
                                sch.op("dve", V.tensor_tensor, ch["kd"][:, cs], A1, A4, ALU.mult, reads=[B1, B4], writes=[B["kd"]])
                                yield
                                sch.op("act", A.activation, A4, A2, AF.Exp, scale=rsg, reads=[B2, B4], writes=[B4])
                                yield
                                sch.op("dve", V.tensor_tensor, ksh, A1, A4, ALU.mult, reads=[B1, B4], writes=[Bks])
                                yield
                                for b4 in range(0, nbh, 4):
                                    nn = min(4, nbh - b4)
                                    pt, Bp = ptk.next()
                                    for j in range(nn):
                                        sch.op("pe", P.transpose, pt[:, j, :], ksh[:, (b4 + j) * 128:(b4 + j + 1) * 128], ident[:],
                                               reads=[Bks, B_const], writes=[Bp])
                                    ei_[0] += 1
                                    evac(ei_[0], ch["kst"][:, b0 + b4:b0 + b4 + nn, :], pt[:, 0:nn, :], [Bp], [B["kst"]])
                                    yield

                            units = [(ch, h) for ch in chains for h in range(4)]
                            gens = []
                            SHIFT = 4
                            active = []
                            ui = 0
                            tick = 0
                            while ui < len(units) or active:
                                if ui < len(units) and len(active) < 4 and (not active or active[-1][1] >= SHIFT):
                                    ch_, h_ = units[ui]
                                    active.append([unit_gen(ch_, h_, Asets[ui % 4]), 0])
                                    ui += 1
                                for a_ in list(active):
                                    try:
                                        next(a_[0])
                                        a_[1] += 1
                                    except StopIteration:
                                        active.remove(a_)
                            sch.barrier()
                        for ch in chains:
                            ch["o"] = sbt(hs, f"o{ch['d']}", [128, T])
                        with ExitStack() as cs_:
                            vexa = sbt(cs_, "vexa", [128, NB, 4, 128], BF16)
                            Bvex = Buf()
                            for nb in range(NB):
                                sch.op("dve", V.tensor_tensor, vexa[:, nb], cm3[:], vtm[:, nb:nb + 1, :].broadcast_to([128, 4, 128]), ALU.mult,
                                       reads=[Bmd, Bv], writes=[Bvex])
                            for ch in chains:
                                d = ch["d"]
                                ch["psU"] = pst(cs_, f"psU{d}", [128, 4, 128]); ch["BpsU"] = Buf()
                                ch["psS"] = [pst(cs_, f"psS{d}{i}", [128, 128]) for i in range(2)]; ch["BpsS"] = [Buf(), Buf()]
                                ch["psO"] = pst(cs_, f"psO{d}", [128, 128]); ch["BpsO"] = Buf()
                                ch["Sf"] = Rot([sbt(cs_, f"Sf{d}{i}", [128, 128]) for i in range(2)])
                                ch["Spad"] = Rot([sbt(cs_, f"Sp{d}{i}", [128, 2, 128], BF16) for i in range(10)])
                                for (tl, Bt_) in ch["Spad"].t:
                                    sch.op("pool", G.memset, tl[:], 0.0, writes=[Bt_])
                                ch["scb"] = Rot([sbt(cs_, f"scb{d}{i}", [128, 2, 128], BF16) for i in range(2)])
                                ch["order"] = ([32, 33] + list(range(32))) if d == 0 else ([33, 32] + list(range(31, -1, -1)))
                                ch["corder"] = [0, 1, 2, 3] if d == 0 else [3, 2, 1, 0]
                                ch["prev"] = None
                                ch["prev_bd"] = None
                                ch["pending"] = None

                            def front_all(step):
                                res = []
                                for ch in chains:
                                    nb = ch["order"][step]
                                    B = ch["B"]
                                    sch.op("pe", P.matmul, ch["psU"][:].rearrange("p c n -> p (c n)"), lhsT=ch["kst"][:, nb, :],
                                           rhs=vexa[:, nb].rearrange("p c n -> p (c n)"), start=True, stop=True,
                                           reads=[B["kst"], Bvex], writes=[ch["BpsU"]])
                                    ch["before"] = []
                                for ci in range(4):
                                    for k_, ch in enumerate(chains):
                                        nb = ch["order"][step]
                                        B = ch["B"]
                                        c = ch["corder"][ci]
                                        n = nb * 4 + c
                                        ch["before"].append((c, ch["prev_bd"]))
                                        sf, Bsf = ch["Sf"].next()
                                        if ch["prev"] is None:
                                            sch.op("dve", V.tensor_copy, sf[:], ch["psU"][:, c, :], reads=[ch["BpsU"]], writes=[Bsf])
                                        else:
                                            pv, Bpv = ch["prev"]
                                            sch.op("dve", V.scalar_tensor_tensor, sf[:], pv[:], ch["dec"][:, n:n + 1], ch["psU"][:, c, :],
                                                   ALU.mult, ALU.add, reads=[Bpv, B["dec"], ch["BpsU"]], writes=[Bsf])
                                        ch["prev"] = (sf, Bsf)
                                        sp_, Bsp = ch["Spad"].next()
                                        base = sp_[:]
                                        ov = bass.AP(tensor=base.tensor, offset=base.offset, ap=[[base.ap[0][0], 128], [192, 2], [1, 64]])
                                        iv = sf[:].rearrange("p (g c) -> p g c", g=2)
                                        sch.op("act", A.copy, ov, iv, reads=[Bsf, Bsp], writes=[Bsp])
                                        ch["prev_bd"] = (sp_, Bsp)
                                for ch in chains:
                                    nb = ch["order"][step]
                                    B = ch["B"]
                                    cols = slice(nb * 128, (nb + 1) * 128)
                                    scb, Bscb = ch["scb"].next()
                                    for hh in range(2):
                                        sch.op("pe", P.matmul, ch["psS"][hh][:], lhsT=ch["kd"][:, cols], rhs=ch["qdm"][:, hh, cols],
                                               start=True, stop=True, reads=[B["kd"], B["qdm"]], writes=[ch["BpsS"][hh]])
                                        sch.op("dve", V.tensor_tensor, scb[:, hh, :], ch["psS"][hh][:], mdir[:, ch["d"], :], ALU.mult,
                                               reads=[ch["BpsS"][hh], Bmd], writes=[Bscb])
                                    res.append((nb, ch["before"], scb, Bscb))
                                return res

                            def back(ch, pend):
                                nb, before, scb, Bscb = pend
                                B = ch["B"]
                                cols = slice(nb * 128, (nb + 1) * 128)
                                inter = [(c, s_) for (c, s_) in before if s_ is not None]
                                for hh in range(2):
                                    sch.op("pe", P.matmul, ch["psO"][:], lhsT=vm[:, hh, nb, :], rhs=scb[:, hh, :], start=(hh == 0),
                                           stop=(hh == 1 and not inter), reads=[Bvm, Bscb], writes=[ch["BpsO"]])
                                for i, (c, (sp_, Bsp)) in enumerate(inter):
                                    for hh in range(2):
                                        sch.op("pe", P.matmul, ch["psO"][:, 32 * c:32 * c + 32], lhsT=sp_[:, hh, :],
                                               rhs=ch["qdm"][:, hh, nb * 128 + 32 * c:nb * 128 + 32 * c + 32], start=False,
                                               stop=(i == len(inter) - 1 and hh == 1), reads=[Bsp, B["qdm"]], writes=[ch["BpsO"]])
                                sch.op("act", A.copy, ch["o"][:, cols], ch["psO"][:], reads=[ch["BpsO"]], writes=[B["o"]])

                            for step in range(NB + 1):
                                if l == 0 and step == 1:
                                    pump(35, make_bg(cs_, 4), drain=True)
                                newp = None
                                if step < NB:
                                    newp = front_all(step)
                                if step >= 1:
                                    for ch in chains:
                                        back(ch, ch["pending"])
                                if step < NB:
                                    for ch, p_ in zip(chains, newp):
                                        ch["pending"] = p_
                            sch.barrier()
                        with ExitStack() as rs:
                            of, ob = chains[0]["o"], chains[1]["o"]
                            Bof, Bob = chains[0]["B"]["o"], chains[1]["B"]["o"]
                            sch.op("dve", V.tensor_tensor, of[:], of[:], ob[:], ALU.add, reads=[Bof, Bob], writes=[Bof])
                            gt = sbt(rs, "hg", [128, T], BF16)
                            Bg = Buf()
                            sch.dma("sp", gt[:], fb[256 + hp * 128:256 + (hp + 1) * 128, :], writes=[Bg])
                            sqr = Rot([sbt(rs, f"hsq{i}", [128, 512]) for i in range(2)])
                            sgr = Rot([sbt(rs, f"hsg{i}", [128, 512]) for i in range(2)])
                            rr = Rot([sbt(rs, f"hr{i}", [128, 512]) for i in range(2)])
                            orr = Rot([sbt(rs, f"hob{i}", [128, 512], BF16) for i in range(2)])
                            pms = Rot([pst(rs, f"pms{i}", [128, 512]) for i in range(2)])
                            for t0 in range(0, T, 512):
                                n = min(512, T - t0)
                                cs = slice(t0, t0 + n)
                                sq, Bsq = sqr.next()
                                sch.op("act", A.activation, sq[:, 0:n], of[:, cs], AF.Square, reads=[Bof], writes=[Bsq])
                                pm_, Bpm = pms.next()
                                sch.op("pe", P.matmul, pm_[:, 0:n], lhsT=bd64[:], rhs=sq[:, 0:n], start=True, stop=True, pemode="f32",
                                       reads=[Bmd, Bsq], writes=[Bpm])
                                r_, Br = rr.next()
                                sch.op("act", A.activation, r_[:, 0:n], pm_[:, 0:n], AF.Ln, bias=epsc[:, 0:1], scale=1.0, reads=[Bpm, Bmd], writes=[Br])
                                sch.op("act", A.activation, r_[:, 0:n], r_[:, 0:n], AF.Exp, scale=-0.5, reads=[Br], writes=[Br])
                                sch.op("dve", V.tensor_tensor, r_[:, 0:n], r_[:, 0:n], of[:, cs], ALU.mult, reads=[Br, Bof], writes=[Br])
                                sg, Bsg = sgr.next()
                                sch.op("act", A.activation, sg[:, 0:n], gt[:, cs], AF.Silu, reads=[Bg], writes=[Bsg])
                                ob_, Bo = orr.next()
                                sch.op("dve", V.scalar_tensor_tensor, ob_[:, 0:n], r_[:, 0:n], ngt[:, l:l + 1], sg[:, 0:n], ALU.mult, ALU.mult,
                                       reads=[Br, Bl, Bsg], writes=[Bo])
                                sch.dma("pool", catT[256 + hp * 128:256 + (hp + 1) * 128, cs], ob_[:, 0:n], reads=[Bo])
                            sch.barrier()

        def phase_e(l):
            with ExitStack() as es:
                sets = []
                for u in range(2):
                    qm = sbt(es, f"naq{u}", [128, 2, T], BF16)
                    kT = sbt(es, f"nak{u}", [128, T], BF16)
                    vm = sbt(es, f"nav{u}", [128, 2, NB, 128], BF16)
                    Bq, Bk, Bv = Buf(), Buf(), Buf()
                    sch.op("pool", G.memset, qm[:], 0.0, writes=[Bq])
                    sch.op("pool", G.memset, vm[:], 0.0, writes=[Bv])
                    sets.append((qm, kT, vm, Bq, Bk, Bv))
                bg_rng = ((8, 14), (0, 8), (14, 20))
                bias = [[sbt(es, f"nab{hh}{g}", [128, bg_rng[g][1] - bg_rng[g][0], 512]) for g in range(3)] for hh in range(2)]
                Bb = [[Buf() for g in range(3)] for hh in range(2)]
                oT = sbt(es, "nao", [128, T], BF16)
                oneh = sbt(es, "oneh", [128, 2, 128], BF16)
                sbr = Rot([sbt(es, f"nas{i}", [128, 512]) for i in range(4)])
                pr = Rot([sbt(es, f"nap{i}", [128, 512], BF16) for i in range(6)])
                rcr = Rot([sbt(es, f"narc{i}", [128, 512]) for i in range(2)])
                psr = Rot([pst(es, f"naS{i}", [128, 512]) for i in range(4)])
                numr = Rot([pst(es, f"naN{i}", [128, 512]) for i in range(2)])
                denr = Rot([pst(es, f"naD{i}", [128, 512]) for i in range(2)])
                Bo, B1 = Buf(), Buf()
                sch.op("pool", G.memset, oneh[:], 0.0, writes=[B1])
                for hh in range(2):
                    sch.op("pool", G.memset, oneh[:, hh, hh * 64:(hh + 1) * 64], 1.0, reads=[B1], writes=[B1])
                LOOK = 3
                ebgs = make_bg(es, 4)
                if l == 0:
                    pump(40, ebgs, drain=True)

                def load_qkv(hp):
                    qm, kT, vm, Bq, Bk, Bv = sets[hp % 2]
                    for hh in range(2):
                        r0 = 512 + hp * 128 + hh * 64
                        sch.dma("sp", qm[hh * 64:(hh + 1) * 64, hh, :], fb[r0:r0 + 64, :], reads=[Bq], writes=[Bq])
                    sch.dma("sp", kT[:], fb[1024 + hp * 128:1024 + (hp + 1) * 128, :], writes=[Bk])
                    for hh in range(2):
                        c0 = 512 + hp * 128 + hh * 64
                        sch.dma("sp", vm[:, hh, :, hh * 64:(hh + 1) * 64], tm[:, c0:c0 + 64].rearrange("(k p) c -> p k c", p=128),
                                reads=[Bv], writes=[Bv])

                def load_bias(hp, g):
                    t0_, t1_ = bg_rng[g]
                    for hh in range(2):
                        sch.dma("sp", bias[hh][g][:], nabias[l, hp * 2 + hh].rearrange("p (t n) -> p t n", n=512)[:, t0_:t1_, :],
                                writes=[Bb[hh][g]])

                load_qkv(0)
                for g in range(3):
                    load_bias(0, g)
                for hp in range(4):
                    qm, kT, vm, Bq, Bk, Bv = sets[hp % 2]
                    if hp + 1 < 4:
                        load_qkv(hp + 1)
                    items = []
                    for qb in range(8):
                        if qb == 0:
                            kts = [(2 * p * 64, (0, p)) for p in range(6)]
                        elif qb == 7:
                            kts = [((52 + 2 * p) * 64, (2, p)) for p in range(6)]
                        else:
                            kts = [((8 * qb - 4 + 2 * po) * 64, (1, po)) for po in range(8)]
                        kts += [(S, None), (S + 128, None)]
                        items.append((qb * 512, 512, kts, qb))
                    if l == 0:
                        items.append((S, L, [(S, None), (S + 128, None)], 8))
                    for (q0, nq, kts, qb) in items:
                        flat = [(hh, kt0, bi) for hh in range(2) for (kt0, bi) in kts]
                        prod = [None] * len(flat)
                        num, Bn = numr.next()
                        den, Bd = denr.next()
                        for i in range(len(flat) + LOOK):
                            if i < len(flat):
                                hh, kt0, bi = flat[i]
                                ps, Bps = psr.next()
                                sch.op("pe", P.matmul, ps[:, 0:nq], lhsT=kT[:, kt0:kt0 + 128], rhs=qm[:, hh, q0:q0 + nq], start=True, stop=True,
                                       reads=[Bk, Bq], writes=[Bps])
                                pt, Bpt = pr.next()
                                if bi is None:
                                    sch.op("act", A.activation, pt[:, 0:nq], ps[:, 0:nq], AF.Exp, reads=[Bps], writes=[Bpt])
                                else:
                                    st, Bst = sbr.next()
                                    sch.op("dve", V.tensor_tensor, st[:, 0:nq], ps[:, 0:nq], bias[hh][bi[0]][:, bi[1], 0:nq], ALU.add,
                                           reads=[Bps, Bb[hh][bi[0]]], writes=[Bst])
                                    sch.op("act", A.activation, pt[:, 0:nq], st[:, 0:nq], AF.Exp, reads=[Bst], writes=[Bpt])
                                prod[i] = (pt, Bpt)
                            j = i - LOOK
                            if j >= 0:
                                hh, kt0, bi = flat[j]
                                pt, Bpt = prod[j]
                                first = (j == 0)
                                last = (j == len(flat) - 1)
                                sch.op("pe", P.matmul, num[:, 0:nq], lhsT=vm[:, hh, kt0 // 128, :], rhs=pt[:, 0:nq],
                                       start=first, stop=last, reads=[Bv, Bpt], writes=[Bn])
                                sch.op("pe", P.matmul, den[:, 0:nq], lhsT=oneh[:, hh, :], rhs=pt[:, 0:nq],
                                       start=first, stop=last, reads=[B1, Bpt], writes=[Bd])
                        rc, Brc = rcr.next()
                        sch.op("dve", V.reciprocal, rc[:, 0:nq], den[:, 0:nq], reads=[Bd], writes=[Brc])
                        sch.op("dve", V.tensor_tensor, oT[:, q0:q0 + nq], num[:, 0:nq], rc[:, 0:nq], ALU.mult, reads=[Bn, Brc], writes=[Bo])
                        if hp + 1 < 4:
                            if qb == 0:
                                load_bias(hp + 1, 0)
                            elif qb == 6:
                                load_bias(hp + 1, 1)
                            elif qb == 7:
                                load_bias(hp + 1, 2)
                    ncol = T if l == 0 else S
                    sch.dma("pool", catT[512 + hp * 128:512 + (hp + 1) * 128, 0:ncol], oT[:, 0:ncol], reads=[Bo])
                    if l == 0 and hp < 3:
                        pump(40, ebgs, drain=True)
                sch.barrier()

        def phase_f1(l):
            xsrc = xin if l == 0 else xres
            with ExitStack() as es:
                wob, Bw = load_weight(es, "woutbf", w_out_b[l].rearrange("(k p) n -> p k n", p=128), 8, D, wcast_buf[("out", l)])
                g1 = [sbt(es, f"g1_{v}", [128, D]) for v in range(2)]
                Bg = Buf()
                for v in range(2):
                    sch.dma("pool", g1[v][:], mod2d[2 * l + v, 2 * D:3 * D].partition_broadcast(128), writes=[Bg])
                fbgs = make_bg(es, 4)
                cr = Rot([sbt(es, f"cat{i}", [128, 8, 512], BF16) for i in range(2)])
                xr = Rot([sbt(es, f"xt{i}", [128, D]) for i in range(3)])
                yr = Rot([sbt(es, f"yt{i}", [128, D]) for i in range(2)])
                pyr = Rot([pst(es, f"py{i}", [128, 512]) for i in range(4)])
                sbs = [(i * 512, 512, 0) for i in range(8)] + ([(S, L, 1)] if l == 0 else [])
                for (t0, ntok, v) in sbs:
                    ct, Bc = cr.next()
                    sch.dma("sp", ct[:, :, 0:ntok], catT[:, t0:t0 + ntok].rearrange("(k p) t -> p k t", p=128), writes=[Bc])
                    for tb in range(ntok // 128):
                        xt, Bx = xr.next()
                        rows = slice(t0 + tb * 128, t0 + (tb + 1) * 128)
                        sch.dma("sp", xt[:], xsrc[rows, :], writes=[Bx])
                        yt, By = yr.next()
                        for nb in range(2):
                            py, Bp = pyr.next()
                            for k in range(8):
                                sch.op("pe", P.matmul, py[:], lhsT=ct[:, k, tb * 128:(tb + 1) * 128], rhs=wob[:, k, nb * 512:(nb + 1) * 512],
                                       start=(k == 0), stop=(k == 7), reads=[Bc, Bw], writes=[Bp])
                            sch.op("dve", V.tensor_tensor, yt[:, nb * 512:(nb + 1) * 512], py[:], g1[v][:, nb * 512:(nb + 1) * 512], ALU.mult,
                                   reads=[Bp, Bg], writes=[By])
                        sch.op("dve", V.tensor_tensor, yt[:], yt[:], xt[:], ALU.add, reads=[By, Bx], writes=[By])
                        sch.dma("pool", xres[rows, :], yt[:], reads=[By])
                        if l == 0:
                            pump(2, fbgs)
                pump(0, fbgs, drain=True)
                sch.barrier()

        def phase_f2(l):
            last = (l == DEPTH - 1)
            with ExitStack() as es:
                wg, Bwg = load_weight(es, "wgbf", w_gate_b[l].rearrange("(k p) n -> p k n", p=128), 8, DFF, wcast_buf[("gate", l)])
                wu, Bwu = load_weight(es, "wubf", w_up_b[l].rearrange("(k p) n -> p k n", p=128), 8, DFF, wcast_buf[("up", l)])
                wd, Bwd = load_weight(es, "wdbf", w_down_b[l].rearrange("(j p) n -> p j n", p=128), NJ, D, wcast_buf[("down", l)])
                junk = sbt(es, "junk", [128, D], BF16)
                nt = (junk, Buf(), Rot([sbt(es, f"ssq{i}", [128, 4]) for i in range(4)]),
                      Rot([sbt(es, f"ntmp{i}", [128, D]) for i in range(1)]))
                Gt = sbt(es, "G2", [128, D])
                SHt = sbt(es, "SH2", [128, D])
                g2 = sbt(es, "g2", [128, D])
                Bm = Buf()

                def load_mods(v):
                    sct, Bsc = nt[3].next()
                    sch.dma("pool", SHt[:], mod2d[2 * l + v, 3 * D:4 * D].partition_broadcast(128), writes=[Bm])
                    sch.dma("pool", Gt[:], norm2_g[l].partition_broadcast(128), writes=[Bm])
                    sch.dma("pool", g2[:], mod2d[2 * l + v, 5 * D:6 * D].partition_broadcast(128), writes=[Bm])
                    sch.dma("pool", sct[:], mod2d[2 * l + v, 4 * D:5 * D].partition_broadcast(128), writes=[Bsc])
                    sch.op("dve", V.scalar_tensor_tensor, Gt[:], sct[:], 1.0, Gt[:], ALU.add, ALU.mult, reads=[Bsc, Bm], writes=[Bm])
                fg = None
                if last:
                    fg = sbt(es, "fg", [128, D])
                    Bfg = Buf()
                    sch.dma("pool", fg[:], final_g.partition_broadcast(128), writes=[Bfg])
                x1 = [(sbt(es, f"x1_{i}", [128, D]), Buf()) for i in range(4)]
                ar = Rot([sbt(es, f"abf{i}", [128, D], BF16) for i in range(1)])
                aT = sbt(es, "aT", [128, 8, 512], BF16)
                BaT = Buf()
                hT = sbt(es, "hT", [128, NJ, 512], BF16)
                BhT = Buf()
                sgr = Rot([sbt(es, f"sg{i}", [128, 512]) for i in range(1)])
                ptrr = Rot([pst(es, f"ptr{i}", [128, 4, 128], BF16) for i in range(1)])
                pgr = Rot([pst(es, f"pg{i}", [128, 512]) for i in range(2)])
                pur = Rot([pst(es, f"pu{i}", [128, 512]) for i in range(2)])
                pdr = Rot([pst(es, f"pd{i}", [128, 512]) for i in range(3)])
                sbs = [(i * 512, 512, 0) for i in range(8)] + ([(S, L, 1)] if l == 0 else [])
                curv = None
                for (t0, ntok, v) in sbs:
                    if v != curv:
                        load_mods(v)
                        curv = v
                    ntb = ntok // 128
                    for tb in range(ntb):
                        xt, Bx = x1[tb]
                        sch.dma("sp", xt[:], xres[t0 + tb * 128:t0 + (tb + 1) * 128, :], writes=[Bx])
                        a_bf, Ba = ar.next()
                        norm_block(nt, xt, Bx, Gt, SHt, Bm, a_bf, Ba)
                        transpose_block(ptrr, a_bf, Ba, aT, BaT, tb, tb)
                    for j in range(NJ):
                        pg, Bpg = pgr.next()
                        pu, Bpu = pur.next()
                        for k in range(8):
                            sch.op("pe", P.matmul, pg[:, 0:ntok], lhsT=wg[:, k, j * 128:(j + 1) * 128], rhs=aT[:, k, 0:ntok],
                                   start=(k == 0), stop=(k == 7), reads=[Bwg, BaT], writes=[Bpg])
                        for k in range(8):
                            sch.op("pe", P.matmul, pu[:, 0:ntok], lhsT=wu[:, k, j * 128:(j + 1) * 128], rhs=aT[:, k, 0:ntok],
                                   start=(k == 0), stop=(k == 7), reads=[Bwu, BaT], writes=[Bpu])
                        sg, Bsg = sgr.next()
                        sch.op("act", A.activation, sg[:, 0:ntok], pg[:, 0:ntok], AF.Silu, reads=[Bpg], writes=[Bsg])
                        sch.op("dve", V.tensor_tensor, hT[:, j, 0:ntok], sg[:, 0:ntok], pu[:, 0:ntok], ALU.mult, reads=[Bsg, Bpu], writes=[BhT])
                    for tb in range(ntb):
                        xt, Bx = x1[tb]
                        rows = slice(t0 + tb * 128, t0 + (tb + 1) * 128)
                        tp, Bt = nt[3].next()
                        for nb in range(2):
                            pd, Bpd = pdr.next()
                            for j in range(NJ):
                                sch.op("pe", P.matmul, pd[:], lhsT=hT[:, j, tb * 128:(tb + 1) * 128], rhs=wd[:, j, nb * 512:(nb + 1) * 512],
                                       start=(j == 0), stop=(j == NJ - 1), reads=[BhT, Bwd], writes=[Bpd])
                            sch.op("dve", V.tensor_tensor, tp[:, nb * 512:(nb + 1) * 512], pd[:], g2[:, nb * 512:(nb + 1) * 512], ALU.mult,
                                   reads=[Bpd, Bm], writes=[Bt])
                        sch.op("dve", V.tensor_tensor, xt[:], xt[:], tp[:], ALU.add, reads=[Bx, Bt], writes=[Bx])
                        if not last:
                            sch.dma("pool", xres[rows, :], xt[:], reads=[Bx])
                        else:
                            sq, Bs = nt[2].next()
                            sch.op("act", A.activation, junk[:], xt[:], AF.Square, accum_out=sq[:, 0:1], reads=[Bx], writes=[nt[1], Bs])
                            sch.op("act", A.activation, sq[:, 1:2], sq[:, 0:1], AF.Sqrt, bias=EPS, scale=1.0 / D, reads=[Bs], writes=[Bs])
                            sch.op("dve", V.reciprocal, sq[:, 2:3], sq[:, 1:2], reads=[Bs], writes=[Bs])
                            sch.op("dve", V.scalar_tensor_tensor, xt[:], xt[:], sq[:, 2:3], fg[:], ALU.mult, ALU.mult,
                                   reads=[Bx, Bs, Bfg], writes=[Bx])
                            sch.dma("pool", out[rows, :], xt[:], reads=[Bx])
                sch.barrier()

        cast_weights(0, ["in"])
        phases = [("mod", phase_mod, None)]
        for l in range(DEPTH):
            phases += [(f"ab{l}", phase_ab, l), (f"c{l}", phase_c, l), (f"d{l}", phase_d, l), (f"e{l}", phase_e, l),
                       (f"f1{l}", phase_f1, l), (f"f2{l}", phase_f2, l)]
        for name, fn, arg in phases:
            if arg is None:
                fn()
            else:
                fn(arg)
            if stop is not None and name == stop:
                break
        sch.finish()
        nc._sched_stats = (sch.nops, sch.nwaits)
    return nc


def _prep(inputs):
    f32 = np.float32
    c = _consts()
    x = np.asarray(inputs["x"], f32)
    ctx = np.asarray(inputs["ctx"], f32)
    cvec = np.asarray(inputs["c"], f32)
    cctx = np.asarray(inputs["c_ctx"], f32)
    lb = np.asarray(inputs["hgrn_lb"], f32)
    lbT = np.ascontiguousarray(lb.reshape(2, 2, 2, 128).transpose(3, 0, 1, 2).reshape(128, 8))
    ng = np.asarray(inputs["hgrn_norm_g"], f32)
    ngT = np.ascontiguousarray(np.concatenate([ng, ng], axis=1).T)
    rpb = np.asarray(inputs["na_rpb"], f32)
    nabias = np.stack([_na_bias_tiles(rpb[l]) for l in range(DEPTH)])
    shared = {
        "w_mod": np.asarray(inputs["w_mod"], f32), "b_mod": np.asarray(inputs["b_mod"], f32),
        "norm1_g": np.asarray(inputs["norm1_g"], f32), "w_in": np.asarray(inputs["w_in"], f32),
        "fourier_w": np.asarray(inputs["fourier_w"], f32), "lbT": lbT, "ngT": ngT, "nabias": nabias,
        "w_out": np.asarray(inputs["w_out"], f32), "norm2_g": np.asarray(inputs["norm2_g"], f32),
        "w_gate": np.asarray(inputs["w_ffn_gate"], f32), "w_up": np.asarray(inputs["w_ffn_up"], f32),
        "w_down": np.asarray(inputs["w_ffn_down"], f32), "final_g": np.asarray(inputs["final_norm_g"], f32),
        "C4": c["C4"], "S4": c["S4"], "C2": c["C2"], "S2": c["S2"], "C64bd": c["C64bd"], "S64bdn": c["S64bdn"],
        "ident": c["ident"], "bdmask": c["bdmask"], "bd64": c["bd64"], "mfwd": c["mfwd"], "mbwd": c["mbwd"], "cm3": c["cm3"],
    }
    maps = []
    for b in range(8):
        m = dict(shared)
        m["xin"] = np.ascontiguousarray(np.concatenate([x[b], ctx[b]], axis=0))
        cv = np.stack([cvec[b], cctx], axis=0)
        m["cT"] = np.ascontiguousarray(cv.reshape(2, 8, 128).transpose(2, 1, 0).reshape(128, 16))
        maps.append(m)
    return maps


def kernel(**inputs):
    maps = _prep(inputs)
    nc = build()
    res = run_bass_kernel_spmd(nc, maps, core_ids=list(range(8)))
    return np.stack([np.asarray(r["out"], np.float32) for r in res.results], axis=0)
```

```python
import numpy as np
import ml_dtypes
from contextlib import ExitStack
import concourse.bass as bass
import concourse.mybir as mybir
from concourse.bass_utils import run_bass_kernel_spmd

F32 = mybir.dt.float32
BF16 = mybir.dt.bfloat16
AF = mybir.ActivationFunctionType
ALU = mybir.AluOpType

D = 1024
S = 4096
L = 256
T = S + L
NB = T // 128
DFF = 2816
NJ = DFF // 128
INW = 3072
EPS = 1e-6
NEG = -1e30
DEPTH = 2


class _Op:
    __slots__ = ("eng", "fn", "args", "kw", "deps", "signal", "sem", "val", "dma", "epoch")


class Buf:
    __slots__ = ("w", "r", "g")

    def __init__(self):
        self.w = {}
        self.r = {}
        self.g = ()


class Sched:
    CENG = ("pe", "act", "dve", "pool")
    SEMCAP = 30000

    def __init__(self, nc, es):
        self.nc = nc
        self.eobj = dict(pe=nc.tensor, act=nc.scalar, dve=nc.vector, pool=nc.gpsimd, sp=nc.sync)
        self.pending = []
        self.lastc = {}
        self.dma_since = []
        self.bar_deps = {}
        ncs = dict(pe=3, act=3, dve=4, pool=2)
        self.csem = {e: [es.enter_context(nc.semaphore(f"c{e}{i}")) for i in range(ncs[e])] for e in self.CENG}
        self.ccount = {e: 0 for e in self.CENG}
        self.dsem = {q: [es.enter_context(nc.semaphore(f"d{q}{i}")) for i in range(8)] for q in ("sp", "pool")}
        self.dcount = {q: [0] * 8 for q in ("sp", "pool")}
        self.drr = {q: 0 for q in ("sp", "pool")}
        self.waited = {e: {} for e in self.eobj}
        self.nops = 0
        self.nwaits = 0
        self.epoch = 0
        self.pe_mode = "f"
        self.bg_ops = []

    def op(self, eng, fn, *args, reads=(), writes=(), dma=False, force=False, bg=False, **kw):
        o = _Op()
        o.eng, o.fn, o.args, o.kw, o.dma = eng, fn, args, kw, dma
        o.signal = dma or force
        o.sem = None
        o.val = 0
        o.epoch = self.epoch
        deps = {}
        if eng == "pe":
            mode = kw.pop("pemode", "f")
            if mode != self.pe_mode and "pe" in self.lastc:
                deps[id(self.lastc["pe"])] = self.lastc["pe"]
            self.pe_mode = mode

        def need(d, same_ok, bar=False):
            if d.epoch < self.epoch and not bar:
                return
            if same_ok and (not d.dma) and (not dma) and d.eng == eng:
                return
            deps[id(d)] = d

        for b in reads:
            for d in b.w.values():
                need(d, eng == "pe")
        for b in writes:
            if b.r:
                b.g = tuple(b.r.values()) + tuple(b.w.values())
                b.r = {}
                b.w = {}
            for d in b.g:
                need(d, True)
        for d in self.bar_deps.pop(eng, ()):
            need(d, True, True)
        for d in deps.values():
            d.signal = True
        o.deps = list(deps.values())
        key = ("d", id(o)) if dma else eng
        for b in reads:
            b.r[key] = o
        for b in writes:
            b.w[key] = o
        if dma and bg:
            o.epoch = 1 << 60
            self.bg_ops.append(o)
        elif dma:
            self.dma_since.append(o)
        else:
            self.lastc[eng] = o
        self.pending.append(o)
        return o

    def dma(self, q, out, in_, reads=(), writes=(), **kw):
        fn = self.nc.sync.dma_start if q == "sp" else self.nc.gpsimd.dma_start
        return self.op(q, fn, reads=reads, writes=writes, dma=True, out=out, in_=in_, **kw)

    def flush(self):
        for o in self.pending:
            e = self.eobj[o.eng]
            w = self.waited[o.eng]
            for d in o.deps:
                assert d.sem is not None, "dependency on unsignalled op"
                k = id(d.sem)
                if w.get(k, 0) >= d.val:
                    continue
                e.wait_ge(d.sem, d.val)
                w[k] = d.val
                self.nwaits += 1
            ins = o.fn(*o.args, **o.kw)
            self.nops += 1
            if o.dma:
                q = o.eng
                i = self.drr[q]
                self.drr[q] = (i + 1) % 8
                self.dcount[q][i] += 16
                o.sem, o.val = self.dsem[q][i], self.dcount[q][i]
                ins.then_inc(o.sem, 16)
            elif o.signal:
                n = self.ccount[o.eng]
                self.ccount[o.eng] = n + 1
                sems = self.csem[o.eng]
                si = n // self.SEMCAP
                assert si < len(sems), "out of compute semaphores"
                o.sem, o.val = sems[si], n % self.SEMCAP + 1
                ins.then_inc(o.sem, 1)
            o.fn = o.args = o.kw = None
        self.pending = []

    def barrier(self):
        deps = [self.lastc[e] for e in self.CENG if e in self.lastc] + self.dma_since
        for d in deps:
            d.signal = True
        self.flush()
        old = self.bar_deps
        self.bar_deps = {e: list(deps) + list(old.get(e, ())) for e in self.eobj}
        self.dma_since = []
        self.epoch += 1

    def finish(self):
        self.barrier()
        sp = self.nc.sync
        for d in list(self.bar_deps["sp"]) + self.bg_ops:
            k = id(d.sem)
            if self.waited["sp"].get(k, 0) >= d.val:
                continue
            sp.wait_ge(d.sem, d.val)
            self.waited["sp"][k] = d.val


class Rot:
    def __init__(self, tiles):
        self.t = [(t, Buf()) for t in tiles]
        self.i = 0

    def next(self):
        r = self.t[self.i]
        self.i = (self.i + 1) % len(self.t)
        return r


_CONST = {}


def _consts():
    if _CONST:
        return _CONST
    bf = ml_dtypes.bfloat16
    t = np.arange(S, dtype=np.int64)
    m = (t[:, None] * t[None, :]) % S
    ang = 2.0 * np.pi * m.astype(np.float64) / S
    _CONST["C4"] = (np.cos(ang) / 64.0).astype(np.float32).astype(bf)
    _CONST["S4"] = (np.sin(ang) / 64.0).astype(np.float32).astype(bf)
    del ang, m
    t = np.arange(L, dtype=np.int64)
    ang = 2.0 * np.pi * ((t[:, None] * t[None, :]) % L).astype(np.float64) / L
    _CONST["C2"] = (np.cos(ang) / 16.0).astype(np.float32).astype(bf)
    _CONST["S2"] = (np.sin(ang) / 16.0).astype(np.float32).astype(bf)
    t = np.arange(64, dtype=np.int64)
    ang = 2.0 * np.pi * ((t[:, None] * t[None, :]) % 64).astype(np.float64) / 64
    c64 = np.cos(ang) / 8.0
    s64 = np.sin(ang) / 8.0
    z = np.zeros((64, 64))
    _CONST["C64bd"] = np.block([[c64, z], [z, c64]]).astype(np.float32)
    _CONST["S64bdn"] = (-np.block([[s64, z], [z, s64]])).astype(np.float32)
    _CONST["ident"] = np.eye(128, dtype=np.float32).astype(bf)
    bd = np.zeros((128, 128), np.float32)
    bd[:64, :64] = 1.0
    bd[64:, 64:] = 1.0
    _CONST["bdmask"] = bd
    _CONST["bd64"] = (bd / 64.0).astype(np.float32)
    s_ = np.arange(128)[:, None]
    t_ = np.arange(128)[None, :]
    same = (s_ // 32) == (t_ // 32)
    _CONST["mfwd"] = (same & (s_ <= t_)).astype(np.float32)
    _CONST["mbwd"] = (same & (s_ >= t_)).astype(np.float32)
    cm = np.zeros((128, 4, 128), np.float32)
    for c_ in range(4):
        cm[32 * c_:32 * c_ + 32, c_, :] = 1.0
    _CONST["cm3"] = cm.reshape(128, 512)
    return _CONST


def _na_bias_tiles(rpb):
    H = rpb.shape[0]
    kc = np.arange(64)[:, None]
    qc = np.arange(64)[None, :]
    wc = np.clip(qc - 8, 0, 48)
    colvalid = (kc >= wc) & (kc < wc + 16)
    cidx = np.clip(kc - qc + 15, 0, 30)
    specs = [(8, 8 - 4 + 2 * po) for po in range(8)] + [(0, 2 * p) for p in range(6)] + [(56, 52 + 2 * p) for p in range(6)]
    out = np.full((H, 128, 20, 512), NEG, np.float32)
    for ti, (r0, kr0) in enumerate(specs):
        for kr2 in range(2):
            kr = kr0 + kr2
            for qr in range(8):
                r = r0 + qr
                rs = min(max(r - 4, 0), 56)
                if not (rs <= kr < rs + 8):
                    continue
                ridx = kr - r + 7
                blk = np.where(colvalid[None], rpb[:, ridx][:, cidx], NEG)
                out[:, kr2 * 64:(kr2 + 1) * 64, ti, qr * 64:(qr + 1) * 64] = blk
    return out.reshape(H, 128, 20 * 512)


def build(debug=False, stop=None):
    nc = bass.Bass("TRN2", target_bir_lowering=False)
    skind = "ExternalOutput" if debug else "Internal"

    def din(name, shape, dt=F32):
        return nc.dram_tensor(name, list(shape), dt, kind="ExternalInput").ap()

    def dscr(name, shape, dt=F32):
        return nc.dram_tensor(name, list(shape), dt, kind=skind).ap()

    xin = din("xin", [T, D])
    cT = din("cT", [128, 16])
    w_mod = din("w_mod", [DEPTH, D, 6 * D])
    b_mod = din("b_mod", [DEPTH, 6 * D])
    norm1_g = din("norm1_g", [DEPTH, D])
    w_in = din("w_in", [DEPTH, D, INW])
    fourier_w = din("fourier_w", [DEPTH, 4, 64, 64])
    lbT = din("lbT", [128, 8])
    ngT = din("ngT", [128, DEPTH])
    nabias = din("nabias", [DEPTH, 8, 128, 20 * 512])
    w_out = din("w_out", [DEPTH, D, D])
    norm2_g = din("norm2_g", [DEPTH, D])
    w_gate = din("w_gate", [DEPTH, D, DFF])
    w_up = din("w_up", [DEPTH, D, DFF])
    w_down = din("w_down", [DEPTH, DFF, D])
    final_g = din("final_g", [D])
    C4 = din("C4", [S, S], BF16)
    S4 = din("S4", [S, S], BF16)
    C2 = din("C2", [L, L], BF16)
    S2 = din("S2", [L, L], BF16)
    C64bd_d = din("C64bd", [128, 128])
    S64bdn_d = din("S64bdn", [128, 128])
    ident_d = din("ident", [128, 128], BF16)
    bdmask_d = din("bdmask", [128, 128])
    bd64_d = din("bd64", [128, 128])
    mfwd_d = din("mfwd", [128, 128])
    mbwd_d = din("mbwd", [128, 128])
    cm3_d = din("cm3", [128, 512])
    out = nc.dram_tensor("out", [S, D], F32, kind="ExternalOutput").ap()

    mod2d = dscr("mod2d", [2 * DEPTH, 6 * D])
    fz = dscr("fz", [512, T])
    fb = dscr("fb", [1536, T], BF16)
    tm = dscr("tm", [T, 1024], BF16)
    catT = dscr("catT", [1024, T], BF16)
    xres = dscr("xres", [T, D])
    w_in_b = dscr("w_in_b", [DEPTH, D, INW], BF16)
    w_out_b = dscr("w_out_b", [DEPTH, D, D], BF16)
    w_gate_b = dscr("w_gate_b", [DEPTH, D, DFF], BF16)
    w_up_b = dscr("w_up_b", [DEPTH, D, DFF], BF16)
    w_down_b = dscr("w_down_b", [DEPTH, DFF, D], BF16)

    with ExitStack() as ges:
        sch = Sched(nc, ges)
        V, A, P, G = nc.vector, nc.scalar, nc.tensor, nc.gpsimd

        uid = [0]

        def sbt(es, name, shape, dt=F32):
            uid[0] += 1
            return es.enter_context(nc.sbuf_tensor(f"s{uid[0]}_{name}", list(shape), dt))

        def pst(es, name, shape, dt=F32):
            uid[0] += 1
            return es.enter_context(nc.psum_tensor(f"p{uid[0]}_{name}", list(shape), dt))

        ident = sbt(ges, "ident", [128, 128], BF16)
        bdmask = sbt(ges, "bdmask", [128, 128])
        onesb = sbt(ges, "onesb", [128, 64], BF16)
        B_const = Buf()
        sch.dma("sp", ident[:], ident_d, writes=[B_const])
        sch.dma("sp", bdmask[:], bdmask_d, writes=[B_const])
        sch.op("pool", G.memset, onesb[:], 1.0, writes=[B_const], force=True)

        def evac(i, outap, inap, reads, writes, scale=None):
            if i % 2 == 0:
                if scale is None:
                    sch.op("act", A.copy, outap, inap, reads=reads, writes=writes)
                else:
                    sch.op("act", A.mul, outap, inap, scale, reads=reads, writes=writes)
            else:
                if scale is None:
                    sch.op("dve", V.tensor_copy, outap, inap, reads=reads, writes=writes)
                else:
                    sch.op("dve", V.tensor_scalar, outap, inap, scale, None, ALU.mult, reads=reads, writes=writes)

        def phase_mod():
            with ExitStack() as es:
                cTt = sbt(es, "cTt", [128, 16])
                sT = sbt(es, "sT", [128, 16])
                mod2s = sbt(es, "mod2s", [2, 6 * D])
                bt = sbt(es, "bt", [2, 6 * D])
                wm = Rot([sbt(es, f"wm{i}", [128, 8, 512]) for i in range(3)])
                pm = Rot([pst(es, f"pm{i}", [128, 512]) for i in range(2)])
                B_c, B_s, B_m, B_b = Buf(), Buf(), Buf(), Buf()
                sch.dma("sp", cTt[:], cT, writes=[B_c])
                sch.op("act", A.activation, sT[:], cTt[:], AF.Silu, reads=[B_c], writes=[B_s])
                for l in range(DEPTH):
                    sch.dma("sp", bt[:], b_mod[l].partition_broadcast(2), writes=[B_b])
                    for nb in range(12):
                        wt, Bw = wm.next()
                        sch.dma("sp", wt[:], w_mod[l][:, nb * 512:(nb + 1) * 512].rearrange("(k p) n -> p k n", p=128), writes=[Bw])
                        pt, Bp = pm.next()
                        for k in range(8):
                            sch.op("pe", P.matmul, pt[0:2, :], lhsT=sT[:, 2 * k:2 * k + 2], rhs=wt[:, k, :],
                                   start=(k == 0), stop=(k == 7), pemode="m2", reads=[B_s, Bw], writes=[Bp])
                        sch.op("dve", V.tensor_tensor, mod2s[:, nb * 512:(nb + 1) * 512], pt[0:2, :], bt[:, nb * 512:(nb + 1) * 512],
                               ALU.add, reads=[Bp, B_b], writes=[B_m])
                    sch.dma("pool", mod2d[2 * l:2 * l + 2, :], mod2s[:], reads=[B_m])
                sch.barrier()

        def norm_block(es_tiles, xt, Bx, Gt, SHt, B_mod, a_bf, Ba):
            junk, Bj, ssqr, tmpr = es_tiles
            sq, Bs = ssqr.next()
            sch.op("act", A.activation, junk[:], xt[:], AF.Square, accum_out=sq[:, 0:1], reads=[Bx], writes=[Bj, Bs])
            sch.op("act", A.activation, sq[:, 1:2], sq[:, 0:1], AF.Sqrt, bias=EPS, scale=1.0 / D, reads=[Bs], writes=[Bs])
            sch.op("dve", V.reciprocal, sq[:, 2:3], sq[:, 1:2], reads=[Bs], writes=[Bs])
            tp, Bt = tmpr.next()
            sch.op("dve", V.scalar_tensor_tensor, tp[:], xt[:], sq[:, 2:3], Gt[:], ALU.mult, ALU.mult,
                   reads=[Bx, Bs, B_mod], writes=[Bt])
            sch.op("dve", V.tensor_tensor, a_bf[:], tp[:], SHt[:], ALU.add, reads=[Bt, B_mod], writes=[Ba])

        def transpose_block(ptrr, a_bf, Ba, aT, BaT, tb, ei):
            for half in range(2):
                pt, Bp = ptrr.next()
                for j in range(4):
                    c = (half * 4 + j) * 128
                    sch.op("pe", P.transpose, pt[:, j, :], a_bf[:, c:c + 128], ident[:], reads=[Ba, B_const], writes=[Bp])
                evac(ei + half, aT[:, half * 4:(half + 1) * 4, tb * 128:(tb + 1) * 128], pt[:], [Bp], [BaT])

        def load_weight(es, name, src_kpn, nk, ncols, Bsrc):
            wb = sbt(es, name, [128, nk, ncols], BF16)
            Bw = Buf()
            for k in range(nk):
                sch.dma("sp", wb[:, k, :], src_kpn[:, k, :], reads=[Bsrc], writes=[Bw])
            return wb, Bw

        wcast_buf = {}

        def cast_weights(l, names):
            table = {"in": (w_in, w_in_b, D), "out": (w_out, w_out_b, D), "gate": (w_gate, w_gate_b, D), "up": (w_up, w_up_b, D),
                     "down": (w_down, w_down_b, DFF)}
            for nm in names:
                src, dst, R_ = table[nm]
                Bw = Buf()
                wcast_buf[(nm, l)] = Bw
                for r0 in range(0, R_, 128):
                    sch.dma("pool", dst[l][r0:r0 + 128, :], src[l][r0:r0 + 128, :], writes=[Bw], bg=True)

        def pump(*a_, **k_):
            return

        def make_bg(es, n):
            return None

        bgq = []

        def mod_tiles(es, l, v, gsrc, i_sh, i_sc, tag):
            Gt = sbt(es, f"G{tag}", [128, D])
            SHt = sbt(es, f"SH{tag}", [128, D])
            Bm = Buf()
            sch.dma("pool", SHt[:], mod2d[2 * l + v, i_sh * D:(i_sh + 1) * D].partition_broadcast(128), writes=[Bm])
            sch.dma("pool", Gt[:], gsrc.partition_broadcast(128), writes=[Bm])
            return Gt, SHt, Bm

        def phase_ab(l):
            xsrc = xin if l == 0 else xres
            if l >= 1 and bgq:
                with ExitStack() as es0:
                    pump(100000, make_bg(es0, 4), drain=True)
                    sch.barrier()
            with ExitStack() as es:
                wbf, Bw = load_weight(es, "winbf", w_in_b[l].rearrange("(k p) n -> p k n", p=128), 8, INW, wcast_buf[("in", l)])
                mods = []
                sct = sbt(es, "sct", [128, D])
                Bsc = Buf()
                for v in range(2):
                    Gt, SHt, Bm = mod_tiles(es, l, v, norm1_g[l], 0, 1, f"1_{v}")
                    sch.dma("pool", sct[:], mod2d[2 * l + v, D:2 * D].partition_broadcast(128), writes=[Bsc])
                    sch.op("dve", V.scalar_tensor_tensor, Gt[:], sct[:], 1.0, Gt[:], ALU.add, ALU.mult, reads=[Bsc, Bm], writes=[Bm])
                    mods.append((Gt, SHt, Bm))
                junk = sbt(es, "junk", [128, D], BF16)
                nt = (junk, Buf(), Rot([sbt(es, f"ssq{i}", [128, 4]) for i in range(4)]),
                      Rot([sbt(es, f"ntmp{i}", [128, D]) for i in range(2)]))
                xr = Rot([sbt(es, f"xt{i}", [128, D]) for i in range(8)])
                ar = Rot([sbt(es, f"abf{i}", [128, D], BF16) for i in range(2)])
                aTr = Rot([sbt(es, f"aT{i}", [128, 8, 512], BF16) for i in range(2)])
                sfr = Rot([sbt(es, f"sf{i}", [128, 512]) for i in range(2)])
                sbr = Rot([sbt(es, f"sb{i}", [128, 512], BF16) for i in range(3)])
                tmr = Rot([sbt(es, f"tms{i}", [128, 1024], BF16) for i in range(2)])
                ptrr = Rot([pst(es, f"ptr{i}", [128, 4, 128], BF16) for i in range(2)])
                pfr = Rot([pst(es, f"pf{i}", [128, 512]) for i in range(3)])
                ptmr = Rot([pst(es, f"ptm{i}", [128, 512]) for i in range(2)])
                fm = []
                for c0, dst, r0, sc in ((256, fb, 0, None), (512, fz, 0, None), (768, fz, 256, None), (1280, fb, 256, None),
                                        (1536, fb, 512, 0.125), (2048, fb, 1024, None)):
                    wd = 512 if c0 >= 1536 else 256
                    for g in range(wd // 128):
                        fm.append((c0 + g * 128, dst, r0 + g * 128, sc))
                sbs = [(i * 512, 512, 0) for i in range(8)] + [(S, L, 1)]
                eic = [0]
                aTs = {}
                abfs = {}

                xts = {}

                def load_part(i):
                    t0, ntok, v = sbs[i]
                    for tb in range(ntok // 128):
                        xt, Bx = xr.next()
                        sch.dma("sp", xt[:], xsrc[t0 + tb * 128:t0 + (tb + 1) * 128, :], writes=[Bx])
                        xts[(i, tb)] = (xt, Bx)

                def norm_part(i, tb):
                    t0, ntok, v = sbs[i]
                    Gt, SHt, Bm = mods[v]
                    if tb == 0:
                        aTs[i] = aTr.next()
                    xt, Bx = xts.pop((i, tb))
                    a_bf, Ba = ar.next()
                    norm_block(nt, xt, Bx, Gt, SHt, Bm, a_bf, Ba)
                    abfs[(i, tb)] = (a_bf, Ba)

                def tr_part(i, tb):
                    aT, BaT = aTs[i]
                    a_bf, Ba = abfs.pop((i, tb))
                    eic[0] += 2
                    transpose_block(ptrr, a_bf, Ba, aT, BaT, tb, eic[0])

                load_part(0)
                for tb in range(sbs[0][1] // 128):
                    norm_part(0, tb)
                    tr_part(0, tb)
                for i, (t0, ntok, v) in enumerate(sbs):
                    aT, BaT = aTs[i]
                    if i + 1 < len(sbs):
                        load_part(i + 1)
                    groups = []
                    for (c0, dst, r0, sc) in fm:
                        groups.append(("fm", c0, dst, r0, sc))
                    for tb in range(ntok // 128):
                        groups.append(("tm", tb))
                    nxt = {}
                    if i + 1 < len(sbs):
                        ntb2 = sbs[i + 1][1] // 128
                        for tb in range(ntb2):
                            nxt.setdefault(1 + 5 * tb, []).append(("n", tb))
                            nxt.setdefault(4 + 5 * tb, []).append(("t", tb))
                    for gi, g in enumerate(groups):
                        for (kind, tb) in nxt.pop(gi, []):
                            (norm_part if kind == "n" else tr_part)(i + 1, tb)
                        if g[0] == "fm":
                            _, c0, dst, r0, sc = g
                            pf, Bp = pfr.next()
                            for k in range(8):
                                sch.op("pe", P.matmul, pf[:, 0:ntok], lhsT=wbf[:, k, c0:c0 + 128], rhs=aT[:, k, 0:ntok],
                                       start=(k == 0), stop=(k == 7), reads=[Bw, BaT], writes=[Bp])
                            if dst is fz:
                                st, Bs = sfr.next()
                            else:
                                st, Bs = sbr.next()
                            evac(0, st[:, 0:ntok], pf[:, 0:ntok], [Bp], [Bs], scale=sc)
                            sch.dma("sp", dst[r0:r0 + 128, t0:t0 + ntok], st[:, 0:ntok], reads=[Bs])
                        else:
                            tb = g[1]
                            tms, Bts = tmr.next()
                            tsl = slice(tb * 128, (tb + 1) * 128)
                            p0, Bp0 = ptmr.next()
                            for (cc, oc) in ((0, 0), (1024, 256)):
                                for k in range(8):
                                    sch.op("pe", P.matmul, p0[:, oc:oc + 256], lhsT=aT[:, k, tsl], rhs=wbf[:, k, cc:cc + 256],
                                           start=(k == 0), stop=(k == 7), reads=[Bw, BaT], writes=[Bp0])
                            evac(0, tms[:, 0:512], p0[:], [Bp0], [Bts])
                            p1, Bp1 = ptmr.next()
                            for k in range(8):
                                sch.op("pe", P.matmul, p1[:], lhsT=aT[:, k, tsl], rhs=wbf[:, k, 2560:3072],
                                       start=(k == 0), stop=(k == 7), reads=[Bw, BaT], writes=[Bp1])
                            evac(0, tms[:, 512:1024], p1[:], [Bp1], [Bts])
                            sch.dma("sp", tm[t0 + tb * 128:t0 + (tb + 1) * 128, :], tms[:], reads=[Bts])
                    for lst in nxt.values():
                        for (kind, tb) in lst:
                            (norm_part if kind == "n" else tr_part)(i + 1, tb)
                sch.barrier()

        def phase_c(l):
            with ExitStack() as es:
                cbgs = make_bg(es, 4)
                U = sbt(es, "fU", [128, NB, 256], BF16)
                BU = Buf()
                sch.dma("sp", U[:], tm[:, 0:256].rearrange("(k p) c -> p k c", p=128), writes=[BU])
                c64 = sbt(es, "c64", [128, 2, 128])
                wf = sbt(es, "wf", [128, 2, 64])
                Mbd = sbt(es, "Mbd", [128, 3, 2, 128], BF16)
                Bc, Bwf, BM = Buf(), Buf(), Buf()
                sch.dma("sp", c64[:, 0, :], C64bd_d, writes=[Bc])
                sch.dma("sp", c64[:, 1, :], S64bdn_d, writes=[Bc])
                for cb in range(2):
                    sch.dma("sp", wf[:, cb, :], fourier_w[l][2 * cb:2 * cb + 2].rearrange("g c d -> (g c) d"), writes=[Bwf])
                sch.op("pool", G.memset, Mbd[:], 0.0, writes=[BM])
                pmr = Rot([pst(es, f"pM{i}", [128, 512]) for i in range(1)])
                pM, BpM = pmr.next()
                for w in range(2):
                    for cb in range(2):
                        o0 = (w * 2 + cb) * 64
                        sch.op("pe", P.matmul, pM[:, o0:o0 + 64], lhsT=c64[:, w, :], rhs=wf[:, cb, :], start=True, stop=True,
                               pemode="f32", reads=[Bc, Bwf], writes=[BpM])
                for w in range(2):
                    for cb in range(2):
                        o0 = (w * 2 + cb) * 64
                        sch.op("dve", V.tensor_copy, Mbd[0:64, w, cb, 0:64], pM[0:64, o0:o0 + 64], reads=[BpM, BM], writes=[BM])
                        sch.op("dve", V.tensor_copy, Mbd[64:128, w, cb, 64:128], pM[64:128, o0:o0 + 64], reads=[BpM, BM], writes=[BM])
                sch.op("dve", V.tensor_scalar, Mbd[:, 2].rearrange("p c n -> p (c n)"), Mbd[:, 1].rearrange("p c n -> p (c n)"), -1.0, None,
                       ALU.mult, reads=[BM], writes=[BM])
                csr = Rot([sbt(es, f"cs{i}", [128, 2, 4, 512], BF16) for i in range(3)])
                pqr = Rot([sbt(es, f"pq{i}", [128, 4, 512], BF16) for i in range(2)])
                osr = Rot([sbt(es, f"fo{i}", [128, 512], BF16) for i in range(3)])
                pacc = [pst(es, f"pacc{i}", [128, 512]) for i in range(4)]
                Bacc = [Buf() for _ in range(4)]
                por = Rot([pst(es, f"po{i}", [128, 512]) for i in range(2)])
                c0col = sbt(es, "c0col", [128, 2], BF16)
                Bc0 = Buf()
                sch.op("pool", G.memset, c0col[:], 1.0 / 64.0, writes=[Bc0])

                def mix(pq, Bpq, n, tcol0, mirror_col0=None, ncols=None):
                    for cb in range(2):
                        po, Bpo = por.next()
                        sch.op("pe", P.matmul, po[:, 0:n], lhsT=Mbd[:, 0, cb, :], rhs=pq[:, cb, 0:n], start=True, stop=False,
                               reads=[BM, Bpq], writes=[Bpo])
                        sch.op("pe", P.matmul, po[:, 0:n], lhsT=Mbd[:, 1, cb, :], rhs=pq[:, 2 + cb, 0:n], start=False, stop=True,
                               reads=[BM, Bpq], writes=[Bpo])
                        ot, Bo = osr.next()
                        evac(cb, ot[:, 0:n], po[:, 0:n], [Bpo], [Bo])
                        nst = n if ncols is None else ncols
                        if nst == 1:
                            sch.dma("pool", catT[cb * 128:(cb + 1) * 128, tcol0:tcol0 + 1], ot[:, 0:1], reads=[Bo],
                                    allow_slow_non_contiguous=True)
                        else:
                            sch.dma("pool", catT[cb * 128:(cb + 1) * 128, tcol0:tcol0 + nst], ot[:, 0:nst], reads=[Bo])
                        if mirror_col0 is not None:
                            po, Bpo = por.next()
                            for (wm, qi) in ((0, cb), (2, 2 + cb)):
                                bq_ = pq[:, qi, 0:n]
                                rev = bass.AP(tensor=bq_.tensor, offset=bq_.offset + n - 1, ap=[[bq_.ap[0][0], 128], [-1, n]])
                                sch.op("pe", P.matmul, po[:, 0:n], lhsT=Mbd[:, wm, cb, :], rhs=rev, start=(wm == 0), stop=(wm == 2),
                                       reads=[BM, Bpq], writes=[Bpo])
                            ot, Bo = osr.next()
                            evac(cb + 1, ot[:, 0:n], po[:, 0:n], [Bpo], [Bo])
                            sch.dma("pool", catT[cb * 128:(cb + 1) * 128, mirror_col0:mirror_col0 + n], ot[:, 0:n], reads=[Bo])

                for cb in range(2):
                    for k in range(32):
                        sch.op("pe", P.matmul, pacc[cb][:, 0:2], lhsT=U[:, k, cb * 128:(cb + 1) * 128], rhs=c0col[:],
                               start=(k == 0), stop=(k == 31), reads=[BU, Bc0], writes=[Bacc[cb]])
                pq, Bpq = pqr.next()
                sch.op("pool", G.memset, pq[:, 2:4, 0:2], 0.0, writes=[Bpq])
                for cb in range(2):
                    evac(cb, pq[:, cb, 0:2], pacc[cb][:, 0:2], [Bacc[cb], Bpq], [Bpq])
                mix(pq, Bpq, 2, 0, ncols=1)
                for nb in range(4):
                    c_lo = 512 * nb + 1
                    for kg in range(8):
                        cs, Bcs = csr.next()
                        sch.dma("sp", cs[:, 0], C4[kg * 512:(kg + 1) * 512, c_lo:c_lo + 512].rearrange("(k p) n -> p k n", p=128), writes=[Bcs])
                        sch.dma("sp", cs[:, 1], S4[kg * 512:(kg + 1) * 512, c_lo:c_lo + 512].rearrange("(k p) n -> p k n", p=128), writes=[Bcs])
                        for kk in range(4):
                            k = kg * 4 + kk
                            for w in range(2):
                                for cb in range(2):
                                    i = w * 2 + cb
                                    sch.op("pe", P.matmul, pacc[i][:], lhsT=U[:, k, cb * 128:(cb + 1) * 128], rhs=cs[:, w, kk, :],
                                           start=(k == 0), stop=(k == 31), reads=[BU, Bcs], writes=[Bacc[i]])
                    pq, Bpq = pqr.next()
                    for i in range(4):
                        evac(i, pq[:, i, :], pacc[i][:], [Bacc[i]], [Bpq])
                    mix(pq, Bpq, 512, c_lo, mirror_col0=3584 - 512 * nb)
                    if l == 0:
                        pump(5, cbgs, drain=True)
                if l == 0:
                    c2 = sbt(es, "c2", [128, 2, 2, 256], BF16)
                    Bc2 = Buf()
                    sch.dma("sp", c2[:, 0], C2.rearrange("(k p) n -> p k n", p=128), writes=[Bc2])
                    sch.dma("sp", c2[:, 1], S2.rearrange("(k p) n -> p k n", p=128), writes=[Bc2])
                    for w in range(2):
                        for cb in range(2):
                            i = w * 2 + cb
                            for k in range(2):
                                sch.op("pe", P.matmul, pacc[i][:, 0:256], lhsT=U[:, 32 + k, cb * 128:(cb + 1) * 128], rhs=c2[:, w, k, :],
                                       start=(k == 0), stop=(k == 1), reads=[BU, Bc2], writes=[Bacc[i]])
                    pq, Bpq = pqr.next()
                    for i in range(4):
                        evac(i, pq[:, i, 0:256], pacc[i][:, 0:256], [Bacc[i]], [Bpq])
                    mix(pq, Bpq, 256, S)
                sch.barrier()

        def phase_d(l):
            HW = 9 * 128
            NCH = T // 32
            PIECES = [(0, 9), (9, 8), (17, 9), (26, 8)]
            with ExitStack() as es:
                lbt = sbt(es, "lbt", [128, 8])
                lbv = sbt(es, "lbv", [128, 8])
                ngt = sbt(es, "ngt", [128, DEPTH])
                mdir = sbt(es, "mdir", [128, 2, 128])
                bd64 = sbt(es, "bd64", [128, 128])
                cm3 = sbt(es, "cm3", [128, 4, 128])
                epsc = sbt(es, "epsc", [128, 1])
                Bl, Bmd = Buf(), Buf()
                sch.op("pool", G.memset, epsc[:], EPS, writes=[Bmd])
                sch.dma("sp", lbt[:], lbT, writes=[Bl])
                sch.dma("sp", ngt[:], ngT, writes=[Bl])
                sch.dma("sp", mdir[:, 0, :], mfwd_d, writes=[Bmd])
                sch.dma("sp", mdir[:, 1, :], mbwd_d, writes=[Bmd])
                sch.dma("sp", bd64[:], bd64_d, writes=[Bmd])
                sch.dma("sp", cm3[:], cm3_d.rearrange("p (c n) -> p c n", n=128), writes=[Bmd])
                if l == 0:
                    sch.op("pool", G.memset, lbv[:, 0:4], 0.0, reads=[Bl], writes=[Bl])
                    sch.op("pool", G.memset, lbv[:, 4:8], 1.0, reads=[Bl], writes=[Bl])
                else:
                    ex = sbt(es, "lbex", [128, 8])
                    sch.op("act", A.activation, ex[:], lbt[:], AF.Exp, reads=[Bl], writes=[Bl])
                    sch.op("dve", V.tensor_tensor, lbv[:, 4:8], ex[:, 0:4], ex[:, 4:8], ALU.add, reads=[Bl], writes=[Bl])
                    sch.op("dve", V.reciprocal, lbv[:, 4:8], lbv[:, 4:8], reads=[Bl], writes=[Bl])
                    sch.op("dve", V.tensor_tensor, lbv[:, 0:4], ex[:, 4:8], lbv[:, 4:8], ALU.mult, reads=[Bl], writes=[Bl])
                    sch.op("dve", V.tensor_scalar, lbv[:, 4:8], lbv[:, 0:4], -1.0, 1.0, ALU.mult, ALU.add, reads=[Bl], writes=[Bl])
                for hp in range(2):
                    with ExitStack() as hs:
                        vtm = sbt(hs, "hv", [128, NB, 128], BF16)
                        vm = sbt(hs, "hvm", [128, 2, NB, 128], BF16)
                        Bv, Bvm = Buf(), Buf()
                        sch.dma("sp", vtm[:], tm[:, 256 + hp * 128:256 + (hp + 1) * 128].rearrange("(k p) c -> p k c", p=128), writes=[Bv])
                        sch.op("pool", G.memset, vm[:], 0.0, writes=[Bvm])
                        sch.op("pool", G.tensor_copy, vm[:, 0, :, 0:64], vtm[:, :, 0:64], reads=[Bv, Bvm], writes=[Bvm])
                        sch.op("pool", G.tensor_copy, vm[:, 1, :, 64:128], vtm[:, :, 64:128], reads=[Bv, Bvm], writes=[Bvm])
                        chains = []
                        for d in range(2):
                            ch = dict(d=d)
                            ch["qdm"] = sbt(hs, f"qdm{d}", [128, 2, T], BF16)
                            ch["kd"] = sbt(hs, f"kd{d}", [128, T], BF16)
                            ch["kst"] = sbt(hs, f"kst{d}", [128, NB, 128], BF16)
                            ch["dec"] = sbt(hs, f"dec{d}", [128, NCH])
                            ch["B"] = {k: Buf() for k in ("qdm", "kd", "kst", "dec", "o")}
                            ch["Bz"] = Buf()
                            sch.op("pool", G.memset, ch["qdm"][:], 0.0, writes=[ch["Bz"]])
                            chains.append(ch)
                        with ExitStack() as gs:
                            Asets = [([sbt(gs, f"gA{u}{i}", [128, HW]) for i in range(4)], [Buf() for _ in range(4)],
                                      sbt(gs, f"gq{u}", [128, HW], BF16), Buf(), sbt(gs, f"gks{u}", [128, HW], BF16), Buf(),
                                      sbt(gs, f"gtot{u}", [128, HW // 32]), Buf()) for u in range(4)]
                            unit = 0
                            msk = sbt(gs, "gmsk", [128, HW])
                            Bmsk = Buf()
                            sch.op("pool", G.memset, msk[:], 1.0, writes=[Bmsk])
                            sch.op("pool", G.memset, msk[:].rearrange("p (n c) -> p n c", c=32)[:, :, 0:1], 0.0, reads=[Bmsk], writes=[Bmsk])
                            ptk = Rot([pst(gs, f"ptk{i}", [128, 4, 128], BF16) for i in range(2)])
                            ei_ = [0]

                            def unit_gen(ch, h, uset):
                                d = ch["d"]
                                B = ch["B"]
                                lbc = lbv[:, d * 2 + hp:d * 2 + hp + 1]
                                omc = lbv[:, 4 + d * 2 + hp:4 + d * 2 + hp + 1]
                                Ar_, BA, qh_, Bq, ksh_, Bks, tot_, Btot = uset
                                b0, nbh = PIECES[h]
                                W_ = nbh * 128
                                c0 = b0 * 128
                                cs = slice(c0, c0 + W_)
                                A1, A2, A3, A4 = [a_[:, 0:W_] for a_ in Ar_]
                                qh = qh_[:, 0:W_]
                                ksh = ksh_[:, 0:W_]
                                tot = tot_[:, 0:W_ // 32]
                                mskv = msk[:, 0:W_]
                                B1, B2, B3, B4 = BA
                                zr = 256 * d + hp * 128
                                sch.dma("sp", A1, fz[zr:zr + 128, cs], writes=[B1])
                                sch.dma("sp", qh, fb[hp * 128:(hp + 1) * 128, cs], writes=[Bq])
                                sch.op("act", A.activation, A1, A1, AF.Sigmoid, reads=[B1], writes=[B1])
                                yield
                                sch.op("act", A.activation, A1, A1, AF.Identity, bias=lbc, scale=omc, reads=[B1, Bl], writes=[B1])
                                yield
                                sch.op("act", A.activation, A2, A1, AF.Ln, reads=[B1], writes=[B2])
                                yield
                                sch.op("act", A.activation, A1, A1, AF.Identity, bias=1.0, scale=-1.0, reads=[B1, B2], writes=[B1])
                                sch.op("dve", V.tensor_tensor_scan, A3, mskv, A2, 0.0, ALU.mult, ALU.add, reads=[Bmsk, B2], writes=[B3])
                                yield
                                A3v = A3.rearrange("p (n c) -> p n c", c=32)
                                A2v = A2.rearrange("p (n c) -> p n c", c=32)
                                totb = tot.rearrange("p (n o) -> p n o", o=1).broadcast_to([128, W_ // 32, 32])
                                sch.op("act", A.copy, tot, A3v[:, :, 31], reads=[B3], writes=[Btot])
                                yield
                                if d == 0:
                                    sch.op("dve", V.tensor_tensor, A2v, A3v, totb, ALU.subtract, reads=[Btot, B3, B2], writes=[B2])
                                    asg, rsg = 1.0, -1.0
                                else:
                                    sch.op("dve", V.tensor_tensor, A2, A3, A2, ALU.subtract, reads=[B3, B2], writes=[B2])
                                    yield
                                    sch.op("dve", V.tensor_tensor, A3v, A2v, totb, ALU.subtract, reads=[Btot, B2, B3], writes=[B3])
                                    asg, rsg = -1.0, 1.0
                                yield
                                sch.op("act", A.activation, ch["dec"][:, c0 // 32:(c0 + W_) // 32], tot, AF.Exp, reads=[Btot], writes=[B["dec"]])
                                sch.op("act", A.activation, A4, A3, AF.Exp, scale=asg, reads=[B3, B4], writes=[B4])
                                yield
                                sch.op("dve", V.tensor_tensor, ch["qdm"][0:64, 0, cs], qh[0:64], A4[0:64], ALU.mult,
                                       reads=[Bq, B4, ch["Bz"]], writes=[B["qdm"]])
                                sch.op("dve", V.tensor_tensor, ch["qdm"][64:128, 1, cs], qh[64:128], A4[64:128], ALU.mult,
                                       reads=[Bq, B4, ch["Bz"]], writes=[B["qdm"]])
                                yield
                                sch.op("act", A.activation, A4, A3, AF.Exp, scale=-asg, reads=[B3, B4], writes=[B4])
                                yield
                                sch.op("dve", V.tensor_tensor, ch["kd"][:, cs], A1, A4, ALU.mult, reads=[B1, B4], writes=[B["kd"]])
                                yield
                                sch.op("act", A.activation, A4, A2, AF.Exp, scale=rsg, reads=[B2, B4], writes=[B4])
                                yield
                                sch.op("dve", V.tensor_tensor, ksh, A1, A4, ALU.mult, reads=[B1, B4], writes=[Bks])
                                yield
                                for b4 in range(0, nbh, 4):
                                    nn = min(4, nbh - b4)
                                    pt, Bp = ptk.next()
                                    for j in range(nn):
                                        sch.op("pe", P.transpose, pt[:, j, :], ksh[:, (b4 + j) * 128:(b4 + j + 1) * 128], ident[:],
                                               reads=[Bks, B_const], writes=[Bp])
                                    ei_[0] += 1
                                    evac(ei_[0], ch["kst"][:, b0 + b4:b0 + b4 + nn, :], pt[:, 0:nn, :], [Bp], [B["kst"]])
                                    yield

                            units = [(ch, h) for ch in chains for h in range(4)]
                            gens = []
                            SHIFT = 4
                            active = []
                            ui = 0
                            tick = 0
                            while ui < len(units) or active:
                                if ui < len(units) and len(active) < 4 and (not active or active[-1][1] >= SHIFT):
                                    ch_, h_ = units[ui]
                                    active.append([unit_gen(ch_, h_, Asets[ui % 4]), 0])
                                    ui += 1
                                for a_ in list(active):
                                    try:
                                        next(a_[0])
                                        a_[1] += 1
                                    except StopIteration:
                                        active.remove(a_)
                            sch.barrier()
                        for ch in chains:
                            ch["o"] = sbt(hs, f"o{ch['d']}", [128, T])
                        with ExitStack() as cs_:
                            vexa = sbt(cs_, "vexa", [128, NB, 4, 128], BF16)
                            Bvex = Buf()
                            for nb in range(NB):
                                sch.op("dve", V.tensor_tensor, vexa[:, nb], cm3[:], vtm[:, nb:nb + 1, :].broadcast_to([128, 4, 128]), ALU.mult,
                                       reads=[Bmd, Bv], writes=[Bvex])
                            for ch in chains:
                                d = ch["d"]
                                ch["psU"] = pst(cs_, f"psU{d}", [128, 4, 128]); ch["BpsU"] = Buf()
                                ch["psS"] = [pst(cs_, f"psS{d}{i}", [128, 128]) for i in range(2)]; ch["BpsS"] = [Buf(), Buf()]
                                ch["psO"] = pst(cs_, f"psO{d}", [128, 128]); ch["BpsO"] = Buf()
                                ch["Sf"] = Rot([sbt(cs_, f"Sf{d}{i}", [128, 128]) for i in range(2)])
                                ch["Spad"] = Rot([sbt(cs_, f"Sp{d}{i}", [128, 2, 128], BF16) for i in range(10)])
                                for (tl, Bt_) in ch["Spad"].t:
                                    sch.op("pool", G.memset, tl[:], 0.0, writes=[Bt_])
                                ch["scb"] = Rot([sbt(cs_, f"scb{d}{i}", [128, 2, 128], BF16) for i in range(2)])
                                ch["order"] = ([32, 33] + list(range(32))) if d == 0 else ([33, 32] + list(range(31, -1, -1)))
                                ch["corder"] = [0, 1, 2, 3] if d == 0 else [3, 2, 1, 0]
                                ch["prev"] = None
                                ch["prev_bd"] = None
                                ch["pending"] = None

                            def front_all(step):
                                res = []
                                for ch in chains:
                                    nb = ch["order"][step]
                                    B = ch["B"]
                                    sch.op("pe", P.matmul, ch["psU"][:].rearrange("p c n -> p (c n)"), lhsT=ch["kst"][:, nb, :],
                                           rhs=vexa[:, nb].rearrange("p c n -> p (c n)"), start=True, stop=True,
                                           reads=[B["kst"], Bvex], writes=[ch["BpsU"]])
                                    ch["before"] = []
                                for ci in range(4):
                                    for k_, ch in enumerate(chains):
                                        nb = ch["order"][step]
                                        B = ch["B"]
                                        c = ch["corder"][ci]
                                        n = nb * 4 + c
                                        ch["before"].append((c, ch["prev_bd"]))
                                        sf, Bsf = ch["Sf"].next()
                                        if ch["prev"] is None:
                                            sch.op("dve", V.tensor_copy, sf[:], ch["psU"][:, c, :], reads=[ch["BpsU"]], writes=[Bsf])
                                        else:
                                            pv, Bpv = ch["prev"]
                                            sch.op("dve", V.scalar_tensor_tensor, sf[:], pv[:], ch["dec"][:, n:n + 1], ch["psU"][:, c, :],
                                                   ALU.mult, ALU.add, reads=[Bpv, B["dec"], ch["BpsU"]], writes=[Bsf])
                                        ch["prev"] = (sf, Bsf)
                                        sp_, Bsp = ch["Spad"].next()
                                        base = sp_[:]
                                        ov = bass.AP(tensor=base.tensor, offset=base.offset, ap=[[base.ap[0][0], 128], [192, 2], [1, 64]])
                                        iv = sf[:].rearrange("p (g c) -> p g c", g=2)
                                        sch.op("act", A.copy, ov, iv, reads=[Bsf, Bsp], writes=[Bsp])
                                        ch["prev_bd"] = (sp_, Bsp)
                                for ch in chains:
                                    nb = ch["order"][step]
                                    B = ch["B"]
                                    cols = slice(nb * 128, (nb + 1) * 128)
                                    scb, Bscb = ch["scb"].next()
                                    for hh in range(2):
                                        sch.op("pe", P.matmul, ch["psS"][hh][:], lhsT=ch["kd"][:, cols], rhs=ch["qdm"][:, hh, cols],
                                               start=True, stop=True, reads=[B["kd"], B["qdm"]], writes=[ch["BpsS"][hh]])
                                        sch.op("dve", V.tensor_tensor, scb[:, hh, :], ch["psS"][hh][:], mdir[:, ch["d"], :], ALU.mult,
                                               reads=[ch["BpsS"][hh], Bmd], writes=[Bscb])
                                    res.append((nb, ch["before"], scb, Bscb))
                                return res

                            def back(ch, pend):
                                nb, before, scb, Bscb = pend
                                B = ch["B"]
                                cols = slice(nb * 128, (nb + 1) * 128)
                                inter = [(c, s_) for (c, s_) in before if s_ is not None]
                                for hh in range(2):
                                    sch.op("pe", P.matmul, ch["psO"][:], lhsT=vm[:, hh, nb, :], rhs=scb[:, hh, :], start=(hh == 0),
                                           stop=(hh == 1 and not inter), reads=[Bvm, Bscb], writes=[ch["BpsO"]])
                                for i, (c, (sp_, Bsp)) in enumerate(inter):
                                    for hh in range(2):
                                        sch.op("pe", P.matmul, ch["psO"][:, 32 * c:32 * c + 32], lhsT=sp_[:, hh, :],
                                               rhs=ch["qdm"][:, hh, nb * 128 + 32 * c:nb * 128 + 32 * c + 32], start=False,
                                               stop=(i == len(inter) - 1 and hh == 1), reads=[Bsp, B["qdm"]], writes=[ch["BpsO"]])
                                sch.op("act", A.copy, ch["o"][:, cols], ch["psO"][:], reads=[ch["BpsO"]], writes=[B["o"]])

                            for step in range(NB + 1):
                                if step == 1:
                                    if l == 0 and hp == 0:
                                        cast_weights(0, ["out", "gate", "up"])
                                    elif l == 0 and hp == 1:
                                        cast_weights(0, ["down"])
                                        cast_weights(1, ["in", "out"])
                                    elif l == 1 and hp == 0:
                                        cast_weights(1, ["gate", "up"])
                                    elif l == 1 and hp == 1:
                                        cast_weights(1, ["down"])
                                newp = None
                                if step < NB:
                                    newp = front_all(step)
                                if step >= 1:
                                    for ch in chains:
                                        back(ch, ch["pending"])
                                if step < NB:
                                    for ch, p_ in zip(chains, newp):
                                        ch["pending"] = p_
                            sch.barrier()
                        with ExitStack() as rs:
                            of, ob = chains[0]["o"], chains[1]["o"]
                            Bof, Bob = chains[0]["B"]["o"], chains[1]["B"]["o"]
                            sch.op("dve", V.tensor_tensor, of[:], of[:], ob[:], ALU.add, reads=[Bof, Bob], writes=[Bof])
                            gt = sbt(rs, "hg", [128, T], BF16)
                            Bg = Buf()
                            sch.dma("sp", gt[:], fb[256 + hp * 128:256 + (hp + 1) * 128, :], writes=[Bg])
                            sqr = Rot([sbt(rs, f"hsq{i}", [128, 512]) for i in range(2)])
                            sgr = Rot([sbt(rs, f"hsg{i}", [128, 512]) for i in range(2)])
                            rr = Rot([sbt(rs, f"hr{i}", [128, 512]) for i in range(2)])
                            orr = Rot([sbt(rs, f"hob{i}", [128, 512], BF16) for i in range(2)])
                            pms = Rot([pst(rs, f"pms{i}", [128, 512]) for i in range(2)])
                            for t0 in range(0, T, 512):
                                n = min(512, T - t0)
                                cs = slice(t0, t0 + n)
                                sq, Bsq = sqr.next()
                                sch.op("act", A.activation, sq[:, 0:n], of[:, cs], AF.Square, reads=[Bof], writes=[Bsq])
                                pm_, Bpm = pms.next()
                                sch.op("pe", P.matmul, pm_[:, 0:n], lhsT=bd64[:], rhs=sq[:, 0:n], start=True, stop=True, pemode="f32",
                                       reads=[Bmd, Bsq], writes=[Bpm])
                                r_, Br = rr.next()
                                sch.op("act", A.activation, r_[:, 0:n], pm_[:, 0:n], AF.Ln, bias=epsc[:, 0:1], scale=1.0, reads=[Bpm, Bmd], writes=[Br])
                                sch.op("act", A.activation, r_[:, 0:n], r_[:, 0:n], AF.Exp, scale=-0.5, reads=[Br], writes=[Br])
                                sch.op("dve", V.tensor_tensor, r_[:, 0:n], r_[:, 0:n], of[:, cs], ALU.mult, reads=[Br, Bof], writes=[Br])
                                sg, Bsg = sgr.next()
                                sch.op("act", A.activation, sg[:, 0:n], gt[:, cs], AF.Silu, reads=[Bg], writes=[Bsg])
                                ob_, Bo = orr.next()
                                sch.op("dve", V.scalar_tensor_tensor, ob_[:, 0:n], r_[:, 0:n], ngt[:, l:l + 1], sg[:, 0:n], ALU.mult, ALU.mult,
                                       reads=[Br, Bl, Bsg], writes=[Bo])
                                sch.dma("pool", catT[256 + hp * 128:256 + (hp + 1) * 128, cs], ob_[:, 0:n], reads=[Bo])
                            sch.barrier()

        def phase_e(l):
            with ExitStack() as es:
                sets = []
                for u in range(2):
                    qm = sbt(es, f"naq{u}", [128, 2, T], BF16)
                    kT = sbt(es, f"nak{u}", [128, T], BF16)
                    vm = sbt(es, f"nav{u}", [128, 2, NB, 128], BF16)
                    Bq, Bk, Bv = Buf(), Buf(), Buf()
                    sch.op("pool", G.memset, qm[:], 0.0, writes=[Bq])
                    sch.op("pool", G.memset, vm[:], 0.0, writes=[Bv])
                    sets.append((qm, kT, vm, Bq, Bk, Bv))
                bg_rng = ((8, 14), (0, 8), (14, 20))
                bias = [[sbt(es, f"nab{hh}{g}", [128, bg_rng[g][1] - bg_rng[g][0], 512]) for g in range(3)] for hh in range(2)]
                Bb = [[Buf() for g in range(3)] for hh in range(2)]
                oT = sbt(es, "nao", [128, T], BF16)
                oneh = sbt(es, "oneh", [128, 2, 128], BF16)
                sbr = Rot([sbt(es, f"nas{i}", [128, 512]) for i in range(4)])
                pr = Rot([sbt(es, f"nap{i}", [128, 512], BF16) for i in range(6)])
                rcr = Rot([sbt(es, f"narc{i}", [128, 512]) for i in range(2)])
                psr = Rot([pst(es, f"naS{i}", [128, 512]) for i in range(4)])
                numr = Rot([pst(es, f"naN{i}", [128, 512]) for i in range(2)])
                denr = Rot([pst(es, f"naD{i}", [128, 512]) for i in range(2)])
                Bo, B1 = Buf(), Buf()
                sch.op("pool", G.memset, oneh[:], 0.0, writes=[B1])
                for hh in range(2):
                    sch.op("pool", G.memset, oneh[:, hh, hh * 64:(hh + 1) * 64], 1.0, reads=[B1], writes=[B1])
                LOOK = 3
                ebgs = make_bg(es, 4)
                if l == 0:
                    pump(40, ebgs, drain=True)

                def load_qkv(hp):
                    qm, kT, vm, Bq, Bk, Bv = sets[hp % 2]
                    for hh in range(2):
                        r0 = 512 + hp * 128 + hh * 64
                        sch.dma("sp", qm[hh * 64:(hh + 1) * 64, hh, :], fb[r0:r0 + 64, :], reads=[Bq], writes=[Bq])
                    sch.dma("sp", kT[:], fb[1024 + hp * 128:1024 + (hp + 1) * 128, :], writes=[Bk])
                    for hh in range(2):
                        c0 = 512 + hp * 128 + hh * 64
                        sch.dma("sp", vm[:, hh, :, hh * 64:(hh + 1) * 64], tm[:, c0:c0 + 64].rearrange("(k p) c -> p k c", p=128),
                                reads=[Bv], writes=[Bv])

                def load_bias(hp, g):
                    t0_, t1_ = bg_rng[g]
                    for hh in range(2):
                        sch.dma("sp", bias[hh][g][:], nabias[l, hp * 2 + hh].rearrange("p (t n) -> p t n", n=512)[:, t0_:t1_, :],
                                writes=[Bb[hh][g]])

                load_qkv(0)
                for g in range(3):
                    load_bias(0, g)
                for hp in range(4):
                    qm, kT, vm, Bq, Bk, Bv = sets[hp % 2]
                    if hp + 1 < 4:
                        load_qkv(hp + 1)
                    items = []
                    for qb in range(8):
                        if qb == 0:
                            kts = [(2 * p * 64, (0, p)) for p in range(6)]
                        elif qb == 7:
                            kts = [((52 + 2 * p) * 64, (2, p)) for p in range(6)]
                        else:
                            kts = [((8 * qb - 4 + 2 * po) * 64, (1, po)) for po in range(8)]
                        kts += [(S, None), (S + 128, None)]
                        items.append((qb * 512, 512, kts, qb))
                    if l == 0:
                        items.append((S, L, [(S, None), (S + 128, None)], 8))
                    for (q0, nq, kts, qb) in items:
                        flat = [(hh, kt0, bi) for hh in range(2) for (kt0, bi) in kts]
                        prod = [None] * len(flat)
                        num, Bn = numr.next()
                        den, Bd = denr.next()
                        for i in range(len(flat) + LOOK):
                            if i < len(flat):
                                hh, kt0, bi = flat[i]
                                ps, Bps = psr.next()
                                sch.op("pe", P.matmul, ps[:, 0:nq], lhsT=kT[:, kt0:kt0 + 128], rhs=qm[:, hh, q0:q0 + nq], start=True, stop=True,
                                       reads=[Bk, Bq], writes=[Bps])
                                pt, Bpt = pr.next()
                                if bi is None:
                                    sch.op("act", A.activation, pt[:, 0:nq], ps[:, 0:nq], AF.Exp, reads=[Bps], writes=[Bpt])
                                else:
                                    st, Bst = sbr.next()
                                    sch.op("dve", V.tensor_tensor, st[:, 0:nq], ps[:, 0:nq], bias[hh][bi[0]][:, bi[1], 0:nq], ALU.add,
                                           reads=[Bps, Bb[hh][bi[0]]], writes=[Bst])
                                    sch.op("act", A.activation, pt[:, 0:nq], st[:, 0:nq], AF.Exp, reads=[Bst], writes=[Bpt])
                                prod[i] = (pt, Bpt)
                            j = i - LOOK
                            if j >= 0:
                                hh, kt0, bi = flat[j]
                                pt, Bpt = prod[j]
                                first = (j == 0)
                                last = (j == len(flat) - 1)
                                sch.op("pe", P.matmul, num[:, 0:nq], lhsT=vm[:, hh, kt0 // 128, :], rhs=pt[:, 0:nq],
                                       start=first, stop=last, reads=[Bv, Bpt], writes=[Bn])
                                sch.op("pe", P.matmul, den[:, 0:nq], lhsT=oneh[:, hh, :], rhs=pt[:, 0:nq],
                                       start=first, stop=last, reads=[B1, Bpt], writes=[Bd])
                        rc, Brc = rcr.next()
                        sch.op("dve", V.reciprocal, rc[:, 0:nq], den[:, 0:nq], reads=[Bd], writes=[Brc])
                        sch.op("dve", V.tensor_tensor, oT[:, q0:q0 + nq], num[:, 0:nq], rc[:, 0:nq], ALU.mult, reads=[Bn, Brc], writes=[Bo])
                        if hp + 1 < 4:
                            if qb == 0:
                                load_bias(hp + 1, 0)
                            elif qb == 6:
                                load_bias(hp + 1, 1)
                            elif qb == 7:
                                load_bias(hp + 1, 2)
                    ncol = T if l == 0 else S
                    sch.dma("pool", catT[512 + hp * 128:512 + (hp + 1) * 128, 0:ncol], oT[:, 0:ncol], reads=[Bo])
                    if l == 0 and hp < 3:
                        pump(40, ebgs, drain=True)
                sch.barrier()

        def phase_f1(l):
            xsrc = xin if l == 0 else xres
            with ExitStack() as es:
                wob, Bw = load_weight(es, "woutbf", w_out_b[l].rearrange("(k p) n -> p k n", p=128), 8, D, wcast_buf[("out", l)])
                g1 = [sbt(es, f"g1_{v}", [128, D]) for v in range(2)]
                Bg = Buf()
                for v in range(2):
                    sch.dma("pool", g1[v][:], mod2d[2 * l + v, 2 * D:3 * D].partition_broadcast(128), writes=[Bg])
                fbgs = make_bg(es, 4)
                cr = Rot([sbt(es, f"cat{i}", [128, 8, 512], BF16) for i in range(2)])
                xr = Rot([sbt(es, f"xt{i}", [128, D]) for i in range(3)])
                yr = Rot([sbt(es, f"yt{i}", [128, D]) for i in range(2)])
                pyr = Rot([pst(es, f"py{i}", [128, 512]) for i in range(4)])
                sbs = [(i * 512, 512, 0) for i in range(8)] + ([(S, L, 1)] if l == 0 else [])
                for (t0, ntok, v) in sbs:
                    ct, Bc = cr.next()
                    sch.dma("sp", ct[:, :, 0:ntok], catT[:, t0:t0 + ntok].rearrange("(k p) t -> p k t", p=128), writes=[Bc])
                    for tb in range(ntok // 128):
                        xt, Bx = xr.next()
                        rows = slice(t0 + tb * 128, t0 + (tb + 1) * 128)
                        sch.dma("sp", xt[:], xsrc[rows, :], writes=[Bx])
                        yt, By = yr.next()
                        for nb in range(2):
                            py, Bp = pyr.next()
                            for k in range(8):
                                sch.op("pe", P.matmul, py[:], lhsT=ct[:, k, tb * 128:(tb + 1) * 128], rhs=wob[:, k, nb * 512:(nb + 1) * 512],
                                       start=(k == 0), stop=(k == 7), reads=[Bc, Bw], writes=[Bp])
                            sch.op("dve", V.tensor_tensor, yt[:, nb * 512:(nb + 1) * 512], py[:], g1[v][:, nb * 512:(nb + 1) * 512], ALU.mult,
                                   reads=[Bp, Bg], writes=[By])
                        sch.op("dve", V.tensor_tensor, yt[:], yt[:], xt[:], ALU.add, reads=[By, Bx], writes=[By])
                        sch.dma("pool", xres[rows, :], yt[:], reads=[By])
                        if l == 0:
                            pump(2, fbgs)
                pump(0, fbgs, drain=True)
                sch.barrier()

        def phase_f2(l):
            last = (l == DEPTH - 1)
            with ExitStack() as es:
                wg = sbt(es, "wgbf", [128, 8, DFF], BF16)
                wu = sbt(es, "wubf", [128, 8, DFF], BF16)
                NWC = 4
                WCW = DFF // NWC
                Bwg_c = [Buf() for _ in range(NWC)]
                Bwu_c = [Buf() for _ in range(NWC)]
                for c_ in range(NWC):
                    csl = slice(c_ * WCW, (c_ + 1) * WCW)
                    sch.dma("sp", wg[:, :, csl], w_gate_b[l][:, csl].rearrange("(k p) n -> p k n", p=128),
                            reads=[wcast_buf[("gate", l)]], writes=[Bwg_c[c_]])
                    sch.dma("sp", wu[:, :, csl], w_up_b[l][:, csl].rearrange("(k p) n -> p k n", p=128),
                            reads=[wcast_buf[("up", l)]], writes=[Bwu_c[c_]])
                wd, Bwd = load_weight(es, "wdbf", w_down_b[l].rearrange("(j p) n -> p j n", p=128), NJ, D, wcast_buf[("down", l)])
                junk = sbt(es, "junk", [128, D], BF16)
                nt = (junk, Buf(), Rot([sbt(es, f"ssq{i}", [128, 4]) for i in range(4)]),
                      Rot([sbt(es, f"ntmp{i}", [128, D]) for i in range(1)]))
                Gt = sbt(es, "G2", [128, D])
                SHt = sbt(es, "SH2", [128, D])
                g2 = sbt(es, "g2", [128, D])
                Bm = Buf()

                def load_mods(v):
                    sct, Bsc = nt[3].next()
                    sch.dma("pool", SHt[:], mod2d[2 * l + v, 3 * D:4 * D].partition_broadcast(128), writes=[Bm])
                    sch.dma("pool", Gt[:], norm2_g[l].partition_broadcast(128), writes=[Bm])
                    sch.dma("pool", g2[:], mod2d[2 * l + v, 5 * D:6 * D].partition_broadcast(128), writes=[Bm])
                    sch.dma("pool", sct[:], mod2d[2 * l + v, 4 * D:5 * D].partition_broadcast(128), writes=[Bsc])
                    sch.op("dve", V.scalar_tensor_tensor, Gt[:], sct[:], 1.0, Gt[:], ALU.add, ALU.mult, reads=[Bsc, Bm], writes=[Bm])
                fg = None
                if last:
                    fg = sbt(es, "fg", [128, D])
                    Bfg = Buf()
                    sch.dma("pool", fg[:], final_g.partition_broadcast(128), writes=[Bfg])
                x1 = [(sbt(es, f"x1_{i}", [128, D]), Buf()) for i in range(4)]
                ar = Rot([sbt(es, f"abf{i}", [128, D], BF16) for i in range(1)])
                aT = sbt(es, "aT", [128, 8, 512], BF16)
                BaT = Buf()
                hT = sbt(es, "hT", [128, NJ, 512], BF16)
                BhT = Buf()
                sgr = Rot([sbt(es, f"sg{i}", [128, 512]) for i in range(1)])
                ptrr = Rot([pst(es, f"ptr{i}", [128, 4, 128], BF16) for i in range(1)])
                pgr = Rot([pst(es, f"pg{i}", [128, 512]) for i in range(2)])
                pur = Rot([pst(es, f"pu{i}", [128, 512]) for i in range(2)])
                pdr = Rot([pst(es, f"pd{i}", [128, 512]) for i in range(3)])
                sbs = [(i * 512, 512, 0) for i in range(8)] + ([(S, L, 1)] if l == 0 else [])
                curv = None
                for (t0, ntok, v) in sbs:
                    if v != curv:
                        load_mods(v)
                        curv = v
                    ntb = ntok // 128
                    for tb in range(ntb):
                        xt, Bx = x1[tb]
                        sch.dma("sp", xt[:], xres[t0 + tb * 128:t0 + (tb + 1) * 128, :], writes=[Bx])
                        a_bf, Ba = ar.next()
                        norm_block(nt, xt, Bx, Gt, SHt, Bm, a_bf, Ba)
                        transpose_block(ptrr, a_bf, Ba, aT, BaT, tb, tb)
                    for j in range(NJ):
                        pg, Bpg = pgr.next()
                        pu, Bpu = pur.next()
                        for k in range(8):
                            sch.op("pe", P.matmul, pg[:, 0:ntok], lhsT=wg[:, k, j * 128:(j + 1) * 128], rhs=aT[:, k, 0:ntok],
                                   start=(k == 0), stop=(k == 7),
                                   reads=[Bwg_c[(j * 128) // WCW], Bwg_c[min(NWC - 1, (j * 128 + 127) // WCW)], BaT], writes=[Bpg])
                        for k in range(8):
                            sch.op("pe", P.matmul, pu[:, 0:ntok], lhsT=wu[:, k, j * 128:(j + 1) * 128], rhs=aT[:, k, 0:ntok],
                                   start=(k == 0), stop=(k == 7),
                                   reads=[Bwu_c[(j * 128) // WCW], Bwu_c[min(NWC - 1, (j * 128 + 127) // WCW)], BaT], writes=[Bpu])
                        sg, Bsg = sgr.next()
                        sch.op("act", A.activation, sg[:, 0:ntok], pg[:, 0:ntok], AF.Silu, reads=[Bpg], writes=[Bsg])
                        sch.op("dve", V.tensor_tensor, hT[:, j, 0:ntok], sg[:, 0:ntok], pu[:, 0:ntok], ALU.mult, reads=[Bsg, Bpu], writes=[BhT])
                    for tb in range(ntb):
                        xt, Bx = x1[tb]
                        rows = slice(t0 + tb * 128, t0 + (tb + 1) * 128)
                        tp, Bt = nt[3].next()
                        for nb in range(2):
                            pd, Bpd = pdr.next()
                            for j in range(NJ):
                                sch.op("pe", P.matmul, pd[:], lhsT=hT[:, j, tb * 128:(tb + 1) * 128], rhs=wd[:, j, nb * 512:(nb + 1) * 512],
                                       start=(j == 0), stop=(j == NJ - 1), reads=[BhT, Bwd], writes=[Bpd])
                            sch.op("dve", V.tensor_tensor, tp[:, nb * 512:(nb + 1) * 512], pd[:], g2[:, nb * 512:(nb + 1) * 512], ALU.mult,
                                   reads=[Bpd, Bm], writes=[Bt])
                        sch.op("dve", V.tensor_tensor, xt[:], xt[:], tp[:], ALU.add, reads=[Bx, Bt], writes=[Bx])
                        if not last:
                            sch.dma("pool", xres[rows, :], xt[:], reads=[Bx])
                        else:
                            sq, Bs = nt[2].next()
                            sch.op("act", A.activation, junk[:], xt[:], AF.Square, accum_out=sq[:, 0:1], reads=[Bx], writes=[nt[1], Bs])
                            sch.op("act", A.activation, sq[:, 1:2], sq[:, 0:1], AF.Sqrt, bias=EPS, scale=1.0 / D, reads=[Bs], writes=[Bs])
                            sch.op("dve", V.reciprocal, sq[:, 2:3], sq[:, 1:2], reads=[Bs], writes=[Bs])
                            sch.op("dve", V.scalar_tensor_tensor, xt[:], xt[:], sq[:, 2:3], fg[:], ALU.mult, ALU.mult,
                                   reads=[Bx, Bs, Bfg], writes=[Bx])
                            sch.dma("pool", out[rows, :], xt[:], reads=[Bx])
                sch.barrier()

        cast_weights(0, ["in"])
        phases = [("mod", phase_mod, None)]
        for l in range(DEPTH):
            phases += [(f"ab{l}", phase_ab, l), (f"c{l}", phase_c, l), (f"d{l}", phase_d, l), (f"e{l}", phase_e, l),
                       (f"f1{l}", phase_f1, l), (f"f2{l}", phase_f2, l)]
        for name, fn, arg in phases:
            if arg is None:
                fn()
            else:
                fn(arg)
            if stop is not None and name == stop:
                break
        sch.finish()
        nc._sched_stats = (sch.nops, sch.nwaits)
    return nc


def _prep(inputs):
    f32 = np.float32
    c = _consts()
    x = np.asarray(inputs["x"], f32)
    ctx = np.asarray(inputs["ctx"], f32)
    cvec = np.asarray(inputs["c"], f32)
    cctx = np.asarray(inputs["c_ctx"], f32)
    lb = np.asarray(inputs["hgrn_lb"], f32)
    lbT = np.ascontiguousarray(lb.reshape(2, 2, 2, 128).transpose(3, 0, 1, 2).reshape(128, 8))
    ng = np.asarray(inputs["hgrn_norm_g"], f32)
    ngT = np.ascontiguousarray(np.concatenate([ng, ng], axis=1).T)
    rpb = np.asarray(inputs["na_rpb"], f32)
    nabias = np.stack([_na_bias_tiles(rpb[l]) for l in range(DEPTH)])
    shared = {
        "w_mod": np.asarray(inputs["w_mod"], f32), "b_mod": np.asarray(inputs["b_mod"], f32),
        "norm1_g": np.asarray(inputs["norm1_g"], f32), "w_in": np.asarray(inputs["w_in"], f32),
        "fourier_w": np.asarray(inputs["fourier_w"], f32), "lbT": lbT, "ngT": ngT, "nabias": nabias,
        "w_out": np.asarray(inputs["w_out"], f32), "norm2_g": np.asarray(inputs["norm2_g"], f32),
        "w_gate": np.asarray(inputs["w_ffn_gate"], f32), "w_up": np.asarray(inputs["w_ffn_up"], f32),
        "w_down": np.asarray(inputs["w_ffn_down"], f32), "final_g": np.asarray(inputs["final_norm_g"], f32),
        "C4": c["C4"], "S4": c["S4"], "C2": c["C2"], "S2": c["S2"], "C64bd": c["C64bd"], "S64bdn": c["S64bdn"],
        "ident": c["ident"], "bdmask": c["bdmask"], "bd64": c["bd64"], "mfwd": c["mfwd"], "mbwd": c["mbwd"], "cm3": c["cm3"],
    }
    maps = []
    for b in range(8):
        m = dict(shared)
        m["xin"] = np.ascontiguousarray(np.concatenate([x[b], ctx[b]], axis=0))
        cv = np.stack([cvec[b], cctx], axis=0)
        m["cT"] = np.ascontiguousarray(cv.reshape(2, 8, 128).transpose(2, 1, 0).reshape(128, 16))
        maps.append(m)
    return maps


def kernel(**inputs):
    maps = _prep(inputs)
    nc = build()
    res = run_bass_kernel_spmd(nc, maps, core_ids=list(range(8)))
    return np.stack([np.asarray(r["out"], np.float32) for r in res.results], axis=0)
```

```python
import numpy as np
import ml_dtypes
from contextlib import ExitStack
import concourse.bass as bass
import concourse.mybir as mybir
from concourse.bass_utils import run_bass_kernel_spmd

F32 = mybir.dt.float32
BF16 = mybir.dt.bfloat16
AF = mybir.ActivationFunctionType
ALU = mybir.AluOpType

D = 1024
S = 4096
L = 256
T = S + L
NB = T // 128
DFF = 2816
NJ = DFF // 128
INW = 3072
EPS = 1e-6
NEG = -1e30
DEPTH = 2


class _Op:
    __slots__ = ("eng", "fn", "args", "kw", "deps", "signal", "sem", "val", "dma", "epoch")


class Buf:
    __slots__ = ("w", "r", "g")

    def __init__(self):
        self.w = {}
        self.r = {}
        self.g = ()


class Sched:
    CENG = ("pe", "act", "dve", "pool")
    SEMCAP = 30000

    def __init__(self, nc, es):
        self.nc = nc
        self.eobj = dict(pe=nc.tensor, act=nc.scalar, dve=nc.vector, pool=nc.gpsimd, sp=nc.sync)
        self.pending = []
        self.lastc = {}
        self.dma_since = []
        self.bar_deps = {}
        ncs = dict(pe=3, act=3, dve=4, pool=2)
        self.csem = {e: [es.enter_context(nc.semaphore(f"c{e}{i}")) for i in range(ncs[e])] for e in self.CENG}
        self.ccount = {e: 0 for e in self.CENG}
        self.dsem = {q: [es.enter_context(nc.semaphore(f"d{q}{i}")) for i in range(8)] for q in ("sp", "pool")}
        self.dcount = {q: [0] * 8 for q in ("sp", "pool")}
        self.drr = {q: 0 for q in ("sp", "pool")}
        self.waited = {e: {} for e in self.eobj}
        self.nops = 0
        self.nwaits = 0
        self.epoch = 0
        self.pe_mode = "f"
        self.bg_ops = []

    def op(self, eng, fn, *args, reads=(), writes=(), dma=False, force=False, bg=False, **kw):
        o = _Op()
        o.eng, o.fn, o.args, o.kw, o.dma = eng, fn, args, kw, dma
        o.signal = dma or force
        o.sem = None
        o.val = 0
        o.epoch = self.epoch
        deps = {}
        if eng == "pe":
            mode = kw.pop("pemode", "f")
            if mode != self.pe_mode and "pe" in self.lastc:
                deps[id(self.lastc["pe"])] = self.lastc["pe"]
            self.pe_mode = mode

        def need(d, same_ok, bar=False):
            if d.epoch < self.epoch and not bar:
                return
            if same_ok and (not d.dma) and (not dma) and d.eng == eng:
                return
            deps[id(d)] = d

        for b in reads:
            for d in b.w.values():
                need(d, eng == "pe")
        for b in writes:
            if b.r:
                b.g = tuple(b.r.values()) + tuple(b.w.values())
                b.r = {}
                b.w = {}
            for d in b.g:
                need(d, True)
        for d in self.bar_deps.pop(eng, ()):
            need(d, True, True)
        for d in deps.values():
            d.signal = True
        o.deps = list(deps.values())
        key = ("d", id(o)) if dma else eng
        for b in reads:
            b.r[key] = o
        for b in writes:
            b.w[key] = o
        if dma and bg:
            o.epoch = 1 << 60
            self.bg_ops.append(o)
        elif dma:
            self.dma_since.append(o)
        else:
            self.lastc[eng] = o
        self.pending.append(o)
        return o

    def dma(self, q, out, in_, reads=(), writes=(), **kw):
        fn = self.nc.sync.dma_start if q == "sp" else self.nc.gpsimd.dma_start
        return self.op(q, fn, reads=reads, writes=writes, dma=True, out=out, in_=in_, **kw)

    def flush(self):
        for o in self.pending:
            e = self.eobj[o.eng]
            w = self.waited[o.eng]
            for d in o.deps:
                assert d.sem is not None, "dependency on unsignalled op"
                k = id(d.sem)
                if w.get(k, 0) >= d.val:
                    continue
                e.wait_ge(d.sem, d.val)
                w[k] = d.val
                self.nwaits += 1
            ins = o.fn(*o.args, **o.kw)
            self.nops += 1
            if o.dma:
                q = o.eng
                i = self.drr[q]
                self.drr[q] = (i + 1) % 8
                self.dcount[q][i] += 16
                o.sem, o.val = self.dsem[q][i], self.dcount[q][i]
                ins.then_inc(o.sem, 16)
            elif o.signal:
                n = self.ccount[o.eng]
                self.ccount[o.eng] = n + 1
                sems = self.csem[o.eng]
                si = n // self.SEMCAP
                assert si < len(sems), "out of compute semaphores"
                o.sem, o.val = sems[si], n % self.SEMCAP + 1
                ins.then_inc(o.sem, 1)
            o.fn = o.args = o.kw = None
        self.pending = []

    def barrier(self):
        deps = [self.lastc[e] for e in self.CENG if e in self.lastc] + self.dma_since
        for d in deps:
            d.signal = True
        self.flush()
        old = self.bar_deps
        self.bar_deps = {e: list(deps) + list(old.get(e, ())) for e in self.eobj}
        self.dma_since = []
        self.epoch += 1

    def finish(self):
        self.barrier()
        sp = self.nc.sync
        for d in list(self.bar_deps["sp"]) + self.bg_ops:
            k = id(d.sem)
            if self.waited["sp"].get(k, 0) >= d.val:
                continue
            sp.wait_ge(d.sem, d.val)
            self.waited["sp"][k] = d.val


class Rot:
    def __init__(self, tiles):
        self.t = [(t, Buf()) for t in tiles]
        self.i = 0

    def next(self):
        r = self.t[self.i]
        self.i = (self.i + 1) % len(self.t)
        return r


_CONST = {}


def _consts():
    if _CONST:
        return _CONST
    bf = ml_dtypes.bfloat16
    t = np.arange(S, dtype=np.int64)
    m = (t[:, None] * t[None, :]) % S
    ang = 2.0 * np.pi * m.astype(np.float64) / S
    _CONST["C4"] = (np.cos(ang) / 64.0).astype(np.float32).astype(bf)
    _CONST["S4"] = (np.sin(ang) / 64.0).astype(np.float32).astype(bf)
    del ang, m
    t = np.arange(L, dtype=np.int64)
    ang = 2.0 * np.pi * ((t[:, None] * t[None, :]) % L).astype(np.float64) / L
    _CONST["C2"] = (np.cos(ang) / 16.0).astype(np.float32).astype(bf)
    _CONST["S2"] = (np.sin(ang) / 16.0).astype(np.float32).astype(bf)
    t = np.arange(64, dtype=np.int64)
    ang = 2.0 * np.pi * ((t[:, None] * t[None, :]) % 64).astype(np.float64) / 64
    c64 = np.cos(ang) / 8.0
    s64 = np.sin(ang) / 8.0
    z = np.zeros((64, 64))
    _CONST["C64bd"] = np.block([[c64, z], [z, c64]]).astype(np.float32)
    _CONST["S64bdn"] = (-np.block([[s64, z], [z, s64]])).astype(np.float32)
    _CONST["ident"] = np.eye(128, dtype=np.float32).astype(bf)
    bd = np.zeros((128, 128), np.float32)
    bd[:64, :64] = 1.0
    bd[64:, 64:] = 1.0
    _CONST["bdmask"] = bd
    _CONST["bd64"] = (bd / 64.0).astype(np.float32)
    s_ = np.arange(128)[:, None]
    t_ = np.arange(128)[None, :]
    same = (s_ // 32) == (t_ // 32)
    _CONST["mfwd"] = (same & (s_ <= t_)).astype(np.float32)
    _CONST["mbwd"] = (same & (s_ >= t_)).astype(np.float32)
    cm = np.zeros((128, 4, 128), np.float32)
    for c_ in range(4):
        cm[32 * c_:32 * c_ + 32, c_, :] = 1.0
    _CONST["cm3"] = cm.reshape(128, 512)
    return _CONST


def _na_bias_tiles(rpb):
    H = rpb.shape[0]
    kc = np.arange(64)[:, None]
    qc = np.arange(64)[None, :]
    wc = np.clip(qc - 8, 0, 48)
    colvalid = (kc >= wc) & (kc < wc + 16)
    cidx = np.clip(kc - qc + 15, 0, 30)
    specs = [(8, 8 - 4 + 2 * po) for po in range(8)] + [(0, 2 * p) for p in range(6)] + [(56, 52 + 2 * p) for p in range(6)]
    out = np.full((H, 128, 20, 512), NEG, np.float32)
    for ti, (r0, kr0) in enumerate(specs):
        for kr2 in range(2):
            kr = kr0 + kr2
            for qr in range(8):
                r = r0 + qr
                rs = min(max(r - 4, 0), 56)
                if not (rs <= kr < rs + 8):
                    continue
                ridx = kr - r + 7
                blk = np.where(colvalid[None], rpb[:, ridx][:, cidx], NEG)
                out[:, kr2 * 64:(kr2 + 1) * 64, ti, qr * 64:(qr + 1) * 64] = blk
    return out.reshape(H, 128, 20 * 512)


def build(debug=False, stop=None):
    nc = bass.Bass("TRN2", target_bir_lowering=False)
    skind = "ExternalOutput" if debug else "Internal"

    def din(name, shape, dt=F32):
        return nc.dram_tensor(name, list(shape), dt, kind="ExternalInput").ap()

    def dscr(name, shape, dt=F32):
        return nc.dram_tensor(name, list(shape), dt, kind=skind).ap()

    xin = din("xin", [T, D])
    cT = din("cT", [128, 16])
    w_mod = din("w_mod", [DEPTH, D, 6 * D])
    b_mod = din("b_mod", [DEPTH, 6 * D])
    norm1_g = din("norm1_g", [DEPTH, D])
    w_in = din("w_in", [DEPTH, D, INW])
    fourier_w = din("fourier_w", [DEPTH, 4, 64, 64])
    lbT = din("lbT", [128, 8])
    ngT = din("ngT", [128, DEPTH])
    nabias = din("nabias", [DEPTH, 8, 128, 20 * 512])
    w_out = din("w_out", [DEPTH, D, D])
    norm2_g = din("norm2_g", [DEPTH, D])
    w_gate = din("w_gate", [DEPTH, D, DFF])
    w_up = din("w_up", [DEPTH, D, DFF])
    w_down = din("w_down", [DEPTH, DFF, D])
    final_g = din("final_g", [D])
    C4 = din("C4", [S, S], BF16)
    S4 = din("S4", [S, S], BF16)
    C2 = din("C2", [L, L], BF16)
    S2 = din("S2", [L, L], BF16)
    C64bd_d = din("C64bd", [128, 128])
    S64bdn_d = din("S64bdn", [128, 128])
    ident_d = din("ident", [128, 128], BF16)
    bdmask_d = din("bdmask", [128, 128])
    bd64_d = din("bd64", [128, 128])
    mfwd_d = din("mfwd", [128, 128])
    mbwd_d = din("mbwd", [128, 128])
    cm3_d = din("cm3", [128, 512])
    out = nc.dram_tensor("out", [S, D], F32, kind="ExternalOutput").ap()

    mod2d = dscr("mod2d", [2 * DEPTH, 6 * D])
    fz = dscr("fz", [512, T])
    fb = dscr("fb", [1536, T], BF16)
    tm = dscr("tm", [T, 1024], BF16)
    catT = dscr("catT", [1024, T], BF16)
    xres = dscr("xres", [T, D])
    w_in_b = dscr("w_in_b", [DEPTH, D, INW], BF16)
    w_out_b = dscr("w_out_b", [DEPTH, D, D], BF16)
    w_gate_b = dscr("w_gate_b", [DEPTH, D, DFF], BF16)
    w_up_b = dscr("w_up_b", [DEPTH, D, DFF], BF16)
    w_down_b = dscr("w_down_b", [DEPTH, DFF, D], BF16)

    with ExitStack() as ges:
        sch = Sched(nc, ges)
        V, A, P, G = nc.vector, nc.scalar, nc.tensor, nc.gpsimd

        uid = [0]

        def sbt(es, name, shape, dt=F32):
            uid[0] += 1
            return es.enter_context(nc.sbuf_tensor(f"s{uid[0]}_{name}", list(shape), dt))

        def pst(es, name, shape, dt=F32):
            uid[0] += 1
            return es.enter_context(nc.psum_tensor(f"p{uid[0]}_{name}", list(shape), dt))

        ident = sbt(ges, "ident", [128, 128], BF16)
        bdmask = sbt(ges, "bdmask", [128, 128])
        onesb = sbt(ges, "onesb", [128, 64], BF16)
        B_const = Buf()
        sch.dma("sp", ident[:], ident_d, writes=[B_const])
        sch.dma("sp", bdmask[:], bdmask_d, writes=[B_const])
        sch.op("pool", G.memset, onesb[:], 1.0, writes=[B_const], force=True)

        def evac(i, outap, inap, reads, writes, scale=None):
            if i % 2 == 0:
                if scale is None:
                    sch.op("act", A.copy, outap, inap, reads=reads, writes=writes)
                else:
                    sch.op("act", A.mul, outap, inap, scale, reads=reads, writes=writes)
            else:
                if scale is None:
                    sch.op("dve", V.tensor_copy, outap, inap, reads=reads, writes=writes)
                else:
                    sch.op("dve", V.tensor_scalar, outap, inap, scale, None, ALU.mult, reads=reads, writes=writes)

        def phase_mod():
            with ExitStack() as es:
                cTt = sbt(es, "cTt", [128, 16])
                sT = sbt(es, "sT", [128, 16])
                mod2s = sbt(es, "mod2s", [2, 6 * D])
                bt = sbt(es, "bt", [2, 6 * D])
                wm = Rot([sbt(es, f"wm{i}", [128, 8, 512]) for i in range(3)])
                pm = Rot([pst(es, f"pm{i}", [128, 512]) for i in range(2)])
                B_c, B_s, B_m, B_b = Buf(), Buf(), Buf(), Buf()
                sch.dma("sp", cTt[:], cT, writes=[B_c])
                sch.op("act", A.activation, sT[:], cTt[:], AF.Silu, reads=[B_c], writes=[B_s])
                for l in range(DEPTH):
                    sch.dma("sp", bt[:], b_mod[l].partition_broadcast(2), writes=[B_b])
                    for nb in range(12):
                        wt, Bw = wm.next()
                        sch.dma("sp", wt[:], w_mod[l][:, nb * 512:(nb + 1) * 512].rearrange("(k p) n -> p k n", p=128), writes=[Bw])
                        pt, Bp = pm.next()
                        for k in range(8):
                            sch.op("pe", P.matmul, pt[0:2, :], lhsT=sT[:, 2 * k:2 * k + 2], rhs=wt[:, k, :],
                                   start=(k == 0), stop=(k == 7), pemode="m2", reads=[B_s, Bw], writes=[Bp])
                        sch.op("dve", V.tensor_tensor, mod2s[:, nb * 512:(nb + 1) * 512], pt[0:2, :], bt[:, nb * 512:(nb + 1) * 512],
                               ALU.add, reads=[Bp, B_b], writes=[B_m])
                    sch.dma("pool", mod2d[2 * l:2 * l + 2, :], mod2s[:], reads=[B_m])
                sch.barrier()

        def norm_block(es_tiles, xt, Bx, Gt, SHt, B_mod, a_bf, Ba):
            junk, Bj, ssqr, tmpr = es_tiles
            sq, Bs = ssqr.next()
            sch.op("act", A.activation, junk[:], xt[:], AF.Square, accum_out=sq[:, 0:1], reads=[Bx], writes=[Bj, Bs])
            sch.op("act", A.activation, sq[:, 1:2], sq[:, 0:1], AF.Sqrt, bias=EPS, scale=1.0 / D, reads=[Bs], writes=[Bs])
            sch.op("dve", V.reciprocal, sq[:, 2:3], sq[:, 1:2], reads=[Bs], writes=[Bs])
            tp, Bt = tmpr.next()
            sch.op("dve", V.scalar_tensor_tensor, tp[:], xt[:], sq[:, 2:3], Gt[:], ALU.mult, ALU.mult,
                   reads=[Bx, Bs, B_mod], writes=[Bt])
            sch.op("dve", V.tensor_tensor, a_bf[:], tp[:], SHt[:], ALU.add, reads=[Bt, B_mod], writes=[Ba])

        def transpose_block(ptrr, a_bf, Ba, aT, BaT, tb, ei):
            for half in range(2):
                pt, Bp = ptrr.next()
                for j in range(4):
                    c = (half * 4 + j) * 128
                    sch.op("pe", P.transpose, pt[:, j, :], a_bf[:, c:c + 128], ident[:], reads=[Ba, B_const], writes=[Bp])
                evac(ei + half, aT[:, half * 4:(half + 1) * 4, tb * 128:(tb + 1) * 128], pt[:], [Bp], [BaT])

        def load_weight(es, name, src_kpn, nk, ncols, Bsrc):
            wb = sbt(es, name, [128, nk, ncols], BF16)
            Bw = Buf()
            for k in range(nk):
                sch.dma("sp", wb[:, k, :], src_kpn[:, k, :], reads=[Bsrc], writes=[Bw])
            return wb, Bw

        wcast_buf = {}

        def cast_weights(l, names):
            table = {"in": (w_in, w_in_b, D), "out": (w_out, w_out_b, D), "gate": (w_gate, w_gate_b, D), "up": (w_up, w_up_b, D),
                     "down": (w_down, w_down_b, DFF)}
            for nm in names:
                src, dst, R_ = table[nm]
                Bw = Buf()
                wcast_buf[(nm, l)] = Bw
                for r0 in range(0, R_, 128):
                    sch.dma("pool", dst[l][r0:r0 + 128, :], src[l][r0:r0 + 128, :], writes=[Bw], bg=True)

        def pump(*a_, **k_):
            return

        def make_bg(es, n):
            return None

        bgq = []

        def mod_tiles(es, l, v, gsrc, i_sh, i_sc, tag):
            Gt = sbt(es, f"G{tag}", [128, D])
            SHt = sbt(es, f"SH{tag}", [128, D])
            Bm = Buf()
            sch.dma("pool", SHt[:], mod2d[2 * l + v, i_sh * D:(i_sh + 1) * D].partition_broadcast(128), writes=[Bm])
            sch.dma("pool", Gt[:], gsrc.partition_broadcast(128), writes=[Bm])
            return Gt, SHt, Bm

        def phase_ab(l):
            xsrc = xin if l == 0 else xres
            if l >= 1 and bgq:
                with ExitStack() as es0:
                    pump(100000, make_bg(es0, 4), drain=True)
                    sch.barrier()
            with ExitStack() as es:
                wbf, Bw = load_weight(es, "winbf", w_in_b[l].rearrange("(k p) n -> p k n", p=128), 8, INW, wcast_buf[("in", l)])
                mods = []
                sct = sbt(es, "sct", [128, D])
                Bsc = Buf()
                for v in range(2):
                    Gt, SHt, Bm = mod_tiles(es, l, v, norm1_g[l], 0, 1, f"1_{v}")
                    sch.dma("pool", sct[:], mod2d[2 * l + v, D:2 * D].partition_broadcast(128), writes=[Bsc])
                    sch.op("dve", V.scalar_tensor_tensor, Gt[:], sct[:], 1.0, Gt[:], ALU.add, ALU.mult, reads=[Bsc, Bm], writes=[Bm])
                    mods.append((Gt, SHt, Bm))
                junk = sbt(es, "junk", [128, D], BF16)
                nt = (junk, Buf(), Rot([sbt(es, f"ssq{i}", [128, 4]) for i in range(4)]),
                      Rot([sbt(es, f"ntmp{i}", [128, D]) for i in range(2)]))
                xr = Rot([sbt(es, f"xt{i}", [128, D]) for i in range(8)])
                ar = Rot([sbt(es, f"abf{i}", [128, D], BF16) for i in range(2)])
                aTr = Rot([sbt(es, f"aT{i}", [128, 8, 512], BF16) for i in range(2)])
                sfr = Rot([sbt(es, f"sf{i}", [128, 512]) for i in range(2)])
                sbr = Rot([sbt(es, f"sb{i}", [128, 512], BF16) for i in range(3)])
                tmr = Rot([sbt(es, f"tms{i}", [128, 1024], BF16) for i in range(2)])
                ptrr = Rot([pst(es, f"ptr{i}", [128, 4, 128], BF16) for i in range(2)])
                pfr = Rot([pst(es, f"pf{i}", [128, 512]) for i in range(3)])
                ptmr = Rot([pst(es, f"ptm{i}", [128, 512]) for i in range(2)])
                fm = []
                for c0, dst, r0, sc in ((256, fb, 0, None), (512, fz, 0, None), (768, fz, 256, None), (1280, fb, 256, None),
                                        (1536, fb, 512, 0.125), (2048, fb, 1024, None)):
                    wd = 512 if c0 >= 1536 else 256
                    for g in range(wd // 128):
                        fm.append((c0 + g * 128, dst, r0 + g * 128, sc))
                sbs = [(i * 512, 512, 0) for i in range(8)] + [(S, L, 1)]
                eic = [0]
                aTs = {}
                abfs = {}

                xts = {}

                def load_part(i):
                    t0, ntok, v = sbs[i]
                    for tb in range(ntok // 128):
                        xt, Bx = xr.next()
                        sch.dma("sp", xt[:], xsrc[t0 + tb * 128:t0 + (tb + 1) * 128, :], writes=[Bx])
                        xts[(i, tb)] = (xt, Bx)

                def norm_part(i, tb):
                    t0, ntok, v = sbs[i]
                    Gt, SHt, Bm = mods[v]
                    if tb == 0:
                        aTs[i] = aTr.next()
                    xt, Bx = xts.pop((i, tb))
                    a_bf, Ba = ar.next()
                    norm_block(nt, xt, Bx, Gt, SHt, Bm, a_bf, Ba)
                    abfs[(i, tb)] = (a_bf, Ba)

                def tr_part(i, tb):
                    aT, BaT = aTs[i]
                    a_bf, Ba = abfs.pop((i, tb))
                    eic[0] += 2
                    transpose_block(ptrr, a_bf, Ba, aT, BaT, tb, eic[0])

                load_part(0)
                for tb in range(sbs[0][1] // 128):
                    norm_part(0, tb)
                    tr_part(0, tb)
                for i, (t0, ntok, v) in enumerate(sbs):
                    aT, BaT = aTs[i]
                    if i + 1 < len(sbs):
                        load_part(i + 1)
                    groups = []
                    for (c0, dst, r0, sc) in fm:
                        groups.append(("fm", c0, dst, r0, sc))
                    for tb in range(ntok // 128):
                        groups.append(("tm", tb))
                    nxt = {}
                    if i + 1 < len(sbs):
                        ntb2 = sbs[i + 1][1] // 128
                        for tb in range(ntb2):
                            nxt.setdefault(1 + 5 * tb, []).append(("n", tb))
                            nxt.setdefault(4 + 5 * tb, []).append(("t", tb))
                    for gi, g in enumerate(groups):
                        for (kind, tb) in nxt.pop(gi, []):
                            (norm_part if kind == "n" else tr_part)(i + 1, tb)
                        if g[0] == "fm":
                            _, c0, dst, r0, sc = g
                            pf, Bp = pfr.next()
                            for k in range(8):
                                sch.op("pe", P.matmul, pf[:, 0:ntok], lhsT=wbf[:, k, c0:c0 + 128], rhs=aT[:, k, 0:ntok],
                                       start=(k == 0), stop=(k == 7), reads=[Bw, BaT], writes=[Bp])
                            if dst is fz:
                                st, Bs = sfr.next()
                            else:
                                st, Bs = sbr.next()
                            evac(0, st[:, 0:ntok], pf[:, 0:ntok], [Bp], [Bs], scale=sc)
                            sch.dma("sp", dst[r0:r0 + 128, t0:t0 + ntok], st[:, 0:ntok], reads=[Bs])
                        else:
                            tb = g[1]
                            tms, Bts = tmr.next()
                            tsl = slice(tb * 128, (tb + 1) * 128)
                            p0, Bp0 = ptmr.next()
                            for (cc, oc) in ((0, 0), (1024, 256)):
                                for k in range(8):
                                    sch.op("pe", P.matmul, p0[:, oc:oc + 256], lhsT=aT[:, k, tsl], rhs=wbf[:, k, cc:cc + 256],
                                           start=(k == 0), stop=(k == 7), reads=[Bw, BaT], writes=[Bp0])
                            evac(0, tms[:, 0:512], p0[:], [Bp0], [Bts])
                            p1, Bp1 = ptmr.next()
                            for k in range(8):
                                sch.op("pe", P.matmul, p1[:], lhsT=aT[:, k, tsl], rhs=wbf[:, k, 2560:3072],
                                       start=(k == 0), stop=(k == 7), reads=[Bw, BaT], writes=[Bp1])
                            evac(0, tms[:, 512:1024], p1[:], [Bp1], [Bts])
                            sch.dma("sp", tm[t0 + tb * 128:t0 + (tb + 1) * 128, :], tms[:], reads=[Bts])
                    for lst in nxt.values():
                        for (kind, tb) in lst:
                            (norm_part if kind == "n" else tr_part)(i + 1, tb)
                sch.barrier()

        def phase_c(l):
            with ExitStack() as es:
                cbgs = make_bg(es, 4)
                U = sbt(es, "fU", [128, NB, 256], BF16)
                BU = Buf()
                sch.dma("sp", U[:], tm[:, 0:256].rearrange("(k p) c -> p k c", p=128), writes=[BU])
                c64 = sbt(es, "c64", [128, 2, 128])
                wf = sbt(es, "wf", [128, 2, 64])
                Mbd = sbt(es, "Mbd", [128, 3, 2, 128], BF16)
                Bc, Bwf, BM = Buf(), Buf(), Buf()
                sch.dma("sp", c64[:, 0, :], C64bd_d, writes=[Bc])
                sch.dma("sp", c64[:, 1, :], S64bdn_d, writes=[Bc])
                for cb in range(2):
                    sch.dma("sp", wf[:, cb, :], fourier_w[l][2 * cb:2 * cb + 2].rearrange("g c d -> (g c) d"), writes=[Bwf])
                sch.op("pool", G.memset, Mbd[:], 0.0, writes=[BM])
                pmr = Rot([pst(es, f"pM{i}", [128, 512]) for i in range(1)])
                pM, BpM = pmr.next()
                for w in range(2):
                    for cb in range(2):
                        o0 = (w * 2 + cb) * 64
                        sch.op("pe", P.matmul, pM[:, o0:o0 + 64], lhsT=c64[:, w, :], rhs=wf[:, cb, :], start=True, stop=True,
                               pemode="f32", reads=[Bc, Bwf], writes=[BpM])
                for w in range(2):
                    for cb in range(2):
                        o0 = (w * 2 + cb) * 64
                        sch.op("dve", V.tensor_copy, Mbd[0:64, w, cb, 0:64], pM[0:64, o0:o0 + 64], reads=[BpM, BM], writes=[BM])
                        sch.op("dve", V.tensor_copy, Mbd[64:128, w, cb, 64:128], pM[64:128, o0:o0 + 64], reads=[BpM, BM], writes=[BM])
                sch.op("dve", V.tensor_scalar, Mbd[:, 2].rearrange("p c n -> p (c n)"), Mbd[:, 1].rearrange("p c n -> p (c n)"), -1.0, None,
                       ALU.mult, reads=[BM], writes=[BM])
                csr = Rot([sbt(es, f"cs{i}", [128, 2, 4, 512], BF16) for i in range(3)])
                pqr = Rot([sbt(es, f"pq{i}", [128, 4, 512], BF16) for i in range(2)])
                osr = Rot([sbt(es, f"fo{i}", [128, 512], BF16) for i in range(3)])
                pacc = [pst(es, f"pacc{i}", [128, 512]) for i in range(4)]
                Bacc = [Buf() for _ in range(4)]
                por = Rot([pst(es, f"po{i}", [128, 512]) for i in range(2)])
                c0col = sbt(es, "c0col", [128, 2], BF16)
                Bc0 = Buf()
                sch.op("pool", G.memset, c0col[:], 1.0 / 64.0, writes=[Bc0])

                def mix(pq, Bpq, n, tcol0, mirror_col0=None, ncols=None):
                    for cb in range(2):
                        po, Bpo = por.next()
                        sch.op("pe", P.matmul, po[:, 0:n], lhsT=Mbd[:, 0, cb, :], rhs=pq[:, cb, 0:n], start=True, stop=False,
                               reads=[BM, Bpq], writes=[Bpo])
                        sch.op("pe", P.matmul, po[:, 0:n], lhsT=Mbd[:, 1, cb, :], rhs=pq[:, 2 + cb, 0:n], start=False, stop=True,
                               reads=[BM, Bpq], writes=[Bpo])
                        ot, Bo = osr.next()
                        evac(cb, ot[:, 0:n], po[:, 0:n], [Bpo], [Bo])
                        nst = n if ncols is None else ncols
                        if nst == 1:
                            sch.dma("pool", catT[cb * 128:(cb + 1) * 128, tcol0:tcol0 + 1], ot[:, 0:1], reads=[Bo],
                                    allow_slow_non_contiguous=True)
                        else:
                            sch.dma("pool", catT[cb * 128:(cb + 1) * 128, tcol0:tcol0 + nst], ot[:, 0:nst], reads=[Bo])
                        if mirror_col0 is not None:
                            po, Bpo = por.next()
                            for (wm, qi) in ((0, cb), (2, 2 + cb)):
                                bq_ = pq[:, qi, 0:n]
                                rev = bass.AP(tensor=bq_.tensor, offset=bq_.offset + n - 1, ap=[[bq_.ap[0][0], 128], [-1, n]])
                                sch.op("pe", P.matmul, po[:, 0:n], lhsT=Mbd[:, wm, cb, :], rhs=rev, start=(wm == 0), stop=(wm == 2),
                                       reads=[BM, Bpq], writes=[Bpo])
                            ot, Bo = osr.next()
                            evac(cb + 1, ot[:, 0:n], po[:, 0:n], [Bpo], [Bo])
                            sch.dma("pool", catT[cb * 128:(cb + 1) * 128, mirror_col0:mirror_col0 + n], ot[:, 0:n], reads=[Bo])

                for cb in range(2):
                    for k in range(32):
                        sch.op("pe", P.matmul, pacc[cb][:, 0:2], lhsT=U[:, k, cb * 128:(cb + 1) * 128], rhs=c0col[:],
                               start=(k == 0), stop=(k == 31), reads=[BU, Bc0], writes=[Bacc[cb]])
                pq, Bpq = pqr.next()
                sch.op("pool", G.memset, pq[:, 2:4, 0:2], 0.0, writes=[Bpq])
                for cb in range(2):
                    evac(cb, pq[:, cb, 0:2], pacc[cb][:, 0:2], [Bacc[cb], Bpq], [Bpq])
                mix(pq, Bpq, 2, 0, ncols=1)
                for nb in range(4):
                    c_lo = 512 * nb + 1
                    for kg in range(8):
                        cs, Bcs = csr.next()
                        sch.dma("sp", cs[:, 0], C4[kg * 512:(kg + 1) * 512, c_lo:c_lo + 512].rearrange("(k p) n -> p k n", p=128), writes=[Bcs])
                        sch.dma("sp", cs[:, 1], S4[kg * 512:(kg + 1) * 512, c_lo:c_lo + 512].rearrange("(k p) n -> p k n", p=128), writes=[Bcs])
                        for kk in range(4):
                            k = kg * 4 + kk
                            for w in range(2):
                                for cb in range(2):
                                    i = w * 2 + cb
                                    sch.op("pe", P.matmul, pacc[i][:], lhsT=U[:, k, cb * 128:(cb + 1) * 128], rhs=cs[:, w, kk, :],
                                           start=(k == 0), stop=(k == 31), reads=[BU, Bcs], writes=[Bacc[i]])
                    pq, Bpq = pqr.next()
                    for i in range(4):
                        evac(i, pq[:, i, :], pacc[i][:], [Bacc[i]], [Bpq])
                    mix(pq, Bpq, 512, c_lo, mirror_col0=3584 - 512 * nb)
                    if l == 0:
                        pump(5, cbgs, drain=True)
                if l == 0:
                    c2 = sbt(es, "c2", [128, 2, 2, 256], BF16)
                    Bc2 = Buf()
                    sch.dma("sp", c2[:, 0], C2.rearrange("(k p) n -> p k n", p=128), writes=[Bc2])
                    sch.dma("sp", c2[:, 1], S2.rearrange("(k p) n -> p k n", p=128), writes=[Bc2])
                    for w in range(2):
                        for cb in range(2):
                            i = w * 2 + cb
                            for k in range(2):
                                sch.op("pe", P.matmul, pacc[i][:, 0:256], lhsT=U[:, 32 + k, cb * 128:(cb + 1) * 128], rhs=c2[:, w, k, :],
                                       start=(k == 0), stop=(k == 1), reads=[BU, Bc2], writes=[Bacc[i]])
                    pq, Bpq = pqr.next()
                    for i in range(4):
                        evac(i, pq[:, i, 0:256], pacc[i][:, 0:256], [Bacc[i]], [Bpq])
                    mix(pq, Bpq, 256, S)
                sch.barrier()

        def phase_d(l):
            HW = 9 * 128
            NCH = T // 32
            PIECES = [(0, 9), (9, 8), (17, 9), (26, 8)]
            with ExitStack() as es:
                lbt = sbt(es, "lbt", [128, 8])
                lbv = sbt(es, "lbv", [128, 8])
                ngt = sbt(es, "ngt", [128, DEPTH])
                mdir = sbt(es, "mdir", [128, 2, 128])
                bd64 = sbt(es, "bd64", [128, 128])
                cm3 = sbt(es, "cm3", [128, 4, 128])
                epsc = sbt(es, "epsc", [128, 1])
                Bl, Bmd = Buf(), Buf()
                sch.op("pool", G.memset, epsc[:], EPS, writes=[Bmd])
                sch.dma("sp", lbt[:], lbT, writes=[Bl])
                sch.dma("sp", ngt[:], ngT, writes=[Bl])
                sch.dma("sp", mdir[:, 0, :], mfwd_d, writes=[Bmd])
                sch.dma("sp", mdir[:, 1, :], mbwd_d, writes=[Bmd])
                sch.dma("sp", bd64[:], bd64_d, writes=[Bmd])
                sch.dma("sp", cm3[:], cm3_d.rearrange("p (c n) -> p c n", n=128), writes=[Bmd])
                if l == 0:
                    sch.op("pool", G.memset, lbv[:, 0:4], 0.0, reads=[Bl], writes=[Bl])
                    sch.op("pool", G.memset, lbv[:, 4:8], 1.0, reads=[Bl], writes=[Bl])
                else:
                    ex = sbt(es, "lbex", [128, 8])
                    sch.op("act", A.activation, ex[:], lbt[:], AF.Exp, reads=[Bl], writes=[Bl])
                    sch.op("dve", V.tensor_tensor, lbv[:, 4:8], ex[:, 0:4], ex[:, 4:8], ALU.add, reads=[Bl], writes=[Bl])
                    sch.op("dve", V.reciprocal, lbv[:, 4:8], lbv[:, 4:8], reads=[Bl], writes=[Bl])
                    sch.op("dve", V.tensor_tensor, lbv[:, 0:4], ex[:, 4:8], lbv[:, 4:8], ALU.mult, reads=[Bl], writes=[Bl])
                    sch.op("dve", V.tensor_scalar, lbv[:, 4:8], lbv[:, 0:4], -1.0, 1.0, ALU.mult, ALU.add, reads=[Bl], writes=[Bl])
                for hp in range(2):
                    with ExitStack() as hs:
                        vtm = sbt(hs, "hv", [128, NB, 128], BF16)
                        vm = sbt(hs, "hvm", [128, 2, NB, 128], BF16)
                        Bv, Bvm = Buf(), Buf()
                        sch.dma("sp", vtm[:], tm[:, 256 + hp * 128:256 + (hp + 1) * 128].rearrange("(k p) c -> p k c", p=128), writes=[Bv])
                        sch.op("dve", V.memset, vm[:], 0.0, writes=[Bvm])
                        sch.op("dve", V.tensor_copy, vm[:, 0, :, 0:64], vtm[:, :, 0:64], reads=[Bv, Bvm], writes=[Bvm])
                        sch.op("dve", V.tensor_copy, vm[:, 1, :, 64:128], vtm[:, :, 64:128], reads=[Bv, Bvm], writes=[Bvm])
                        chains = []
                        for d in range(2):
                            ch = dict(d=d)
                            ch["qdm"] = sbt(hs, f"qdm{d}", [128, 2, T], BF16)
                            ch["kd"] = sbt(hs, f"kd{d}", [128, T], BF16)
                            ch["kst"] = sbt(hs, f"kst{d}", [128, NB, 128], BF16)
                            ch["dec"] = sbt(hs, f"dec{d}", [128, NCH])
                            ch["B"] = {k: Buf() for k in ("qdm", "kd", "kst", "dec", "o")}
                            ch["Bz"] = Buf()
                            sch.op("dve" if d == 0 else "act", (V.memset if d == 0 else A.memzero), ch["qdm"][:], *((0.0,) if d == 0 else ()),
                                   writes=[ch["Bz"]])
                            chains.append(ch)
                        with ExitStack() as gs:
                            Asets = [([sbt(gs, f"gA{u}{i}", [128, HW]) for i in range(4)], [Buf() for _ in range(4)],
                                      sbt(gs, f"gq{u}", [128, HW], BF16), Buf(), sbt(gs, f"gks{u}", [128, HW], BF16), Buf(),
                                      sbt(gs, f"gtot{u}", [128, HW // 32]), Buf()) for u in range(4)]
                            unit = 0
                            msk = sbt(gs, "gmsk", [128, HW])
                            Bmsk = Buf()
                            sch.op("pool", G.memset, msk[:], 1.0, writes=[Bmsk])
                            sch.op("pool", G.memset, msk[:].rearrange("p (n c) -> p n c", c=32)[:, :, 0:1], 0.0, reads=[Bmsk], writes=[Bmsk])
                            ptk = Rot([pst(gs, f"ptk{i}", [128, 4, 128], BF16) for i in range(2)])
                            ei_ = [0]

                            def unit_gen(ch, h, uset):
                                d = ch["d"]
                                B = ch["B"]
                                lbc = lbv[:, d * 2 + hp:d * 2 + hp + 1]
                                omc = lbv[:, 4 + d * 2 + hp:4 + d * 2 + hp + 1]
                                Ar_, BA, qh_, Bq, ksh_, Bks, tot_, Btot = uset
                                b0, nbh = PIECES[h]
                                W_ = nbh * 128
                                c0 = b0 * 128
                                cs = slice(c0, c0 + W_)
                                A1, A2, A3, A4 = [a_[:, 0:W_] for a_ in Ar_]
                                qh = qh_[:, 0:W_]
                                ksh = ksh_[:, 0:W_]
                                tot = tot_[:, 0:W_ // 32]
                                mskv = msk[:, 0:W_]
                                B1, B2, B3, B4 = BA
                                zr = 256 * d + hp * 128
                                sch.dma("sp", A1, fz[zr:zr + 128, cs], writes=[B1])
                                sch.dma("sp", qh, fb[hp * 128:(hp + 1) * 128, cs], writes=[Bq])
                                sch.op("act", A.activation, A1, A1, AF.Sigmoid, reads=[B1], writes=[B1])
                                yield
                                sch.op("act", A.activation, A1, A1, AF.Identity, bias=lbc, scale=omc, reads=[B1, Bl], writes=[B1])
                                yield
                                sch.op("act", A.activation, A2, A1, AF.Ln, reads=[B1], writes=[B2])
                                yield
                                sch.op("act", A.activation, A1, A1, AF.Identity, bias=1.0, scale=-1.0, reads=[B1, B2], writes=[B1])
                                sch.op("dve", V.tensor_tensor_scan, A3, mskv, A2, 0.0, ALU.mult, ALU.add, reads=[Bmsk, B2], writes=[B3])
                                yield
                                A3v = A3.rearrange("p (n c) -> p n c", c=32)
                                A2v = A2.rearrange("p (n c) -> p n c", c=32)
                                totb = tot.rearrange("p (n o) -> p n o", o=1).broadcast_to([128, W_ // 32, 32])
                                sch.op("act", A.copy, tot, A3v[:, :, 31], reads=[B3], writes=[Btot])
                                yield
                                if d == 0:
                                    sch.op("dve", V.tensor_tensor, A2v, A3v, totb, ALU.subtract, reads=[Btot, B3, B2], writes=[B2])
                                    asg, rsg = 1.0, -1.0
                                else:
                                    sch.op("dve", V.tensor_tensor, A2, A3, A2, ALU.subtract, reads=[B3, B2], writes=[B2])
                                    yield
                                    sch.op("dve", V.tensor_tensor, A3v, A2v, totb, ALU.subtract, reads=[Btot, B2, B3], writes=[B3])
                                    asg, rsg = -1.0, 1.0
                                yield
                                sch.op("act", A.activation, ch["dec"][:, c0 // 32:(c0 + W_) // 32], tot, AF.Exp, reads=[Btot], writes=[B["dec"]])
                                sch.op("act", A.activation, A4, A3, AF.Exp, scale=asg, reads=[B3, B4], writes=[B4])
                                yield
                                sch.op("dve", V.tensor_tensor, ch["qdm"][0:64, 0, cs], qh[0:64], A4[0:64], ALU.mult,
                                       reads=[Bq, B4, ch["Bz"]], writes=[B["qdm"]])
                                sch.op("dve", V.tensor_tensor, ch["qdm"][64:128, 1, cs], qh[64:128], A4[64:128], ALU.mult,
                                       reads=[Bq, B4, ch["Bz"]], writes=[B["qdm"]])
                                yield
                                sch.op("act", A.activation, A4, A3, AF.Exp, scale=-asg, reads=[B3, B4], writes=[B4])
                                yield
                                sch.op("dve", V.tensor_tensor, ch["kd"][:, cs], A1, A4, ALU.mult, reads=[B1, B4], writes=[B["kd"]])
                                yield
                                sch.op("act", A.activation, A4, A2, AF.Exp, scale=rsg, reads=[B2, B4], writes=[B4])
                                yield
                                sch.op("dve", V.tensor_tensor, ksh, A1, A4, ALU.mult, reads=[B1, B4], writes=[Bks])
                                yield
                                for b4 in range(0, nbh, 4):
                                    nn = min(4, nbh - b4)
                                    pt, Bp = ptk.next()
                                    for j in range(nn):
                                        sch.op("pe", P.transpose, pt[:, j, :], ksh[:, (b4 + j) * 128:(b4 + j + 1) * 128], ident[:],
                                               reads=[Bks, B_const], writes=[Bp])
                                    ei_[0] += 1
                                    evac(ei_[0], ch["kst"][:, b0 + b4:b0 + b4 + nn, :], pt[:, 0:nn, :], [Bp], [B["kst"]])
                                    yield

                            units = [(ch, h) for ch in chains for h in range(4)]
                            gens = []
                            SHIFT = 4
                            active = []
                            ui = 0
                            tick = 0
                            while ui < len(units) or active:
                                if ui < len(units) and len(active) < 4 and (not active or active[-1][1] >= SHIFT):
                                    ch_, h_ = units[ui]
                                    active.append([unit_gen(ch_, h_, Asets[ui % 4]), 0])
                                    ui += 1
                                for a_ in list(active):
                                    try:
                                        next(a_[0])
                                        a_[1] += 1
                                    except StopIteration:
                                        active.remove(a_)
                            sch.barrier()
                        for ch in chains:
                            ch["o"] = sbt(hs, f"o{ch['d']}", [128, T])
                        with ExitStack() as cs_:
                            vexa = sbt(cs_, "vexa", [128, NB, 4, 128], BF16)
                            Bvex = Buf()
                            for nb in range(NB):
                                sch.op("dve", V.tensor_tensor, vexa[:, nb], cm3[:], vtm[:, nb:nb + 1, :].broadcast_to([128, 4, 128]), ALU.mult,
                                       reads=[Bmd, Bv], writes=[Bvex])
                            for ch in chains:
                                d = ch["d"]
                                ch["psU"] = pst(cs_, f"psU{d}", [128, 4, 128]); ch["BpsU"] = Buf()
                                ch["psS"] = [pst(cs_, f"psS{d}{i}", [128, 128]) for i in range(2)]; ch["BpsS"] = [Buf(), Buf()]
                                ch["psO"] = pst(cs_, f"psO{d}", [128, 128]); ch["BpsO"] = Buf()
                                ch["Sf"] = Rot([sbt(cs_, f"Sf{d}{i}", [128, 128]) for i in range(2)])
                                ch["Spad"] = Rot([sbt(cs_, f"Sp{d}{i}", [128, 2, 128], BF16) for i in range(10)])
                                for (tl, Bt_) in ch["Spad"].t:
                                    sch.op("dve", V.memset, tl[:], 0.0, writes=[Bt_])
                                ch["scb"] = Rot([sbt(cs_, f"scb{d}{i}", [128, 2, 128], BF16) for i in range(2)])
                                ch["order"] = ([32, 33] + list(range(32))) if d == 0 else ([33, 32] + list(range(31, -1, -1)))
                                ch["corder"] = [0, 1, 2, 3] if d == 0 else [3, 2, 1, 0]
                                ch["prev"] = None
                                ch["prev_bd"] = None
                                ch["pending"] = None

                            def front_all(step):
                                res = []
                                for ch in chains:
                                    nb = ch["order"][step]
                                    B = ch["B"]
                                    sch.op("pe", P.matmul, ch["psU"][:].rearrange("p c n -> p (c n)"), lhsT=ch["kst"][:, nb, :],
                                           rhs=vexa[:, nb].rearrange("p c n -> p (c n)"), start=True, stop=True,
                                           reads=[B["kst"], Bvex], writes=[ch["BpsU"]])
                                    ch["before"] = []
                                for ci in range(4):
                                    for k_, ch in enumerate(chains):
                                        nb = ch["order"][step]
                                        B = ch["B"]
                                        c = ch["corder"][ci]
                                        n = nb * 4 + c
                                        ch["before"].append((c, ch["prev_bd"]))
                                        sf, Bsf = ch["Sf"].next()
                                        if ch["prev"] is None:
                                            sch.op("dve", V.tensor_copy, sf[:], ch["psU"][:, c, :], reads=[ch["BpsU"]], writes=[Bsf])
                                        else:
                                            pv, Bpv = ch["prev"]
                                            sch.op("dve", V.scalar_tensor_tensor, sf[:], pv[:], ch["dec"][:, n:n + 1], ch["psU"][:, c, :],
                                                   ALU.mult, ALU.add, reads=[Bpv, B["dec"], ch["BpsU"]], writes=[Bsf])
                                        ch["prev"] = (sf, Bsf)
                                        sp_, Bsp = ch["Spad"].next()
                                        base = sp_[:]
                                        ov = bass.AP(tensor=base.tensor, offset=base.offset, ap=[[base.ap[0][0], 128], [192, 2], [1, 64]])
                                        iv = sf[:].rearrange("p (g c) -> p g c", g=2)
                                        sch.op("act", A.copy, ov, iv, reads=[Bsf, Bsp], writes=[Bsp])
                                        ch["prev_bd"] = (sp_, Bsp)
                                for ch in chains:
                                    nb = ch["order"][step]
                                    B = ch["B"]
                                    cols = slice(nb * 128, (nb + 1) * 128)
                                    scb, Bscb = ch["scb"].next()
                                    for hh in range(2):
                                        sch.op("pe", P.matmul, ch["psS"][hh][:], lhsT=ch["kd"][:, cols], rhs=ch["qdm"][:, hh, cols],
                                               start=True, stop=True, reads=[B["kd"], B["qdm"]], writes=[ch["BpsS"][hh]])
                                        sch.op("dve", V.tensor_tensor, scb[:, hh, :], ch["psS"][hh][:], mdir[:, ch["d"], :], ALU.mult,
                                               reads=[ch["BpsS"][hh], Bmd], writes=[Bscb])
                                    res.append((nb, ch["before"], scb, Bscb))
                                return res

                            def back(ch, pend):
                                nb, before, scb, Bscb = pend
                                B = ch["B"]
                                cols = slice(nb * 128, (nb + 1) * 128)
                                inter = [(c, s_) for (c, s_) in before if s_ is not None]
                                for hh in range(2):
                                    sch.op("pe", P.matmul, ch["psO"][:], lhsT=vm[:, hh, nb, :], rhs=scb[:, hh, :], start=(hh == 0),
                                           stop=(hh == 1 and not inter), reads=[Bvm, Bscb], writes=[ch["BpsO"]])
                                for i, (c, (sp_, Bsp)) in enumerate(inter):
                                    for hh in range(2):
                                        sch.op("pe", P.matmul, ch["psO"][:, 32 * c:32 * c + 32], lhsT=sp_[:, hh, :],
                                               rhs=ch["qdm"][:, hh, nb * 128 + 32 * c:nb * 128 + 32 * c + 32], start=False,
                                               stop=(i == len(inter) - 1 and hh == 1), reads=[Bsp, B["qdm"]], writes=[ch["BpsO"]])
                                sch.op("act", A.copy, ch["o"][:, cols], ch["psO"][:], reads=[ch["BpsO"]], writes=[B["o"]])

                            for step in range(NB + 1):
                                if step == 1:
                                    if l == 0 and hp == 0:
                                        cast_weights(0, ["out", "gate", "up"])
                                    elif l == 0 and hp == 1:
                                        cast_weights(0, ["down"])
                                        cast_weights(1, ["in", "out"])
                                    elif l == 1 and hp == 0:
                                        cast_weights(1, ["gate", "up"])
                                    elif l == 1 and hp == 1:
                                        cast_weights(1, ["down"])
                                newp = None
                                if step < NB:
                                    newp = front_all(step)
                                if step >= 1:
                                    for ch in chains:
                                        back(ch, ch["pending"])
                                if step < NB:
                                    for ch, p_ in zip(chains, newp):
                                        ch["pending"] = p_
                            sch.barrier()
                        with ExitStack() as rs:
                            of, ob = chains[0]["o"], chains[1]["o"]
                            Bof, Bob = chains[0]["B"]["o"], chains[1]["B"]["o"]
                            sch.op("dve", V.tensor_tensor, of[:], of[:], ob[:], ALU.add, reads=[Bof, Bob], writes=[Bof])
                            gt = sbt(rs, "hg", [128, T], BF16)
                            Bg = Buf()
                            sch.dma("sp", gt[:], fb[256 + hp * 128:256 + (hp + 1) * 128, :], writes=[Bg])
                            sqa = sbt(rs, "hsqa", [128, T])
                            sga = sbt(rs, "hsga", [128, T])
                            Bsq, Bsg = Buf(), Buf()
                            rr = Rot([sbt(rs, f"hr{i}", [128, 512]) for i in range(2)])
                            orr = Rot([sbt(rs, f"hob{i}", [128, 512], BF16) for i in range(2)])
                            pms = Rot([pst(rs, f"pms{i}", [128, 512]) for i in range(2)])
                            for t0 in range(0, T, 1088):
                                sch.op("act", A.activation, sga[:, t0:t0 + 1088], gt[:, t0:t0 + 1088], AF.Silu, reads=[Bg], writes=[Bsg])
                                sch.op("dve", V.tensor_tensor, sqa[:, t0:t0 + 1088], of[:, t0:t0 + 1088], of[:, t0:t0 + 1088], ALU.mult,
                                       reads=[Bof], writes=[Bsq])
                            for t0 in range(0, T, 512):
                                n = min(512, T - t0)
                                cs = slice(t0, t0 + n)
                                pm_, Bpm = pms.next()
                                sch.op("pe", P.matmul, pm_[:, 0:n], lhsT=bd64[:], rhs=sqa[:, cs], start=True, stop=True, pemode="f32",
                                       reads=[Bmd, Bsq], writes=[Bpm])
                                r_, Br = rr.next()
                                sch.op("act", A.activation, r_[:, 0:n], pm_[:, 0:n], AF.Ln, bias=epsc[:, 0:1], scale=1.0, reads=[Bpm, Bmd], writes=[Br])
                                sch.op("act", A.activation, r_[:, 0:n], r_[:, 0:n], AF.Exp, scale=-0.5, reads=[Br], writes=[Br])
                                sch.op("dve", V.tensor_tensor, r_[:, 0:n], r_[:, 0:n], of[:, cs], ALU.mult, reads=[Br, Bof], writes=[Br])
                                ob_, Bo = orr.next()
                                sch.op("dve", V.scalar_tensor_tensor, ob_[:, 0:n], r_[:, 0:n], ngt[:, l:l + 1], sga[:, cs], ALU.mult, ALU.mult,
                                       reads=[Br, Bl, Bsg], writes=[Bo])
                                sch.dma("pool", catT[256 + hp * 128:256 + (hp + 1) * 128, cs], ob_[:, 0:n], reads=[Bo])
                            sch.barrier()

        def phase_e(l):
            with ExitStack() as es:
                sets = []
                for u in range(2):
                    qm = sbt(es, f"naq{u}", [128, 2, T], BF16)
                    kT = sbt(es, f"nak{u}", [128, T], BF16)
                    vm = sbt(es, f"nav{u}", [128, 2, NB, 128], BF16)
                    Bq, Bk, Bv = Buf(), Buf(), Buf()
                    sch.op("pool", G.memset, qm[:], 0.0, writes=[Bq])
                    sch.op("pool", G.memset, vm[:], 0.0, writes=[Bv])
                    sets.append((qm, kT, vm, Bq, Bk, Bv))
                bg_rng = ((8, 14), (0, 8), (14, 20))
                bias = [[sbt(es, f"nab{hh}{g}", [128, bg_rng[g][1] - bg_rng[g][0], 512]) for g in range(3)] for hh in range(2)]
                Bb = [[Buf() for g in range(3)] for hh in range(2)]
                oT = sbt(es, "nao", [128, T], BF16)
                oneh = sbt(es, "oneh", [128, 2, 128], BF16)
                sbr = Rot([sbt(es, f"nas{i}", [128, 512]) for i in range(4)])
                pr = Rot([sbt(es, f"nap{i}", [128, 512], BF16) for i in range(6)])
                rcr = Rot([sbt(es, f"narc{i}", [128, 512]) for i in range(2)])
                psr = Rot([pst(es, f"naS{i}", [128, 512]) for i in range(4)])
                numr = Rot([pst(es, f"naN{i}", [128, 512]) for i in range(2)])
                denr = Rot([pst(es, f"naD{i}", [128, 512]) for i in range(2)])
                Bo, B1 = Buf(), Buf()
                sch.op("pool", G.memset, oneh[:], 0.0, writes=[B1])
                for hh in range(2):
                    sch.op("pool", G.memset, oneh[:, hh, hh * 64:(hh + 1) * 64], 1.0, reads=[B1], writes=[B1])
                LOOK = 3
                ebgs = make_bg(es, 4)
                if l == 0:
                    pump(40, ebgs, drain=True)

                def load_qkv(hp):
                    qm, kT, vm, Bq, Bk, Bv = sets[hp % 2]
                    for hh in range(2):
                        r0 = 512 + hp * 128 + hh * 64
                        sch.dma("sp", qm[hh * 64:(hh + 1) * 64, hh, :], fb[r0:r0 + 64, :], reads=[Bq], writes=[Bq])
                    sch.dma("sp", kT[:], fb[1024 + hp * 128:1024 + (hp + 1) * 128, :], writes=[Bk])
                    for hh in range(2):
                        c0 = 512 + hp * 128 + hh * 64
                        sch.dma("sp", vm[:, hh, :, hh * 64:(hh + 1) * 64], tm[:, c0:c0 + 64].rearrange("(k p) c -> p k c", p=128),
                                reads=[Bv], writes=[Bv])

                def load_bias(hp, g):
                    t0_, t1_ = bg_rng[g]
                    for hh in range(2):
                        sch.dma("sp", bias[hh][g][:], nabias[l, hp * 2 + hh].rearrange("p (t n) -> p t n", n=512)[:, t0_:t1_, :],
                                writes=[Bb[hh][g]])

                load_qkv(0)
                for g in range(3):
                    load_bias(0, g)
                for hp in range(4):
                    qm, kT, vm, Bq, Bk, Bv = sets[hp % 2]
                    if hp + 1 < 4:
                        load_qkv(hp + 1)
                    items = []
                    for qb in range(8):
                        if qb == 0:
                            kts = [(2 * p * 64, (0, p)) for p in range(6)]
                        elif qb == 7:
                            kts = [((52 + 2 * p) * 64, (2, p)) for p in range(6)]
                        else:
                            kts = [((8 * qb - 4 + 2 * po) * 64, (1, po)) for po in range(8)]
                        kts += [(S, None), (S + 128, None)]
                        items.append((qb * 512, 512, kts, qb))
                    if l == 0:
                        items.append((S, L, [(S, None), (S + 128, None)], 8))
                    for (q0, nq, kts, qb) in items:
                        flat = [(hh, kt0, bi) for hh in range(2) for (kt0, bi) in kts]
                        prod = [None] * len(flat)
                        num, Bn = numr.next()
                        den, Bd = denr.next()
                        for i in range(len(flat) + LOOK):
                            if i < len(flat):
                                hh, kt0, bi = flat[i]
                                ps, Bps = psr.next()
                                sch.op("pe", P.matmul, ps[:, 0:nq], lhsT=kT[:, kt0:kt0 + 128], rhs=qm[:, hh, q0:q0 + nq], start=True, stop=True,
                                       reads=[Bk, Bq], writes=[Bps])
                                pt, Bpt = pr.next()
                                if bi is None:
                                    sch.op("act", A.activation, pt[:, 0:nq], ps[:, 0:nq], AF.Exp, reads=[Bps], writes=[Bpt])
                                else:
                                    st, Bst = sbr.next()
                                    sch.op("dve", V.tensor_tensor, st[:, 0:nq], ps[:, 0:nq], bias[hh][bi[0]][:, bi[1], 0:nq], ALU.add,
                                           reads=[Bps, Bb[hh][bi[0]]], writes=[Bst])
                                    sch.op("act", A.activation, pt[:, 0:nq], st[:, 0:nq], AF.Exp, reads=[Bst], writes=[Bpt])
                                prod[i] = (pt, Bpt)
                            j = i - LOOK
                            if j >= 0:
                                hh, kt0, bi = flat[j]
                                pt, Bpt = prod[j]
                                first = (j == 0)
                                last = (j == len(flat) - 1)
                                sch.op("pe", P.matmul, num[:, 0:nq], lhsT=vm[:, hh, kt0 // 128, :], rhs=pt[:, 0:nq],
                                       start=first, stop=last, reads=[Bv, Bpt], writes=[Bn])
                                sch.op("pe", P.matmul, den[:, 0:nq], lhsT=oneh[:, hh, :], rhs=pt[:, 0:nq],
                                       start=first, stop=last, reads=[B1, Bpt], writes=[Bd])
                        rc, Brc = rcr.next()
                        sch.op("dve", V.reciprocal, rc[:, 0:nq], den[:, 0:nq], reads=[Bd], writes=[Brc])
                        sch.op("dve", V.tensor_tensor, oT[:, q0:q0 + nq], num[:, 0:nq], rc[:, 0:nq], ALU.mult, reads=[Bn, Brc], writes=[Bo])
                        if hp + 1 < 4:
                            if qb == 0:
                                load_bias(hp + 1, 0)
                            elif qb == 6:
                                load_bias(hp + 1, 1)
                            elif qb == 7:
                                load_bias(hp + 1, 2)
                    ncol = T if l == 0 else S
                    sch.dma("pool", catT[512 + hp * 128:512 + (hp + 1) * 128, 0:ncol], oT[:, 0:ncol], reads=[Bo])
                    if l == 0 and hp < 3:
                        pump(40, ebgs, drain=True)
                sch.barrier()

        def phase_f1(l):
            xsrc = xin if l == 0 else xres
            with ExitStack() as es:
                wob, Bw = load_weight(es, "woutbf", w_out_b[l].rearrange("(k p) n -> p k n", p=128), 8, D, wcast_buf[("out", l)])
                g1 = [sbt(es, f"g1_{v}", [128, D]) for v in range(2)]
                Bg = Buf()
                for v in range(2):
                    sch.dma("pool", g1[v][:], mod2d[2 * l + v, 2 * D:3 * D].partition_broadcast(128), writes=[Bg])
                fbgs = make_bg(es, 4)
                cr = Rot([sbt(es, f"cat{i}", [128, 8, 512], BF16) for i in range(2)])
                xr = Rot([sbt(es, f"xt{i}", [128, D]) for i in range(3)])
                yr = Rot([sbt(es, f"yt{i}", [128, D]) for i in range(2)])
                pyr = Rot([pst(es, f"py{i}", [128, 512]) for i in range(4)])
                sbs = [(i * 512, 512, 0) for i in range(8)] + ([(S, L, 1)] if l == 0 else [])
                for (t0, ntok, v) in sbs:
                    ct, Bc = cr.next()
                    sch.dma("sp", ct[:, :, 0:ntok], catT[:, t0:t0 + ntok].rearrange("(k p) t -> p k t", p=128), writes=[Bc])
                    for tb in range(ntok // 128):
                        xt, Bx = xr.next()
                        rows = slice(t0 + tb * 128, t0 + (tb + 1) * 128)
                        sch.dma("sp", xt[:], xsrc[rows, :], writes=[Bx])
                        yt, By = yr.next()
                        for nb in range(2):
                            py, Bp = pyr.next()
                            for k in range(8):
                                sch.op("pe", P.matmul, py[:], lhsT=ct[:, k, tb * 128:(tb + 1) * 128], rhs=wob[:, k, nb * 512:(nb + 1) * 512],
                                       start=(k == 0), stop=(k == 7), reads=[Bc, Bw], writes=[Bp])
                            sch.op("dve", V.tensor_tensor, yt[:, nb * 512:(nb + 1) * 512], py[:], g1[v][:, nb * 512:(nb + 1) * 512], ALU.mult,
                                   reads=[Bp, Bg], writes=[By])
                        sch.op("dve", V.tensor_tensor, yt[:], yt[:], xt[:], ALU.add, reads=[By, Bx], writes=[By])
                        sch.dma("pool", xres[rows, :], yt[:], reads=[By])
                        if l == 0:
                            pump(2, fbgs)
                pump(0, fbgs, drain=True)
                sch.barrier()

        def phase_f2(l):
            last = (l == DEPTH - 1)
            with ExitStack() as es:
                wg = sbt(es, "wgbf", [128, 8, DFF], BF16)
                wu = sbt(es, "wubf", [128, 8, DFF], BF16)
                NWC = 4
                WCW = DFF // NWC
                Bwg_c = [Buf() for _ in range(NWC)]
                Bwu_c = [Buf() for _ in range(NWC)]
                for c_ in range(NWC):
                    csl = slice(c_ * WCW, (c_ + 1) * WCW)
                    sch.dma("sp", wg[:, :, csl], w_gate_b[l][:, csl].rearrange("(k p) n -> p k n", p=128),
                            reads=[wcast_buf[("gate", l)]], writes=[Bwg_c[c_]])
                    sch.dma("sp", wu[:, :, csl], w_up_b[l][:, csl].rearrange("(k p) n -> p k n", p=128),
                            reads=[wcast_buf[("up", l)]], writes=[Bwu_c[c_]])
                wd, Bwd = load_weight(es, "wdbf", w_down_b[l].rearrange("(j p) n -> p j n", p=128), NJ, D, wcast_buf[("down", l)])
                junk = sbt(es, "junk", [128, D], BF16)
                nt = (junk, Buf(), Rot([sbt(es, f"ssq{i}", [128, 4]) for i in range(4)]),
                      Rot([sbt(es, f"ntmp{i}", [128, D]) for i in range(1)]))
                Gt = sbt(es, "G2", [128, D])
                SHt = sbt(es, "SH2", [128, D])
                g2 = sbt(es, "g2", [128, D])
                Bm = Buf()

                def load_mods(v):
                    sct, Bsc = nt[3].next()
                    sch.dma("pool", SHt[:], mod2d[2 * l + v, 3 * D:4 * D].partition_broadcast(128), writes=[Bm])
                    sch.dma("pool", Gt[:], norm2_g[l].partition_broadcast(128), writes=[Bm])
                    sch.dma("pool", g2[:], mod2d[2 * l + v, 5 * D:6 * D].partition_broadcast(128), writes=[Bm])
                    sch.dma("pool", sct[:], mod2d[2 * l + v, 4 * D:5 * D].partition_broadcast(128), writes=[Bsc])
                    sch.op("dve", V.scalar_tensor_tensor, Gt[:], sct[:], 1.0, Gt[:], ALU.add, ALU.mult, reads=[Bsc, Bm], writes=[Bm])
                fg = None
                if last:
                    fg = sbt(es, "fg", [128, D])
                    Bfg = Buf()
                    sch.dma("pool", fg[:], final_g.partition_broadcast(128), writes=[Bfg])
                x1 = [(sbt(es, f"x1_{i}", [128, D]), Buf()) for i in range(4)]
                ar = Rot([sbt(es, f"abf{i}", [128, D], BF16) for i in range(1)])
                aT = sbt(es, "aT", [128, 8, 512], BF16)
                BaT = Buf()
                hT = sbt(es, "hT", [128, NJ, 512], BF16)
                BhT = Buf()
                sgr = Rot([sbt(es, f"sg{i}", [128, 512]) for i in range(1)])
                ptrr = Rot([pst(es, f"ptr{i}", [128, 4, 128], BF16) for i in range(1)])
                pgr = Rot([pst(es, f"pg{i}", [128, 512]) for i in range(2)])
                pur = Rot([pst(es, f"pu{i}", [128, 512]) for i in range(2)])
                pdr = Rot([pst(es, f"pd{i}", [128, 512]) for i in range(3)])
                sbs = [(i * 512, 512, 0) for i in range(8)] + ([(S, L, 1)] if l == 0 else [])
                curv = None
                for (t0, ntok, v) in sbs:
                    if v != curv:
                        load_mods(v)
                        curv = v
                    ntb = ntok // 128
                    for tb in range(ntb):
                        xt, Bx = x1[tb]
                        sch.dma("sp", xt[:], xres[t0 + tb * 128:t0 + (tb + 1) * 128, :], writes=[Bx])
                        a_bf, Ba = ar.next()
                        norm_block(nt, xt, Bx, Gt, SHt, Bm, a_bf, Ba)
                        transpose_block(ptrr, a_bf, Ba, aT, BaT, tb, tb)
                    for j in range(NJ):
                        pg, Bpg = pgr.next()
                        pu, Bpu = pur.next()
                        for k in range(8):
                            sch.op("pe", P.matmul, pg[:, 0:ntok], lhsT=wg[:, k, j * 128:(j + 1) * 128], rhs=aT[:, k, 0:ntok],
                                   start=(k == 0), stop=(k == 7),
                                   reads=[Bwg_c[(j * 128) // WCW], Bwg_c[min(NWC - 1, (j * 128 + 127) // WCW)], BaT], writes=[Bpg])
                        for k in range(8):
                            sch.op("pe", P.matmul, pu[:, 0:ntok], lhsT=wu[:, k, j * 128:(j + 1) * 128], rhs=aT[:, k, 0:ntok],
                                   start=(k == 0), stop=(k == 7),
                                   reads=[Bwu_c[(j * 128) // WCW], Bwu_c[min(NWC - 1, (j * 128 + 127) // WCW)], BaT], writes=[Bpu])
                        sg, Bsg = sgr.next()
                        sch.op("act", A.activation, sg[:, 0:ntok], pg[:, 0:ntok], AF.Silu, reads=[Bpg], writes=[Bsg])
                        sch.op("dve", V.tensor_tensor, hT[:, j, 0:ntok], sg[:, 0:ntok], pu[:, 0:ntok], ALU.mult, reads=[Bsg, Bpu], writes=[BhT])
                    for tb in range(ntb):
                        xt, Bx = x1[tb]
                        rows = slice(t0 + tb * 128, t0 + (tb + 1) * 128)
                        tp, Bt = nt[3].next()
                        for nb in range(2):
                            pd, Bpd = pdr.next()
                            for j in range(NJ):
                                sch.op("pe", P.matmul, pd[:], lhsT=hT[:, j, tb * 128:(tb + 1) * 128], rhs=wd[:, j, nb * 512:(nb + 1) * 512],
                                       start=(j == 0), stop=(j == NJ - 1), reads=[BhT, Bwd], writes=[Bpd])
                            sch.op("dve", V.tensor_tensor, tp[:, nb * 512:(nb + 1) * 512], pd[:], g2[:, nb * 512:(nb + 1) * 512], ALU.mult,
                                   reads=[Bpd, Bm], writes=[Bt])
                        sch.op("dve", V.tensor_tensor, xt[:], xt[:], tp[:], ALU.add, reads=[Bx, Bt], writes=[Bx])
                        if not last:
                            sch.dma("pool", xres[rows, :], xt[:], reads=[Bx])
                        else:
                            sq, Bs = nt[2].next()
                            sch.op("act", A.activation, junk[:], xt[:], AF.Square, accum_out=sq[:, 0:1], reads=[Bx], writes=[nt[1], Bs])
                            sch.op("act", A.activation, sq[:, 1:2], sq[:, 0:1], AF.Sqrt, bias=EPS, scale=1.0 / D, reads=[Bs], writes=[Bs])
                            sch.op("dve", V.reciprocal, sq[:, 2:3], sq[:, 1:2], reads=[Bs], writes=[Bs])
                            sch.op("dve", V.scalar_tensor_tensor, xt[:], xt[:], sq[:, 2:3], fg[:], ALU.mult, ALU.mult,
                                   reads=[Bx, Bs, Bfg], writes=[Bx])
                            sch.dma("pool", out[rows, :], xt[:], reads=[Bx])
                sch.barrier()

        cast_weights(0, ["in"])
        phases = [("mod", phase_mod, None)]
        for l in range(DEPTH):
            phases += [(f"ab{l}", phase_ab, l), (f"c{l}", phase_c, l), (f"d{l}", phase_d, l), (f"e{l}", phase_e, l),
                       (f"f1{l}", phase_f1, l), (f"f2{l}", phase_f2, l)]
        for name, fn, arg in phases:
            if arg is None:
                fn()
            else:
                fn(arg)
            if stop is not None and name == stop:
                break
        sch.finish()
        nc._sched_stats = (sch.nops, sch.nwaits)
    return nc


def _prep(inputs):
    f32 = np.float32
    c = _consts()
    x = np.asarray(inputs["x"], f32)
    ctx = np.asarray(inputs["ctx"], f32)
    cvec = np.asarray(inputs["c"], f32)
    cctx = np.asarray(inputs["c_ctx"], f32)
    lb = np.asarray(inputs["hgrn_lb"], f32)
    lbT = np.ascontiguousarray(lb.reshape(2, 2, 2, 128).transpose(3, 0, 1, 2).reshape(128, 8))
    ng = np.asarray(inputs["hgrn_norm_g"], f32)
    ngT = np.ascontiguousarray(np.concatenate([ng, ng], axis=1).T)
    rpb = np.asarray(inputs["na_rpb"], f32)
    nabias = np.stack([_na_bias_tiles(rpb[l]) for l in range(DEPTH)])
    shared = {
        "w_mod": np.asarray(inputs["w_mod"], f32), "b_mod": np.asarray(inputs["b_mod"], f32),
        "norm1_g": np.asarray(inputs["norm1_g"], f32), "w_in": np.asarray(inputs["w_in"], f32),
        "fourier_w": np.asarray(inputs["fourier_w"], f32), "lbT": lbT, "ngT": ngT, "nabias": nabias,
        "w_out": np.asarray(inputs["w_out"], f32), "norm2_g": np.asarray(inputs["norm2_g"], f32),
        "w_gate": np.asarray(inputs["w_ffn_gate"], f32), "w_up": np.asarray(inputs["w_ffn_up"], f32),
        "w_down": np.asarray(inputs["w_ffn_down"], f32), "final_g": np.asarray(inputs["final_norm_g"], f32),
        "C4": c["C4"], "S4": c["S4"], "C2": c["C2"], "S2": c["S2"], "C64bd": c["C64bd"], "S64bdn": c["S64bdn"],
        "ident": c["ident"], "bdmask": c["bdmask"], "bd64": c["bd64"], "mfwd": c["mfwd"], "mbwd": c["mbwd"], "cm3": c["cm3"],
    }
    maps = []
    for b in range(8):
        m = dict(shared)
        m["xin"] = np.ascontiguousarray(np.concatenate([x[b], ctx[b]], axis=0))
        cv = np.stack([cvec[b], cctx], axis=0)
        m["cT"] = np.ascontiguousarray(cv.reshape(2, 8, 128).transpose(2, 1, 0).reshape(128, 16))
        maps.append(m)
    return maps


def kernel(**inputs):
    maps = _prep(inputs)
    nc = build()
    res = run_bass_kernel_spmd(nc, maps, core_ids=list(range(8)))
    return np.stack([np.asarray(r["out"], np.float32) for r in res.results], axis=0)
```

```python
import numpy as np
import ml_dtypes
from contextlib import ExitStack
import concourse.bass as bass
import concourse.mybir as mybir
from concourse.bass_utils import run_bass_kernel_spmd

F32 = mybir.dt.float32
BF16 = mybir.dt.bfloat16
AF = mybir.ActivationFunctionType
ALU = mybir.AluOpType

D = 1024
S = 4096
L = 256
T = S + L
NB = T // 128
DFF = 2816
NJ = DFF // 128
INW = 3072
EPS = 1e-6
NEG = -1e30
DEPTH = 2


class _Op:
    __slots__ = ("eng", "fn", "args", "kw", "deps", "signal", "sem", "val", "dma", "epoch")


class Buf:
    __slots__ = ("w", "r", "g")

    def __init__(self):
        self.w = {}
        self.r = {}
        self.g = ()


class Sched:
    CENG = ("pe", "act", "dve", "pool")
    SEMCAP = 30000

    def __init__(self, nc, es):
        self.nc = nc
        self.eobj = dict(pe=nc.tensor, act=nc.scalar, dve=nc.vector, pool=nc.gpsimd, sp=nc.sync)
        self.pending = []
        self.lastc = {}
        self.dma_since = []
        self.bar_deps = {}
        ncs = dict(pe=3, act=3, dve=4, pool=2)
        self.csem = {e: [es.enter_context(nc.semaphore(f"c{e}{i}")) for i in range(ncs[e])] for e in self.CENG}
        self.ccount = {e: 0 for e in self.CENG}
        self.dsem = {q: [es.enter_context(nc.semaphore(f"d{q}{i}")) for i in range(8)] for q in ("sp", "pool")}
        self.dcount = {q: [0] * 8 for q in ("sp", "pool")}
        self.drr = {q: 0 for q in ("sp", "pool")}
        self.waited = {e: {} for e in self.eobj}
        self.nops = 0
        self.nwaits = 0
        self.epoch = 0
        self.pe_mode = "f"
        self.bg_ops = []

    def op(self, eng, fn, *args, reads=(), writes=(), dma=False, force=False, bg=False, **kw):
        o = _Op()
        o.eng, o.fn, o.args, o.kw, o.dma = eng, fn, args, kw, dma
        o.signal = dma or force
        o.sem = None
        o.val = 0
        o.epoch = self.epoch
        deps = {}
        if eng == "pe":
            mode = kw.pop("pemode", "f")
            if mode != self.pe_mode and "pe" in self.lastc:
                deps[id(self.lastc["pe"])] = self.lastc["pe"]
            self.pe_mode = mode

        def need(d, same_ok, bar=False):
            if d.epoch < self.epoch and not bar:
                return
            if same_ok and (not d.dma) and (not dma) and d.eng == eng:
                return
            deps[id(d)] = d

        for b in reads:
            for d in b.w.values():
                need(d, eng == "pe")
        for b in writes:
            if b.r:
                b.g = tuple(b.r.values()) + tuple(b.w.values())
                b.r = {}
                b.w = {}
            for d in b.g:
                need(d, True)
        for d in self.bar_deps.pop(eng, ()):
            need(d, True, True)
        for d in deps.values():
            d.signal = True
        o.deps = list(deps.values())
        key = ("d", id(o)) if dma else eng
        for b in reads:
            b.r[key] = o
        for b in writes:
            b.w[key] = o
        if dma and bg:
            o.epoch = 1 << 60
            self.bg_ops.append(o)
        elif dma:
            self.dma_since.append(o)
        else:
            self.lastc[eng] = o
        self.pending.append(o)
        return o

    def dma(self, q, out, in_, reads=(), writes=(), **kw):
        fn = self.nc.sync.dma_start if q == "sp" else self.nc.gpsimd.dma_start
        return self.op(q, fn, reads=reads, writes=writes, dma=True, out=out, in_=in_, **kw)

    def flush(self):
        for o in self.pending:
            e = self.eobj[o.eng]
            w = self.waited[o.eng]
            for d in o.deps:
                assert d.sem is not None, "dependency on unsignalled op"
                k = id(d.sem)
                if w.get(k, 0) >= d.val:
                    continue
                e.wait_ge(d.sem, d.val)
                w[k] = d.val
                self.nwaits += 1
            ins = o.fn(*o.args, **o.kw)
            self.nops += 1
            if o.dma:
                q = o.eng
                i = self.drr[q]
                self.drr[q] = (i + 1) % 8
                self.dcount[q][i] += 16
                o.sem, o.val = self.dsem[q][i], self.dcount[q][i]
                ins.then_inc(o.sem, 16)
            elif o.signal:
                n = self.ccount[o.eng]
                self.ccount[o.eng] = n + 1
                sems = self.csem[o.eng]
                si = n // self.SEMCAP
                assert si < len(sems), "out of compute semaphores"
                o.sem, o.val = sems[si], n % self.SEMCAP + 1
                ins.then_inc(o.sem, 1)
            o.fn = o.args = o.kw = None
        self.pending = []

    def barrier(self):
        deps = [self.lastc[e] for e in self.CENG if e in self.lastc] + self.dma_since
        for d in deps:
            d.signal = True
        self.flush()
        old = self.bar_deps
        self.bar_deps = {e: list(deps) + list(old.get(e, ())) for e in self.eobj}
        self.dma_since = []
        self.epoch += 1

    def finish(self):
        self.barrier()
        sp = self.nc.sync
        for d in list(self.bar_deps["sp"]) + self.bg_ops:
            k = id(d.sem)
            if self.waited["sp"].get(k, 0) >= d.val:
                continue
            sp.wait_ge(d.sem, d.val)
            self.waited["sp"][k] = d.val


class Rot:
    def __init__(self, tiles):
        self.t = [(t, Buf()) for t in tiles]
        self.i = 0

    def next(self):
        r = self.t[self.i]
        self.i = (self.i + 1) % len(self.t)
        return r


_CONST = {}


def _consts():
    if _CONST:
        return _CONST
    bf = ml_dtypes.bfloat16
    t = np.arange(S, dtype=np.int64)
    m = (t[:, None] * t[None, :]) % S
    ang = 2.0 * np.pi * m.astype(np.float64) / S
    _CONST["C4"] = (np.cos(ang) / 64.0).astype(np.float32).astype(bf)
    _CONST["S4"] = (np.sin(ang) / 64.0).astype(np.float32).astype(bf)
    del ang, m
    t = np.arange(L, dtype=np.int64)
    ang = 2.0 * np.pi * ((t[:, None] * t[None, :]) % L).astype(np.float64) / L
    _CONST["C2"] = (np.cos(ang) / 16.0).astype(np.float32).astype(bf)
    _CONST["S2"] = (np.sin(ang) / 16.0).astype(np.float32).astype(bf)
    t = np.arange(64, dtype=np.int64)
    ang = 2.0 * np.pi * ((t[:, None] * t[None, :]) % 64).astype(np.float64) / 64
    c64 = np.cos(ang) / 8.0
    s64 = np.sin(ang) / 8.0
    z = np.zeros((64, 64))
    _CONST["C64bd"] = np.block([[c64, z], [z, c64]]).astype(np.float32)
    _CONST["S64bdn"] = (-np.block([[s64, z], [z, s64]])).astype(np.float32)
    _CONST["ident"] = np.eye(128, dtype=np.float32).astype(bf)
    bd = np.zeros((128, 128), np.float32)
    bd[:64, :64] = 1.0
    bd[64:, 64:] = 1.0
    _CONST["bdmask"] = bd
    _CONST["bd64"] = (bd / 64.0).astype(np.float32)
    s_ = np.arange(128)[:, None]
    t_ = np.arange(128)[None, :]
    same = (s_ // 32) == (t_ // 32)
    _CONST["mfwd"] = (same & (s_ <= t_)).astype(np.float32)
    _CONST["mbwd"] = (same & (s_ >= t_)).astype(np.float32)
    cm = np.zeros((128, 4, 128), np.float32)
    for c_ in range(4):
        cm[32 * c_:32 * c_ + 32, c_, :] = 1.0
    _CONST["cm3"] = cm.reshape(128, 512)
    jp = np.zeros((128, 2, 128), np.float32)
    for p_ in range(1, 128):
        jp[128 - p_, 0, p_] = 1.0
    jp[0, 1, 0] = 1.0
    _CONST["jperm"] = jp.reshape(128, 256).astype(bf)
    return _CONST


def _na_bias_tiles(rpb):
    H = rpb.shape[0]
    kc = np.arange(64)[:, None]
    qc = np.arange(64)[None, :]
    wc = np.clip(qc - 8, 0, 48)
    colvalid = (kc >= wc) & (kc < wc + 16)
    cidx = np.clip(kc - qc + 15, 0, 30)
    specs = [(8, 8 - 4 + 2 * po) for po in range(8)] + [(0, 2 * p) for p in range(6)] + [(56, 52 + 2 * p) for p in range(6)]
    out = np.full((H, 128, 20, 512), NEG, np.float32)
    for ti, (r0, kr0) in enumerate(specs):
        for kr2 in range(2):
            kr = kr0 + kr2
            for qr in range(8):
                r = r0 + qr
                rs = min(max(r - 4, 0), 56)
                if not (rs <= kr < rs + 8):
                    continue
                ridx = kr - r + 7
                blk = np.where(colvalid[None], rpb[:, ridx][:, cidx], NEG)
                out[:, kr2 * 64:(kr2 + 1) * 64, ti, qr * 64:(qr + 1) * 64] = blk
    return out.reshape(H, 128, 20 * 512)


def build(debug=False, stop=None):
    nc = bass.Bass("TRN2", target_bir_lowering=False)
    skind = "ExternalOutput" if debug else "Internal"

    def din(name, shape, dt=F32):
        return nc.dram_tensor(name, list(shape), dt, kind="ExternalInput").ap()

    def dscr(name, shape, dt=F32):
        return nc.dram_tensor(name, list(shape), dt, kind=skind).ap()

    xin = din("xin", [T, D])
    cT = din("cT", [128, 16])
    w_mod = din("w_mod", [DEPTH, D, 6 * D])
    b_mod = din("b_mod", [DEPTH, 6 * D])
    norm1_g = din("norm1_g", [DEPTH, D])
    w_in = din("w_in", [DEPTH, D, INW])
    fourier_w = din("fourier_w", [DEPTH, 4, 64, 64])
    lbT = din("lbT", [128, 8])
    ngT = din("ngT", [128, DEPTH])
    nabias = din("nabias", [DEPTH, 8, 128, 20 * 512])
    w_out = din("w_out", [DEPTH, D, D])
    norm2_g = din("norm2_g", [DEPTH, D])
    w_gate = din("w_gate", [DEPTH, D, DFF])
    w_up = din("w_up", [DEPTH, D, DFF])
    w_down = din("w_down", [DEPTH, DFF, D])
    final_g = din("final_g", [D])
    C4 = din("C4", [S, S], BF16)
    S4 = din("S4", [S, S], BF16)
    C2 = din("C2", [L, L], BF16)
    S2 = din("S2", [L, L], BF16)
    C64bd_d = din("C64bd", [128, 128])
    S64bdn_d = din("S64bdn", [128, 128])
    ident_d = din("ident", [128, 128], BF16)
    bdmask_d = din("bdmask", [128, 128])
    bd64_d = din("bd64", [128, 128])
    mfwd_d = din("mfwd", [128, 128])
    mbwd_d = din("mbwd", [128, 128])
    cm3_d = din("cm3", [128, 512])
    jperm_d = din("jperm", [128, 256], BF16)
    out = nc.dram_tensor("out", [S, D], F32, kind="ExternalOutput").ap()

    mod2d = dscr("mod2d", [2 * DEPTH, 6 * D])
    fz = dscr("fz", [512, T])
    fb = dscr("fb", [1536, T], BF16)
    tm = dscr("tm", [T, 1024], BF16)
    catT = dscr("catT", [1024, T], BF16)
    xres = dscr("xres", [T, D])
    w_in_b = dscr("w_in_b", [DEPTH, D, INW], BF16)
    w_out_b = dscr("w_out_b", [DEPTH, D, D], BF16)
    w_gate_b = dscr("w_gate_b", [DEPTH, D, DFF], BF16)
    w_up_b = dscr("w_up_b", [DEPTH, D, DFF], BF16)
    w_down_b = dscr("w_down_b", [DEPTH, DFF, D], BF16)

    with ExitStack() as ges:
        sch = Sched(nc, ges)
        V, A, P, G = nc.vector, nc.scalar, nc.tensor, nc.gpsimd

        uid = [0]

        def sbt(es, name, shape, dt=F32):
            uid[0] += 1
            return es.enter_context(nc.sbuf_tensor(f"s{uid[0]}_{name}", list(shape), dt))

        def pst(es, name, shape, dt=F32):
            uid[0] += 1
            return es.enter_context(nc.psum_tensor(f"p{uid[0]}_{name}", list(shape), dt))

        ident = sbt(ges, "ident", [128, 128], BF16)
        bdmask = sbt(ges, "bdmask", [128, 128])
        onesb = sbt(ges, "onesb", [128, 64], BF16)
        B_const = Buf()
        sch.dma("sp", ident[:], ident_d, writes=[B_const])
        sch.dma("sp", bdmask[:], bdmask_d, writes=[B_const])
        sch.op("pool", G.memset, onesb[:], 1.0, writes=[B_const], force=True)

        def evac(i, outap, inap, reads, writes, scale=None):
            if i % 2 == 0:
                if scale is None:
                    sch.op("act", A.copy, outap, inap, reads=reads, writes=writes)
                else:
                    sch.op("act", A.mul, outap, inap, scale, reads=reads, writes=writes)
            else:
                if scale is None:
                    sch.op("dve", V.tensor_copy, outap, inap, reads=reads, writes=writes)
                else:
                    sch.op("dve", V.tensor_scalar, outap, inap, scale, None, ALU.mult, reads=reads, writes=writes)

        def phase_mod():
            with ExitStack() as es:
                cTt = sbt(es, "cTt", [128, 16])
                sT = sbt(es, "sT", [128, 16])
                mod2s = sbt(es, "mod2s", [2, 6 * D])
                bt = sbt(es, "bt", [2, 6 * D])
                wm = Rot([sbt(es, f"wm{i}", [128, 8, 512]) for i in range(3)])
                pm = Rot([pst(es, f"pm{i}", [128, 512]) for i in range(2)])
                B_c, B_s, B_m, B_b = Buf(), Buf(), Buf(), Buf()
                sch.dma("sp", cTt[:], cT, writes=[B_c])
                sch.op("act", A.activation, sT[:], cTt[:], AF.Silu, reads=[B_c], writes=[B_s])
                for l in range(DEPTH):
                    sch.dma("sp", bt[:], b_mod[l].partition_broadcast(2), writes=[B_b])
                    for nb in range(12):
                        wt, Bw = wm.next()
                        sch.dma("sp", wt[:], w_mod[l][:, nb * 512:(nb + 1) * 512].rearrange("(k p) n -> p k n", p=128), writes=[Bw])
                        pt, Bp = pm.next()
                        for k in range(8):
                            sch.op("pe", P.matmul, pt[0:2, :], lhsT=sT[:, 2 * k:2 * k + 2], rhs=wt[:, k, :],
                                   start=(k == 0), stop=(k == 7), pemode="m2", reads=[B_s, Bw], writes=[Bp])
                        sch.op("dve", V.tensor_tensor, mod2s[:, nb * 512:(nb + 1) * 512], pt[0:2, :], bt[:, nb * 512:(nb + 1) * 512],
                               ALU.add, reads=[Bp, B_b], writes=[B_m])
                    sch.dma("pool", mod2d[2 * l:2 * l + 2, :], mod2s[:], reads=[B_m])
                sch.barrier()

        def norm_block(es_tiles, xt, Bx, Gt, SHt, B_mod, a_bf, Ba):
            junk, Bj, ssqr, tmpr = es_tiles
            sq, Bs = ssqr.next()
            sch.op("act", A.activation, junk[:], xt[:], AF.Square, accum_out=sq[:, 0:1], reads=[Bx], writes=[Bj, Bs])
            sch.op("act", A.activation, sq[:, 1:2], sq[:, 0:1], AF.Sqrt, bias=EPS, scale=1.0 / D, reads=[Bs], writes=[Bs])
            sch.op("dve", V.reciprocal, sq[:, 2:3], sq[:, 1:2], reads=[Bs], writes=[Bs])
            tp, Bt = tmpr.next()
            sch.op("dve", V.scalar_tensor_tensor, tp[:], xt[:], sq[:, 2:3], Gt[:], ALU.mult, ALU.mult,
                   reads=[Bx, Bs, B_mod], writes=[Bt])
            sch.op("dve", V.tensor_tensor, a_bf[:], tp[:], SHt[:], ALU.add, reads=[Bt, B_mod], writes=[Ba])

        def transpose_block(ptrr, a_bf, Ba, aT, BaT, tb, ei):
            for half in range(2):
                pt, Bp = ptrr.next()
                for j in range(4):
                    c = (half * 4 + j) * 128
                    sch.op("pe", P.transpose, pt[:, j, :], a_bf[:, c:c + 128], ident[:], reads=[Ba, B_const], writes=[Bp])
                evac(ei + half, aT[:, half * 4:(half + 1) * 4, tb * 128:(tb + 1) * 128], pt[:], [Bp], [BaT])

        def load_weight(es, name, src_kpn, nk, ncols, Bsrc):
            wb = sbt(es, name, [128, nk, ncols], BF16)
            Bw = Buf()
            for k in range(nk):
                sch.dma("sp", wb[:, k, :], src_kpn[:, k, :], reads=[Bsrc], writes=[Bw])
            return wb, Bw

        wcast_buf = {}

        def cast_weights(l, names):
            table = {"in": (w_in, w_in_b, D), "out": (w_out, w_out_b, D), "gate": (w_gate, w_gate_b, D), "up": (w_up, w_up_b, D),
                     "down": (w_down, w_down_b, DFF)}
            for nm in names:
                src, dst, R_ = table[nm]
                Bw = Buf()
                wcast_buf[(nm, l)] = Bw
                for r0 in range(0, R_, 128):
                    sch.dma("pool", dst[l][r0:r0 + 128, :], src[l][r0:r0 + 128, :], writes=[Bw], bg=True)

        def pump(*a_, **k_):
            return

        def make_bg(es, n):
            return None

        bgq = []

        def mod_tiles(es, l, v, gsrc, i_sh, i_sc, tag):
            Gt = sbt(es, f"G{tag}", [128, D])
            SHt = sbt(es, f"SH{tag}", [128, D])
            Bm = Buf()
            sch.dma("pool", SHt[:], mod2d[2 * l + v, i_sh * D:(i_sh + 1) * D].partition_broadcast(128), writes=[Bm])
            sch.dma("pool", Gt[:], gsrc.partition_broadcast(128), writes=[Bm])
            return Gt, SHt, Bm

        def phase_ab(l):
            xsrc = xin if l == 0 else xres
            if l >= 1 and bgq:
                with ExitStack() as es0:
                    pump(100000, make_bg(es0, 4), drain=True)
                    sch.barrier()
            with ExitStack() as es:
                wbf, Bw = load_weight(es, "winbf", w_in_b[l].rearrange("(k p) n -> p k n", p=128), 8, INW, wcast_buf[("in", l)])
                mods = []
                sct = sbt(es, "sct", [128, D])
                Bsc = Buf()
                for v in range(2):
                    Gt, SHt, Bm = mod_tiles(es, l, v, norm1_g[l], 0, 1, f"1_{v}")
                    sch.dma("pool", sct[:], mod2d[2 * l + v, D:2 * D].partition_broadcast(128), writes=[Bsc])
                    sch.op("dve", V.scalar_tensor_tensor, Gt[:], sct[:], 1.0, Gt[:], ALU.add, ALU.mult, reads=[Bsc, Bm], writes=[Bm])
                    mods.append((Gt, SHt, Bm))
                junk = sbt(es, "junk", [128, D], BF16)
                nt = (junk, Buf(), Rot([sbt(es, f"ssq{i}", [128, 4]) for i in range(4)]),
                      Rot([sbt(es, f"ntmp{i}", [128, D]) for i in range(2)]))
                xr = Rot([sbt(es, f"xt{i}", [128, D]) for i in range(8)])
                ar = Rot([sbt(es, f"abf{i}", [128, D], BF16) for i in range(2)])
                aTr = Rot([sbt(es, f"aT{i}", [128, 8, 512], BF16) for i in range(2)])
                sfr = Rot([sbt(es, f"sf{i}", [128, 512]) for i in range(2)])
                sbr = Rot([sbt(es, f"sb{i}", [128, 512], BF16) for i in range(3)])
                tmr = Rot([sbt(es, f"tms{i}", [128, 1024], BF16) for i in range(2)])
                ptrr = Rot([pst(es, f"ptr{i}", [128, 4, 128], BF16) for i in range(2)])
                pfr = Rot([pst(es, f"pf{i}", [128, 512]) for i in range(3)])
                ptmr = Rot([pst(es, f"ptm{i}", [128, 512]) for i in range(2)])
                fm = []
                for c0, dst, r0, sc in ((256, fb, 0, None), (512, fz, 0, None), (768, fz, 256, None), (1280, fb, 256, None),
                                        (1536, fb, 512, 0.125), (2048, fb, 1024, None)):
                    wd = 512 if c0 >= 1536 else 256
                    for g in range(wd // 128):
                        fm.append((c0 + g * 128, dst, r0 + g * 128, sc))
                sbs = [(i * 512, 512, 0) for i in range(8)] + [(S, L, 1)]
                eic = [0]
                aTs = {}
                abfs = {}

                xts = {}

                def load_part(i):
                    t0, ntok, v = sbs[i]
                    for tb in range(ntok // 128):
                        xt, Bx = xr.next()
                        sch.dma("sp", xt[:], xsrc[t0 + tb * 128:t0 + (tb + 1) * 128, :], writes=[Bx])
                        xts[(i, tb)] = (xt, Bx)

                def norm_part(i, tb):
                    t0, ntok, v = sbs[i]
                    Gt, SHt, Bm = mods[v]
                    if tb == 0:
                        aTs[i] = aTr.next()
                    xt, Bx = xts.pop((i, tb))
                    a_bf, Ba = ar.next()
                    norm_block(nt, xt, Bx, Gt, SHt, Bm, a_bf, Ba)
                    abfs[(i, tb)] = (a_bf, Ba)

                def tr_part(i, tb):
                    aT, BaT = aTs[i]
                    a_bf, Ba = abfs.pop((i, tb))
                    eic[0] += 2
                    transpose_block(ptrr, a_bf, Ba, aT, BaT, tb, eic[0])

                load_part(0)
                for tb in range(sbs[0][1] // 128):
                    norm_part(0, tb)
                    tr_part(0, tb)
                for i, (t0, ntok, v) in enumerate(sbs):
                    aT, BaT = aTs[i]
                    if i + 1 < len(sbs):
                        load_part(i + 1)
                    groups = []
                    for (c0, dst, r0, sc) in fm:
                        groups.append(("fm", c0, dst, r0, sc))
                    for tb in range(ntok // 128):
                        groups.append(("tm", tb))
                    nxt = {}
                    if i + 1 < len(sbs):
                        ntb2 = sbs[i + 1][1] // 128
                        for tb in range(ntb2):
                            nxt.setdefault(1 + 5 * tb, []).append(("n", tb))
                            nxt.setdefault(4 + 5 * tb, []).append(("t", tb))
                    for gi, g in enumerate(groups):
                        for (kind, tb) in nxt.pop(gi, []):
                            (norm_part if kind == "n" else tr_part)(i + 1, tb)
                        if g[0] == "fm":
                            _, c0, dst, r0, sc = g
                            pf, Bp = pfr.next()
                            for k in range(8):
                                sch.op("pe", P.matmul, pf[:, 0:ntok], lhsT=wbf[:, k, c0:c0 + 128], rhs=aT[:, k, 0:ntok],
                                       start=(k == 0), stop=(k == 7), reads=[Bw, BaT], writes=[Bp])
                            if dst is fz:
                                st, Bs = sfr.next()
                            else:
                                st, Bs = sbr.next()
                            evac(0, st[:, 0:ntok], pf[:, 0:ntok], [Bp], [Bs], scale=sc)
                            sch.dma("sp", dst[r0:r0 + 128, t0:t0 + ntok], st[:, 0:ntok], reads=[Bs])
                        else:
                            tb = g[1]
                            tms, Bts = tmr.next()
                            tsl = slice(tb * 128, (tb + 1) * 128)
                            p0, Bp0 = ptmr.next()
                            for (cc, oc) in ((0, 0), (1024, 256)):
                                for k in range(8):
                                    sch.op("pe", P.matmul, p0[:, oc:oc + 256], lhsT=aT[:, k, tsl], rhs=wbf[:, k, cc:cc + 256],
                                           start=(k == 0), stop=(k == 7), reads=[Bw, BaT], writes=[Bp0])
                            evac(0, tms[:, 0:512], p0[:], [Bp0], [Bts])
                            p1, Bp1 = ptmr.next()
                            for k in range(8):
                                sch.op("pe", P.matmul, p1[:], lhsT=aT[:, k, tsl], rhs=wbf[:, k, 2560:3072],
                                       start=(k == 0), stop=(k == 7), reads=[Bw, BaT], writes=[Bp1])
                            evac(0, tms[:, 512:1024], p1[:], [Bp1], [Bts])
                            sch.dma("sp", tm[t0 + tb * 128:t0 + (tb + 1) * 128, :], tms[:], reads=[Bts])
                    for lst in nxt.values():
                        for (kind, tb) in lst:
                            (norm_part if kind == "n" else tr_part)(i + 1, tb)
                sch.barrier()

        def phase_c(l):
            with ExitStack() as es:
                cbgs = make_bg(es, 4)
                U = sbt(es, "fU", [128, NB, 256], BF16)
                BU = Buf()
                sch.dma("sp", U[:], tm[:, 0:256].rearrange("(k p) c -> p k c", p=128), writes=[BU])
                c64 = sbt(es, "c64", [128, 2, 128])
                wf = sbt(es, "wf", [128, 2, 64])
                Mbd = sbt(es, "Mbd", [128, 3, 2, 128], BF16)
                Bc, Bwf, BM = Buf(), Buf(), Buf()
                sch.dma("sp", c64[:, 0, :], C64bd_d, writes=[Bc])
                sch.dma("sp", c64[:, 1, :], S64bdn_d, writes=[Bc])
                for cb in range(2):
                    sch.dma("sp", wf[:, cb, :], fourier_w[l][2 * cb:2 * cb + 2].rearrange("g c d -> (g c) d"), writes=[Bwf])
                sch.op("pool", G.memset, Mbd[:], 0.0, writes=[BM])
                pmr = Rot([pst(es, f"pM{i}", [128, 512]) for i in range(1)])
                pM, BpM = pmr.next()
                for w in range(2):
                    for cb in range(2):
                        o0 = (w * 2 + cb) * 64
                        sch.op("pe", P.matmul, pM[:, o0:o0 + 64], lhsT=c64[:, w, :], rhs=wf[:, cb, :], start=True, stop=True,
                               pemode="f32", reads=[Bc, Bwf], writes=[BpM])
                for w in range(2):
                    for cb in range(2):
                        o0 = (w * 2 + cb) * 64
                        sch.op("dve", V.tensor_copy, Mbd[0:64, w, cb, 0:64], pM[0:64, o0:o0 + 64], reads=[BpM, BM], writes=[BM])
                        sch.op("dve", V.tensor_copy, Mbd[64:128, w, cb, 64:128], pM[64:128, o0:o0 + 64], reads=[BpM, BM], writes=[BM])
                sch.op("dve", V.tensor_scalar, Mbd[:, 2].rearrange("p c n -> p (c n)"), Mbd[:, 1].rearrange("p c n -> p (c n)"), -1.0, None,
                       ALU.mult, reads=[BM], writes=[BM])
                csr = Rot([sbt(es, f"cs{i}", [128, 2, 4, 512], BF16) for i in range(3)])
                pqr = Rot([sbt(es, f"pq{i}", [128, 4, 512], BF16) for i in range(2)])
                osr = Rot([sbt(es, f"fo{i}", [128, 512], BF16) for i in range(3)])
                pacc = [pst(es, f"pacc{i}", [128, 512]) for i in range(4)]
                Bacc = [Buf() for _ in range(4)]
                por = Rot([pst(es, f"po{i}", [128, 512]) for i in range(2)])
                c0col = sbt(es, "c0col", [128, 2], BF16)
                Bc0 = Buf()
                sch.op("pool", G.memset, c0col[:], 1.0 / 64.0, writes=[Bc0])

                def mix(pq, Bpq, n, tcol0, mirror_col0=None, ncols=None):
                    for cb in range(2):
                        po, Bpo = por.next()
                        sch.op("pe", P.matmul, po[:, 0:n], lhsT=Mbd[:, 0, cb, :], rhs=pq[:, cb, 0:n], start=True, stop=False,
                               reads=[BM, Bpq], writes=[Bpo])
                        sch.op("pe", P.matmul, po[:, 0:n], lhsT=Mbd[:, 1, cb, :], rhs=pq[:, 2 + cb, 0:n], start=False, stop=True,
                               reads=[BM, Bpq], writes=[Bpo])
                        ot, Bo = osr.next()
                        evac(cb, ot[:, 0:n], po[:, 0:n], [Bpo], [Bo])
                        nst = n if ncols is None else ncols
                        if nst == 1:
                            sch.dma("pool", catT[cb * 128:(cb + 1) * 128, tcol0:tcol0 + 1], ot[:, 0:1], reads=[Bo],
                                    allow_slow_non_contiguous=True)
                        else:
                            sch.dma("pool", catT[cb * 128:(cb + 1) * 128, tcol0:tcol0 + nst], ot[:, 0:nst], reads=[Bo])
                        if mirror_col0 is not None:
                            po, Bpo = por.next()
                            for (wm, qi) in ((0, cb), (2, 2 + cb)):
                                bq_ = pq[:, qi, 0:n]
                                rev = bass.AP(tensor=bq_.tensor, offset=bq_.offset + n - 1, ap=[[bq_.ap[0][0], 128], [-1, n]])
                                sch.op("pe", P.matmul, po[:, 0:n], lhsT=Mbd[:, wm, cb, :], rhs=rev, start=(wm == 0), stop=(wm == 2),
                                       reads=[BM, Bpq], writes=[Bpo])
                            ot, Bo = osr.next()
                            evac(cb + 1, ot[:, 0:n], po[:, 0:n], [Bpo], [Bo])
                            sch.dma("pool", catT[cb * 128:(cb + 1) * 128, mirror_col0:mirror_col0 + n], ot[:, 0:n], reads=[Bo])

                for cb in range(2):
                    for k in range(32):
                        sch.op("pe", P.matmul, pacc[cb][:, 0:2], lhsT=U[:, k, cb * 128:(cb + 1) * 128], rhs=c0col[:],
                               start=(k == 0), stop=(k == 31), reads=[BU, Bc0], writes=[Bacc[cb]])
                pq, Bpq = pqr.next()
                sch.op("pool", G.memset, pq[:, 2:4, 0:2], 0.0, writes=[Bpq])
                for cb in range(2):
                    evac(cb, pq[:, cb, 0:2], pacc[cb][:, 0:2], [Bacc[cb], Bpq], [Bpq])
                mix(pq, Bpq, 2, 0, ncols=1)
                urev = sbt(es, "urev", [128, 16, 256], BF16)
                Ue = sbt(es, "Ue", [128, 16, 256], BF16)
                Uo = sbt(es, "Uo", [128, 16, 256], BF16)
                Ux = sbt(es, "Ux", [128, 256], BF16)
                Brev, BUe, BUx = Buf(), Buf(), Buf()
                jm = sbt(es, "jm", [128, 2, 128], BF16)
                Bjm = Buf()
                sch.dma("sp", jm[:], jperm_d.rearrange("p (a n) -> p a n", n=128), writes=[Bjm])
                for k in range(16):
                    pr_, Bpr = por.next()
                    sch.op("pe", P.matmul, pr_[:, 0:256], lhsT=jm[:, 0, :], rhs=U[:, 31 - k, :], start=True, stop=(k == 0),
                           reads=[Bjm, BU], writes=[Bpr])
                    if k >= 1:
                        sch.op("pe", P.matmul, pr_[:, 0:256], lhsT=jm[:, 1, :], rhs=U[:, 32 - k, :], start=False, stop=True,
                               reads=[Bjm, BU], writes=[Bpr])
                    evac(k, urev[:, k, :], pr_[:, 0:256], [Bpr], [Brev])
                sch.op("dve", V.tensor_tensor, Ue[:], U[:, 0:16, :], urev[:], ALU.add, reads=[BU, Brev], writes=[BUe])
                sch.op("dve", V.tensor_tensor, Uo[:], U[:, 0:16, :], urev[:], ALU.subtract, reads=[BU, Brev], writes=[BUe])
                sch.op("dve", V.memset, Ux[:], 0.0, writes=[BUx])
                sch.op("dve", V.tensor_copy, Ux[0:1, :], U[0:1, 16, :], reads=[BU, BUx], writes=[BUx])
                crow = Rot([sbt(es, f"crow{i}", [128, 512], BF16) for i in range(2)])
                for (tl_, Bt_) in crow.t:
                    sch.op("dve", V.memset, tl_[:], 0.0, writes=[Bt_])
                for nb in range(4):
                    c_lo = 512 * nb + 1
                    cr_, Bcr = crow.next()
                    sch.dma("sp", cr_[0:1, :], C4[2048:2049, c_lo:c_lo + 512], reads=[Bcr], writes=[Bcr])
                    for kg in range(4):
                        cs, Bcs = csr.next()
                        sch.dma("sp", cs[:, 0], C4[kg * 512:(kg + 1) * 512, c_lo:c_lo + 512].rearrange("(k p) n -> p k n", p=128), writes=[Bcs])
                        sch.dma("sp", cs[:, 1], S4[kg * 512:(kg + 1) * 512, c_lo:c_lo + 512].rearrange("(k p) n -> p k n", p=128), writes=[Bcs])
                        for kk in range(4):
                            k = kg * 4 + kk
                            for w in range(2):
                                src_u = Ue if w == 0 else Uo
                                for cb in range(2):
                                    i = w * 2 + cb
                                    sch.op("pe", P.matmul, pacc[i][:], lhsT=src_u[:, k, cb * 128:(cb + 1) * 128], rhs=cs[:, w, kk, :],
                                           start=(k == 0), stop=(w == 1 and k == 15), reads=[BUe, Bcs], writes=[Bacc[i]])
                    for cb in range(2):
                        sch.op("pe", P.matmul, pacc[cb][:], lhsT=Ux[:, cb * 128:(cb + 1) * 128], rhs=cr_[:], start=False, stop=True,
                               reads=[BUx, Bcr], writes=[Bacc[cb]])
                    pq, Bpq = pqr.next()
                    for i in range(4):
                        evac(i, pq[:, i, :], pacc[i][:], [Bacc[i]], [Bpq])
                    mix(pq, Bpq, 512, c_lo, mirror_col0=3584 - 512 * nb)
                    if l == 0:
                        pump(5, cbgs, drain=True)
                if l == 0:
                    c2 = sbt(es, "c2", [128, 2, 2, 256], BF16)
                    Bc2 = Buf()
                    sch.dma("sp", c2[:, 0], C2.rearrange("(k p) n -> p k n", p=128), writes=[Bc2])
                    sch.dma("sp", c2[:, 1], S2.rearrange("(k p) n -> p k n", p=128), writes=[Bc2])
                    for w in range(2):
                        for cb in range(2):
                            i = w * 2 + cb
                            for k in range(2):
                                sch.op("pe", P.matmul, pacc[i][:, 0:256], lhsT=U[:, 32 + k, cb * 128:(cb + 1) * 128], rhs=c2[:, w, k, :],
                                       start=(k == 0), stop=(k == 1), reads=[BU, Bc2], writes=[Bacc[i]])
                    pq, Bpq = pqr.next()
                    for i in range(4):
                        evac(i, pq[:, i, 0:256], pacc[i][:, 0:256], [Bacc[i]], [Bpq])
                    mix(pq, Bpq, 256, S)
                sch.barrier()

        def phase_d(l):
            HW = 9 * 128
            NCH = T // 32
            PIECES = [(0, 9), (9, 8), (17, 9), (26, 8)]
            with ExitStack() as es:
                lbt = sbt(es, "lbt", [128, 8])
                lbv = sbt(es, "lbv", [128, 8])
                ngt = sbt(es, "ngt", [128, DEPTH])
                mdir = sbt(es, "mdir", [128, 2, 128])
                bd64 = sbt(es, "bd64", [128, 128])
                cm3 = sbt(es, "cm3", [128, 4, 128])
                epsc = sbt(es, "epsc", [128, 1])
                Bl, Bmd = Buf(), Buf()
                sch.op("pool", G.memset, epsc[:], EPS, writes=[Bmd])
                sch.dma("sp", lbt[:], lbT, writes=[Bl])
                sch.dma("sp", ngt[:], ngT, writes=[Bl])
                sch.dma("sp", mdir[:, 0, :], mfwd_d, writes=[Bmd])
                sch.dma("sp", mdir[:, 1, :], mbwd_d, writes=[Bmd])
                sch.dma("sp", bd64[:], bd64_d, writes=[Bmd])
                sch.dma("sp", cm3[:], cm3_d.rearrange("p (c n) -> p c n", n=128), writes=[Bmd])
                if l == 0:
                    sch.op("pool", G.memset, lbv[:, 0:4], 0.0, reads=[Bl], writes=[Bl])
                    sch.op("pool", G.memset, lbv[:, 4:8], 1.0, reads=[Bl], writes=[Bl])
                else:
                    ex = sbt(es, "lbex", [128, 8])
                    sch.op("act", A.activation, ex[:], lbt[:], AF.Exp, reads=[Bl], writes=[Bl])
                    sch.op("dve", V.tensor_tensor, lbv[:, 4:8], ex[:, 0:4], ex[:, 4:8], ALU.add, reads=[Bl], writes=[Bl])
                    sch.op("dve", V.reciprocal, lbv[:, 4:8], lbv[:, 4:8], reads=[Bl], writes=[Bl])
                    sch.op("dve", V.tensor_tensor, lbv[:, 0:4], ex[:, 4:8], lbv[:, 4:8], ALU.mult, reads=[Bl], writes=[Bl])
                    sch.op("dve", V.tensor_scalar, lbv[:, 4:8], lbv[:, 0:4], -1.0, 1.0, ALU.mult, ALU.add, reads=[Bl], writes=[Bl])
                for hp in range(2):
                    with ExitStack() as hs:
                        vtm = sbt(hs, "hv", [128, NB, 128], BF16)
                        vm = sbt(hs, "hvm", [128, 2, NB, 128], BF16)
                        Bv, Bvm = Buf(), Buf()
                        sch.dma("sp", vtm[:], tm[:, 256 + hp * 128:256 + (hp + 1) * 128].rearrange("(k p) c -> p k c", p=128), writes=[Bv])
                        sch.op("dve", V.memset, vm[:], 0.0, writes=[Bvm])
                        sch.op("dve", V.tensor_copy, vm[:, 0, :, 0:64], vtm[:, :, 0:64], reads=[Bv, Bvm], writes=[Bvm])
                        sch.op("dve", V.tensor_copy, vm[:, 1, :, 64:128], vtm[:, :, 64:128], reads=[Bv, Bvm], writes=[Bvm])
                        chains = []
                        for d in range(2):
                            ch = dict(d=d)
                            ch["qdm"] = sbt(hs, f"qdm{d}", [128, 2, T], BF16)
                            ch["kd"] = sbt(hs, f"kd{d}", [128, T], BF16)
                            ch["kst"] = sbt(hs, f"kst{d}", [128, NB, 128], BF16)
                            ch["dec"] = sbt(hs, f"dec{d}", [128, NCH])
                            ch["B"] = {k: Buf() for k in ("qdm", "kd", "kst", "dec", "o")}
                            ch["Bz"] = Buf()
                            sch.op("dve" if d == 0 else "act", (V.memset if d == 0 else A.memzero), ch["qdm"][:], *((0.0,) if d == 0 else ()),
                                   writes=[ch["Bz"]])
                            chains.append(ch)
                        with ExitStack() as gs:
                            Asets = [([sbt(gs, f"gA{u}{i}", [128, HW]) for i in range(4)], [Buf() for _ in range(4)],
                                      sbt(gs, f"gq{u}", [128, HW], BF16), Buf(), sbt(gs, f"gks{u}", [128, HW], BF16), Buf(),
                                      sbt(gs, f"gtot{u}", [128, HW // 32]), Buf()) for u in range(4)]
                            unit = 0
                            msk = sbt(gs, "gmsk", [128, HW])
                            Bmsk = Buf()
                            sch.op("pool", G.memset, msk[:], 1.0, writes=[Bmsk])
                            sch.op("pool", G.memset, msk[:].rearrange("p (n c) -> p n c", c=32)[:, :, 0:1], 0.0, reads=[Bmsk], writes=[Bmsk])
                            ptk = Rot([pst(gs, f"ptk{i}", [128, 4, 128], BF16) for i in range(2)])
                            ei_ = [0]

                            def unit_gen(ch, h, uset):
                                d = ch["d"]
                                B = ch["B"]
                                lbc = lbv[:, d * 2 + hp:d * 2 + hp + 1]
                                omc = lbv[:, 4 + d * 2 + hp:4 + d * 2 + hp + 1]
                                Ar_, BA, qh_, Bq, ksh_, Bks, tot_, Btot = uset
                                b0, nbh = PIECES[h]
                                W_ = nbh * 128
                                c0 = b0 * 128
                                cs = slice(c0, c0 + W_)
                                A1, A2, A3, A4 = [a_[:, 0:W_] for a_ in Ar_]
                                qh = qh_[:, 0:W_]
                                ksh = ksh_[:, 0:W_]
                                tot = tot_[:, 0:W_ // 32]
                                mskv = msk[:, 0:W_]
                                B1, B2, B3, B4 = BA
                                zr = 256 * d + hp * 128
                                sch.dma("sp", A1, fz[zr:zr + 128, cs], writes=[B1])
                                sch.dma("sp", qh, fb[hp * 128:(hp + 1) * 128, cs], writes=[Bq])
                                sch.op("act", A.activation, A1, A1, AF.Sigmoid, reads=[B1], writes=[B1])
                                yield
                                sch.op("act", A.activation, A1, A1, AF.Identity, bias=lbc, scale=omc, reads=[B1, Bl], writes=[B1])
                                yield
                                sch.op("act", A.activation, A2, A1, AF.Ln, reads=[B1], writes=[B2])
                                yield
                                sch.op("act", A.activation, A1, A1, AF.Identity, bias=1.0, scale=-1.0, reads=[B1, B2], writes=[B1])
                                sch.op("dve", V.tensor_tensor_scan, A3, mskv, A2, 0.0, ALU.mult, ALU.add, reads=[Bmsk, B2], writes=[B3])
                                yield
                                A3v = A3.rearrange("p (n c) -> p n c", c=32)
                                A2v = A2.rearrange("p (n c) -> p n c", c=32)
                                totb = tot.rearrange("p (n o) -> p n o", o=1).broadcast_to([128, W_ // 32, 32])
                                sch.op("act", A.copy, tot, A3v[:, :, 31], reads=[B3], writes=[Btot])
                                yield
                                if d == 0:
                                    sch.op("dve", V.tensor_tensor, A2v, A3v, totb, ALU.subtract, reads=[Btot, B3, B2], writes=[B2])
                                    asg, rsg = 1.0, -1.0
                                else:
                                    sch.op("dve", V.tensor_tensor, A2, A3, A2, ALU.subtract, reads=[B3, B2], writes=[B2])
                                    yield
                                    sch.op("dve", V.tensor_tensor, A3v, A2v, totb, ALU.subtract, reads=[Btot, B2, B3], writes=[B3])
                                    asg, rsg = -1.0, 1.0
                                yield
                                sch.op("act", A.activation, ch["dec"][:, c0 // 32:(c0 + W_) // 32], tot, AF.Exp, reads=[Btot], writes=[B["dec"]])
                                sch.op("act", A.activation, A4, A3, AF.Exp, scale=asg, reads=[B3, B4], writes=[B4])
                                yield
                                sch.op("dve", V.tensor_tensor, ch["qdm"][0:64, 0, cs], qh[0:64], A4[0:64], ALU.mult,
                                       reads=[Bq, B4, ch["Bz"]], writes=[B["qdm"]])
                                sch.op("dve", V.tensor_tensor, ch["qdm"][64:128, 1, cs], qh[64:128], A4[64:128], ALU.mult,
                                       reads=[Bq, B4, ch["Bz"]], writes=[B["qdm"]])
                                yield
                                sch.op("act", A.activation, A4, A3, AF.Exp, scale=-asg, reads=[B3, B4], writes=[B4])
                                yield
                                sch.op("dve", V.tensor_tensor, ch["kd"][:, cs], A1, A4, ALU.mult, reads=[B1, B4], writes=[B["kd"]])
                                yield
                                sch.op("act", A.activation, A4, A2, AF.Exp, scale=rsg, reads=[B2, B4], writes=[B4])
                                yield
                                sch.op("dve", V.tensor_tensor, ksh, A1, A4, ALU.mult, reads=[B1, B4], writes=[Bks])
                                yield
                                for b4 in range(0, nbh, 4):
                                    nn = min(4, nbh - b4)
                                    pt, Bp = ptk.next()
                                    for j in range(nn):
                                        sch.op("pe", P.transpose, pt[:, j, :], ksh[:, (b4 + j) * 128:(b4 + j + 1) * 128], ident[:],
                                               reads=[Bks, B_const], writes=[Bp])
                                    ei_[0] += 1
                                    evac(ei_[0], ch["kst"][:, b0 + b4:b0 + b4 + nn, :], pt[:, 0:nn, :], [Bp], [B["kst"]])
                                    yield

                            units = [(ch, h) for ch in chains for h in range(4)]
                            gens = []
                            SHIFT = 4
                            active = []
                            ui = 0
                            tick = 0
                            while ui < len(units) or active:
                                if ui < len(units) and len(active) < 4 and (not active or active[-1][1] >= SHIFT):
                                    ch_, h_ = units[ui]
                                    active.append([unit_gen(ch_, h_, Asets[ui % 4]), 0])
                                    ui += 1
                                for a_ in list(active):
                                    try:
                                        next(a_[0])
                                        a_[1] += 1
                                    except StopIteration:
                                        active.remove(a_)
                            sch.barrier()
                        for ch in chains:
                            ch["o"] = sbt(hs, f"o{ch['d']}", [128, T])
                        with ExitStack() as cs_:
                            vexa = sbt(cs_, "vexa", [128, NB, 4, 128], BF16)
                            Bvex = Buf()
                            for nb in range(NB):
                                sch.op("dve", V.tensor_tensor, vexa[:, nb], cm3[:], vtm[:, nb:nb + 1, :].broadcast_to([128, 4, 128]), ALU.mult,
                                       reads=[Bmd, Bv], writes=[Bvex])
                            for ch in chains:
                                d = ch["d"]
                                ch["psU"] = pst(cs_, f"psU{d}", [128, 4, 128]); ch["BpsU"] = Buf()
                                ch["psS"] = [pst(cs_, f"psS{d}{i}", [128, 128]) for i in range(2)]; ch["BpsS"] = [Buf(), Buf()]
                                ch["psO"] = pst(cs_, f"psO{d}", [128, 128]); ch["BpsO"] = Buf()
                                ch["Sf"] = Rot([sbt(cs_, f"Sf{d}{i}", [128, 128]) for i in range(2)])
                                ch["Spad"] = Rot([sbt(cs_, f"Sp{d}{i}", [128, 2, 128], BF16) for i in range(10)])
                                for (tl, Bt_) in ch["Spad"].t:
                                    sch.op("dve", V.memset, tl[:], 0.0, writes=[Bt_])
                                ch["scb"] = Rot([sbt(cs_, f"scb{d}{i}", [128, 2, 128], BF16) for i in range(2)])
                                ch["order"] = ([32, 33] + list(range(32))) if d == 0 else ([33, 32] + list(range(31, -1, -1)))
                                ch["corder"] = [0, 1, 2, 3] if d == 0 else [3, 2, 1, 0]
                                ch["prev"] = None
                                ch["prev_bd"] = None
                                ch["pending"] = None

                            def front_all(step):
                                res = []
                                for ch in chains:
                                    nb = ch["order"][step]
                                    B = ch["B"]
                                    sch.op("pe", P.matmul, ch["psU"][:].rearrange("p c n -> p (c n)"), lhsT=ch["kst"][:, nb, :],
                                           rhs=vexa[:, nb].rearrange("p c n -> p (c n)"), start=True, stop=True,
                                           reads=[B["kst"], Bvex], writes=[ch["BpsU"]])
                                    ch["before"] = []
                                for ci in range(4):
                                    for k_, ch in enumerate(chains):
                                        nb = ch["order"][step]
                                        B = ch["B"]
                                        c = ch["corder"][ci]
                                        n = nb * 4 + c
                                        ch["before"].append((c, ch["prev_bd"]))
                                        sf, Bsf = ch["Sf"].next()
                                        if ch["prev"] is None:
                                            sch.op("dve", V.tensor_copy, sf[:], ch["psU"][:, c, :], reads=[ch["BpsU"]], writes=[Bsf])
                                        else:
                                            pv, Bpv = ch["prev"]
                                            sch.op("dve", V.scalar_tensor_tensor, sf[:], pv[:], ch["dec"][:, n:n + 1], ch["psU"][:, c, :],
                                                   ALU.mult, ALU.add, reads=[Bpv, B["dec"], ch["BpsU"]], writes=[Bsf])
                                        ch["prev"] = (sf, Bsf)
                                        sp_, Bsp = ch["Spad"].next()
                                        base = sp_[:]
                                        ov = bass.AP(tensor=base.tensor, offset=base.offset, ap=[[base.ap[0][0], 128], [192, 2], [1, 64]])
                                        iv = sf[:].rearrange("p (g c) -> p g c", g=2)
                                        sch.op("act", A.copy, ov, iv, reads=[Bsf, Bsp], writes=[Bsp])
                                        ch["prev_bd"] = (sp_, Bsp)
                                for ch in chains:
                                    nb = ch["order"][step]
                                    B = ch["B"]
                                    cols = slice(nb * 128, (nb + 1) * 128)
                                    scb, Bscb = ch["scb"].next()
                                    for hh in range(2):
                                        sch.op("pe", P.matmul, ch["psS"][hh][:], lhsT=ch["kd"][:, cols], rhs=ch["qdm"][:, hh, cols],
                                               start=True, stop=True, reads=[B["kd"], B["qdm"]], writes=[ch["BpsS"][hh]])
                                        sch.op("dve", V.tensor_tensor, scb[:, hh, :], ch["psS"][hh][:], mdir[:, ch["d"], :], ALU.mult,
                                               reads=[ch["BpsS"][hh], Bmd], writes=[Bscb])
                                    res.append((nb, ch["before"], scb, Bscb))
                                return res

                            def back(ch, pend):
                                nb, before, scb, Bscb = pend
                                B = ch["B"]
                                cols = slice(nb * 128, (nb + 1) * 128)
                                inter = [(c, s_) for (c, s_) in before if s_ is not None]
                                for hh in range(2):
                                    sch.op("pe", P.matmul, ch["psO"][:], lhsT=vm[:, hh, nb, :], rhs=scb[:, hh, :], start=(hh == 0),
                                           stop=(hh == 1 and not inter), reads=[Bvm, Bscb], writes=[ch["BpsO"]])
                                for i, (c, (sp_, Bsp)) in enumerate(inter):
                                    for hh in range(2):
                                        sch.op("pe", P.matmul, ch["psO"][:, 32 * c:32 * c + 32], lhsT=sp_[:, hh, :],
                                               rhs=ch["qdm"][:, hh, nb * 128 + 32 * c:nb * 128 + 32 * c + 32], start=False,
                                               stop=(i == len(inter) - 1 and hh == 1), reads=[Bsp, B["qdm"]], writes=[ch["BpsO"]])
                                sch.op("act", A.copy, ch["o"][:, cols], ch["psO"][:], reads=[ch["BpsO"]], writes=[B["o"]])

                            for step in range(NB + 1):
                                if step == 1:
                                    if l == 0 and hp == 0:
                                        cast_weights(0, ["out", "gate", "up"])
                                    elif l == 0 and hp == 1:
                                        cast_weights(0, ["down"])
                                        cast_weights(1, ["in", "out"])
                                    elif l == 1 and hp == 0:
                                        cast_weights(1, ["gate", "up"])
                                    elif l == 1 and hp == 1:
                                        cast_weights(1, ["down"])
                                newp = None
                                if step < NB:
                                    newp = front_all(step)
                                if step >= 1:
                                    for ch in chains:
                                        back(ch, ch["pending"])
                                if step < NB:
                                    for ch, p_ in zip(chains, newp):
                                        ch["pending"] = p_
                            sch.barrier()
                        with ExitStack() as rs:
                            of, ob = chains[0]["o"], chains[1]["o"]
                            Bof, Bob = chains[0]["B"]["o"], chains[1]["B"]["o"]
                            sch.op("dve", V.tensor_tensor, of[:], of[:], ob[:], ALU.add, reads=[Bof, Bob], writes=[Bof])
                            gt = sbt(rs, "hg", [128, T], BF16)
                            Bg = Buf()
                            sch.dma("sp", gt[:], fb[256 + hp * 128:256 + (hp + 1) * 128, :], writes=[Bg])
                            sqa = sbt(rs, "hsqa", [128, T])
                            sga = sbt(rs, "hsga", [128, T])
                            Bsq, Bsg = Buf(), Buf()
                            rr = Rot([sbt(rs, f"hr{i}", [128, 512]) for i in range(2)])
                            orr = Rot([sbt(rs, f"hob{i}", [128, 512], BF16) for i in range(2)])
                            pms = Rot([pst(rs, f"pms{i}", [128, 512]) for i in range(2)])
                            for t0 in range(0, T, 1088):
                                sch.op("act", A.activation, sga[:, t0:t0 + 1088], gt[:, t0:t0 + 1088], AF.Silu, reads=[Bg], writes=[Bsg])
                                sch.op("dve", V.tensor_tensor, sqa[:, t0:t0 + 1088], of[:, t0:t0 + 1088], of[:, t0:t0 + 1088], ALU.mult,
                                       reads=[Bof], writes=[Bsq])
                            for t0 in range(0, T, 512):
                                n = min(512, T - t0)
                                cs = slice(t0, t0 + n)
                                pm_, Bpm = pms.next()
                                sch.op("pe", P.matmul, pm_[:, 0:n], lhsT=bd64[:], rhs=sqa[:, cs], start=True, stop=True, pemode="f32",
                                       reads=[Bmd, Bsq], writes=[Bpm])
                                r_, Br = rr.next()
                                sch.op("act", A.activation, r_[:, 0:n], pm_[:, 0:n], AF.Ln, bias=epsc[:, 0:1], scale=1.0, reads=[Bpm, Bmd], writes=[Br])
                                sch.op("act", A.activation, r_[:, 0:n], r_[:, 0:n], AF.Exp, scale=-0.5, reads=[Br], writes=[Br])
                                sch.op("dve", V.tensor_tensor, r_[:, 0:n], r_[:, 0:n], of[:, cs], ALU.mult, reads=[Br, Bof], writes=[Br])
                                ob_, Bo = orr.next()
                                sch.op("dve", V.scalar_tensor_tensor, ob_[:, 0:n], r_[:, 0:n], ngt[:, l:l + 1], sga[:, cs], ALU.mult, ALU.mult,
                                       reads=[Br, Bl, Bsg], writes=[Bo])
                                sch.dma("pool", catT[256 + hp * 128:256 + (hp + 1) * 128, cs], ob_[:, 0:n], reads=[Bo])
                            sch.barrier()

        def phase_e(l):
            with ExitStack() as es:
                sets = []
                for u in range(2):
                    qm = sbt(es, f"naq{u}", [128, 2, T], BF16)
                    kT = sbt(es, f"nak{u}", [128, T], BF16)
                    vm = sbt(es, f"nav{u}", [128, 2, NB, 128], BF16)
                    Bq, Bk, Bv = Buf(), Buf(), Buf()
                    sch.op("pool", G.memset, qm[:], 0.0, writes=[Bq])
                    sch.op("pool", G.memset, vm[:], 0.0, writes=[Bv])
                    sets.append((qm, kT, vm, Bq, Bk, Bv))
                bg_rng = ((8, 14), (0, 8), (14, 20))
                bias = [[sbt(es, f"nab{hh}{g}", [128, bg_rng[g][1] - bg_rng[g][0], 512]) for g in range(3)] for hh in range(2)]
                Bb = [[Buf() for g in range(3)] for hh in range(2)]
                oT = sbt(es, "nao", [128, T], BF16)
                oneh = sbt(es, "oneh", [128, 2, 128], BF16)
                sbr = Rot([sbt(es, f"nas{i}", [128, 512]) for i in range(4)])
                pr = Rot([sbt(es, f"nap{i}", [128, 512], BF16) for i in range(6)])
                rcr = Rot([sbt(es, f"narc{i}", [128, 512]) for i in range(2)])
                psr = Rot([pst(es, f"naS{i}", [128, 512]) for i in range(4)])
                numr = Rot([pst(es, f"naN{i}", [128, 512]) for i in range(2)])
                denr = Rot([pst(es, f"naD{i}", [128, 512]) for i in range(2)])
                Bo, B1 = Buf(), Buf()
                sch.op("pool", G.memset, oneh[:], 0.0, writes=[B1])
                for hh in range(2):
                    sch.op("pool", G.memset, oneh[:, hh, hh * 64:(hh + 1) * 64], 1.0, reads=[B1], writes=[B1])
                LOOK = 3
                ebgs = make_bg(es, 4)
                if l == 0:
                    pump(40, ebgs, drain=True)

                def load_qkv(hp):
                    qm, kT, vm, Bq, Bk, Bv = sets[hp % 2]
                    for hh in range(2):
                        r0 = 512 + hp * 128 + hh * 64
                        sch.dma("sp", qm[hh * 64:(hh + 1) * 64, hh, :], fb[r0:r0 + 64, :], reads=[Bq], writes=[Bq])
                    sch.dma("sp", kT[:], fb[1024 + hp * 128:1024 + (hp + 1) * 128, :], writes=[Bk])
                    for hh in range(2):
                        c0 = 512 + hp * 128 + hh * 64
                        sch.dma("sp", vm[:, hh, :, hh * 64:(hh + 1) * 64], tm[:, c0:c0 + 64].rearrange("(k p) c -> p k c", p=128),
                                reads=[Bv], writes=[Bv])

                def load_bias(hp, g):
                    t0_, t1_ = bg_rng[g]
                    for hh in range(2):
                        sch.dma("sp", bias[hh][g][:], nabias[l, hp * 2 + hh].rearrange("p (t n) -> p t n", n=512)[:, t0_:t1_, :],
                                writes=[Bb[hh][g]])

                load_qkv(0)
                for g in range(3):
                    load_bias(0, g)
                for hp in range(4):
                    qm, kT, vm, Bq, Bk, Bv = sets[hp % 2]
                    if hp + 1 < 4:
                        load_qkv(hp + 1)
                    items = []
                    for qb in range(8):
                        if qb == 0:
                            kts = [(2 * p * 64, (0, p)) for p in range(6)]
                        elif qb == 7:
                            kts = [((52 + 2 * p) * 64, (2, p)) for p in range(6)]
                        else:
                            kts = [((8 * qb - 4 + 2 * po) * 64, (1, po)) for po in range(8)]
                        kts += [(S, None), (S + 128, None)]
                        items.append((qb * 512, 512, kts, qb))
                    if l == 0:
                        items.append((S, L, [(S, None), (S + 128, None)], 8))
                    for (q0, nq, kts, qb) in items:
                        flat = [(hh, kt0, bi) for hh in range(2) for (kt0, bi) in kts]
                        prod = [None] * len(flat)
                        num, Bn = numr.next()
                        den, Bd = denr.next()
                        for i in range(len(flat) + LOOK):
                            if i < len(flat):
                                hh, kt0, bi = flat[i]
                                ps, Bps = psr.next()
                                sch.op("pe", P.matmul, ps[:, 0:nq], lhsT=kT[:, kt0:kt0 + 128], rhs=qm[:, hh, q0:q0 + nq], start=True, stop=True,
                                       reads=[Bk, Bq], writes=[Bps])
                                pt, Bpt = pr.next()
                                if bi is None:
                                    sch.op("act", A.activation, pt[:, 0:nq], ps[:, 0:nq], AF.Exp, reads=[Bps], writes=[Bpt])
                                else:
                                    st, Bst = sbr.next()
                                    sch.op("dve", V.tensor_tensor, st[:, 0:nq], ps[:, 0:nq], bias[hh][bi[0]][:, bi[1], 0:nq], ALU.add,
                                           reads=[Bps, Bb[hh][bi[0]]], writes=[Bst])
                                    sch.op("act", A.activation, pt[:, 0:nq], st[:, 0:nq], AF.Exp, reads=[Bst], writes=[Bpt])
                                prod[i] = (pt, Bpt)
                            j = i - LOOK
                            if j >= 0:
                                hh, kt0, bi = flat[j]
                                pt, Bpt = prod[j]
                                first = (j == 0)
                                last = (j == len(flat) - 1)
                                sch.op("pe", P.matmul, num[:, 0:nq], lhsT=vm[:, hh, kt0 // 128, :], rhs=pt[:, 0:nq],
                                       start=first, stop=last, reads=[Bv, Bpt], writes=[Bn])
                                sch.op("pe", P.matmul, den[:, 0:nq], lhsT=oneh[:, hh, :], rhs=pt[:, 0:nq],
                                       start=first, stop=last, reads=[B1, Bpt], writes=[Bd])
                        rc, Brc = rcr.next()
                        sch.op("act", A.activation, rc[:, 0:nq], den[:, 0:nq], AF.Ln, reads=[Bd], writes=[Brc])
                        sch.op("act", A.activation, rc[:, 0:nq], rc[:, 0:nq], AF.Exp, scale=-1.0, reads=[Brc], writes=[Brc])
                        sch.op("dve", V.tensor_tensor, oT[:, q0:q0 + nq], num[:, 0:nq], rc[:, 0:nq], ALU.mult, reads=[Bn, Brc], writes=[Bo])
                        if hp + 1 < 4:
                            if qb == 0:
                                load_bias(hp + 1, 0)
                            elif qb == 6:
                                load_bias(hp + 1, 1)
                            elif qb == 7:
                                load_bias(hp + 1, 2)
                    ncol = T if l == 0 else S
                    sch.dma("pool", catT[512 + hp * 128:512 + (hp + 1) * 128, 0:ncol], oT[:, 0:ncol], reads=[Bo])
                    if l == 0 and hp < 3:
                        pump(40, ebgs, drain=True)
                sch.barrier()

        def phase_f1(l):
            xsrc = xin if l == 0 else xres
            with ExitStack() as es:
                wob, Bw = load_weight(es, "woutbf", w_out_b[l].rearrange("(k p) n -> p k n", p=128), 8, D, wcast_buf[("out", l)])
                g1 = [sbt(es, f"g1_{v}", [128, D]) for v in range(2)]
                Bg = Buf()
                for v in range(2):
                    sch.dma("pool", g1[v][:], mod2d[2 * l + v, 2 * D:3 * D].partition_broadcast(128), writes=[Bg])
                fbgs = make_bg(es, 4)
                cr = Rot([sbt(es, f"cat{i}", [128, 8, 512], BF16) for i in range(2)])
                xr = Rot([sbt(es, f"xt{i}", [128, D]) for i in range(3)])
                yr = Rot([sbt(es, f"yt{i}", [128, D]) for i in range(2)])
                pyr = Rot([pst(es, f"py{i}", [128, 512]) for i in range(4)])
                sbs = [(i * 512, 512, 0) for i in range(8)] + ([(S, L, 1)] if l == 0 else [])
                for (t0, ntok, v) in sbs:
                    ct, Bc = cr.next()
                    sch.dma("sp", ct[:, :, 0:ntok], catT[:, t0:t0 + ntok].rearrange("(k p) t -> p k t", p=128), writes=[Bc])
                    for tb in range(ntok // 128):
                        xt, Bx = xr.next()
                        rows = slice(t0 + tb * 128, t0 + (tb + 1) * 128)
                        sch.dma("sp", xt[:], xsrc[rows, :], writes=[Bx])
                        yt, By = yr.next()
                        for nb in range(2):
                            py, Bp = pyr.next()
                            for k in range(8):
                                sch.op("pe", P.matmul, py[:], lhsT=ct[:, k, tb * 128:(tb + 1) * 128], rhs=wob[:, k, nb * 512:(nb + 1) * 512],
                                       start=(k == 0), stop=(k == 7), reads=[Bc, Bw], writes=[Bp])
                            sch.op("dve", V.tensor_tensor, yt[:, nb * 512:(nb + 1) * 512], py[:], g1[v][:, nb * 512:(nb + 1) * 512], ALU.mult,
                                   reads=[Bp, Bg], writes=[By])
                        sch.op("dve", V.tensor_tensor, yt[:], yt[:], xt[:], ALU.add, reads=[By, Bx], writes=[By])
                        sch.dma("pool", xres[rows, :], yt[:], reads=[By])
                        if l == 0:
                            pump(2, fbgs)
                pump(0, fbgs, drain=True)
                sch.barrier()

        def phase_f2(l):
            last = (l == DEPTH - 1)
            with ExitStack() as es:
                wg = sbt(es, "wgbf", [128, 8, DFF], BF16)
                wu = sbt(es, "wubf", [128, 8, DFF], BF16)
                NWC = 4
                WCW = DFF // NWC
                Bwg_c = [Buf() for _ in range(NWC)]
                Bwu_c = [Buf() for _ in range(NWC)]
                for c_ in range(NWC):
                    csl = slice(c_ * WCW, (c_ + 1) * WCW)
                    sch.dma("sp", wg[:, :, csl], w_gate_b[l][:, csl].rearrange("(k p) n -> p k n", p=128),
                            reads=[wcast_buf[("gate", l)]], writes=[Bwg_c[c_]])
                    sch.dma("sp", wu[:, :, csl], w_up_b[l][:, csl].rearrange("(k p) n -> p k n", p=128),
                            reads=[wcast_buf[("up", l)]], writes=[Bwu_c[c_]])
                wd, Bwd = load_weight(es, "wdbf", w_down_b[l].rearrange("(j p) n -> p j n", p=128), NJ, D, wcast_buf[("down", l)])
                junk = sbt(es, "junk", [128, D], BF16)
                nt = (junk, Buf(), Rot([sbt(es, f"ssq{i}", [128, 4]) for i in range(4)]),
                      Rot([sbt(es, f"ntmp{i}", [128, D]) for i in range(1)]))
                Gt = sbt(es, "G2", [128, D])
                SHt = sbt(es, "SH2", [128, D])
                g2 = sbt(es, "g2", [128, D])
                Bm = Buf()

                def load_mods(v):
                    sct, Bsc = nt[3].next()
                    sch.dma("pool", SHt[:], mod2d[2 * l + v, 3 * D:4 * D].partition_broadcast(128), writes=[Bm])
                    sch.dma("pool", Gt[:], norm2_g[l].partition_broadcast(128), writes=[Bm])
                    sch.dma("pool", g2[:], mod2d[2 * l + v, 5 * D:6 * D].partition_broadcast(128), writes=[Bm])
                    sch.dma("pool", sct[:], mod2d[2 * l + v, 4 * D:5 * D].partition_broadcast(128), writes=[Bsc])
                    sch.op("dve", V.scalar_tensor_tensor, Gt[:], sct[:], 1.0, Gt[:], ALU.add, ALU.mult, reads=[Bsc, Bm], writes=[Bm])
                fg = None
                if last:
                    fg = sbt(es, "fg", [128, D])
                    Bfg = Buf()
                    sch.dma("pool", fg[:], final_g.partition_broadcast(128), writes=[Bfg])
                x1 = [(sbt(es, f"x1_{i}", [128, D]), Buf()) for i in range(4)]
                ar = Rot([sbt(es, f"abf{i}", [128, D], BF16) for i in range(1)])
                aT = sbt(es, "aT", [128, 8, 512], BF16)
                BaT = Buf()
                hT = sbt(es, "hT", [128, NJ, 512], BF16)
                BhT = Buf()
                sgr = Rot([sbt(es, f"sg{i}", [128, 512]) for i in range(1)])
                ptrr = Rot([pst(es, f"ptr{i}", [128, 4, 128], BF16) for i in range(1)])
                pgr = Rot([pst(es, f"pg{i}", [128, 512]) for i in range(2)])
                pur = Rot([pst(es, f"pu{i}", [128, 512]) for i in range(2)])
                pdr = Rot([pst(es, f"pd{i}", [128, 512]) for i in range(3)])
                sbs = [(i * 512, 512, 0) for i in range(8)] + ([(S, L, 1)] if l == 0 else [])
                curv = None
                for (t0, ntok, v) in sbs:
                    if v != curv:
                        load_mods(v)
                        curv = v
                    ntb = ntok // 128
                    for tb in range(ntb):
                        xt, Bx = x1[tb]
                        sch.dma("sp", xt[:], xres[t0 + tb * 128:t0 + (tb + 1) * 128, :], writes=[Bx])
                        a_bf, Ba = ar.next()
                        norm_block(nt, xt, Bx, Gt, SHt, Bm, a_bf, Ba)
                        transpose_block(ptrr, a_bf, Ba, aT, BaT, tb, tb)
                    for j in range(NJ):
                        pg, Bpg = pgr.next()
                        pu, Bpu = pur.next()
                        for k in range(8):
                            sch.op("pe", P.matmul, pg[:, 0:ntok], lhsT=wg[:, k, j * 128:(j + 1) * 128], rhs=aT[:, k, 0:ntok],
                                   start=(k == 0), stop=(k == 7),
                                   reads=[Bwg_c[(j * 128) // WCW], Bwg_c[min(NWC - 1, (j * 128 + 127) // WCW)], BaT], writes=[Bpg])
                        for k in range(8):
                            sch.op("pe", P.matmul, pu[:, 0:ntok], lhsT=wu[:, k, j * 128:(j + 1) * 128], rhs=aT[:, k, 0:ntok],
                                   start=(k == 0), stop=(k == 7),
                                   reads=[Bwu_c[(j * 128) // WCW], Bwu_c[min(NWC - 1, (j * 128 + 127) // WCW)], BaT], writes=[Bpu])
                        sg, Bsg = sgr.next()
                        sch.op("act", A.activation, sg[:, 0:ntok], pg[:, 0:ntok], AF.Silu, reads=[Bpg], writes=[Bsg])
                        sch.op("dve", V.tensor_tensor, hT[:, j, 0:ntok], sg[:, 0:ntok], pu[:, 0:ntok], ALU.mult, reads=[Bsg, Bpu], writes=[BhT])
                    for tb in range(ntb):
                        xt, Bx = x1[tb]
                        rows = slice(t0 + tb * 128, t0 + (tb + 1) * 128)
                        tp, Bt = nt[3].next()
                        for nb in range(2):
                            pd, Bpd = pdr.next()
                            for j in range(NJ):
                                sch.op("pe", P.matmul, pd[:], lhsT=hT[:, j, tb * 128:(tb + 1) * 128], rhs=wd[:, j, nb * 512:(nb + 1) * 512],
                                       start=(j == 0), stop=(j == NJ - 1), reads=[BhT, Bwd], writes=[Bpd])
                            sch.op("dve", V.tensor_tensor, tp[:, nb * 512:(nb + 1) * 512], pd[:], g2[:, nb * 512:(nb + 1) * 512], ALU.mult,
                                   reads=[Bpd, Bm], writes=[Bt])
                        sch.op("dve", V.tensor_tensor, xt[:], xt[:], tp[:], ALU.add, reads=[Bx, Bt], writes=[Bx])
                        if not last:
                            sch.dma("pool", xres[rows, :], xt[:], reads=[Bx])
                        else:
                            sq, Bs = nt[2].next()
                            sch.op("act", A.activation, junk[:], xt[:], AF.Square, accum_out=sq[:, 0:1], reads=[Bx], writes=[nt[1], Bs])
                            sch.op("act", A.activation, sq[:, 1:2], sq[:, 0:1], AF.Sqrt, bias=EPS, scale=1.0 / D, reads=[Bs], writes=[Bs])
                            sch.op("dve", V.reciprocal, sq[:, 2:3], sq[:, 1:2], reads=[Bs], writes=[Bs])
                            sch.op("dve", V.scalar_tensor_tensor, xt[:], xt[:], sq[:, 2:3], fg[:], ALU.mult, ALU.mult,
                                   reads=[Bx, Bs, Bfg], writes=[Bx])
                            sch.dma("pool", out[rows, :], xt[:], reads=[Bx])
                sch.barrier()

        cast_weights(0, ["in"])
        phases = [("mod", phase_mod, None)]
        for l in range(DEPTH):
            phases += [(f"ab{l}", phase_ab, l), (f"c{l}", phase_c, l), (f"d{l}", phase_d, l), (f"e{l}", phase_e, l),
                       (f"f1{l}", phase_f1, l), (f"f2{l}", phase_f2, l)]
        for name, fn, arg in phases:
            if arg is None:
                fn()
            else:
                fn(arg)
            if stop is not None and name == stop:
                break
        sch.finish()
        nc._sched_stats = (sch.nops, sch.nwaits)
    return nc


def _prep(inputs):
    f32 = np.float32
    c = _consts()
    x = np.asarray(inputs["x"], f32)
    ctx = np.asarray(inputs["ctx"], f32)
    cvec = np.asarray(inputs["c"], f32)
    cctx = np.asarray(inputs["c_ctx"], f32)
    lb = np.asarray(inputs["hgrn_lb"], f32)
    lbT = np.ascontiguousarray(lb.reshape(2, 2, 2, 128).transpose(3, 0, 1, 2).reshape(128, 8))
    ng = np.asarray(inputs["hgrn_norm_g"], f32)
    ngT = np.ascontiguousarray(np.concatenate([ng, ng], axis=1).T)
    rpb = np.asarray(inputs["na_rpb"], f32)
    nabias = np.stack([_na_bias_tiles(rpb[l]) for l in range(DEPTH)])
    shared = {
        "w_mod": np.asarray(inputs["w_mod"], f32), "b_mod": np.asarray(inputs["b_mod"], f32),
        "norm1_g": np.asarray(inputs["norm1_g"], f32), "w_in": np.asarray(inputs["w_in"], f32),
        "fourier_w": np.asarray(inputs["fourier_w"], f32), "lbT": lbT, "ngT": ngT, "nabias": nabias,
        "w_out": np.asarray(inputs["w_out"], f32), "norm2_g": np.asarray(inputs["norm2_g"], f32),
        "w_gate": np.asarray(inputs["w_ffn_gate"], f32), "w_up": np.asarray(inputs["w_ffn_up"], f32),
        "w_down": np.asarray(inputs["w_ffn_down"], f32), "final_g": np.asarray(inputs["final_norm_g"], f32),
        "C4": c["C4"], "S4": c["S4"], "C2": c["C2"], "S2": c["S2"], "C64bd": c["C64bd"], "S64bdn": c["S64bdn"],
        "ident": c["ident"], "bdmask": c["bdmask"], "bd64": c["bd64"], "mfwd": c["mfwd"], "mbwd": c["mbwd"], "cm3": c["cm3"], "jperm": c["jperm"],
    }
    maps = []
    for b in range(8):
        m = dict(shared)
        m["xin"] = np.ascontiguousarray(np.concatenate([x[b], ctx[b]], axis=0))
        cv = np.stack([cvec[b], cctx], axis=0)
        m["cT"] = np.ascontiguousarray(cv.reshape(2, 8, 128).transpose(2, 1, 0).reshape(128, 16))
        maps.append(m)
    return maps


def kernel(**inputs):
    maps = _prep(inputs)
    nc = build()
    res = run_bass_kernel_spmd(nc, maps, core_ids=list(range(8)))
    return np.stack([np.asarray(r["out"], np.float32) for r in res.results], axis=0)
```

```python
import numpy as np
import ml_dtypes
from contextlib import ExitStack
import concourse.bass as bass
import concourse.mybir as mybir
from concourse.bass_utils import run_bass_kernel_spmd

F32 = mybir.dt.float32
BF16 = mybir.dt.bfloat16
AF = mybir.ActivationFunctionType
ALU = mybir.AluOpType

D = 1024
S = 4096
L = 256
T = S + L
NB = T // 128
DFF = 2816
NJ = DFF // 128
INW = 3072
EPS = 1e-6
NEG = -1e30
DEPTH = 2


class _Op:
    __slots__ = ("eng", "fn", "args", "kw", "deps", "signal", "sem", "val", "dma", "epoch")


class Buf:
    __slots__ = ("w", "r", "g")

    def __init__(self):
        self.w = {}
        self.r = {}
        self.g = ()


class Sched:
    CENG = ("pe", "act", "dve", "pool")
    SEMCAP = 30000

    def __init__(self, nc, es):
        self.nc = nc
        self.eobj = dict(pe=nc.tensor, act=nc.scalar, dve=nc.vector, pool=nc.gpsimd, sp=nc.sync)
        self.pending = []
        self.lastc = {}
        self.dma_since = []
        self.bar_deps = {}
        ncs = dict(pe=3, act=3, dve=4, pool=2)
        self.csem = {e: [es.enter_context(nc.semaphore(f"c{e}{i}")) for i in range(ncs[e])] for e in self.CENG}
        self.ccount = {e: 0 for e in self.CENG}
        self.dsem = {q: [es.enter_context(nc.semaphore(f"d{q}{i}")) for i in range(8)] for q in ("sp", "pool", "act")}
        self.dcount = {q: [0] * 8 for q in ("sp", "pool", "act")}
        self.drr = {q: 0 for q in ("sp", "pool", "act")}
        self.waited = {e: {} for e in self.eobj}
        self.nops = 0
        self.nwaits = 0
        self.epoch = 0
        self.pe_mode = "f"
        self.bg_ops = []

    def op(self, eng, fn, *args, reads=(), writes=(), dma=False, force=False, bg=False, **kw):
        o = _Op()
        o.eng, o.fn, o.args, o.kw, o.dma = eng, fn, args, kw, dma
        o.signal = dma or force
        o.sem = None
        o.val = 0
        o.epoch = self.epoch
        deps = {}
        if eng == "pe":
            mode = kw.pop("pemode", "f")
            if mode != self.pe_mode and "pe" in self.lastc:
                deps[id(self.lastc["pe"])] = self.lastc["pe"]
            self.pe_mode = mode

        def need(d, same_ok, bar=False):
            if d.epoch < self.epoch and not bar:
                return
            if same_ok and (not d.dma) and (not dma) and d.eng == eng:
                return
            deps[id(d)] = d

        for b in reads:
            for d in b.w.values():
                need(d, eng == "pe")
        for b in writes:
            if b.r:
                b.g = tuple(b.r.values()) + tuple(b.w.values())
                b.r = {}
                b.w = {}
            for d in b.g:
                need(d, True)
        for d in self.bar_deps.pop(eng, ()):
            need(d, True, True)
        for d in deps.values():
            d.signal = True
        o.deps = list(deps.values())
        key = ("d", id(o)) if dma else eng
        for b in reads:
            b.r[key] = o
        for b in writes:
            b.w[key] = o
        if dma and bg:
            o.epoch = 1 << 60
            self.bg_ops.append(o)
        elif dma:
            self.dma_since.append(o)
        else:
            self.lastc[eng] = o
        self.pending.append(o)
        return o

    def dma(self, q, out, in_, reads=(), writes=(), **kw):
        fn = {"sp": self.nc.sync.dma_start, "pool": self.nc.gpsimd.dma_start, "act": self.nc.scalar.dma_start}[q]
        return self.op(q, fn, reads=reads, writes=writes, dma=True, out=out, in_=in_, **kw)

    def flush(self):
        for o in self.pending:
            e = self.eobj[o.eng]
            w = self.waited[o.eng]
            for d in o.deps:
                assert d.sem is not None, "dependency on unsignalled op"
                k = id(d.sem)
                if w.get(k, 0) >= d.val:
                    continue
                e.wait_ge(d.sem, d.val)
                w[k] = d.val
                self.nwaits += 1
            ins = o.fn(*o.args, **o.kw)
            self.nops += 1
            if o.dma:
                q = o.eng
                i = self.drr[q]
                self.drr[q] = (i + 1) % 8
                self.dcount[q][i] += 16
                o.sem, o.val = self.dsem[q][i], self.dcount[q][i]
                ins.then_inc(o.sem, 16)
            elif o.signal:
                n = self.ccount[o.eng]
                self.ccount[o.eng] = n + 1
                sems = self.csem[o.eng]
                si = n // self.SEMCAP
                assert si < len(sems), "out of compute semaphores"
                o.sem, o.val = sems[si], n % self.SEMCAP + 1
                ins.then_inc(o.sem, 1)
            o.fn = o.args = o.kw = None
        self.pending = []

    def barrier(self):
        deps = [self.lastc[e] for e in self.CENG if e in self.lastc] + self.dma_since
        for d in deps:
            d.signal = True
        self.flush()
        old = self.bar_deps
        self.bar_deps = {e: list(deps) + list(old.get(e, ())) for e in self.eobj}
        self.dma_since = []
        self.epoch += 1

    def finish(self):
        self.barrier()
        sp = self.nc.sync
        for d in list(self.bar_deps["sp"]) + self.bg_ops:
            k = id(d.sem)
            if self.waited["sp"].get(k, 0) >= d.val:
                continue
            sp.wait_ge(d.sem, d.val)
            self.waited["sp"][k] = d.val


class Rot:
    def __init__(self, tiles):
        self.t = [(t, Buf()) for t in tiles]
        self.i = 0

    def next(self):
        r = self.t[self.i]
        self.i = (self.i + 1) % len(self.t)
        return r


_CONST = {}


def _consts():
    if _CONST:
        return _CONST
    bf = ml_dtypes.bfloat16
    t = np.arange(S, dtype=np.int64)
    m = (t[:, None] * t[None, :]) % S
    ang = 2.0 * np.pi * m.astype(np.float64) / S
    _CONST["C4"] = (np.cos(ang) / 64.0).astype(np.float32).astype(bf)
    _CONST["S4"] = (np.sin(ang) / 64.0).astype(np.float32).astype(bf)
    del ang, m
    t = np.arange(L, dtype=np.int64)
    ang = 2.0 * np.pi * ((t[:, None] * t[None, :]) % L).astype(np.float64) / L
    _CONST["C2"] = (np.cos(ang) / 16.0).astype(np.float32).astype(bf)
    _CONST["S2"] = (np.sin(ang) / 16.0).astype(np.float32).astype(bf)
    t = np.arange(64, dtype=np.int64)
    ang = 2.0 * np.pi * ((t[:, None] * t[None, :]) % 64).astype(np.float64) / 64
    c64 = np.cos(ang) / 8.0
    s64 = np.sin(ang) / 8.0
    z = np.zeros((64, 64))
    _CONST["C64bd"] = np.block([[c64, z], [z, c64]]).astype(np.float32)
    _CONST["S64bdn"] = (-np.block([[s64, z], [z, s64]])).astype(np.float32)
    _CONST["ident"] = np.eye(128, dtype=np.float32).astype(bf)
    bd = np.zeros((128, 128), np.float32)
    bd[:64, :64] = 1.0
    bd[64:, 64:] = 1.0
    _CONST["bdmask"] = bd
    _CONST["bd64"] = (bd / 64.0).astype(np.float32)
    s_ = np.arange(128)[:, None]
    t_ = np.arange(128)[None, :]
    same = (s_ // 32) == (t_ // 32)
    _CONST["mfwd"] = (same & (s_ <= t_)).astype(np.float32)
    _CONST["mbwd"] = (same & (s_ >= t_)).astype(np.float32)
    cm = np.zeros((128, 4, 128), np.float32)
    for c_ in range(4):
        cm[32 * c_:32 * c_ + 32, c_, :] = 1.0
    _CONST["cm3"] = cm.reshape(128, 512)
    jp = np.zeros((128, 2, 128), np.float32)
    for p_ in range(1, 128):
        jp[128 - p_, 0, p_] = 1.0
    jp[0, 1, 0] = 1.0
    _CONST["jperm"] = jp.reshape(128, 256).astype(bf)
    return _CONST


def _na_bias_tiles(rpb):
    H = rpb.shape[0]
    kc = np.arange(64)[:, None]
    qc = np.arange(64)[None, :]
    wc = np.clip(qc - 8, 0, 48)
    colvalid = (kc >= wc) & (kc < wc + 16)
    cidx = np.clip(kc - qc + 15, 0, 30)
    specs = [(8, 8 - 4 + 2 * po) for po in range(8)] + [(0, 2 * p) for p in range(6)] + [(56, 52 + 2 * p) for p in range(6)]
    out = np.full((H, 128, 20, 512), NEG, np.float32)
    for ti, (r0, kr0) in enumerate(specs):
        for kr2 in range(2):
            kr = kr0 + kr2
            for qr in range(8):
                r = r0 + qr
                rs = min(max(r - 4, 0), 56)
                if not (rs <= kr < rs + 8):
                    continue
                ridx = kr - r + 7
                blk = np.where(colvalid[None], rpb[:, ridx][:, cidx], NEG)
                out[:, kr2 * 64:(kr2 + 1) * 64, ti, qr * 64:(qr + 1) * 64] = blk
    return out.reshape(H, 128, 20 * 512)


def build(debug=False, stop=None):
    nc = bass.Bass("TRN2", target_bir_lowering=False)
    skind = "ExternalOutput" if debug else "Internal"

    def din(name, shape, dt=F32):
        return nc.dram_tensor(name, list(shape), dt, kind="ExternalInput").ap()

    def dscr(name, shape, dt=F32):
        return nc.dram_tensor(name, list(shape), dt, kind=skind).ap()

    xin = din("xin", [T, D])
    cT = din("cT", [128, 16])
    w_mod = din("w_mod", [DEPTH, D, 6 * D])
    b_mod = din("b_mod", [DEPTH, 6 * D])
    norm1_g = din("norm1_g", [DEPTH, D])
    w_in = din("w_in", [DEPTH, D, INW])
    fourier_w = din("fourier_w", [DEPTH, 4, 64, 64])
    lbT = din("lbT", [128, 8])
    ngT = din("ngT", [128, DEPTH])
    nabias = din("nabias", [DEPTH, 8, 128, 20 * 512])
    w_out = din("w_out", [DEPTH, D, D])
    norm2_g = din("norm2_g", [DEPTH, D])
    w_gate = din("w_gate", [DEPTH, D, DFF])
    w_up = din("w_up", [DEPTH, D, DFF])
    w_down = din("w_down", [DEPTH, DFF, D])
    final_g = din("final_g", [D])
    C4 = din("C4", [S, S], BF16)
    S4 = din("S4", [S, S], BF16)
    C2 = din("C2", [L, L], BF16)
    S2 = din("S2", [L, L], BF16)
    C64bd_d = din("C64bd", [128, 128])
    S64bdn_d = din("S64bdn", [128, 128])
    ident_d = din("ident", [128, 128], BF16)
    bdmask_d = din("bdmask", [128, 128])
    bd64_d = din("bd64", [128, 128])
    mfwd_d = din("mfwd", [128, 128])
    mbwd_d = din("mbwd", [128, 128])
    cm3_d = din("cm3", [128, 512])
    jperm_d = din("jperm", [128, 256], BF16)
    out = nc.dram_tensor("out", [S, D], F32, kind="ExternalOutput").ap()

    mod2d = dscr("mod2d", [2 * DEPTH, 6 * D])
    fz = dscr("fz", [512, T])
    fb = dscr("fb", [1536, T], BF16)
    tm = dscr("tm", [T, 1024], BF16)
    catT = dscr("catT", [1024, T], BF16)
    xres = dscr("xres", [T, D])
    w_in_b = dscr("w_in_b", [DEPTH, D, INW], BF16)
    w_out_b = dscr("w_out_b", [DEPTH, D, D], BF16)
    w_gate_b = dscr("w_gate_b", [DEPTH, D, DFF], BF16)
    w_up_b = dscr("w_up_b", [DEPTH, D, DFF], BF16)
    w_down_b = dscr("w_down_b", [DEPTH, DFF, D], BF16)

    with ExitStack() as ges:
        sch = Sched(nc, ges)
        V, A, P, G = nc.vector, nc.scalar, nc.tensor, nc.gpsimd

        uid = [0]

        def sbt(es, name, shape, dt=F32):
            uid[0] += 1
            return es.enter_context(nc.sbuf_tensor(f"s{uid[0]}_{name}", list(shape), dt))

        def pst(es, name, shape, dt=F32):
            uid[0] += 1
            return es.enter_context(nc.psum_tensor(f"p{uid[0]}_{name}", list(shape), dt))

        ident = sbt(ges, "ident", [128, 128], BF16)
        bdmask = sbt(ges, "bdmask", [128, 128])
        onesb = sbt(ges, "onesb", [128, 64], BF16)
        B_const = Buf()
        sch.dma("sp", ident[:], ident_d, writes=[B_const])
        sch.dma("sp", bdmask[:], bdmask_d, writes=[B_const])
        sch.op("pool", G.memset, onesb[:], 1.0, writes=[B_const], force=True)

        def evac(i, outap, inap, reads, writes, scale=None):
            if i % 2 == 0:
                if scale is None:
                    sch.op("act", A.copy, outap, inap, reads=reads, writes=writes)
                else:
                    sch.op("act", A.mul, outap, inap, scale, reads=reads, writes=writes)
            else:
                if scale is None:
                    sch.op("dve", V.tensor_copy, outap, inap, reads=reads, writes=writes)
                else:
                    sch.op("dve", V.tensor_scalar, outap, inap, scale, None, ALU.mult, reads=reads, writes=writes)

        def phase_mod():
            with ExitStack() as es:
                cTt = sbt(es, "cTt", [128, 16])
                sT = sbt(es, "sT", [128, 16])
                mod2s = sbt(es, "mod2s", [2, 6 * D])
                bt = sbt(es, "bt", [2, 6 * D])
                wm = Rot([sbt(es, f"wm{i}", [128, 8, 512]) for i in range(3)])
                pm = Rot([pst(es, f"pm{i}", [128, 512]) for i in range(2)])
                B_c, B_s, B_m, B_b = Buf(), Buf(), Buf(), Buf()
                sch.dma("sp", cTt[:], cT, writes=[B_c])
                sch.op("act", A.activation, sT[:], cTt[:], AF.Silu, reads=[B_c], writes=[B_s])
                for l in range(DEPTH):
                    sch.dma("sp", bt[:], b_mod[l].partition_broadcast(2), writes=[B_b])
                    for nb in range(12):
                        wt, Bw = wm.next()
                        sch.dma("sp", wt[:], w_mod[l][:, nb * 512:(nb + 1) * 512].rearrange("(k p) n -> p k n", p=128), writes=[Bw])
                        pt, Bp = pm.next()
                        for k in range(8):
                            sch.op("pe", P.matmul, pt[0:2, :], lhsT=sT[:, 2 * k:2 * k + 2], rhs=wt[:, k, :],
                                   start=(k == 0), stop=(k == 7), pemode="m2", reads=[B_s, Bw], writes=[Bp])
                        sch.op("dve", V.tensor_tensor, mod2s[:, nb * 512:(nb + 1) * 512], pt[0:2, :], bt[:, nb * 512:(nb + 1) * 512],
                               ALU.add, reads=[Bp, B_b], writes=[B_m])
                    sch.dma("pool", mod2d[2 * l:2 * l + 2, :], mod2s[:], reads=[B_m])
                sch.barrier()

        def norm_block(es_tiles, xt, Bx, Gt, SHt, B_mod, a_bf, Ba):
            junk, Bj, ssqr, tmpr = es_tiles
            sq, Bs = ssqr.next()
            sch.op("act", A.activation, junk[:], xt[:], AF.Square, accum_out=sq[:, 0:1], reads=[Bx], writes=[Bj, Bs])
            sch.op("act", A.activation, sq[:, 1:2], sq[:, 0:1], AF.Sqrt, bias=EPS, scale=1.0 / D, reads=[Bs], writes=[Bs])
            sch.op("dve", V.reciprocal, sq[:, 2:3], sq[:, 1:2], reads=[Bs], writes=[Bs])
            tp, Bt = tmpr.next()
            sch.op("dve", V.scalar_tensor_tensor, tp[:], xt[:], sq[:, 2:3], Gt[:], ALU.mult, ALU.mult,
                   reads=[Bx, Bs, B_mod], writes=[Bt])
            sch.op("dve", V.tensor_tensor, a_bf[:], tp[:], SHt[:], ALU.add, reads=[Bt, B_mod], writes=[Ba])

        def transpose_block(ptrr, a_bf, Ba, aT, BaT, tb, ei):
            for half in range(2):
                pt, Bp = ptrr.next()
                for j in range(4):
                    c = (half * 4 + j) * 128
                    sch.op("pe", P.transpose, pt[:, j, :], a_bf[:, c:c + 128], ident[:], reads=[Ba, B_const], writes=[Bp])
                evac(ei + half, aT[:, half * 4:(half + 1) * 4, tb * 128:(tb + 1) * 128], pt[:], [Bp], [BaT])

        def load_weight(es, name, src_kpn, nk, ncols, Bsrc):
            wb = sbt(es, name, [128, nk, ncols], BF16)
            Bw = Buf()
            for k in range(nk):
                sch.dma("sp", wb[:, k, :], src_kpn[:, k, :], reads=[Bsrc], writes=[Bw])
            return wb, Bw

        wcast_buf = {}

        def cast_weights(l, names):
            table = {"in": (w_in, w_in_b, D), "out": (w_out, w_out_b, D), "gate": (w_gate, w_gate_b, D), "up": (w_up, w_up_b, D),
                     "down": (w_down, w_down_b, DFF)}
            for nm in names:
                src, dst, R_ = table[nm]
                Bw = Buf()
                wcast_buf[(nm, l)] = Bw
                for r0 in range(0, R_, 128):
                    sch.dma("pool", dst[l][r0:r0 + 128, :], src[l][r0:r0 + 128, :], writes=[Bw], bg=True)

        def pump(*a_, **k_):
            return

        def make_bg(es, n):
            return None

        bgq = []

        def mod_tiles(es, l, v, gsrc, i_sh, i_sc, tag):
            Gt = sbt(es, f"G{tag}", [128, D])
            SHt = sbt(es, f"SH{tag}", [128, D])
            Bm = Buf()
            sch.dma("pool", SHt[:], mod2d[2 * l + v, i_sh * D:(i_sh + 1) * D].partition_broadcast(128), writes=[Bm])
            sch.dma("pool", Gt[:], gsrc.partition_broadcast(128), writes=[Bm])
            return Gt, SHt, Bm

        def phase_ab(l):
            xsrc = xin if l == 0 else xres
            if l >= 1 and bgq:
                with ExitStack() as es0:
                    pump(100000, make_bg(es0, 4), drain=True)
                    sch.barrier()
            with ExitStack() as es:
                wbf, Bw = load_weight(es, "winbf", w_in_b[l].rearrange("(k p) n -> p k n", p=128), 8, INW, wcast_buf[("in", l)])
                mods = []
                sct = sbt(es, "sct", [128, D])
                Bsc = Buf()
                for v in range(2):
                    Gt, SHt, Bm = mod_tiles(es, l, v, norm1_g[l], 0, 1, f"1_{v}")
                    sch.dma("pool", sct[:], mod2d[2 * l + v, D:2 * D].partition_broadcast(128), writes=[Bsc])
                    sch.op("dve", V.scalar_tensor_tensor, Gt[:], sct[:], 1.0, Gt[:], ALU.add, ALU.mult, reads=[Bsc, Bm], writes=[Bm])
                    mods.append((Gt, SHt, Bm))
                junk = sbt(es, "junk", [128, D], BF16)
                nt = (junk, Buf(), Rot([sbt(es, f"ssq{i}", [128, 4]) for i in range(4)]),
                      Rot([sbt(es, f"ntmp{i}", [128, D]) for i in range(2)]))
                xr = Rot([sbt(es, f"xt{i}", [128, D]) for i in range(8)])
                ar = Rot([sbt(es, f"abf{i}", [128, D], BF16) for i in range(2)])
                aTr = Rot([sbt(es, f"aT{i}", [128, 8, 512], BF16) for i in range(2)])
                sfr = Rot([sbt(es, f"sf{i}", [128, 512]) for i in range(2)])
                sbr = Rot([sbt(es, f"sb{i}", [128, 512], BF16) for i in range(3)])
                tmr = Rot([sbt(es, f"tms{i}", [128, 1024], BF16) for i in range(2)])
                ptrr = Rot([pst(es, f"ptr{i}", [128, 4, 128], BF16) for i in range(2)])
                pfr = Rot([pst(es, f"pf{i}", [128, 512]) for i in range(3)])
                ptmr = Rot([pst(es, f"ptm{i}", [128, 512]) for i in range(2)])
                fm = []
                for c0, dst, r0, sc in ((256, fb, 0, None), (512, fz, 0, None), (768, fz, 256, None), (1280, fb, 256, None),
                                        (1536, fb, 512, 0.125), (2048, fb, 1024, None)):
                    wd = 512 if c0 >= 1536 else 256
                    for g in range(wd // 128):
                        fm.append((c0 + g * 128, dst, r0 + g * 128, sc))
                sbs = [(i * 512, 512, 0) for i in range(8)] + [(S, L, 1)]
                eic = [0]
                aTs = {}
                abfs = {}

                xts = {}

                def load_part(i):
                    t0, ntok, v = sbs[i]
                    for tb in range(ntok // 128):
                        xt, Bx = xr.next()
                        sch.dma("sp", xt[:], xsrc[t0 + tb * 128:t0 + (tb + 1) * 128, :], writes=[Bx])
                        xts[(i, tb)] = (xt, Bx)

                def norm_part(i, tb):
                    t0, ntok, v = sbs[i]
                    Gt, SHt, Bm = mods[v]
                    if tb == 0:
                        aTs[i] = aTr.next()
                    xt, Bx = xts.pop((i, tb))
                    a_bf, Ba = ar.next()
                    norm_block(nt, xt, Bx, Gt, SHt, Bm, a_bf, Ba)
                    abfs[(i, tb)] = (a_bf, Ba)

                def tr_part(i, tb):
                    aT, BaT = aTs[i]
                    a_bf, Ba = abfs.pop((i, tb))
                    eic[0] += 2
                    transpose_block(ptrr, a_bf, Ba, aT, BaT, tb, eic[0])

                load_part(0)
                for tb in range(sbs[0][1] // 128):
                    norm_part(0, tb)
                    tr_part(0, tb)
                for i, (t0, ntok, v) in enumerate(sbs):
                    aT, BaT = aTs[i]
                    if i + 1 < len(sbs):
                        load_part(i + 1)
                    groups = []
                    for (c0, dst, r0, sc) in fm:
                        groups.append(("fm", c0, dst, r0, sc))
                    for tb in range(ntok // 128):
                        groups.append(("tm", tb))
                    nxt = {}
                    if i + 1 < len(sbs):
                        ntb2 = sbs[i + 1][1] // 128
                        for tb in range(ntb2):
                            nxt.setdefault(1 + 5 * tb, []).append(("n", tb))
                            nxt.setdefault(4 + 5 * tb, []).append(("t", tb))
                    for gi, g in enumerate(groups):
                        for (kind, tb) in nxt.pop(gi, []):
                            (norm_part if kind == "n" else tr_part)(i + 1, tb)
                        if g[0] == "fm":
                            _, c0, dst, r0, sc = g
                            pf, Bp = pfr.next()
                            for k in range(8):
                                sch.op("pe", P.matmul, pf[:, 0:ntok], lhsT=wbf[:, k, c0:c0 + 128], rhs=aT[:, k, 0:ntok],
                                       start=(k == 0), stop=(k == 7), reads=[Bw, BaT], writes=[Bp])
                            if dst is fz:
                                st, Bs = sfr.next()
                            else:
                                st, Bs = sbr.next()
                            evac(0, st[:, 0:ntok], pf[:, 0:ntok], [Bp], [Bs], scale=sc)
                            sch.dma("sp", dst[r0:r0 + 128, t0:t0 + ntok], st[:, 0:ntok], reads=[Bs])
                        else:
                            tb = g[1]
                            tms, Bts = tmr.next()
                            tsl = slice(tb * 128, (tb + 1) * 128)
                            p0, Bp0 = ptmr.next()
                            for (cc, oc) in ((0, 0), (1024, 256)):
                                for k in range(8):
                                    sch.op("pe", P.matmul, p0[:, oc:oc + 256], lhsT=aT[:, k, tsl], rhs=wbf[:, k, cc:cc + 256],
                                           start=(k == 0), stop=(k == 7), reads=[Bw, BaT], writes=[Bp0])
                            evac(0, tms[:, 0:512], p0[:], [Bp0], [Bts])
                            p1, Bp1 = ptmr.next()
                            for k in range(8):
                                sch.op("pe", P.matmul, p1[:], lhsT=aT[:, k, tsl], rhs=wbf[:, k, 2560:3072],
                                       start=(k == 0), stop=(k == 7), reads=[Bw, BaT], writes=[Bp1])
                            evac(0, tms[:, 512:1024], p1[:], [Bp1], [Bts])
                            sch.dma("sp", tm[t0 + tb * 128:t0 + (tb + 1) * 128, :], tms[:], reads=[Bts])
                    for lst in nxt.values():
                        for (kind, tb) in lst:
                            (norm_part if kind == "n" else tr_part)(i + 1, tb)
                sch.barrier()

        def phase_c(l):
            with ExitStack() as es:
                cbgs = make_bg(es, 4)
                U = sbt(es, "fU", [128, NB, 256], BF16)
                BU = Buf()
                sch.dma("sp", U[:], tm[:, 0:256].rearrange("(k p) c -> p k c", p=128), writes=[BU])
                c64 = sbt(es, "c64", [128, 2, 128])
                wf = sbt(es, "wf", [128, 2, 64])
                Mbd = sbt(es, "Mbd", [128, 3, 2, 128], BF16)
                Bc, Bwf, BM = Buf(), Buf(), Buf()
                sch.dma("sp", c64[:, 0, :], C64bd_d, writes=[Bc])
                sch.dma("sp", c64[:, 1, :], S64bdn_d, writes=[Bc])
                for cb in range(2):
                    sch.dma("sp", wf[:, cb, :], fourier_w[l][2 * cb:2 * cb + 2].rearrange("g c d -> (g c) d"), writes=[Bwf])
                sch.op("pool", G.memset, Mbd[:], 0.0, writes=[BM])
                pmr = Rot([pst(es, f"pM{i}", [128, 512]) for i in range(1)])
                pM, BpM = pmr.next()
                for w in range(2):
                    for cb in range(2):
                        o0 = (w * 2 + cb) * 64
                        sch.op("pe", P.matmul, pM[:, o0:o0 + 64], lhsT=c64[:, w, :], rhs=wf[:, cb, :], start=True, stop=True,
                               pemode="f32", reads=[Bc, Bwf], writes=[BpM])
                for w in range(2):
                    for cb in range(2):
                        o0 = (w * 2 + cb) * 64
                        sch.op("dve", V.tensor_copy, Mbd[0:64, w, cb, 0:64], pM[0:64, o0:o0 + 64], reads=[BpM, BM], writes=[BM])
                        sch.op("dve", V.tensor_copy, Mbd[64:128, w, cb, 64:128], pM[64:128, o0:o0 + 64], reads=[BpM, BM], writes=[BM])
                sch.op("dve", V.tensor_scalar, Mbd[:, 2].rearrange("p c n -> p (c n)"), Mbd[:, 1].rearrange("p c n -> p (c n)"), -1.0, None,
                       ALU.mult, reads=[BM], writes=[BM])
                csr = Rot([sbt(es, f"cs{i}", [128, 2, 4, 512], BF16) for i in range(3)])
                pqr = Rot([sbt(es, f"pq{i}", [128, 4, 512], BF16) for i in range(2)])
                osr = Rot([sbt(es, f"fo{i}", [128, 512], BF16) for i in range(3)])
                pacc = [pst(es, f"pacc{i}", [128, 512]) for i in range(4)]
                Bacc = [Buf() for _ in range(4)]
                por = Rot([pst(es, f"po{i}", [128, 512]) for i in range(2)])
                c0col = sbt(es, "c0col", [128, 2], BF16)
                Bc0 = Buf()
                sch.op("pool", G.memset, c0col[:], 1.0 / 64.0, writes=[Bc0])

                def mix(pq, Bpq, n, tcol0, mirror_col0=None, ncols=None):
                    for cb in range(2):
                        po, Bpo = por.next()
                        sch.op("pe", P.matmul, po[:, 0:n], lhsT=Mbd[:, 0, cb, :], rhs=pq[:, cb, 0:n], start=True, stop=False,
                               reads=[BM, Bpq], writes=[Bpo])
                        sch.op("pe", P.matmul, po[:, 0:n], lhsT=Mbd[:, 1, cb, :], rhs=pq[:, 2 + cb, 0:n], start=False, stop=True,
                               reads=[BM, Bpq], writes=[Bpo])
                        ot, Bo = osr.next()
                        evac(cb, ot[:, 0:n], po[:, 0:n], [Bpo], [Bo])
                        nst = n if ncols is None else ncols
                        if nst == 1:
                            sch.dma("pool", catT[cb * 128:(cb + 1) * 128, tcol0:tcol0 + 1], ot[:, 0:1], reads=[Bo],
                                    allow_slow_non_contiguous=True)
                        else:
                            sch.dma("pool", catT[cb * 128:(cb + 1) * 128, tcol0:tcol0 + nst], ot[:, 0:nst], reads=[Bo])
                        if mirror_col0 is not None:
                            po, Bpo = por.next()
                            for (wm, qi) in ((0, cb), (2, 2 + cb)):
                                bq_ = pq[:, qi, 0:n]
                                rev = bass.AP(tensor=bq_.tensor, offset=bq_.offset + n - 1, ap=[[bq_.ap[0][0], 128], [-1, n]])
                                sch.op("pe", P.matmul, po[:, 0:n], lhsT=Mbd[:, wm, cb, :], rhs=rev, start=(wm == 0), stop=(wm == 2),
                                       reads=[BM, Bpq], writes=[Bpo])
                            ot, Bo = osr.next()
                            evac(cb + 1, ot[:, 0:n], po[:, 0:n], [Bpo], [Bo])
                            sch.dma("pool", catT[cb * 128:(cb + 1) * 128, mirror_col0:mirror_col0 + n], ot[:, 0:n], reads=[Bo])

                for cb in range(2):
                    for k in range(32):
                        sch.op("pe", P.matmul, pacc[cb][:, 0:2], lhsT=U[:, k, cb * 128:(cb + 1) * 128], rhs=c0col[:],
                               start=(k == 0), stop=(k == 31), reads=[BU, Bc0], writes=[Bacc[cb]])
                pq, Bpq = pqr.next()
                sch.op("pool", G.memset, pq[:, 2:4, 0:2], 0.0, writes=[Bpq])
                for cb in range(2):
                    evac(cb, pq[:, cb, 0:2], pacc[cb][:, 0:2], [Bacc[cb], Bpq], [Bpq])
                mix(pq, Bpq, 2, 0, ncols=1)
                urev = sbt(es, "urev", [128, 16, 256], BF16)
                Ue = sbt(es, "Ue", [128, 16, 256], BF16)
                Uo = sbt(es, "Uo", [128, 16, 256], BF16)
                Ux = sbt(es, "Ux", [128, 256], BF16)
                Brev, BUe, BUx = Buf(), Buf(), Buf()
                jm = sbt(es, "jm", [128, 2, 128], BF16)
                Bjm = Buf()
                sch.dma("sp", jm[:], jperm_d.rearrange("p (a n) -> p a n", n=128), writes=[Bjm])
                for k in range(16):
                    pr_, Bpr = por.next()
                    sch.op("pe", P.matmul, pr_[:, 0:256], lhsT=jm[:, 0, :], rhs=U[:, 31 - k, :], start=True, stop=(k == 0),
                           reads=[Bjm, BU], writes=[Bpr])
                    if k >= 1:
                        sch.op("pe", P.matmul, pr_[:, 0:256], lhsT=jm[:, 1, :], rhs=U[:, 32 - k, :], start=False, stop=True,
                               reads=[Bjm, BU], writes=[Bpr])
                    evac(k, urev[:, k, :], pr_[:, 0:256], [Bpr], [Brev])
                sch.op("dve", V.tensor_tensor, Ue[:], U[:, 0:16, :], urev[:], ALU.add, reads=[BU, Brev], writes=[BUe])
                sch.op("dve", V.tensor_tensor, Uo[:], U[:, 0:16, :], urev[:], ALU.subtract, reads=[BU, Brev], writes=[BUe])
                sch.op("dve", V.memset, Ux[:], 0.0, writes=[BUx])
                sch.op("dve", V.tensor_copy, Ux[0:1, :], U[0:1, 16, :], reads=[BU, BUx], writes=[BUx])
                crow = Rot([sbt(es, f"crow{i}", [128, 512], BF16) for i in range(2)])
                for (tl_, Bt_) in crow.t:
                    sch.op("dve", V.memset, tl_[:], 0.0, writes=[Bt_])
                for nb in range(4):
                    c_lo = 512 * nb + 1
                    cr_, Bcr = crow.next()
                    sch.dma("sp", cr_[0:1, :], C4[2048:2049, c_lo:c_lo + 512], reads=[Bcr], writes=[Bcr])
                    for kg in range(4):
                        cs, Bcs = csr.next()
                        sch.dma("sp", cs[:, 0], C4[kg * 512:(kg + 1) * 512, c_lo:c_lo + 512].rearrange("(k p) n -> p k n", p=128), writes=[Bcs])
                        sch.dma("sp", cs[:, 1], S4[kg * 512:(kg + 1) * 512, c_lo:c_lo + 512].rearrange("(k p) n -> p k n", p=128), writes=[Bcs])
                        for kk in range(4):
                            k = kg * 4 + kk
                            for w in range(2):
                                src_u = Ue if w == 0 else Uo
                                for cb in range(2):
                                    i = w * 2 + cb
                                    sch.op("pe", P.matmul, pacc[i][:], lhsT=src_u[:, k, cb * 128:(cb + 1) * 128], rhs=cs[:, w, kk, :],
                                           start=(k == 0), stop=(w == 1 and k == 15), reads=[BUe, Bcs], writes=[Bacc[i]])
                    for cb in range(2):
                        sch.op("pe", P.matmul, pacc[cb][:], lhsT=Ux[:, cb * 128:(cb + 1) * 128], rhs=cr_[:], start=False, stop=True,
                               reads=[BUx, Bcr], writes=[Bacc[cb]])
                    pq, Bpq = pqr.next()
                    for i in range(4):
                        evac(i, pq[:, i, :], pacc[i][:], [Bacc[i]], [Bpq])
                    mix(pq, Bpq, 512, c_lo, mirror_col0=3584 - 512 * nb)
                    if l == 0:
                        pump(5, cbgs, drain=True)
                if l == 0:
                    c2 = sbt(es, "c2", [128, 2, 2, 256], BF16)
                    Bc2 = Buf()
                    sch.dma("sp", c2[:, 0], C2.rearrange("(k p) n -> p k n", p=128), writes=[Bc2])
                    sch.dma("sp", c2[:, 1], S2.rearrange("(k p) n -> p k n", p=128), writes=[Bc2])
                    for w in range(2):
                        for cb in range(2):
                            i = w * 2 + cb
                            for k in range(2):
                                sch.op("pe", P.matmul, pacc[i][:, 0:256], lhsT=U[:, 32 + k, cb * 128:(cb + 1) * 128], rhs=c2[:, w, k, :],
                                       start=(k == 0), stop=(k == 1), reads=[BU, Bc2], writes=[Bacc[i]])
                    pq, Bpq = pqr.next()
                    for i in range(4):
                        evac(i, pq[:, i, 0:256], pacc[i][:, 0:256], [Bacc[i]], [Bpq])
                    mix(pq, Bpq, 256, S)
                sch.barrier()

        def phase_d(l):
            HW = 9 * 128
            NCH = T // 32
            PIECES = [(0, 9), (9, 8), (17, 9), (26, 8)]
            with ExitStack() as es:
                lbt = sbt(es, "lbt", [128, 8])
                lbv = sbt(es, "lbv", [128, 8])
                ngt = sbt(es, "ngt", [128, DEPTH])
                mdir = sbt(es, "mdir", [128, 2, 128])
                bd64 = sbt(es, "bd64", [128, 128])
                cm3 = sbt(es, "cm3", [128, 4, 128])
                epsc = sbt(es, "epsc", [128, 1])
                Bl, Bmd = Buf(), Buf()
                sch.op("pool", G.memset, epsc[:], EPS, writes=[Bmd])
                sch.dma("sp", lbt[:], lbT, writes=[Bl])
                sch.dma("sp", ngt[:], ngT, writes=[Bl])
                sch.dma("sp", mdir[:, 0, :], mfwd_d, writes=[Bmd])
                sch.dma("sp", mdir[:, 1, :], mbwd_d, writes=[Bmd])
                sch.dma("sp", bd64[:], bd64_d, writes=[Bmd])
                sch.dma("sp", cm3[:], cm3_d.rearrange("p (c n) -> p c n", n=128), writes=[Bmd])
                if l == 0:
                    sch.op("pool", G.memset, lbv[:, 0:4], 0.0, reads=[Bl], writes=[Bl])
                    sch.op("pool", G.memset, lbv[:, 4:8], 1.0, reads=[Bl], writes=[Bl])
                else:
                    ex = sbt(es, "lbex", [128, 8])
                    sch.op("act", A.activation, ex[:], lbt[:], AF.Exp, reads=[Bl], writes=[Bl])
                    sch.op("dve", V.tensor_tensor, lbv[:, 4:8], ex[:, 0:4], ex[:, 4:8], ALU.add, reads=[Bl], writes=[Bl])
                    sch.op("dve", V.reciprocal, lbv[:, 4:8], lbv[:, 4:8], reads=[Bl], writes=[Bl])
                    sch.op("dve", V.tensor_tensor, lbv[:, 0:4], ex[:, 4:8], lbv[:, 4:8], ALU.mult, reads=[Bl], writes=[Bl])
                    sch.op("dve", V.tensor_scalar, lbv[:, 4:8], lbv[:, 0:4], -1.0, 1.0, ALU.mult, ALU.add, reads=[Bl], writes=[Bl])
                for hp in range(2):
                    with ExitStack() as hs:
                        vtm = sbt(hs, "hv", [128, NB, 128], BF16)
                        vm = sbt(hs, "hvm", [128, 2, NB, 128], BF16)
                        Bv, Bvm = Buf(), Buf()
                        sch.dma("sp", vtm[:], tm[:, 256 + hp * 128:256 + (hp + 1) * 128].rearrange("(k p) c -> p k c", p=128), writes=[Bv])
                        sch.op("dve", V.memset, vm[:], 0.0, writes=[Bvm])
                        sch.op("dve", V.tensor_copy, vm[:, 0, :, 0:64], vtm[:, :, 0:64], reads=[Bv, Bvm], writes=[Bvm])
                        sch.op("dve", V.tensor_copy, vm[:, 1, :, 64:128], vtm[:, :, 64:128], reads=[Bv, Bvm], writes=[Bvm])
                        chains = []
                        for d in range(2):
                            ch = dict(d=d)
                            ch["qdm"] = sbt(hs, f"qdm{d}", [128, 2, T], BF16)
                            ch["kd"] = sbt(hs, f"kd{d}", [128, T], BF16)
                            ch["kst"] = sbt(hs, f"kst{d}", [128, NB, 128], BF16)
                            ch["dec"] = sbt(hs, f"dec{d}", [128, NCH])
                            ch["B"] = {k: Buf() for k in ("qdm", "kd", "kst", "dec", "o")}
                            ch["Bz"] = Buf()
                            sch.op("dve" if d == 0 else "act", (V.memset if d == 0 else A.memzero), ch["qdm"][:], *((0.0,) if d == 0 else ()),
                                   writes=[ch["Bz"]])
                            chains.append(ch)
                        with ExitStack() as gs:
                            Asets = [([sbt(gs, f"gA{u}{i}", [128, HW]) for i in range(4)], [Buf() for _ in range(4)],
                                      sbt(gs, f"gq{u}", [128, HW], BF16), Buf(), sbt(gs, f"gks{u}", [128, HW], BF16), Buf(),
                                      sbt(gs, f"gtot{u}", [128, HW // 32]), Buf()) for u in range(4)]
                            unit = 0
                            msk = sbt(gs, "gmsk", [128, HW])
                            Bmsk = Buf()
                            sch.op("pool", G.memset, msk[:], 1.0, writes=[Bmsk])
                            sch.op("pool", G.memset, msk[:].rearrange("p (n c) -> p n c", c=32)[:, :, 0:1], 0.0, reads=[Bmsk], writes=[Bmsk])
                            ptk = Rot([pst(gs, f"ptk{i}", [128, 4, 128], BF16) for i in range(2)])
                            ei_ = [0]

                            def unit_gen(ch, h, uset):
                                d = ch["d"]
                                B = ch["B"]
                                lbc = lbv[:, d * 2 + hp:d * 2 + hp + 1]
                                omc = lbv[:, 4 + d * 2 + hp:4 + d * 2 + hp + 1]
                                Ar_, BA, qh_, Bq, ksh_, Bks, tot_, Btot = uset
                                b0, nbh = PIECES[h]
                                W_ = nbh * 128
                                c0 = b0 * 128
                                cs = slice(c0, c0 + W_)
                                A1, A2, A3, A4 = [a_[:, 0:W_] for a_ in Ar_]
                                qh = qh_[:, 0:W_]
                                ksh = ksh_[:, 0:W_]
                                tot = tot_[:, 0:W_ // 32]
                                mskv = msk[:, 0:W_]
                                B1, B2, B3, B4 = BA
                                zr = 256 * d + hp * 128
                                sch.dma("sp", A1, fz[zr:zr + 128, cs], writes=[B1])
                                sch.dma("sp", qh, fb[hp * 128:(hp + 1) * 128, cs], writes=[Bq])
                                sch.op("act", A.activation, A1, A1, AF.Sigmoid, reads=[B1], writes=[B1])
                                yield
                                sch.op("act", A.activation, A1, A1, AF.Identity, bias=lbc, scale=omc, reads=[B1, Bl], writes=[B1])
                                yield
                                sch.op("act", A.activation, A2, A1, AF.Ln, reads=[B1], writes=[B2])
                                yield
                                sch.op("act", A.activation, A1, A1, AF.Identity, bias=1.0, scale=-1.0, reads=[B1, B2], writes=[B1])
                                sch.op("dve", V.tensor_tensor_scan, A3, mskv, A2, 0.0, ALU.mult, ALU.add, reads=[Bmsk, B2], writes=[B3])
                                yield
                                A3v = A3.rearrange("p (n c) -> p n c", c=32)
                                A2v = A2.rearrange("p (n c) -> p n c", c=32)
                                totb = tot.rearrange("p (n o) -> p n o", o=1).broadcast_to([128, W_ // 32, 32])
                                sch.op("act", A.copy, tot, A3v[:, :, 31], reads=[B3], writes=[Btot])
                                yield
                                if d == 0:
                                    sch.op("dve", V.tensor_tensor, A2v, A3v, totb, ALU.subtract, reads=[Btot, B3, B2], writes=[B2])
                                    asg, rsg = 1.0, -1.0
                                else:
                                    sch.op("dve", V.tensor_tensor, A2, A3, A2, ALU.subtract, reads=[B3, B2], writes=[B2])
                                    yield
                                    sch.op("dve", V.tensor_tensor, A3v, A2v, totb, ALU.subtract, reads=[Btot, B2, B3], writes=[B3])
                                    asg, rsg = -1.0, 1.0
                                yield
                                sch.op("act", A.activation, ch["dec"][:, c0 // 32:(c0 + W_) // 32], tot, AF.Exp, reads=[Btot], writes=[B["dec"]])
                                sch.op("act", A.activation, A4, A3, AF.Exp, scale=asg, reads=[B3, B4], writes=[B4])
                                yield
                                sch.op("dve", V.tensor_tensor, ch["qdm"][0:64, 0, cs], qh[0:64], A4[0:64], ALU.mult,
                                       reads=[Bq, B4, ch["Bz"]], writes=[B["qdm"]])
                                sch.op("dve", V.tensor_tensor, ch["qdm"][64:128, 1, cs], qh[64:128], A4[64:128], ALU.mult,
                                       reads=[Bq, B4, ch["Bz"]], writes=[B["qdm"]])
                                yield
                                sch.op("act", A.activation, A4, A3, AF.Exp, scale=-asg, reads=[B3, B4], writes=[B4])
                                yield
                                sch.op("dve", V.tensor_tensor, ch["kd"][:, cs], A1, A4, ALU.mult, reads=[B1, B4], writes=[B["kd"]])
                                yield
                                sch.op("act", A.activation, A4, A2, AF.Exp, scale=rsg, reads=[B2, B4], writes=[B4])
                                yield
                                sch.op("dve", V.tensor_tensor, ksh, A1, A4, ALU.mult, reads=[B1, B4], writes=[Bks])
                                yield
                                for b4 in range(0, nbh, 4):
                                    nn = min(4, nbh - b4)
                                    pt, Bp = ptk.next()
                                    for j in range(nn):
                                        sch.op("pe", P.transpose, pt[:, j, :], ksh[:, (b4 + j) * 128:(b4 + j + 1) * 128], ident[:],
                                               reads=[Bks, B_const], writes=[Bp])
                                    ei_[0] += 1
                                    evac(ei_[0], ch["kst"][:, b0 + b4:b0 + b4 + nn, :], pt[:, 0:nn, :], [Bp], [B["kst"]])
                                    yield

                            units = [(ch, h) for ch in chains for h in range(4)]
                            gens = []
                            SHIFT = 4
                            active = []
                            ui = 0
                            tick = 0
                            while ui < len(units) or active:
                                if ui < len(units) and len(active) < 4 and (not active or active[-1][1] >= SHIFT):
                                    ch_, h_ = units[ui]
                                    active.append([unit_gen(ch_, h_, Asets[ui % 4]), 0])
                                    ui += 1
                                for a_ in list(active):
                                    try:
                                        next(a_[0])
                                        a_[1] += 1
                                    except StopIteration:
                                        active.remove(a_)
                            sch.barrier()
                        for ch in chains:
                            ch["o"] = sbt(hs, f"o{ch['d']}", [128, T])
                        with ExitStack() as cs_:
                            vexa = sbt(cs_, "vexa", [128, NB, 4, 128], BF16)
                            Bvex = Buf()
                            for nb in range(NB):
                                sch.op("dve", V.tensor_tensor, vexa[:, nb], cm3[:], vtm[:, nb:nb + 1, :].broadcast_to([128, 4, 128]), ALU.mult,
                                       reads=[Bmd, Bv], writes=[Bvex])
                            for ch in chains:
                                d = ch["d"]
                                ch["psU"] = pst(cs_, f"psU{d}", [128, 4, 128]); ch["BpsU"] = Buf()
                                ch["psS"] = [pst(cs_, f"psS{d}{i}", [128, 128]) for i in range(2)]; ch["BpsS"] = [Buf(), Buf()]
                                ch["psO"] = pst(cs_, f"psO{d}", [128, 128]); ch["BpsO"] = Buf()
                                ch["Sf"] = Rot([sbt(cs_, f"Sf{d}{i}", [128, 128]) for i in range(2)])
                                ch["Spad"] = Rot([sbt(cs_, f"Sp{d}{i}", [128, 2, 128], BF16) for i in range(10)])
                                for (tl, Bt_) in ch["Spad"].t:
                                    sch.op("dve", V.memset, tl[:], 0.0, writes=[Bt_])
                                ch["scb"] = Rot([sbt(cs_, f"scb{d}{i}", [128, 2, 128], BF16) for i in range(2)])
                                ch["order"] = ([32, 33] + list(range(32))) if d == 0 else ([33, 32] + list(range(31, -1, -1)))
                                ch["corder"] = [0, 1, 2, 3] if d == 0 else [3, 2, 1, 0]
                                ch["prev"] = None
                                ch["prev_bd"] = None
                                ch["pending"] = None

                            def front_all(step):
                                res = []
                                for ch in chains:
                                    nb = ch["order"][step]
                                    B = ch["B"]
                                    sch.op("pe", P.matmul, ch["psU"][:].rearrange("p c n -> p (c n)"), lhsT=ch["kst"][:, nb, :],
                                           rhs=vexa[:, nb].rearrange("p c n -> p (c n)"), start=True, stop=True,
                                           reads=[B["kst"], Bvex], writes=[ch["BpsU"]])
                                    ch["before"] = []
                                for ci in range(4):
                                    for k_, ch in enumerate(chains):
                                        nb = ch["order"][step]
                                        B = ch["B"]
                                        c = ch["corder"][ci]
                                        n = nb * 4 + c
                                        ch["before"].append((c, ch["prev_bd"]))
                                        sf, Bsf = ch["Sf"].next()
                                        if ch["prev"] is None:
                                            sch.op("dve", V.tensor_copy, sf[:], ch["psU"][:, c, :], reads=[ch["BpsU"]], writes=[Bsf])
                                        else:
                                            pv, Bpv = ch["prev"]
                                            sch.op("dve", V.scalar_tensor_tensor, sf[:], pv[:], ch["dec"][:, n:n + 1], ch["psU"][:, c, :],
                                                   ALU.mult, ALU.add, reads=[Bpv, B["dec"], ch["BpsU"]], writes=[Bsf])
                                        ch["prev"] = (sf, Bsf)
                                        sp_, Bsp = ch["Spad"].next()
                                        base = sp_[:]
                                        ov = bass.AP(tensor=base.tensor, offset=base.offset, ap=[[base.ap[0][0], 128], [192, 2], [1, 64]])
                                        iv = sf[:].rearrange("p (g c) -> p g c", g=2)
                                        sch.op("act", A.copy, ov, iv, reads=[Bsf, Bsp], writes=[Bsp])
                                        ch["prev_bd"] = (sp_, Bsp)
                                for ch in chains:
                                    nb = ch["order"][step]
                                    B = ch["B"]
                                    cols = slice(nb * 128, (nb + 1) * 128)
                                    scb, Bscb = ch["scb"].next()
                                    for hh in range(2):
                                        sch.op("pe", P.matmul, ch["psS"][hh][:], lhsT=ch["kd"][:, cols], rhs=ch["qdm"][:, hh, cols],
                                               start=True, stop=True, reads=[B["kd"], B["qdm"]], writes=[ch["BpsS"][hh]])
                                        sch.op("dve", V.tensor_tensor, scb[:, hh, :], ch["psS"][hh][:], mdir[:, ch["d"], :], ALU.mult,
                                               reads=[ch["BpsS"][hh], Bmd], writes=[Bscb])
                                    res.append((nb, ch["before"], scb, Bscb))
                                return res

                            def back(ch, pend):
                                nb, before, scb, Bscb = pend
                                B = ch["B"]
                                cols = slice(nb * 128, (nb + 1) * 128)
                                inter = [(c, s_) for (c, s_) in before if s_ is not None]
                                for hh in range(2):
                                    sch.op("pe", P.matmul, ch["psO"][:], lhsT=vm[:, hh, nb, :], rhs=scb[:, hh, :], start=(hh == 0),
                                           stop=(hh == 1 and not inter), reads=[Bvm, Bscb], writes=[ch["BpsO"]])
                                for i, (c, (sp_, Bsp)) in enumerate(inter):
                                    for hh in range(2):
                                        sch.op("pe", P.matmul, ch["psO"][:, 32 * c:32 * c + 32], lhsT=sp_[:, hh, :],
                                               rhs=ch["qdm"][:, hh, nb * 128 + 32 * c:nb * 128 + 32 * c + 32], start=False,
                                               stop=(i == len(inter) - 1 and hh == 1), reads=[Bsp, B["qdm"]], writes=[ch["BpsO"]])
                                sch.op("act", A.copy, ch["o"][:, cols], ch["psO"][:], reads=[ch["BpsO"]], writes=[B["o"]])

                            for step in range(NB + 1):
                                if step == 1:
                                    if l == 0 and hp == 0:
                                        cast_weights(0, ["out", "gate", "up"])
                                    elif l == 0 and hp == 1:
                                        cast_weights(0, ["down"])
                                        cast_weights(1, ["in", "out"])
                                    elif l == 1 and hp == 0:
                                        cast_weights(1, ["gate", "up"])
                                    elif l == 1 and hp == 1:
                                        cast_weights(1, ["down"])
                                newp = None
                                if step < NB:
                                    newp = front_all(step)
                                if step >= 1:
                                    for ch in chains:
                                        back(ch, ch["pending"])
                                if step < NB:
                                    for ch, p_ in zip(chains, newp):
                                        ch["pending"] = p_
                            sch.barrier()
                        with ExitStack() as rs:
                            of, ob = chains[0]["o"], chains[1]["o"]
                            Bof, Bob = chains[0]["B"]["o"], chains[1]["B"]["o"]
                            sch.op("dve", V.tensor_tensor, of[:], of[:], ob[:], ALU.add, reads=[Bof, Bob], writes=[Bof])
                            gt = sbt(rs, "hg", [128, T], BF16)
                            Bg = Buf()
                            sch.dma("sp", gt[:], fb[256 + hp * 128:256 + (hp + 1) * 128, :], writes=[Bg])
                            sqa = sbt(rs, "hsqa", [128, T])
                            sga = sbt(rs, "hsga", [128, T])
                            Bsq, Bsg = Buf(), Buf()
                            rr = Rot([sbt(rs, f"hr{i}", [128, 512]) for i in range(2)])
                            orr = Rot([sbt(rs, f"hob{i}", [128, 512], BF16) for i in range(2)])
                            pms = Rot([pst(rs, f"pms{i}", [128, 512]) for i in range(2)])
                            for t0 in range(0, T, 1088):
                                sch.op("act", A.activation, sga[:, t0:t0 + 1088], gt[:, t0:t0 + 1088], AF.Silu, reads=[Bg], writes=[Bsg])
                                sch.op("dve", V.tensor_tensor, sqa[:, t0:t0 + 1088], of[:, t0:t0 + 1088], of[:, t0:t0 + 1088], ALU.mult,
                                       reads=[Bof], writes=[Bsq])
                            for t0 in range(0, T, 512):
                                n = min(512, T - t0)
                                cs = slice(t0, t0 + n)
                                pm_, Bpm = pms.next()
                                sch.op("pe", P.matmul, pm_[:, 0:n], lhsT=bd64[:], rhs=sqa[:, cs], start=True, stop=True, pemode="f32",
                                       reads=[Bmd, Bsq], writes=[Bpm])
                                r_, Br = rr.next()
                                sch.op("act", A.activation, r_[:, 0:n], pm_[:, 0:n], AF.Ln, bias=epsc[:, 0:1], scale=1.0, reads=[Bpm, Bmd], writes=[Br])
                                sch.op("act", A.activation, r_[:, 0:n], r_[:, 0:n], AF.Exp, scale=-0.5, reads=[Br], writes=[Br])
                                sch.op("dve", V.tensor_tensor, r_[:, 0:n], r_[:, 0:n], of[:, cs], ALU.mult, reads=[Br, Bof], writes=[Br])
                                ob_, Bo = orr.next()
                                sch.op("dve", V.scalar_tensor_tensor, ob_[:, 0:n], r_[:, 0:n], ngt[:, l:l + 1], sga[:, cs], ALU.mult, ALU.mult,
                                       reads=[Br, Bl, Bsg], writes=[Bo])
                                sch.dma("pool", catT[256 + hp * 128:256 + (hp + 1) * 128, cs], ob_[:, 0:n], reads=[Bo])
                            sch.barrier()

        def phase_e(l):
            with ExitStack() as es:
                sets = []
                for u in range(2):
                    qm = sbt(es, f"naq{u}", [128, 2, T], BF16)
                    kT = sbt(es, f"nak{u}", [128, T], BF16)
                    vm = sbt(es, f"nav{u}", [128, 2, NB, 128], BF16)
                    Bq, Bk, Bv = Buf(), Buf(), Buf()
                    sch.op("pool", G.memset, qm[:], 0.0, writes=[Bq])
                    sch.op("pool", G.memset, vm[:], 0.0, writes=[Bv])
                    sets.append((qm, kT, vm, Bq, Bk, Bv))
                bg_rng = ((8, 14), (0, 8), (14, 20))
                bias = [[sbt(es, f"nab{hh}{g}", [128, bg_rng[g][1] - bg_rng[g][0], 512]) for g in range(3)] for hh in range(2)]
                Bb = [[Buf() for g in range(3)] for hh in range(2)]
                oT = sbt(es, "nao", [128, T], BF16)
                oneh = sbt(es, "oneh", [128, 2, 128], BF16)
                sbr = Rot([sbt(es, f"nas{i}", [128, 512]) for i in range(4)])
                pr = Rot([sbt(es, f"nap{i}", [128, 512], BF16) for i in range(6)])
                rcr = Rot([sbt(es, f"narc{i}", [128, 512]) for i in range(2)])
                psr = Rot([pst(es, f"naS{i}", [128, 512]) for i in range(4)])
                numr = Rot([pst(es, f"naN{i}", [128, 512]) for i in range(2)])
                denr = Rot([pst(es, f"naD{i}", [128, 512]) for i in range(2)])
                Bo, B1 = Buf(), Buf()
                sch.op("pool", G.memset, oneh[:], 0.0, writes=[B1])
                for hh in range(2):
                    sch.op("pool", G.memset, oneh[:, hh, hh * 64:(hh + 1) * 64], 1.0, reads=[B1], writes=[B1])
                LOOK = 3
                ebgs = make_bg(es, 4)
                if l == 0:
                    pump(40, ebgs, drain=True)

                def load_qkv(hp):
                    qm, kT, vm, Bq, Bk, Bv = sets[hp % 2]
                    for hh in range(2):
                        r0 = 512 + hp * 128 + hh * 64
                        sch.dma("sp", qm[hh * 64:(hh + 1) * 64, hh, :], fb[r0:r0 + 64, :], reads=[Bq], writes=[Bq])
                    sch.dma("sp", kT[:], fb[1024 + hp * 128:1024 + (hp + 1) * 128, :], writes=[Bk])
                    for hh in range(2):
                        c0 = 512 + hp * 128 + hh * 64
                        sch.dma("sp", vm[:, hh, :, hh * 64:(hh + 1) * 64], tm[:, c0:c0 + 64].rearrange("(k p) c -> p k c", p=128),
                                reads=[Bv], writes=[Bv])

                def load_bias(hp, g):
                    t0_, t1_ = bg_rng[g]
                    for hh in range(2):
                        sch.dma("sp", bias[hh][g][:], nabias[l, hp * 2 + hh].rearrange("p (t n) -> p t n", n=512)[:, t0_:t1_, :],
                                writes=[Bb[hh][g]])

                load_qkv(0)
                for g in range(3):
                    load_bias(0, g)
                for hp in range(4):
                    qm, kT, vm, Bq, Bk, Bv = sets[hp % 2]
                    if hp + 1 < 4:
                        load_qkv(hp + 1)
                    items = []
                    for qb in range(8):
                        if qb == 0:
                            kts = [(2 * p * 64, (0, p)) for p in range(6)]
                        elif qb == 7:
                            kts = [((52 + 2 * p) * 64, (2, p)) for p in range(6)]
                        else:
                            kts = [((8 * qb - 4 + 2 * po) * 64, (1, po)) for po in range(8)]
                        kts += [(S, None), (S + 128, None)]
                        items.append((qb * 512, 512, kts, qb))
                    if l == 0:
                        items.append((S, L, [(S, None), (S + 128, None)], 8))
                    for (q0, nq, kts, qb) in items:
                        flat = [(hh, kt0, bi) for hh in range(2) for (kt0, bi) in kts]
                        prod = [None] * len(flat)
                        num, Bn = numr.next()
                        den, Bd = denr.next()
                        for i in range(len(flat) + LOOK):
                            if i < len(flat):
                                hh, kt0, bi = flat[i]
                                ps, Bps = psr.next()
                                sch.op("pe", P.matmul, ps[:, 0:nq], lhsT=kT[:, kt0:kt0 + 128], rhs=qm[:, hh, q0:q0 + nq], start=True, stop=True,
                                       reads=[Bk, Bq], writes=[Bps])
                                pt, Bpt = pr.next()
                                if bi is None:
                                    sch.op("act", A.activation, pt[:, 0:nq], ps[:, 0:nq], AF.Exp, reads=[Bps], writes=[Bpt])
                                else:
                                    st, Bst = sbr.next()
                                    sch.op("dve", V.tensor_tensor, st[:, 0:nq], ps[:, 0:nq], bias[hh][bi[0]][:, bi[1], 0:nq], ALU.add,
                                           reads=[Bps, Bb[hh][bi[0]]], writes=[Bst])
                                    sch.op("act", A.activation, pt[:, 0:nq], st[:, 0:nq], AF.Exp, reads=[Bst], writes=[Bpt])
                                prod[i] = (pt, Bpt)
                            j = i - LOOK
                            if j >= 0:
                                hh, kt0, bi = flat[j]
                                pt, Bpt = prod[j]
                                first = (j == 0)
                                last = (j == len(flat) - 1)
                                sch.op("pe", P.matmul, num[:, 0:nq], lhsT=vm[:, hh, kt0 // 128, :], rhs=pt[:, 0:nq],
                                       start=first, stop=last, reads=[Bv, Bpt], writes=[Bn])
                                sch.op("pe", P.matmul, den[:, 0:nq], lhsT=oneh[:, hh, :], rhs=pt[:, 0:nq],
                                       start=first, stop=last, reads=[B1, Bpt], writes=[Bd])
                        rc, Brc = rcr.next()
                        sch.op("act", A.activation, rc[:, 0:nq], den[:, 0:nq], AF.Ln, reads=[Bd], writes=[Brc])
                        sch.op("act", A.activation, rc[:, 0:nq], rc[:, 0:nq], AF.Exp, scale=-1.0, reads=[Brc], writes=[Brc])
                        sch.op("dve", V.tensor_tensor, oT[:, q0:q0 + nq], num[:, 0:nq], rc[:, 0:nq], ALU.mult, reads=[Bn, Brc], writes=[Bo])
                        if hp + 1 < 4:
                            if qb == 0:
                                load_bias(hp + 1, 0)
                            elif qb == 6:
                                load_bias(hp + 1, 1)
                            elif qb == 7:
                                load_bias(hp + 1, 2)
                    ncol = T if l == 0 else S
                    sch.dma("pool", catT[512 + hp * 128:512 + (hp + 1) * 128, 0:ncol], oT[:, 0:ncol], reads=[Bo])
                    if l == 0 and hp < 3:
                        pump(40, ebgs, drain=True)
                sch.barrier()

        def phase_f1(l):
            xsrc = xin if l == 0 else xres
            with ExitStack() as es:
                wob, Bw = load_weight(es, "woutbf", w_out_b[l].rearrange("(k p) n -> p k n", p=128), 8, D, wcast_buf[("out", l)])
                g1 = [sbt(es, f"g1_{v}", [128, D]) for v in range(2)]
                Bg = Buf()
                for v in range(2):
                    sch.dma("pool", g1[v][:], mod2d[2 * l + v, 2 * D:3 * D].partition_broadcast(128), writes=[Bg])
                fbgs = make_bg(es, 4)
                cr = Rot([sbt(es, f"cat{i}", [128, 8, 512], BF16) for i in range(2)])
                xr = Rot([sbt(es, f"xt{i}", [128, D]) for i in range(3)])
                yr = Rot([sbt(es, f"yt{i}", [128, D]) for i in range(2)])
                pyr = Rot([pst(es, f"py{i}", [128, 512]) for i in range(4)])
                sbs = [(i * 512, 512, 0) for i in range(8)] + ([(S, L, 1)] if l == 0 else [])
                for (t0, ntok, v) in sbs:
                    ct, Bc = cr.next()
                    sch.dma("sp", ct[:, :, 0:ntok], catT[:, t0:t0 + ntok].rearrange("(k p) t -> p k t", p=128), writes=[Bc])
                    for tb in range(ntok // 128):
                        xt, Bx = xr.next()
                        rows = slice(t0 + tb * 128, t0 + (tb + 1) * 128)
                        sch.dma("sp", xt[:], xsrc[rows, :], writes=[Bx])
                        yt, By = yr.next()
                        for nb in range(2):
                            py, Bp = pyr.next()
                            for k in range(8):
                                sch.op("pe", P.matmul, py[:], lhsT=ct[:, k, tb * 128:(tb + 1) * 128], rhs=wob[:, k, nb * 512:(nb + 1) * 512],
                                       start=(k == 0), stop=(k == 7), reads=[Bc, Bw], writes=[Bp])
                            sch.op("dve", V.tensor_tensor, yt[:, nb * 512:(nb + 1) * 512], py[:], g1[v][:, nb * 512:(nb + 1) * 512], ALU.mult,
                                   reads=[Bp, Bg], writes=[By])
                        sch.op("dve", V.tensor_tensor, yt[:], yt[:], xt[:], ALU.add, reads=[By, Bx], writes=[By])
                        sch.dma("act", xres[rows, :], yt[:], reads=[By])
                        if l == 0:
                            pump(2, fbgs)
                pump(0, fbgs, drain=True)
                sch.barrier()

        def phase_f2(l):
            last = (l == DEPTH - 1)
            with ExitStack() as es:
                wg = sbt(es, "wgbf", [128, 8, DFF], BF16)
                wu = sbt(es, "wubf", [128, 8, DFF], BF16)
                NWC = 4
                WCW = DFF // NWC
                Bwg_c = [Buf() for _ in range(NWC)]
                Bwu_c = [Buf() for _ in range(NWC)]
                for c_ in range(NWC):
                    csl = slice(c_ * WCW, (c_ + 1) * WCW)
                    sch.dma("sp", wg[:, :, csl], w_gate_b[l][:, csl].rearrange("(k p) n -> p k n", p=128),
                            reads=[wcast_buf[("gate", l)]], writes=[Bwg_c[c_]])
                    sch.dma("sp", wu[:, :, csl], w_up_b[l][:, csl].rearrange("(k p) n -> p k n", p=128),
                            reads=[wcast_buf[("up", l)]], writes=[Bwu_c[c_]])
                wd, Bwd = load_weight(es, "wdbf", w_down_b[l].rearrange("(j p) n -> p j n", p=128), NJ, D, wcast_buf[("down", l)])
                junk = sbt(es, "junk", [128, D], BF16)
                nt = (junk, Buf(), Rot([sbt(es, f"ssq{i}", [128, 4]) for i in range(4)]),
                      Rot([sbt(es, f"ntmp{i}", [128, D]) for i in range(1)]))
                Gt = sbt(es, "G2", [128, D])
                SHt = sbt(es, "SH2", [128, D])
                g2 = sbt(es, "g2", [128, D])
                Bm = Buf()

                def load_mods(v):
                    sct, Bsc = nt[3].next()
                    sch.dma("pool", SHt[:], mod2d[2 * l + v, 3 * D:4 * D].partition_broadcast(128), writes=[Bm])
                    sch.dma("pool", Gt[:], norm2_g[l].partition_broadcast(128), writes=[Bm])
                    sch.dma("pool", g2[:], mod2d[2 * l + v, 5 * D:6 * D].partition_broadcast(128), writes=[Bm])
                    sch.dma("pool", sct[:], mod2d[2 * l + v, 4 * D:5 * D].partition_broadcast(128), writes=[Bsc])
                    sch.op("dve", V.scalar_tensor_tensor, Gt[:], sct[:], 1.0, Gt[:], ALU.add, ALU.mult, reads=[Bsc, Bm], writes=[Bm])
                fg = None
                if last:
                    fg = sbt(es, "fg", [128, D])
                    Bfg = Buf()
                    sch.dma("pool", fg[:], final_g.partition_broadcast(128), writes=[Bfg])
                x1 = [(sbt(es, f"x1_{i}", [128, D]), Buf()) for i in range(4)]
                ar = Rot([sbt(es, f"abf{i}", [128, D], BF16) for i in range(1)])
                aT = sbt(es, "aT", [128, 8, 512], BF16)
                BaT = Buf()
                hT = sbt(es, "hT", [128, NJ, 512], BF16)
                BhT = Buf()
                sgr = Rot([sbt(es, f"sg{i}", [128, 512]) for i in range(1)])
                ptrr = Rot([pst(es, f"ptr{i}", [128, 4, 128], BF16) for i in range(1)])
                pgr = Rot([pst(es, f"pg{i}", [128, 512]) for i in range(2)])
                pur = Rot([pst(es, f"pu{i}", [128, 512]) for i in range(2)])
                pdr = Rot([pst(es, f"pd{i}", [128, 512]) for i in range(3)])
                sbs = [(i * 512, 512, 0) for i in range(8)] + ([(S, L, 1)] if l == 0 else [])
                curv = None
                for (t0, ntok, v) in sbs:
                    if v != curv:
                        load_mods(v)
                        curv = v
                    ntb = ntok // 128
                    for tb in range(ntb):
                        xt, Bx = x1[tb]
                        sch.dma("sp", xt[:], xres[t0 + tb * 128:t0 + (tb + 1) * 128, :], writes=[Bx])
                        a_bf, Ba = ar.next()
                        norm_block(nt, xt, Bx, Gt, SHt, Bm, a_bf, Ba)
                        transpose_block(ptrr, a_bf, Ba, aT, BaT, tb, tb)
                    for j in range(NJ):
                        pg, Bpg = pgr.next()
                        pu, Bpu = pur.next()
                        for k in range(8):
                            sch.op("pe", P.matmul, pg[:, 0:ntok], lhsT=wg[:, k, j * 128:(j + 1) * 128], rhs=aT[:, k, 0:ntok],
                                   start=(k == 0), stop=(k == 7),
                                   reads=[Bwg_c[(j * 128) // WCW], Bwg_c[min(NWC - 1, (j * 128 + 127) // WCW)], BaT], writes=[Bpg])
                        for k in range(8):
                            sch.op("pe", P.matmul, pu[:, 0:ntok], lhsT=wu[:, k, j * 128:(j + 1) * 128], rhs=aT[:, k, 0:ntok],
                                   start=(k == 0), stop=(k == 7),
                                   reads=[Bwu_c[(j * 128) // WCW], Bwu_c[min(NWC - 1, (j * 128 + 127) // WCW)], BaT], writes=[Bpu])
                        sg, Bsg = sgr.next()
                        sch.op("act", A.activation, sg[:, 0:ntok], pg[:, 0:ntok], AF.Silu, reads=[Bpg], writes=[Bsg])
                        sch.op("dve", V.tensor_tensor, hT[:, j, 0:ntok], sg[:, 0:ntok], pu[:, 0:ntok], ALU.mult, reads=[Bsg, Bpu], writes=[BhT])
                    for tb in range(ntb):
                        xt, Bx = x1[tb]
                        rows = slice(t0 + tb * 128, t0 + (tb + 1) * 128)
                        tp, Bt = nt[3].next()
                        for nb in range(2):
                            pd, Bpd = pdr.next()
                            for j in range(NJ):
                                sch.op("pe", P.matmul, pd[:], lhsT=hT[:, j, tb * 128:(tb + 1) * 128], rhs=wd[:, j, nb * 512:(nb + 1) * 512],
                                       start=(j == 0), stop=(j == NJ - 1), reads=[BhT, Bwd], writes=[Bpd])
                            sch.op("dve", V.tensor_tensor, tp[:, nb * 512:(nb + 1) * 512], pd[:], g2[:, nb * 512:(nb + 1) * 512], ALU.mult,
                                   reads=[Bpd, Bm], writes=[Bt])
                        sch.op("dve", V.tensor_tensor, xt[:], xt[:], tp[:], ALU.add, reads=[Bx, Bt], writes=[Bx])
                        if not last:
                            sch.dma("pool", xres[rows, :], xt[:], reads=[Bx])
                        else:
                            sq, Bs = nt[2].next()
                            sch.op("act", A.activation, junk[:], xt[:], AF.Square, accum_out=sq[:, 0:1], reads=[Bx], writes=[nt[1], Bs])
                            sch.op("act", A.activation, sq[:, 1:2], sq[:, 0:1], AF.Sqrt, bias=EPS, scale=1.0 / D, reads=[Bs], writes=[Bs])
                            sch.op("dve", V.reciprocal, sq[:, 2:3], sq[:, 1:2], reads=[Bs], writes=[Bs])
                            sch.op("dve", V.scalar_tensor_tensor, xt[:], xt[:], sq[:, 2:3], fg[:], ALU.mult, ALU.mult,
                                   reads=[Bx, Bs, Bfg], writes=[Bx])
                            sch.dma("pool", out[rows, :], xt[:], reads=[Bx])
                sch.barrier()

        cast_weights(0, ["in"])
        phases = [("mod", phase_mod, None)]
        for l in range(DEPTH):
            phases += [(f"ab{l}", phase_ab, l), (f"c{l}", phase_c, l), (f"d{l}", phase_d, l), (f"e{l}", phase_e, l),
                       (f"f1{l}", phase_f1, l), (f"f2{l}", phase_f2, l)]
        for name, fn, arg in phases:
            if arg is None:
                fn()
            else:
                fn(arg)
            if stop is not None and name == stop:
                break
        sch.finish()
        nc._sched_stats = (sch.nops, sch.nwaits)
    return nc


def _prep(inputs):
    f32 = np.float32
    c = _consts()
    x = np.asarray(inputs["x"], f32)
    ctx = np.asarray(inputs["ctx"], f32)
    cvec = np.asarray(inputs["c"], f32)
    cctx = np.asarray(inputs["c_ctx"], f32)
    lb = np.asarray(inputs["hgrn_lb"], f32)
    lbT = np.ascontiguousarray(lb.reshape(2, 2, 2, 128).transpose(3, 0, 1, 2).reshape(128, 8))
    ng = np.asarray(inputs["hgrn_norm_g"], f32)
    ngT = np.ascontiguousarray(np.concatenate([ng, ng], axis=1).T)
    rpb = np.asarray(inputs["na_rpb"], f32)
    nabias = np.stack([_na_bias_tiles(rpb[l]) for l in range(DEPTH)])
    shared = {
        "w_mod": np.asarray(inputs["w_mod"], f32), "b_mod": np.asarray(inputs["b_mod"], f32),
        "norm1_g": np.asarray(inputs["norm1_g"], f32), "w_in": np.asarray(inputs["w_in"], f32),
        "fourier_w": np.asarray(inputs["fourier_w"], f32), "lbT": lbT, "ngT": ngT, "nabias": nabias,
        "w_out": np.asarray(inputs["w_out"], f32), "norm2_g": np.asarray(inputs["norm2_g"], f32),
        "w_gate": np.asarray(inputs["w_ffn_gate"], f32), "w_up": np.asarray(inputs["w_ffn_up"], f32),
        "w_down": np.asarray(inputs["w_ffn_down"], f32), "final_g": np.asarray(inputs["final_norm_g"], f32),
        "C4": c["C4"], "S4": c["S4"], "C2": c["C2"], "S2": c["S2"], "C64bd": c["C64bd"], "S64bdn": c["S64bdn"],
        "ident": c["ident"], "bdmask": c["bdmask"], "bd64": c["bd64"], "mfwd": c["mfwd"], "mbwd": c["mbwd"], "cm3": c["cm3"], "jperm": c["jperm"],
    }
    maps = []
    for b in range(8):
        m = dict(shared)
        m["xin"] = np.ascontiguousarray(np.concatenate([x[b], ctx[b]], axis=0))
        cv = np.stack([cvec[b], cctx], axis=0)
        m["cT"] = np.ascontiguousarray(cv.reshape(2, 8, 128).transpose(2, 1, 0).reshape(128, 16))
        maps.append(m)
    return maps


def kernel(**inputs):
    maps = _prep(inputs)
    nc = build()
    res = run_bass_kernel_spmd(nc, maps, core_ids=list(range(8)))
    return np.stack([np.asarray(r["out"], np.float32) for r in res.results], axis=0)
```
